# Optimizing a Trainium2 kernel written in Bass

```python
import jax, jax.numpy as jnp
from jax import lax
import numpy as np

D_MODEL = 1024
BATCH = 16
SEQ = 2048
DEPTH = 2

N_MIXERS = 2
EPS = 1e-6
LRU_WIDTH = 1280
LRU_BLOCKS = 10
LRU_BLOCK_W = LRU_WIDTH // LRU_BLOCKS
LRU_CONV = 4
LRU_C = 8.0
N_HEADS = 16
HEAD_DIM = 64
N_KV = 4
GROUP = N_HEADS // N_KV
N_BRANCH = 3
CMP_BLOCK = 32
CMP_STRIDE = 16
CMP_HIDDEN = 256
SEL_BLOCK = 64
N_SELECT = 8
WINDOW = 512
Q_BLOCK = 64
FORCED_BONUS = 1e4
NEG = -1e30
Q_COLS = N_HEADS * HEAD_DIM
KV_COLS = 2 * N_KV * HEAD_DIM
NSA_IN = Q_COLS + N_BRANCH * KV_COLS + N_BRANCH * N_HEADS
D_FF = 3072
FFN_CONV = 3

kernel_name = "hybrid_rglru_nsa_convffn"


def rmsnorm(x, g):
    xf = x.astype(jnp.float32)
    y = xf * lax.rsqrt(jnp.mean(xf * xf, axis=-1, keepdims=True) + EPS)
    return (y * g.astype(jnp.float32)).astype(x.dtype)


def causal_dwconv(x, w, b):
    K, C = w.shape
    y = lax.conv_general_dilated(x, w[:, None, :].astype(x.dtype), window_strides=(1,),
                                 padding=[(K - 1, 0)], dimension_numbers=('NWC', 'WIO', 'NWC'),
                                 feature_group_count=C)
    return y + b.astype(x.dtype)


def alibi_slopes():
    s = 2.0 ** (-8.0 * np.arange(1, N_HEADS + 1) / N_HEADS)
    return jnp.asarray(s, dtype=jnp.float32).reshape(N_KV, GROUP)


def masked_softmax(s, m):
    return jax.nn.softmax(jnp.where(m, s, NEG), axis=-1)


def rglru_mixer(x, w_in, conv_w, conv_b, gate_w, gate_b, a_param, w_out):
    B, S, _ = x.shape
    y_br, x_br = jnp.split(x @ w_in, 2, axis=-1)
    y_br = jax.nn.gelu(y_br)
    xc = causal_dwconv(x_br, conv_w, conv_b)
    xb = xc.reshape(B, S, LRU_BLOCKS, LRU_BLOCK_W)
    g = jnp.einsum('bsnc,gncd->gbsnd', xb, gate_w).reshape(2, B, S, LRU_WIDTH)
    g = jax.nn.sigmoid(g.astype(jnp.float32) + gate_b.astype(jnp.float32)[:, None, None, :])
    r_gate, i_gate = g[0], g[1]
    log_a = -LRU_C * r_gate * jax.nn.softplus(-a_param.astype(jnp.float32))
    a = jnp.exp(log_a)
    mult = jnp.sqrt(-jnp.expm1(2.0 * log_a))
    bterm = mult * i_gate * xc.astype(jnp.float32)

    def combine(left, right):
        a1, b1 = left
        a2, b2 = right
        return a1 * a2, a2 * b1 + b2

    _, h = lax.associative_scan(combine, (a, bterm), axis=1)
    return (h.astype(x.dtype) * y_br) @ w_out


def nsa_mixer(x, w_in, cmp_pos, cmp_w1, cmp_b1, cmp_w2, w_out):
    B, S, _ = x.shape
    dt = x.dtype
    f32 = jnp.float32
    proj = x @ w_in
    q = proj[..., :Q_COLS].reshape(B, S, N_KV, GROUP, HEAD_DIM).transpose(0, 2, 3, 1, 4)
    q = q * (HEAD_DIM ** -0.5)
    kv = proj[..., Q_COLS:Q_COLS + N_BRANCH * KV_COLS].reshape(B, S, N_BRANCH, 2, N_KV, HEAD_DIM)
    kv = kv.transpose(2, 3, 0, 4, 1, 5)
    gates = jax.nn.sigmoid(proj[..., Q_COLS + N_BRANCH * KV_COLS:].astype(f32))
    gates = gates.reshape(B, S, N_BRANCH, N_KV, GROUP).transpose(2, 0, 3, 4, 1)
    slopes = alibi_slopes()

    n_cmp = (S - CMP_BLOCK) // CMP_STRIDE + 1
    cmp_start = jnp.arange(n_cmp) * CMP_STRIDE
    tok_idx = cmp_start[:, None] + jnp.arange(CMP_BLOCK)[None, :]

    def compress(t, pos, w1, b1, w2):
        blk = (t[:, :, tok_idx] + pos.astype(dt)).reshape(B, N_KV, n_cmp, CMP_BLOCK * HEAD_DIM)
        return jax.nn.gelu(blk @ w1 + b1.astype(dt)) @ w2

    k_cmp = compress(kv[0, 0], cmp_pos[0], cmp_w1[0], cmp_b1[0], cmp_w2[0])
    v_cmp = compress(kv[0, 1], cmp_pos[1], cmp_w1[1], cmp_b1[1], cmp_w2[1])
    cmp_end = cmp_start + CMP_BLOCK - 1
    cmp_center = cmp_start.astype(f32) + (CMP_BLOCK - 1) / 2.0

    n_sel = S // SEL_BLOCK
    n_top = min(N_SELECT, n_sel)
    sel_j = jnp.arange(n_sel)
    overlap = ((cmp_start[:, None] < (sel_j[None, :] + 1) * SEL_BLOCK) &
               (cmp_start[:, None] + CMP_BLOCK > sel_j[None, :] * SEL_BLOCK)).astype(f32)
    k_blocks = kv[1, 0].reshape(B, N_KV, n_sel, SEL_BLOCK, HEAD_DIM)
    v_blocks = kv[1, 1].reshape(B, N_KV, n_sel, SEL_BLOCK, HEAD_DIM)
    gather = jax.vmap(jax.vmap(lambda blk, ix: blk[ix]))

    pad = ((0, 0), (0, 0), (WINDOW, 0), (0, 0))
    k_win = jnp.pad(kv[2, 0], pad)
    v_win = jnp.pad(kv[2, 1], pad)

    def query_block(q0):
        qc = lax.dynamic_slice_in_dim(q, q0, Q_BLOCK, axis=3)
        gc = lax.dynamic_slice_in_dim(gates, q0, Q_BLOCK, axis=4)
        t = q0 + jnp.arange(Q_BLOCK)
        tf = t.astype(f32)
        sl = slopes[:, :, None, None]

        m_c = cmp_end[None, :] <= t[:, None]
        s = jnp.einsum('bgrqd,bgcd->bgrqc', qc, k_cmp, preferred_element_type=f32)
        s = s - sl * (tf[:, None] - cmp_center[None, :])
        p_cmp = masked_softmax(s, m_c) * m_c
        o_cmp = jnp.einsum('bgrqc,bgcd->bgrqd', p_cmp.astype(dt), v_cmp)

        imp = jnp.einsum('bgrqc,cj->bgqj', p_cmp, overlap)
        cur = t // SEL_BLOCK
        forced = (sel_j[None, :] == 0) | (sel_j[None, :] == cur[:, None]) | (sel_j[None, :] == cur[:, None] - 1)
        future = sel_j[None, :] > cur[:, None]
        score = jnp.where(forced, FORCED_BONUS, jnp.where(future, -1.0, imp))
        _, idx = lax.top_k(score, n_top)
        kg = gather(k_blocks, idx).reshape(B, N_KV, Q_BLOCK, n_top * SEL_BLOCK, HEAD_DIM)
        vg = gather(v_blocks, idx).reshape(B, N_KV, Q_BLOCK, n_top * SEL_BLOCK, HEAD_DIM)
        pos = (idx[..., None] * SEL_BLOCK + jnp.arange(SEL_BLOCK)).reshape(B, N_KV, Q_BLOCK, n_top * SEL_BLOCK)
        dist = (t[:, None] - pos)[:, :, None]
        s = jnp.einsum('bgrqd,bgqkd->bgrqk', qc, kg, preferred_element_type=f32)
        s = s - sl * dist.astype(f32)
        p = masked_softmax(s, dist >= 0)
        o_sel = jnp.einsum('bgrqk,bgqkd->bgrqd', p.astype(dt), vg)

        kw = lax.dynamic_slice_in_dim(k_win, q0, WINDOW + Q_BLOCK, axis=2)
        vw = lax.dynamic_slice_in_dim(v_win, q0, WINDOW + Q_BLOCK, axis=2)
        pos_w = q0 - WINDOW + jnp.arange(WINDOW + Q_BLOCK)
        dist_w = t[:, None] - pos_w[None, :]
        m_w = (dist_w >= 0) & (dist_w < WINDOW) & (pos_w[None, :] >= 0)
        s = jnp.einsum('bgrqd,bgkd->bgrqk', qc, kw, preferred_element_type=f32)
        s = s - sl * dist_w.astype(f32)
        p = masked_softmax(s, m_w)
        o_win = jnp.einsum('bgrqk,bgkd->bgrqd', p.astype(dt), vw)

        o = (gc[0][..., None] * o_cmp.astype(f32) + gc[1][..., None] * o_sel.astype(f32)
             + gc[2][..., None] * o_win.astype(f32)).astype(dt)
        return o.transpose(0, 3, 1, 2, 4).reshape(B, Q_BLOCK, Q_COLS)

    starts = jnp.arange(S // Q_BLOCK) * Q_BLOCK
    o = lax.map(query_block, starts)
    o = o.transpose(1, 0, 2, 3).reshape(B, S, Q_COLS)
    return o @ w_out


def conv_ffn(x, w_in, conv_w, conv_b, w_out):
    a, b = jnp.split(x @ w_in, 2, axis=-1)
    a = causal_dwconv(a, conv_w, conv_b)
    return (jax.nn.gelu(a) * b) @ w_out


def setup_inputs(seed: int = 0) -> dict:
    key = jax.random.key(seed)
    ks = iter(jax.random.split(key, 32))
    n_a = len(range(0, DEPTH, N_MIXERS))
    n_b = len(range(1, DEPTH, N_MIXERS))

    def nrm(shape, scale):
        return jax.random.normal(next(ks), shape, jnp.float32) * scale

    def gain(shape):
        return 1.0 + nrm(shape, 0.05)

    x = nrm((BATCH, SEQ, D_MODEL), 1.0)
    lru_norm_g = gain((n_a, D_MODEL))
    lru_w_in = nrm((n_a, D_MODEL, 2 * LRU_WIDTH), D_MODEL ** -0.5)
    lru_conv_w = nrm((n_a, LRU_CONV, LRU_WIDTH), LRU_CONV ** -0.5)
    lru_conv_b = nrm((n_a, LRU_WIDTH), 0.02)
    lru_gate_w = nrm((n_a, 2, LRU_BLOCKS, LRU_BLOCK_W, LRU_BLOCK_W), LRU_BLOCK_W ** -0.5)
    lru_gate_b = nrm((n_a, 2, LRU_WIDTH), 0.1)
    a_c = jax.random.uniform(next(ks), (n_a, LRU_WIDTH), jnp.float32, 0.9, 0.999)
    s = a_c ** (1.0 / LRU_C)
    lru_a_param = jnp.log(s) - jnp.log1p(-s)
    lru_w_out = nrm((n_a, LRU_WIDTH, D_MODEL), LRU_WIDTH ** -0.5)
    nsa_norm_g = gain((n_b, D_MODEL))
    nsa_w_in = nrm((n_b, D_MODEL, NSA_IN), D_MODEL ** -0.5)
    nsa_cmp_pos = nrm((n_b, 2, CMP_BLOCK, HEAD_DIM), 0.1)
    nsa_cmp_w1 = nrm((n_b, 2, CMP_BLOCK * HEAD_DIM, CMP_HIDDEN), (CMP_BLOCK * HEAD_DIM) ** -0.5)
    nsa_cmp_b1 = nrm((n_b, 2, CMP_HIDDEN), 0.02)
    nsa_cmp_w2 = nrm((n_b, 2, CMP_HIDDEN, HEAD_DIM), CMP_HIDDEN ** -0.5)
    nsa_w_out = nrm((n_b, Q_COLS, D_MODEL), Q_COLS ** -0.5)
    ffn_norm_g = gain((DEPTH, D_MODEL))
    ffn_w_in = nrm((DEPTH, D_MODEL, 2 * D_FF), D_MODEL ** -0.5)
    ffn_conv_w = nrm((DEPTH, FFN_CONV, D_FF), FFN_CONV ** -0.5)
    ffn_conv_b = nrm((DEPTH, D_FF), 0.02)
    ffn_w_out = nrm((DEPTH, D_FF, D_MODEL), D_FF ** -0.5)
    final_norm_g = gain((D_MODEL,))
    return {"x": x, "lru_norm_g": lru_norm_g, "lru_w_in": lru_w_in, "lru_conv_w": lru_conv_w,
            "lru_conv_b": lru_conv_b, "lru_gate_w": lru_gate_w, "lru_gate_b": lru_gate_b,
            "lru_a_param": lru_a_param, "lru_w_out": lru_w_out, "nsa_norm_g": nsa_norm_g,
            "nsa_w_in": nsa_w_in, "nsa_cmp_pos": nsa_cmp_pos, "nsa_cmp_w1": nsa_cmp_w1,
            "nsa_cmp_b1": nsa_cmp_b1, "nsa_cmp_w2": nsa_cmp_w2, "nsa_w_out": nsa_w_out,
            "ffn_norm_g": ffn_norm_g, "ffn_w_in": ffn_w_in, "ffn_conv_w": ffn_conv_w,
            "ffn_conv_b": ffn_conv_b, "ffn_w_out": ffn_w_out, "final_norm_g": final_norm_g}


def reference(x, lru_norm_g, lru_w_in, lru_conv_w, lru_conv_b, lru_gate_w, lru_gate_b,
              lru_a_param, lru_w_out, nsa_norm_g, nsa_w_in, nsa_cmp_pos, nsa_cmp_w1,
              nsa_cmp_b1, nsa_cmp_w2, nsa_w_out, ffn_norm_g, ffn_w_in, ffn_conv_w,
              ffn_conv_b, ffn_w_out, final_norm_g):
    h = x
    for layer in range(DEPTH):
        mixer, j = layer % N_MIXERS, layer // N_MIXERS
        if mixer == 0:
            h = h + rglru_mixer(rmsnorm(h, lru_norm_g[j]), lru_w_in[j], lru_conv_w[j], lru_conv_b[j],
                                lru_gate_w[j], lru_gate_b[j], lru_a_param[j], lru_w_out[j])
        else:
            h = h + nsa_mixer(rmsnorm(h, nsa_norm_g[j]), nsa_w_in[j], nsa_cmp_pos[j], nsa_cmp_w1[j],
                              nsa_cmp_b1[j], nsa_cmp_w2[j], nsa_w_out[j])
        h = h + conv_ffn(rmsnorm(h, ffn_norm_g[layer]), ffn_w_in[layer], ffn_conv_w[layer],
                         ffn_conv_b[layer], ffn_w_out[layer])
    return rmsnorm(h, final_norm_g)
```

```python
import contextlib
import numpy as np
import concourse.bass as bass
import concourse.mybir as mybir
from concourse.bass_utils import run_bass_kernel_spmd

F32 = mybir.dt.float32
BF16 = mybir.dt.bfloat16
AF = mybir.ActivationFunctionType
ALU = mybir.AluOpType
AX = mybir.AxisListType

D = 1024
KC = 8
NT = 2048
TT = 512
NTT = 4
LW = 1280
LN = 10
DFF = 3072
FJ = 24
EPS = 1e-6
GELU_K = 1.5957691216057308


class Sched:
    ENGS = ("tensor", "vector", "scalar", "gpsimd", "sync")

    def __init__(self, nc, stack):
        self.nc = nc
        self.stack = stack
        self.eng = {e: getattr(nc, e) for e in self.ENGS}
        self.sem = {}
        self.cnt = {}
        self.known = {e: {} for e in self.ENGS}
        self.res = {}
        self.n_inst = 0
        self.n_wait = 0
        for e in ("tensor", "vector", "scalar", "gpsimd"):
            self._mksem(e)

    def _mksem(self, key):
        if key not in self.sem:
            name = "s_" + key.replace(":", "_")
            self.sem[key] = self.stack.enter_context(self.nc.semaphore(name))
            self.cnt[key] = 0
        return self.sem[key]

    def _deps(self, engine, reads, writes):
        deps = {}

        def add(k, v):
            if v > deps.get(k, 0):
                deps[k] = v
        for r in reads:
            st = self.res.get(r)
            if st and st["w"]:
                add(*st["w"])
        for w in writes:
            st = self.res.get(w)
            if st:
                if st["w"]:
                    add(*st["w"])
                for k, v in st["r"].items():
                    add(k, v)
        kn = self.known[engine]
        for k, v in deps.items():
            if k == engine and engine == "tensor":
                continue
            if kn.get(k, 0) >= v:
                continue
            self.eng[engine].wait_ge(self.sem[k], v)
            self.n_wait += 1
            kn[k] = v

    def _mark(self, key, val, reads, writes):
        for r in reads:
            st = self.res.setdefault(r, {"w": None, "r": {}})
            st["r"][key] = val
        for w in writes:
            self.res[w] = {"w": (key, val), "r": {}}

    def op(self, engine, fn, reads=(), writes=()):
        self._deps(engine, reads, writes)
        ins = fn(self.eng[engine])
        self.cnt[engine] += 1
        ins.then_inc(self.sem[engine], 1)
        self.n_inst += 1
        self._mark(engine, self.cnt[engine], reads, writes)
        return ins

    def dma(self, queue, key, fn, reads=(), writes=()):
        k = "dma:" + key
        self._mksem(k)
        self._deps(queue, reads, writes)
        ins = fn(self.eng[queue])
        self.cnt[k] += 16
        ins.then_inc(self.sem[k], 16)
        self.n_inst += 1
        self._mark(k, self.cnt[k], reads, writes)
        return ins

    def barrier(self):
        for e in self.ENGS:
            for k, v in self.cnt.items():
                if v == 0 or (k == e and e == "tensor"):
                    continue
                if self.known[e].get(k, 0) >= v:
                    continue
                self.eng[e].wait_ge(self.sem[k], v)
                self.known[e][k] = v

    def finish(self):
        for k, v in self.cnt.items():
            if v and self.known["sync"].get(k, 0) < v:
                self.nc.sync.wait_ge(self.sem[k], v)
                self.known["sync"][k] = v


class Ctx:
    pass


def tsl(tt):
    return slice(tt * TT, (tt + 1) * TT)


def build(nseq=2, upto=99):
    nc = bass.Bass("TRN2", target_bir_lowering=False)
    C = Ctx()
    C.nc = nc

    def din(name, shape):
        return nc.dram_tensor(name, list(shape), F32, kind="ExternalInput").ap()
    C.xT = din("xT", [nseq, 128, KC, NT])
    C.gains = din("gains", [128, 5, KC])
    C.lru_win = din("lru_win", [LN, 128, KC, 256])
    C.lru_gw = din("lru_gw", [LN, 128, 2, 128])
    C.lru_wout = din("lru_wout", [LN, 128, D])
    C.lru_vec = din("lru_vec", [128, LN, 8])
    C.ffn_win = din("ffn_win", [2, FJ, 128, KC, 256])
    C.ffn_wout = din("ffn_wout", [2, FJ, 128, D])
    C.ffn_vec = din("ffn_vec", [128, 2, FJ, 4])
    C.nsa_wch = din("nsa_wch", [24, 128, KC, 128])
    C.nsa_wtok = din("nsa_wtok", [128, KC, 560])
    C.nsa_w1 = din("nsa_w1", [2, 128, 32, 256])
    C.nsa_posT = din("nsa_posT", [128, 2, 32])
    C.nsa_b1 = din("nsa_b1", [128, 2, 2])
    C.nsa_w2k = din("nsa_w2k", [128, 2, 128])
    C.nsa_w2v = din("nsa_w2v", [128, 2, 64])
    C.c_dtab = nc.dram_tensor("c_dtab", [128, 13, TT], mybir.dt.int16, kind="ExternalInput").ap()
    C.c_etab = din("c_etab", [32, 16, 128])
    C.c_biasc = din("c_biasc", [128, 16, 16])
    C.c_keep = din("c_keep", [128, 16, 32])
    C.c_addc = din("c_addc", [128, 16, 32])
    C.c_ident = din("c_ident", [128, 128])
    C.c_valid = din("c_valid", [128, 16, 1])
    C.c_ovl = din("c_ovl", [128, 33])
    C.outT = nc.dram_tensor("outT", [nseq, 128, KC, NT], F32, kind="ExternalOutput").ap()

    with contextlib.ExitStack() as st:
        S = Sched(nc, st)
        C.S = S

        uid = [0]

        def sb(name, shape, dt=F32, stack=st):
            uid[0] += 1
            return stack.enter_context(nc.sbuf_tensor("%s_u%d" % (name, uid[0]), list(shape), dt))
        C.sb = sb
        C.hT = sb("hT", [128, KC, NT])
        C.xn = sb("xn", [128, KC, NT], BF16)
        C.ones = sb("ones", [128, 128])
        C.gn = sb("gn", [128, 5, KC])
        C.lvec = sb("lvec", [128, LN, 8])
        C.lca = sb("lca", [128, LN, 2])
        C.fvec = sb("fvec", [128, 2, FJ, 4])
        C.psum = [st.enter_context(nc.psum_tensor("ps%d" % i, [128, TT], F32)) for i in range(8)]
        C.psi = 0
        C.nrot = 8

        def nextps():
            i = C.psi % C.nrot
            C.psi += 1
            return C.psum[i], ("ps", i)
        C.nextps = nextps

        S.op("vector", lambda e: e.memset(C.ones[:], 1.0), writes=["ones"])
        S.dma("sync", "c0", lambda e: e.dma_start(out=C.gn[:], in_=C.gains), writes=["gn"])
        S.dma("sync", "c1", lambda e: e.dma_start(out=C.lvec[:], in_=C.lru_vec), writes=["lvec"])
        S.dma("sync", "c2", lambda e: e.dma_start(out=C.fvec[:], in_=C.ffn_vec), writes=["fvec"])
        lru_consts(C)

        for s in range(nseq):
            for kc in range(KC):
                S.dma("sync", "x%d" % kc, lambda e, kc=kc: e.dma_start(out=C.hT[:, kc, :], in_=C.xT[s, :, kc, :]),
                      writes=[("hT", kc, tt) for tt in range(NTT)])
            if upto >= 1:
                rmsnorm(C, 0)
                lru_mixer(C)
            if upto >= 2:
                rmsnorm(C, 1)
                conv_ffn(C, 0)
            if upto >= 3:
                rmsnorm(C, 2)
                nsa_mixer(C)
            if upto >= 4:
                rmsnorm(C, 3)
                conv_ffn(C, 1)
            if upto >= 5:
                rmsnorm(C, 4, final=True)
            for kc in range(KC):
                S.dma("sync", "o%d" % kc, lambda e, kc=kc: e.dma_start(out=C.outT[s, :, kc, :], in_=C.hT[:, kc, :]),
                      reads=[("hT", kc, tt) for tt in range(NTT)])
        S.finish()
    C.n_inst = S.n_inst
    C.n_wait = S.n_wait
    return nc, C


def lru_consts(C):
    S, sb = C.S, C.sb
    with contextlib.ExitStack() as ph:
        t = [sb("lc%d" % i, [128, LN], F32, ph) for i in range(6)]
        ap = C.lvec[:, :, 7]
        S.op("scalar", lambda e: e.activation(out=t[0][:], in_=ap, func=AF.Abs), reads=["lvec"], writes=["lc0"])
        S.op("scalar", lambda e: e.activation(out=t[1][:], in_=t[0][:], func=AF.Exp, scale=-1.0), reads=["lc0"], writes=["lc1"])
        S.op("scalar", lambda e: e.activation(out=t[2][:], in_=t[1][:], func=AF.Ln, bias=1.0), reads=["lc1"], writes=["lc2"])
        S.op("vector", lambda e: e.tensor_scalar(out=t[3][:], in0=t[1][:], scalar1=1.0 / 3.0, scalar2=-0.5, op0=ALU.mult, op1=ALU.add), reads=["lc1"], writes=["lc3"])
        S.op("vector", lambda e: e.tensor_tensor(out=t[3][:], in0=t[3][:], in1=t[1][:], op=ALU.mult), reads=["lc3", "lc1"], writes=["lc3"])
        S.op("vector", lambda e: e.tensor_scalar(out=t[3][:], in0=t[3][:], scalar1=1.0, scalar2=None, op0=ALU.add), reads=["lc3"], writes=["lc3"])
        S.op("vector", lambda e: e.tensor_tensor(out=t[3][:], in0=t[3][:], in1=t[1][:], op=ALU.mult), reads=["lc3", "lc1"], writes=["lc3"])
        S.op("vector", lambda e: e.tensor_single_scalar(out=t[4][:], in_=t[1][:], scalar=0.03, op=ALU.is_lt), reads=["lc1"], writes=["lc4"])
        S.op("vector", lambda e: e.tensor_tensor(out=t[3][:], in0=t[3][:], in1=t[2][:], op=ALU.subtract), reads=["lc3", "lc2"], writes=["lc3"])
        S.op("vector", lambda e: e.tensor_tensor(out=t[3][:], in0=t[3][:], in1=t[4][:], op=ALU.mult), reads=["lc3", "lc4"], writes=["lc3"])
        S.op("vector", lambda e: e.tensor_tensor(out=t[3][:], in0=t[3][:], in1=t[2][:], op=ALU.add), reads=["lc3", "lc2"], writes=["lc3"])
        S.op("vector", lambda e: e.tensor_scalar(out=t[5][:], in0=ap, scalar1=-1.0, scalar2=0.0, op0=ALU.mult, op1=ALU.max), reads=["lvec"], writes=["lc5"])
        S.op("vector", lambda e: e.tensor_tensor(out=t[3][:], in0=t[3][:], in1=t[5][:], op=ALU.add), reads=["lc3", "lc5"], writes=["lc3"])
        S.op("vector", lambda e: e.tensor_scalar(out=C.lca[:, :, 0], in0=t[3][:], scalar1=-8.0, scalar2=None, op0=ALU.mult), reads=["lc3"], writes=["lca"])
        S.op("vector", lambda e: e.tensor_scalar(out=C.lca[:, :, 1], in0=t[3][:], scalar1=-16.0, scalar2=None, op0=ALU.mult), reads=["lc3"], writes=["lca"])
        S.barrier()


def rmsnorm(C, gi, final=False):
    S = C.S
    ph = contextlib.ExitStack()
    C.sq = [C.sb("sq%d" % i, [128, TT], F32, ph) for i in range(2)]
    C.rs = C.sb("rs", [128, TT], F32, ph)
    for tt in range(NTT):
        ps, pk = C.nextps()
        for kc in range(KC):
            sq = C.sq[kc % 2]
            S.op("scalar", lambda e: e.activation(out=sq[:], in_=C.hT[:, kc, tsl(tt)], func=AF.Square),
                 reads=[("hT", kc, tt)], writes=[("sq", kc % 2)])
            S.op("tensor", lambda e: e.matmul(ps[:], lhsT=C.ones[:], rhs=sq[:], start=(kc == 0), stop=(kc == KC - 1)),
                 reads=["ones", ("sq", kc % 2)], writes=[pk])
        S.op("vector", lambda e: e.tensor_scalar(out=C.rs[:], in0=ps[:], scalar1=1.0 / D, scalar2=EPS, op0=ALU.mult, op1=ALU.add),
             reads=[pk], writes=["rs"])
        S.op("scalar", lambda e: e.activation(out=C.rs[:], in_=C.rs[:], func=AF.Sqrt), reads=["rs"], writes=["rs"])
        S.op("vector", lambda e: e.reciprocal(out=C.rs[:], in_=C.rs[:]), reads=["rs"], writes=["rs"])
        for kc in range(KC):
            if final:
                S.op("vector", lambda e: e.scalar_tensor_tensor(out=C.hT[:, kc, tsl(tt)], in0=C.hT[:, kc, tsl(tt)], scalar=C.gn[:, gi, kc:kc + 1],
                                                               in1=C.rs[:], op0=ALU.mult, op1=ALU.mult),
                     reads=[("hT", kc, tt), "rs", "gn"], writes=[("hT", kc, tt)])
            else:
                S.op("vector", lambda e: e.scalar_tensor_tensor(out=C.xn[:, kc, tsl(tt)], in0=C.hT[:, kc, tsl(tt)], scalar=C.gn[:, gi, kc:kc + 1],
                                                               in1=C.rs[:], op0=ALU.mult, op1=ALU.mult),
                     reads=[("hT", kc, tt), "rs", "gn"], writes=[("xn", kc, tt)])
    S.barrier()
    ph.close()


def inproj(C, w, col0, tt, evac):
    S = C.S
    ps, pk = C.nextps()
    wt, wk = w
    for kc in range(KC):
        S.op("tensor", lambda e: e.matmul(ps[:], lhsT=wt[:, kc, col0:col0 + 128], rhs=C.xn[:, kc, tsl(tt)], start=(kc == 0), stop=(kc == KC - 1)),
             reads=[wk, ("xn", kc, tt)], writes=[pk])
    evac(ps, pk)


def gelu_inplace(C, x, xk, t1, t1k, tt):
    S = C.S
    sl = tsl(tt)
    S.op("scalar", lambda e: e.activation(out=t1[:, sl], in_=x[:, sl], func=AF.Square), reads=[(xk, tt)], writes=[(t1k, tt)])
    S.op("vector", lambda e: e.tensor_scalar(out=t1[:, sl], in0=t1[:, sl], scalar1=0.044715, scalar2=1.0, op0=ALU.mult, op1=ALU.add),
         reads=[(t1k, tt)], writes=[(t1k, tt)])
    S.op("vector", lambda e: e.tensor_tensor(out=t1[:, sl], in0=t1[:, sl], in1=x[:, sl], op=ALU.mult), reads=[(t1k, tt), (xk, tt)], writes=[(t1k, tt)])
    S.op("scalar", lambda e: e.activation(out=t1[:, sl], in_=t1[:, sl], func=AF.Sigmoid, scale=GELU_K), reads=[(t1k, tt)], writes=[(t1k, tt)])
    S.op("vector", lambda e: e.tensor_tensor(out=x[:, sl], in0=x[:, sl], in1=t1[:, sl], op=ALU.mult), reads=[(t1k, tt), (xk, tt)], writes=[(xk, tt)])


def outproj_group(C, acts, wouts):
    S = C.S
    n = len(acts)
    for tt in range(NTT):
        for m in range(KC):
            ps, pk = C.nextps()
            for i in range(n):
                a_ap, a_k = acts[i]
                wt, wk = wouts[i]
                S.op("tensor", lambda e: e.matmul(ps[:], lhsT=wt[:, m * 128:(m + 1) * 128], rhs=a_ap(tt), start=(i == 0), stop=(i == n - 1)),
                     reads=[wk, a_k(tt)], writes=[pk])
            S.op("vector", lambda e: e.tensor_tensor(out=C.hT[:, m, tsl(tt)], in0=C.hT[:, m, tsl(tt)], in1=ps[:], op=ALU.add),
                 reads=[pk, ("hT", m, tt)], writes=[("hT", m, tt)])


def lru_mixer(C):
    S, sb, nc = C.S, C.sb, C.nc
    G = 2
    with contextlib.ExitStack() as ph:
        win = [sb("l_win%d" % i, [128, KC, 256], BF16, ph) for i in range(2)]
        gw = [sb("l_gw%d" % i, [128, 2, 128], BF16, ph) for i in range(2)]
        wo = [sb("l_wo%d" % i, [128, D], BF16, ph) for i in range(2 * G)]
        hy = [sb("l_hy%d" % i, [128, NT], BF16, ph) for i in range(2 * G)]
        ysb = sb("l_ysb", [128, NT], F32, ph)
        t1 = sb("l_t1", [128, NT], F32, ph)
        xpad = sb("l_xpad", [128, 3 + NT], F32, ph)
        xc = sb("l_xc", [128, NT], F32, ph)
        xcb = sb("l_xcb", [128, NT], BF16, ph)
        rr = sb("l_r", [128, NT], F32, ph)
        a2 = sb("l_a2", [128, NT], F32, ph)
        ig = sb("l_ig", [128, NT], F32, ph)
        hh = sb("l_h", [128, NT], F32, ph)
        S.op("vector", lambda e: e.memset(xpad[:, 0:3], 0.0), writes=["l_xpad0"])
        pending = None
        for n in range(LN):
            wi, gwi = win[n % 2], gw[n % 2]
            woi, hyi = wo[n % (2 * G)], hy[n % (2 * G)]
            wik, gwk, wok, hyk = "l_win%d" % (n % 2), "l_gw%d" % (n % 2), "l_wo%d" % (n % (2 * G)), "l_hy%d" % (n % (2 * G))
            S.dma("gpsimd", wik, lambda e: e.dma_start(out=wi[:], in_=C.lru_win[n]), writes=[wik])
            S.dma("gpsimd", gwk, lambda e: e.dma_start(out=gwi[:], in_=C.lru_gw[n]), writes=[gwk])
            S.dma("gpsimd", wok, lambda e: e.dma_start(out=woi[:], in_=C.lru_wout[n]), writes=[wok])
            for tt in range(NTT):
                inproj(C, (wi, wik), 0, tt, lambda ps, pk: S.op(
                    "scalar", lambda e: e.activation(out=ysb[:, tsl(tt)], in_=ps[:], func=AF.Identity), reads=[pk], writes=[("l_ysb", tt)]))
            for tt in range(NTT):
                inproj(C, (wi, wik), 128, tt, lambda ps, pk: S.op(
                    "scalar", lambda e: e.activation(out=xpad[:, 3 + tt * TT:3 + (tt + 1) * TT], in_=ps[:], func=AF.Identity),
                    reads=[pk], writes=[("l_xpad", tt)]))
            if pending is not None:
                outproj_group(C, *pending)
                pending = None
            for tt in range(NTT):
                gelu_inplace(C, ysb, "l_ysb", t1, "l_t1", tt)
            for tt in range(NTT):
                rd = [("l_xpad", tt), ("l_xpad", tt - 1) if tt > 0 else "l_xpad0", "lvec"]
                S.op("vector", lambda e: e.tensor_scalar(out=xc[:, tsl(tt)], in0=xpad[:, 3 + tt * TT:3 + (tt + 1) * TT], scalar1=C.lvec[:, n, 3:4],
                                                        scalar2=C.lvec[:, n, 4:5], op0=ALU.mult, op1=ALU.add), reads=rd, writes=[("l_xc", tt)])
                for k in (2, 1, 0):
                    S.op("vector", lambda e: e.scalar_tensor_tensor(out=xc[:, tsl(tt)], in0=xpad[:, k + tt * TT:k + (tt + 1) * TT], scalar=C.lvec[:, n, k:k + 1],
                                                                   in1=xc[:, tsl(tt)], op0=ALU.mult, op1=ALU.add),
                         reads=rd + [("l_xc", tt)], writes=[("l_xc", tt)])
                S.op("scalar", lambda e: e.activation(out=xcb[:, tsl(tt)], in_=xc[:, tsl(tt)], func=AF.Identity), reads=[("l_xc", tt)], writes=[("l_xcb", tt)])
            for g, (dst, dk) in enumerate(((rr, "l_r"), (ig, "l_ig"))):
                for tt in range(NTT):
                    ps, pk = C.nextps()
                    S.op("tensor", lambda e: e.matmul(ps[:], lhsT=gwi[:, g, :], rhs=xcb[:, tsl(tt)], start=True, stop=True),
                         reads=[gwk, ("l_xcb", tt)], writes=[pk])
                    S.op("scalar", lambda e: e.activation(out=dst[:, tsl(tt)], in_=ps[:], func=AF.Sigmoid, bias=C.lvec[:, n, 5 + g:6 + g]),
                         reads=[pk, "lvec"], writes=[(dk, tt)])
            for tt in range(NTT):
                sl = tsl(tt)
                S.op("scalar", lambda e: e.activation(out=a2[:, sl], in_=rr[:, sl], func=AF.Exp, scale=C.lca[:, n, 1:2]), reads=[("l_r", tt), "lca"], writes=[("l_a2", tt)])
                S.op("scalar", lambda e: e.activation(out=rr[:, sl], in_=rr[:, sl], func=AF.Exp, scale=C.lca[:, n, 0:1]), reads=[("l_r", tt), "lca"], writes=[("l_r", tt)])
                S.op("vector", lambda e: e.tensor_scalar(out=a2[:, sl], in0=a2[:, sl], scalar1=-1.0, scalar2=1.0, op0=ALU.mult, op1=ALU.add),
                     reads=[("l_a2", tt)], writes=[("l_a2", tt)])
                S.op("scalar", lambda e: e.activation(out=a2[:, sl], in_=a2[:, sl], func=AF.Sqrt), reads=[("l_a2", tt)], writes=[("l_a2", tt)])
                S.op("vector", lambda e: e.tensor_tensor(out=ig[:, sl], in0=ig[:, sl], in1=xc[:, sl], op=ALU.mult), reads=[("l_ig", tt), ("l_xc", tt)], writes=[("l_ig", tt)])
                S.op("vector", lambda e: e.tensor_tensor(out=ig[:, sl], in0=ig[:, sl], in1=a2[:, sl], op=ALU.mult), reads=[("l_ig", tt), ("l_a2", tt)], writes=[("l_ig", tt)])
                init = 0.0 if tt == 0 else hh[:, tt * TT - 1:tt * TT]
                S.op("vector", lambda e: e.tensor_tensor_scan(out=hh[:, sl], data0=rr[:, sl], data1=ig[:, sl], initial=init, op0=ALU.mult, op1=ALU.add),
                     reads=[("l_r", tt), ("l_ig", tt)] + ([("l_h", tt - 1)] if tt else []), writes=[("l_h", tt)])
                S.op("vector", lambda e: e.tensor_tensor(out=hyi[:, sl], in0=hh[:, sl], in1=ysb[:, sl], op=ALU.mult),
                     reads=[("l_h", tt), ("l_ysb", tt)], writes=[(hyk, tt)])
            if n % G == G - 1:
                idx = [(n - G + 1 + i) % (2 * G) for i in range(G)]
                pending = ([((lambda tt, i=i: hy[i][:, tsl(tt)]), (lambda tt, i=i: ("l_hy%d" % i, tt))) for i in idx],
                           [(wo[i], "l_wo%d" % i) for i in idx])
        outproj_group(C, *pending)
        S.barrier()


def conv_ffn(C, L):
    S, sb, nc = C.S, C.sb, C.nc
    G = 4
    with contextlib.ExitStack() as ph:
        win = [sb("f_win%d" % i, [128, KC, 256], BF16, ph) for i in range(2)]
        wo = [sb("f_wo%d" % i, [128, D], BF16, ph) for i in range(2 * G)]
        act = [sb("f_act%d" % i, [128, NT], BF16, ph) for i in range(2 * G)]
        apad = sb("f_apad", [128, 2 + NT], F32, ph)
        ac = sb("f_ac", [128, NT], F32, ph)
        t1 = sb("f_t1", [128, NT], F32, ph)
        bsb = sb("f_bsb", [128, NT], BF16, ph)
        S.op("vector", lambda e: e.memset(apad[:, 0:2], 0.0), writes=["f_apad0"])
        pending = None
        for j in range(FJ):
            wi = win[j % 2]
            woi, acti = wo[j % (2 * G)], act[j % (2 * G)]
            wik, wok, actk = "f_win%d" % (j % 2), "f_wo%d" % (j % (2 * G)), "f_act%d" % (j % (2 * G))
            S.dma("gpsimd", wik, lambda e: e.dma_start(out=wi[:], in_=C.ffn_win[L, j]), writes=[wik])
            S.dma("gpsimd", wok, lambda e: e.dma_start(out=woi[:], in_=C.ffn_wout[L, j]), writes=[wok])
            for tt in range(NTT):
                inproj(C, (wi, wik), 0, tt, lambda ps, pk: S.op(
                    "scalar", lambda e: e.activation(out=apad[:, 2 + tt * TT:2 + (tt + 1) * TT], in_=ps[:], func=AF.Identity),
                    reads=[pk], writes=[("f_apad", tt)]))
            for tt in range(NTT):
                inproj(C, (wi, wik), 128, tt, lambda ps, pk: S.op(
                    "scalar", lambda e: e.activation(out=bsb[:, tsl(tt)], in_=ps[:], func=AF.Identity), reads=[pk], writes=[("f_bsb", tt)]))
            if pending is not None:
                outproj_group(C, *pending)
                pending = None
            for tt in range(NTT):
                sl = tsl(tt)
                rd = [("f_apad", tt), ("f_apad", tt - 1) if tt > 0 else "f_apad0", "fvec"]
                S.op("vector", lambda e: e.tensor_scalar(out=ac[:, sl], in0=apad[:, 2 + tt * TT:2 + (tt + 1) * TT], scalar1=C.fvec[:, L, j, 2:3],
                                                        scalar2=C.fvec[:, L, j, 3:4], op0=ALU.mult, op1=ALU.add), reads=rd, writes=[("f_ac", tt)])
                for k in (1, 0):
                    S.op("vector", lambda e: e.scalar_tensor_tensor(out=ac[:, sl], in0=apad[:, k + tt * TT:k + (tt + 1) * TT], scalar=C.fvec[:, L, j, k:k + 1],
                                                                   in1=ac[:, sl], op0=ALU.mult, op1=ALU.add),
                         reads=rd + [("f_ac", tt)], writes=[("f_ac", tt)])
                gelu_inplace(C, ac, "f_ac", t1, "f_t1", tt)
                S.op("vector", lambda e: e.tensor_tensor(out=acti[:, sl], in0=ac[:, sl], in1=bsb[:, sl], op=ALU.mult),
                     reads=[("f_ac", tt), ("f_bsb", tt)], writes=[(actk, tt)])
            if j % G == G - 1:
                idx = [(j - G + 1 + i) % (2 * G) for i in range(G)]
                pending = ([((lambda tt, i=i: act[i][:, tsl(tt)]), (lambda tt, i=i: ("f_act%d" % i, tt))) for i in idx],
                           [(wo[i], "f_wo%d" % i) for i in idx])
        outproj_group(C, *pending)
        S.barrier()


SLOPES = [2.0 ** (-8.0 * (h + 1) / 16.0) for h in range(16)]
BIGNEG = 30000.0


def gelu_ap(C, x, t, xk, tk):
    S = C.S
    S.op("scalar", lambda e: e.activation(out=t, in_=x, func=AF.Square), reads=[xk], writes=[tk])
    S.op("vector", lambda e: e.tensor_scalar(out=t, in0=t, scalar1=0.044715, scalar2=1.0, op0=ALU.mult, op1=ALU.add), reads=[tk], writes=[tk])
    S.op("vector", lambda e: e.tensor_tensor(out=t, in0=t, in1=x, op=ALU.mult), reads=[tk, xk], writes=[tk])
    S.op("scalar", lambda e: e.activation(out=t, in_=t, func=AF.Sigmoid, scale=GELU_K), reads=[tk], writes=[tk])
    S.op("vector", lambda e: e.tensor_tensor(out=x, in0=x, in1=t, op=ALU.mult), reads=[tk, xk], writes=[xk])


def nsa_mixer(C):
    S, sb, nc = C.S, C.sb, C.nc
    C.nrot = 6
    po_banks = [(C.psum[6], ("ps", 6)), (C.psum[7], ("ps", 7))]
    po_i = [0]

    def nextpo():
        r = po_banks[po_i[0] % 2]
        po_i[0] += 1
        return r
    with contextlib.ExitStack() as ph:
        KCT = sb("n_KCT", [128, 4, 128], BF16, ph)
        VCa = sb("n_VCa", [128, 4, 97], BF16, ph)
        wch = [None, None]
        wci = [0]

        def alloc_wch(stack):
            for i in range(2):
                wch[i] = sb("n_wch%d" % i, [128, KC, 128], BF16, stack)

        def load_wch(idx):
            i = wci[0] % 2
            wci[0] += 1
            k = "n_wch%d" % i
            S.dma("gpsimd", k, lambda e: e.dma_start(out=wch[i][:], in_=C.nsa_wch[idx]), writes=[k])
            return wch[i], k

        with contextlib.ExitStack() as p1:
            alloc_wch(p1)
            KV0 = sb("n_KV0", [128, 4, NT], BF16, p1)
            W1 = sb("n_W1", [128, 2, 32, 256], BF16, p1)
            posT = sb("n_posT", [128, 2, 32], BF16, p1)
            b1 = sb("n_b1", [128, 2, 2], F32, p1)
            W2k = sb("n_W2k", [128, 2, 128], BF16, p1)
            W2v = sb("n_W2v", [128, 2, 64], BF16, p1)
            ovl = sb("n_ovl", [128, 33], F32, p1)
            hid = sb("n_hid", [128, 2, 128], F32, p1)
            hsc = sb("n_hsc", [128, 2, 128], F32, p1)
            ghb = sb("n_ghb", [128, 2, 128], BF16, p1)
            bcol = sb("n_bcol", [128, 2], F32, p1)
            for kvi in range(2):
                S.dma("gpsimd", "n_W1_%d" % kvi, lambda e: e.dma_start(out=W1[:, kvi], in_=C.nsa_w1[kvi]), writes=[("n_W1", kvi)])
            S.dma("gpsimd", "n_posT", lambda e: e.dma_start(out=posT[:], in_=C.nsa_posT), writes=["n_posT"])
            S.dma("sync", "n_b1", lambda e: e.dma_start(out=b1[:], in_=C.nsa_b1), writes=["n_b1"])
            S.dma("gpsimd", "n_W2k", lambda e: e.dma_start(out=W2k[:], in_=C.nsa_w2k), writes=["n_W2k"])
            S.dma("gpsimd", "n_W2v", lambda e: e.dma_start(out=W2v[:], in_=C.nsa_w2v), writes=["n_W2v"])
            S.dma("sync", "n_ovl", lambda e: e.dma_start(out=ovl[:], in_=C.c_ovl), writes=["n_ovl"])
            S.op("vector", lambda e: e.memset(KCT[:], 0.0), writes=[("n_KCT", g) for g in range(4)])
            S.op("vector", lambda e: e.memset(VCa[:], 0.0), writes=[("n_VCa", g) for g in range(4)])
            for g in range(4):
                S.op("vector", lambda e: e.tensor_copy(out=VCa[:, g, 64:97], in_=ovl[:]), reads=["n_ovl"], writes=[("n_VCa", g)])
            for c4 in range(4):
                wt, wk = load_wch(c4)
                for tt in range(NTT):
                    inproj(C, (wt, wk), 0, tt, lambda ps, pk: S.op(
                        "scalar", lambda e: e.activation(out=KV0[:, c4, tsl(tt)], in_=ps[:], func=AF.Identity), reads=[pk], writes=[("n_KV0", c4)]))
            for kvi in range(2):
                for g in range(4):
                    c4 = kvi * 2 + g // 2
                    rows = slice((g % 2) * 64, (g % 2) * 64 + 64)
                    for mh in range(2):
                        ps, pk = C.nextps()
                        for i in range(32):
                            S.op("tensor", lambda e: e.matmul(ps[:, 0:127], lhsT=W1[rows, kvi, i, mh * 128:(mh + 1) * 128],
                                                              rhs=KV0[rows, c4, i:i + 2017:16], start=(i == 0), stop=False),
                                 reads=[("n_W1", kvi), ("n_KV0", c4)], writes=[pk])
                        for i in range(32):
                            S.op("tensor", lambda e: e.matmul(ps[:, 127:128], lhsT=W1[rows, kvi, i, mh * 128:(mh + 1) * 128],
                                                              rhs=posT[rows, kvi, i:i + 1], start=False, stop=(i == 31)),
                                 reads=[("n_W1", kvi), "n_posT"], writes=[pk])
                        S.op("vector", lambda e: e.tensor_tensor(out=bcol[:, mh:mh + 1], in0=ps[:, 127:128], in1=b1[:, kvi, mh:mh + 1], op=ALU.add),
                             reads=[pk, "n_b1"], writes=[("n_bcol", mh)])
                        S.op("scalar", lambda e: e.activation(out=hid[:, mh, 0:127], in_=ps[:, 0:127], func=AF.Identity, bias=bcol[:, mh:mh + 1]),
                             reads=[pk, ("n_bcol", mh)], writes=[("n_hid", mh)])
                        gelu_ap(C, hid[:, mh, 0:127], hsc[:, mh, 0:127], ("n_hid", mh), ("n_hsc", mh))
                        S.op("scalar", lambda e: e.activation(out=ghb[:, mh, 0:127], in_=hid[:, mh, 0:127], func=AF.Identity),
                             reads=[("n_hid", mh)], writes=[("n_ghb", mh)])
                    ps, pk = C.nextps()
                    if kvi == 0:
                        for mh in range(2):
                            S.op("tensor", lambda e: e.matmul(ps[:, 0:127], lhsT=W2k[:, mh, :], rhs=ghb[:, mh, 0:127], start=(mh == 0), stop=(mh == 1)),
                                 reads=["n_W2k", ("n_ghb", mh)], writes=[pk])
                        S.op("scalar", lambda e: e.activation(out=KCT[:, g, 0:127], in_=ps[:, 0:127], func=AF.Identity), reads=[pk], writes=[("n_KCT", g)])
                    else:
                        for mh in range(2):
                            S.op("tensor", lambda e: e.matmul(ps[0:127, 0:64], lhsT=ghb[:, mh, 0:127], rhs=W2v[:, mh, :], start=(mh == 0), stop=(mh == 1)),
                                 reads=["n_W2v", ("n_ghb", mh)], writes=[pk])
                        S.op("scalar", lambda e: e.activation(out=VCa[0:127, g, 0:64], in_=ps[0:127, 0:64], func=AF.Identity), reads=[pk], writes=[("n_VCa", g)])
            S.barrier()

        pA = ph.enter_context(contextlib.ExitStack())
        QT = sb("n_QT", [128, 8, NT], BF16, pA)
        K12 = sb("n_K12", [128, 2, 2, NT], BF16, pA)
        Vaug = sb("n_Vaug", [128, 2, 16, 4, 65], BF16, pA)
        gts = sb("n_gts", [128, 16, 48], F32, pA)
        with contextlib.ExitStack() as p2:
            alloc_wch(p2)
            wtok = sb("n_wtok", [128, KC, 560], BF16, p2)
            S.dma("gpsimd", "n_wtok", lambda e: e.dma_start(out=wtok[:], in_=C.nsa_wtok), writes=["n_wtok"])
            for mq in range(8):
                wt, wk = load_wch(4 + mq)
                for tt in range(NTT):
                    inproj(C, (wt, wk), 0, tt, lambda ps, pk: S.op(
                        "scalar", lambda e: e.activation(out=QT[:, mq, tsl(tt)], in_=ps[:], func=AF.Identity, scale=0.125), reads=[pk], writes=[("n_QT", mq, tt)]))
            for br in range(2):
                for c2 in range(2):
                    wt, wk = load_wch(12 + br * 2 + c2)
                    for tt in range(NTT):
                        inproj(C, (wt, wk), 0, tt, lambda ps, pk: S.op(
                            "scalar", lambda e: e.activation(out=K12[:, br, c2, tsl(tt)], in_=ps[:], func=AF.Identity), reads=[pk], writes=[("n_K12", br, c2, tt)]))
            S.op("vector", lambda e: e.memset(Vaug[:].rearrange("p a b c d -> p (a b c) d")[:, :, 64:65], 1.0), writes=["n_Vone"])
            for t16 in range(16):
                tok = slice(t16 * 128, (t16 + 1) * 128)
                ps, pk = C.nextps()
                for kc in range(KC):
                    S.op("tensor", lambda e: e.matmul(ps[:, 0:512], lhsT=C.xn[:, kc, tok], rhs=wtok[:, kc, 0:512], start=(kc == 0), stop=(kc == KC - 1)),
                         reads=["n_wtok", ("xn", kc, t16 // 4)], writes=[pk])
                for br in range(2):
                    S.op("scalar", lambda e: e.activation(out=Vaug[:, br, t16, :, 0:64], in_=ps[:, br * 256:(br + 1) * 256].rearrange("p (g d) -> p g d", d=64),
                                                          func=AF.Identity), reads=[pk], writes=[("n_Vaug", br, t16)])
                ps, pk = C.nextps()
                for kc in range(KC):
                    S.op("tensor", lambda e: e.matmul(ps[:, 0:48], lhsT=C.xn[:, kc, tok], rhs=wtok[:, kc, 512:560], start=(kc == 0), stop=(kc == KC - 1)),
                         reads=["n_wtok", ("xn", kc, t16 // 4)], writes=[pk])
                S.op("scalar", lambda e: e.activation(out=gts[:, t16, :], in_=ps[:, 0:48], func=AF.Sigmoid), reads=[pk], writes=[("n_gts", t16)])
            S.barrier()

        with contextlib.ExitStack() as p3:
            dtab = sb("n_dtab", [128, 10, TT], mybir.dt.int16, p3)
            etab = sb("n_etab", [32, 16, 128], BF16, p3)
            biasc = sb("n_biasc", [128, 16, 16], F32, p3)
            keep = sb("n_keep", [128, 16, 32], BF16, p3)
            addc = sb("n_addc", [128, 16, 32], BF16, p3)
            ident = sb("n_ident", [128, 128], F32, p3)
            sm = [sb("n_sm%d" % i, [128, TT], F32, p3) for i in range(2)]
            pT = [sb("n_pT%d" % i, [128, TT], BF16, p3) for i in range(3)]
            otoks = [sb("n_otok%d" % i, [128, 4, 256], F32, p3) for i in range(1)]
            valid = sb("n_valid", [128, 16, 1], F32, p3)
            otmp = sb("n_otmp", [128, 4, 64], F32, p3)
            rden = sb("n_rden", [128, 4, 1], F32, p3)
            ff = sb("n_ff", [128, 4, 1], F32, p3)
            imp = sb("n_imp", [128, 4, 32], F32, p3)
            itmp = sb("n_itmp", [128, 4, 32], F32, p3)
            top8 = sb("n_top8", [128, 4, 8], F32, p3)
            selb = sb("n_selb", [128, 4, 32], F32, p3)
            selbT = sb("n_selbT", [32, 4, TT], BF16, p3)
            S.dma("sync", "n_dtab", lambda e: e.dma_start(out=dtab[:, 0:9, :], in_=C.c_dtab[:, 0:9, :]), writes=["n_dtab"])
            S.dma("gpsimd", "n_etab", lambda e: e.dma_start(out=etab[:], in_=C.c_etab), writes=["n_etab"])
            S.dma("sync", "n_biasc", lambda e: e.dma_start(out=biasc[:], in_=C.c_biasc), writes=["n_biasc"])
            S.dma("gpsimd", "n_keep", lambda e: e.dma_start(out=keep[:], in_=C.c_keep), writes=["n_keep"])
            S.dma("gpsimd", "n_addc", lambda e: e.dma_start(out=addc[:], in_=C.c_addc), writes=["n_addc"])
            S.dma("sync", "n_ident", lambda e: e.dma_start(out=ident[:], in_=C.c_ident), writes=["n_ident"])
            S.dma("sync", "n_valid", lambda e: e.dma_start(out=valid[:], in_=C.c_valid), writes=["n_valid"])
            smi = [0]

            def score_tile(mm_fn, mm_reads, dti, scal, bias_ap):
                dkey = "n_dtabc" if dti == 9 else "n_dtab"
                i = smi[0] % 2
                j = smi[0] % 3
                smi[0] += 1
                ps, pk = C.nextps()
                mm_fn(ps, pk)
                S.op("vector", lambda e: e.scalar_tensor_tensor(out=sm[i][:], in0=dtab[:, dti, :], scalar=scal, in1=ps[:], op0=ALU.mult, op1=ALU.add),
                     reads=[pk, dkey], writes=[("n_sm", i)])
                if bias_ap is None:
                    S.op("scalar", lambda e: e.activation(out=pT[j][:], in_=sm[i][:], func=AF.Exp), reads=[("n_sm", i)], writes=[("n_pT", j)])
                else:
                    S.op("scalar", lambda e: e.activation(out=pT[j][:], in_=sm[i][:], func=AF.Exp, bias=bias_ap), reads=[("n_sm", i), "n_biasc"], writes=[("n_pT", j)])
                return pT[j], ("n_pT", j)

            def run_jobs(jobs, LA=2):
                staged = []
                for idx in range(len(jobs) + LA):
                    if idx < len(jobs):
                        jb = jobs[idx]
                        staged.append(score_tile(jb["mm"], None, jb["dti"], jb["scal"], jb["bias"]))
                    k = idx - LA
                    if k >= 0:
                        jobs[k]["pv"](*staged[k])

            def accum_out(po, pok, ncol, otok, okey, r, gate_col, qt, first):
                po3 = po[:, 0:4 * ncol].rearrange("p (s c) -> p s c", c=ncol)
                S.op("vector", lambda e: e.tensor_scalar(out=rden[:], in0=po3[:, :, 64:65], scalar1=1e-30, scalar2=None, op0=ALU.max), reads=[pok], writes=["n_rden"])
                S.op("vector", lambda e: e.reciprocal(out=rden[:], in_=rden[:]), reads=["n_rden"], writes=["n_rden"])
                if first and qt == 0:
                    S.op("vector", lambda e: e.tensor_tensor(out=rden[:], in0=rden[:], in1=valid[:, 0:4, :], op=ALU.mult), reads=["n_rden", "n_valid"], writes=["n_rden"])
                S.op("vector", lambda e: e.tensor_tensor(out=ff[:], in0=rden[:], in1=gts[:, qt * 4:(qt + 1) * 4, gate_col:gate_col + 1], op=ALU.mult),
                     reads=["n_rden"] + [("n_gts", qt * 4 + i) for i in range(4)], writes=["n_ff"])
                dst = otok[:, :, r * 64:(r + 1) * 64]
                if first:
                    S.op("vector", lambda e: e.tensor_tensor(out=dst, in0=po3[:, :, 0:64], in1=ff[:].to_broadcast([128, 4, 64]), op=ALU.mult),
                         reads=[pok, "n_ff"], writes=[(okey, r)])
                else:
                    S.op("vector", lambda e: e.tensor_tensor(out=otmp[:], in0=po3[:, :, 0:64], in1=ff[:].to_broadcast([128, 4, 64]), op=ALU.mult),
                         reads=[pok, "n_ff"], writes=["n_otmp"])
                    S.op("vector", lambda e: e.tensor_tensor(out=dst, in0=dst, in1=otmp[:], op=ALU.add), reads=["n_otmp", (okey, r)], writes=[(okey, r)])

            for qt in range(NTT):
                qs = tsl(qt)
                S.dma("sync", "n_dtabc", lambda e: e.dma_start(out=dtab[:, 9, :], in_=C.c_dtab[:, 9 + qt, :]), writes=["n_dtabc"])
                for g in range(4):
                    half = g % 2
                    rows = slice(half * 64, half * 64 + 64)
                    c2 = g // 2
                    otok = otoks[0]
                    okey = "n_otok0"
                    jobs = []
                    for r in range(4):
                        hh = g * 4 + r
                        mq = (g // 2) * 4 + r

                        def mm(ps, pk, mq=mq):
                            S.op("tensor", lambda e: e.matmul(ps[:], lhsT=KCT[rows, g, :], rhs=QT[rows, mq, qs], start=True, stop=True),
                                 reads=[("n_KCT", g), ("n_QT", mq, qt)], writes=[pk])

                        def pv(p_t, p_k, r=r, hh=hh):
                            po, pok = nextpo()
                            for sub in range(4):
                                S.op("tensor", lambda e: e.matmul(po[:, sub * 97:(sub + 1) * 97], lhsT=p_t[:, sub * 128:(sub + 1) * 128], rhs=VCa[:, g, :], start=True, stop=True),
                                     reads=[p_k, ("n_VCa", g)], writes=[pok])
                            accum_out(po, pok, 97, otok, okey, r, hh, qt, True)
                            po3 = po[:, 0:388].rearrange("p (s c) -> p s c", c=97)
                            if r == 0:
                                S.op("vector", lambda e: e.tensor_tensor(out=imp[:], in0=po3[:, :, 65:97], in1=rden[:].to_broadcast([128, 4, 32]), op=ALU.mult),
                                     reads=[pok, "n_rden"], writes=["n_imp"])
                            else:
                                S.op("vector", lambda e: e.tensor_tensor(out=itmp[:], in0=po3[:, :, 65:97], in1=rden[:].to_broadcast([128, 4, 32]), op=ALU.mult),
                                     reads=[pok, "n_rden"], writes=["n_itmp"])
                                S.op("vector", lambda e: e.tensor_tensor(out=imp[:], in0=imp[:], in1=itmp[:], op=ALU.add), reads=["n_itmp", "n_imp"], writes=["n_imp"])
                        jobs.append(dict(mm=mm, dti=9, scal=-SLOPES[hh] / 2.0, bias=None, pv=pv))
                    run_jobs(jobs)
                    S.op("vector", lambda e: e.tensor_tensor(out=imp[:], in0=imp[:], in1=keep[:, qt * 4:(qt + 1) * 4, :], op=ALU.mult), reads=["n_imp", "n_keep"], writes=["n_imp"])
                    S.op("vector", lambda e: e.tensor_tensor(out=imp[:], in0=imp[:], in1=addc[:, qt * 4:(qt + 1) * 4, :], op=ALU.add), reads=["n_imp", "n_addc"], writes=["n_imp"])
                    for sub in range(4):
                        S.op("vector", lambda e: e.max(out=top8[:, sub, :], in_=imp[:, sub, :]), reads=["n_imp"], writes=["n_top8"])
                    for sub in range(4):
                        S.op("vector", lambda e: e.tensor_scalar(out=selb[:, sub, :], in0=imp[:, sub, :], scalar1=top8[:, sub, 7:8], scalar2=-BIGNEG, op0=ALU.is_lt, op1=ALU.mult),
                             reads=["n_imp", "n_top8"], writes=["n_selb"])
                    ps, pk = C.nextps()
                    for sub in range(4):
                        S.op("tensor", lambda e: e.transpose(out=ps[0:32, sub * 128:(sub + 1) * 128], in_=selb[:, sub, :], identity=ident[:]),
                             reads=["n_selb", "n_ident"], writes=[pk])
                    S.op("scalar", lambda e: e.activation(out=selbT[:, g, :], in_=ps[0:32, :], func=AF.Identity), reads=[pk], writes=[("n_selbT", g)])
                    jobs = []
                    for r in range(4):
                        hh = g * 4 + r
                        mq = (g // 2) * 4 + r
                        for br in range(2):
                            kts = list(range(0, qt * 4 + 4)) if br == 0 else list(range(max(0, qt * 4 - 4), qt * 4 + 4))
                            state = {}
                            for n_k, kt in enumerate(kts):
                                delta = qt * TT - kt * 128
                                bias_ap = None
                                if delta <= 0:
                                    dti = (-delta) // 128
                                elif br == 1:
                                    dti = 3 + delta // 128
                                else:
                                    dti = 8
                                    bias_ap = biasc[:, hh, delta // 128:delta // 128 + 1]

                                def mm(ps, pk, br=br, kt=kt, mq=mq):
                                    ks = slice(kt * 128, (kt + 1) * 128)
                                    S.op("tensor", lambda e: e.matmul(ps[:], lhsT=K12[rows, br, c2, ks], rhs=QT[rows, mq, qs], start=True, stop=(br == 1)),
                                         reads=[("n_K12", br, c2, kt // 4), ("n_QT", mq, qt)], writes=[pk])
                                    if br == 0:
                                        S.op("tensor", lambda e: e.matmul(ps[:], lhsT=etab[:, kt, :], rhs=selbT[:, g, :], start=False, stop=True),
                                             reads=["n_etab", ("n_selbT", g)], writes=[pk])

                                def pv(p_t, p_k, br=br, kt=kt, n_k=n_k, nk=len(kts), state=state, r=r, hh=hh):
                                    if n_k == 0:
                                        state["po"] = nextpo()
                                    po, pok = state["po"]
                                    for sub in range(4):
                                        S.op("tensor", lambda e: e.matmul(po[:, sub * 65:(sub + 1) * 65], lhsT=p_t[:, sub * 128:(sub + 1) * 128], rhs=Vaug[:, br, kt, g, :],
                                                                          start=(n_k == 0 and sub == 0), stop=(n_k == nk - 1 and sub == 3)),
                                             reads=[p_k, ("n_Vaug", br, kt), "n_Vone"], writes=[pok])
                                    if n_k == nk - 1:
                                        accum_out(po, pok, 65, otok, okey, r, (1 + br) * 16 + hh, qt, False)
                                jobs.append(dict(mm=mm, dti=dti, scal=-SLOPES[hh], bias=bias_ap, pv=pv))
                    run_jobs(jobs)
                    for kk in range(2):
                        kc = 2 * g + kk
                        ps, pk = C.nextps()
                        for sub in range(4):
                            S.op("tensor", lambda e: e.transpose(out=ps[:, sub * 128:(sub + 1) * 128], in_=otok[:, sub, kk * 128:(kk + 1) * 128], identity=ident[:]),
                                 reads=[(okey, 2 * kk), (okey, 2 * kk + 1), "n_ident"], writes=[pk])
                        S.op("scalar", lambda e: e.activation(out=C.xn[:, kc, qs], in_=ps[:], func=AF.Identity), reads=[pk], writes=[("xn", kc, qt)])
            S.barrier()

        pA.close()
        alloc_wch(ph)
        for m in range(KC):
            wt, wk = load_wch(16 + m)
            for tt in range(NTT):
                def ev(ps, pk):
                    S.op("vector", lambda e: e.tensor_tensor(out=C.hT[:, m, tsl(tt)], in0=C.hT[:, m, tsl(tt)], in1=ps[:], op=ALU.add),
                         reads=[pk, ("hT", m, tt)], writes=[("hT", m, tt)])
                inproj(C, (wt, wk), 0, tt, ev)
        S.barrier()
    C.nrot = 8


def prep_weights(inp):
    f = np.ascontiguousarray
    w = {}
    g = np.stack([inp["lru_norm_g"][0], inp["ffn_norm_g"][0], inp["nsa_norm_g"][0], inp["ffn_norm_g"][1], inp["final_norm_g"]], 0)
    w["gains"] = f(g.reshape(5, KC, 128).transpose(2, 0, 1))
    wi = inp["lru_w_in"][0].reshape(KC, 128, 2, LN, 128)
    w["lru_win"] = f(wi.transpose(3, 1, 0, 2, 4).reshape(LN, 128, KC, 256))
    w["lru_gw"] = f(inp["lru_gate_w"][0].transpose(1, 2, 0, 3))
    w["lru_wout"] = f(inp["lru_w_out"][0].reshape(LN, 128, D))
    vec = np.concatenate([inp["lru_conv_w"][0], inp["lru_conv_b"][0][None], inp["lru_gate_b"][0], inp["lru_a_param"][0][None]], 0)
    w["lru_vec"] = f(vec.reshape(8, LN, 128).transpose(2, 1, 0))
    fw = inp["ffn_w_in"].reshape(2, KC, 128, 2, FJ, 128)
    w["ffn_win"] = f(fw.transpose(0, 4, 2, 1, 3, 5).reshape(2, FJ, 128, KC, 256))
    w["ffn_wout"] = f(inp["ffn_w_out"].reshape(2, FJ, 128, D))
    fv = np.concatenate([inp["ffn_conv_w"], inp["ffn_conv_b"][:, None]], 1)
    w["ffn_vec"] = f(fv.reshape(2, 4, FJ, 128).transpose(3, 0, 2, 1))
    w.update(nsa_host(inp))
    w.update(const_tables())
    return w


def nsa_host(inp):
    f = np.ascontiguousarray
    w = {}
    W = inp["nsa_w_in"][0]
    Wr = W.reshape(KC, 128, 2608)

    def chunk(cols):
        return Wr[:, :, cols].transpose(1, 0, 2)
    chunks = []
    for c4 in range(4):
        kvi, gp = c4 // 2, c4 % 2
        base = 1024 + kvi * 256 + gp * 128
        chunks.append(chunk(np.arange(base, base + 128)))
    for mq in range(8):
        p, r = mq // 4, mq % 4
        ha, hb = 4 * (2 * p) + r, 4 * (2 * p + 1) + r
        chunks.append(chunk(np.concatenate([np.arange(ha * 64, ha * 64 + 64), np.arange(hb * 64, hb * 64 + 64)])))
    for br in (1, 2):
        for c2 in range(2):
            base = 1024 + br * 512 + c2 * 128
            chunks.append(chunk(np.arange(base, base + 128)))
    Wo = inp["nsa_w_out"][0].reshape(KC, 128, D)
    for m in range(KC):
        chunks.append(Wo[:, :, m * 128:(m + 1) * 128].transpose(1, 0, 2))
    w["nsa_wch"] = f(np.stack(chunks, 0))
    tokcols = np.concatenate([np.arange(1024 + 512 + 256, 1024 + 512 + 512), np.arange(1024 + 1024 + 256, 1024 + 1024 + 512), np.arange(2560, 2608)])
    w["nsa_wtok"] = f(Wr[:, :, tokcols].transpose(1, 0, 2))
    w1 = inp["nsa_cmp_w1"][0].reshape(2, 32, 64, 256).transpose(0, 2, 1, 3)
    w["nsa_w1"] = f(np.concatenate([w1, w1], 1))
    pT = inp["nsa_cmp_pos"][0].transpose(2, 0, 1)
    w["nsa_posT"] = f(np.concatenate([pT, pT], 0))
    w["nsa_b1"] = f(inp["nsa_cmp_b1"][0].reshape(2, 2, 128).transpose(2, 0, 1))
    w2 = inp["nsa_cmp_w2"][0]
    w2k = w2[0].reshape(2, 128, 64).transpose(1, 0, 2)
    w["nsa_w2k"] = f(np.concatenate([w2k, w2k], 2))
    w["nsa_w2v"] = f(w2[1].reshape(2, 128, 64).transpose(1, 0, 2))
    return w


def const_tables():
    c = {}
    HUGE = 30000
    k = np.arange(128)[:, None]
    q = np.arange(TT)[None, :]
    dt = np.zeros((128, 13, TT), np.int64)
    for i in range(4):
        d = -128 * i + q - k
        dt[:, i] = np.where(d >= 0, d, HUGE)
    for i in range(1, 5):
        d = 128 * i + q - k
        dt[:, 3 + i] = np.where(d < 512, d, HUGE)
    dt[:, 8] = q - k
    cc = np.arange(128)[:, None]
    for qt in range(4):
        t = qt * TT + q
        d2 = 2 * t - 32 * cc - 31
        ok = (16 * cc + 31 <= t) & (cc < 127)
        dt[:, 9 + qt] = np.where(ok, d2, HUGE)
    c["c_dtab"] = dt.astype(np.int16)
    et = np.zeros((32, 16, 128), np.float32)
    for kt in range(16):
        for kk in range(128):
            et[(kt * 128 + kk) // 64, kt, kk] = 1.0
    c["c_etab"] = et
    sl = np.array(SLOPES, np.float64)
    bc = -(sl[:, None] * (128.0 * np.arange(16))[None, :])
    c["c_biasc"] = np.ascontiguousarray(np.broadcast_to(bc[None], (128, 16, 16))).astype(np.float32)
    t = (np.arange(16)[None, :, None] * 128 + np.arange(128)[:, None, None])
    j = np.arange(32)[None, None, :]
    cur = t // 64
    forced = (j == 0) | (j == cur) | (j == cur - 1)
    future = j > cur
    c["c_keep"] = np.where(forced | future, 0.0, 1.0).astype(np.float32)
    c["c_addc"] = np.where(forced, 1e4, np.where(future, -1.0, 0.0)).astype(np.float32)
    c["c_ident"] = np.eye(128, dtype=np.float32)
    c["c_valid"] = (t >= 31).astype(np.float32)
    ov = np.zeros((128, 33), np.float32)
    ov[:127, 0] = 1.0
    cs = np.arange(127)[:, None] * 16
    sj = np.arange(32)[None, :]
    ov[:127, 1:] = ((cs < (sj + 1) * 64) & (cs + 32 > sj * 64)).astype(np.float32)
    c["c_ovl"] = ov
    return c


_CACHE = {}


def kernel(**inp):
    inp = {k: np.asarray(v) for k, v in inp.items()}
    ncores, nseq = 8, 2
    if "nc" not in _CACHE:
        _CACHE["nc"] = build(nseq)[0]
    nc = _CACHE["nc"]
    w = prep_weights(inp)
    x = inp["x"]
    xT = np.ascontiguousarray(x.reshape(ncores, nseq, NT, KC, 128).transpose(0, 1, 4, 3, 2))
    in_maps = [dict(w, xT=xT[c]) for c in range(ncores)]
    res = run_bass_kernel_spmd(nc, in_maps, core_ids=list(range(ncores)))
    o = np.stack([r["outT"] for r in res.results], 0)
    return np.ascontiguousarray(o.transpose(0, 1, 4, 3, 2)).reshape(16, NT, D).astype(np.float32)
```

```python
import contextlib
import numpy as np
import concourse.bass as bass
import concourse.mybir as mybir
from concourse.bass_utils import run_bass_kernel_spmd

F32 = mybir.dt.float32
BF16 = mybir.dt.bfloat16
AF = mybir.ActivationFunctionType
ALU = mybir.AluOpType
AX = mybir.AxisListType

D = 1024
KC = 8
NT = 2048
TT = 512
NTT = 4
LW = 1280
LN = 10
DFF = 3072
FJ = 24
EPS = 1e-6
GELU_K = 1.5957691216057308


class Sched:
    ENGS = ("tensor", "vector", "scalar", "gpsimd", "sync")

    def __init__(self, nc, stack):
        self.nc = nc
        self.stack = stack
        self.eng = {e: getattr(nc, e) for e in self.ENGS}
        self.sem = {}
        self.cnt = {}
        self.known = {e: {} for e in self.ENGS}
        self.res = {}
        self.n_inst = 0
        self.n_wait = 0
        for e in ("tensor", "vector", "scalar", "gpsimd"):
            self._mksem(e)

    def _mksem(self, key):
        if key not in self.sem:
            name = "s_" + key.replace(":", "_")
            self.sem[key] = self.stack.enter_context(self.nc.semaphore(name))
            self.cnt[key] = 0
        return self.sem[key]

    def _deps(self, engine, reads, writes):
        deps = {}

        def add(k, v):
            if v > deps.get(k, 0):
                deps[k] = v
        for r in reads:
            st = self.res.get(r)
            if st and st["w"]:
                add(*st["w"])
        for w in writes:
            st = self.res.get(w)
            if st:
                if st["w"]:
                    add(*st["w"])
                for k, v in st["r"].items():
                    add(k, v)
        kn = self.known[engine]
        for k, v in deps.items():
            if k == engine and engine == "tensor":
                continue
            if kn.get(k, 0) >= v:
                continue
            self.eng[engine].wait_ge(self.sem[k], v)
            self.n_wait += 1
            kn[k] = v

    def _mark(self, key, val, reads, writes):
        for r in reads:
            st = self.res.setdefault(r, {"w": None, "r": {}})
            st["r"][key] = val
        for w in writes:
            self.res[w] = {"w": (key, val), "r": {}}

    def op(self, engine, fn, reads=(), writes=()):
        self._deps(engine, reads, writes)
        ins = fn(self.eng[engine])
        self.cnt[engine] += 1
        ins.then_inc(self.sem[engine], 1)
        self.n_inst += 1
        self._mark(engine, self.cnt[engine], reads, writes)
        return ins

    def dma(self, queue, key, fn, reads=(), writes=()):
        k = "dma:" + key
        self._mksem(k)
        self._deps(queue, reads, writes)
        ins = fn(self.eng[queue])
        self.cnt[k] += 16
        ins.then_inc(self.sem[k], 16)
        self.n_inst += 1
        self._mark(k, self.cnt[k], reads, writes)
        return ins

    def barrier(self):
        for e in self.ENGS:
            for k, v in self.cnt.items():
                if v == 0 or (k == e and e == "tensor"):
                    continue
                if self.known[e].get(k, 0) >= v:
                    continue
                self.eng[e].wait_ge(self.sem[k], v)
                self.known[e][k] = v

    def finish(self):
        for k, v in self.cnt.items():
            if v and self.known["sync"].get(k, 0) < v:
                self.nc.sync.wait_ge(self.sem[k], v)
                self.known["sync"][k] = v


class Ctx:
    pass


def tsl(tt):
    return slice(tt * TT, (tt + 1) * TT)


def build(nseq=2, upto=99):
    nc = bass.Bass("TRN2", target_bir_lowering=False)
    C = Ctx()
    C.nc = nc

    def din(name, shape):
        return nc.dram_tensor(name, list(shape), F32, kind="ExternalInput").ap()
    C.xT = din("xT", [nseq, 128, KC, NT])
    C.gains = din("gains", [128, 5, KC])
    C.lru_win = din("lru_win", [LN, 128, KC, 256])
    C.lru_gw = din("lru_gw", [LN, 128, 2, 128])
    C.lru_wout = din("lru_wout", [LN, 128, D])
    C.lru_vec = din("lru_vec", [128, LN, 8])
    C.ffn_win = din("ffn_win", [2, FJ, 128, KC, 256])
    C.ffn_wout = din("ffn_wout", [2, FJ, 128, D])
    C.ffn_vec = din("ffn_vec", [128, 2, FJ, 4])
    C.nsa_wch = din("nsa_wch", [24, 128, KC, 128])
    C.nsa_wtok = din("nsa_wtok", [128, KC, 560])
    C.nsa_w1 = din("nsa_w1", [2, 128, 32, 256])
    C.nsa_posT = din("nsa_posT", [128, 2, 32])
    C.nsa_b1 = din("nsa_b1", [128, 2, 2])
    C.nsa_w2k = din("nsa_w2k", [128, 2, 128])
    C.nsa_w2v = din("nsa_w2v", [128, 2, 64])
    C.c_dtab = nc.dram_tensor("c_dtab", [128, 13, TT], mybir.dt.int16, kind="ExternalInput").ap()
    C.c_etab = din("c_etab", [32, 16, 128])
    C.c_biasc = din("c_biasc", [128, 16, 16])
    C.c_keep = din("c_keep", [128, 16, 32])
    C.c_addc = din("c_addc", [128, 16, 32])
    C.c_ident = din("c_ident", [128, 128])
    C.c_valid = din("c_valid", [128, 16, 1])
    C.c_ovl = din("c_ovl", [128, 33])
    C.outT = nc.dram_tensor("outT", [nseq, 128, KC, NT], F32, kind="ExternalOutput").ap()

    with contextlib.ExitStack() as st:
        S = Sched(nc, st)
        C.S = S

        uid = [0]

        def sb(name, shape, dt=F32, stack=st):
            uid[0] += 1
            return stack.enter_context(nc.sbuf_tensor("%s_u%d" % (name, uid[0]), list(shape), dt))
        C.sb = sb
        C.hT = sb("hT", [128, KC, NT])
        C.xn = sb("xn", [128, KC, NT], BF16)
        C.ones = sb("ones", [128, 128])
        C.gn = sb("gn", [128, 5, KC])
        C.lvec = sb("lvec", [128, LN, 8])
        C.lca = sb("lca", [128, LN, 2])
        C.fvec = sb("fvec", [128, 2, FJ, 4])
        C.psum = [st.enter_context(nc.psum_tensor("ps%d" % i, [128, TT], F32)) for i in range(8)]
        C.psi = 0
        C.nrot = 8

        def nextps():
            i = C.psi % C.nrot
            C.psi += 1
            return C.psum[i], ("ps", i)
        C.nextps = nextps

        S.op("vector", lambda e: e.memset(C.ones[:], 1.0), writes=["ones"])
        S.dma("sync", "c0", lambda e: e.dma_start(out=C.gn[:], in_=C.gains), writes=["gn"])
        S.dma("sync", "c1", lambda e: e.dma_start(out=C.lvec[:], in_=C.lru_vec), writes=["lvec"])
        S.dma("sync", "c2", lambda e: e.dma_start(out=C.fvec[:], in_=C.ffn_vec), writes=["fvec"])
        lru_consts(C)

        for s in range(nseq):
            for kc in range(KC):
                S.dma("sync", "x%d" % kc, lambda e, kc=kc: e.dma_start(out=C.hT[:, kc, :], in_=C.xT[s, :, kc, :]),
                      writes=[("hT", kc, tt) for tt in range(NTT)])
            if upto >= 1:
                rmsnorm(C, 0)
                lru_mixer(C)
            if upto >= 2:
                rmsnorm(C, 1)
                conv_ffn(C, 0)
            if upto >= 3:
                rmsnorm(C, 2)
                nsa_mixer(C)
            if upto >= 4:
                rmsnorm(C, 3)
                conv_ffn(C, 1)
            if upto >= 5:
                rmsnorm(C, 4, final=True)
            for kc in range(KC):
                S.dma("sync", "o%d" % kc, lambda e, kc=kc: e.dma_start(out=C.outT[s, :, kc, :], in_=C.hT[:, kc, :]),
                      reads=[("hT", kc, tt) for tt in range(NTT)])
        S.finish()
    C.n_inst = S.n_inst
    C.n_wait = S.n_wait
    return nc, C


def lru_consts(C):
    S, sb = C.S, C.sb
    with contextlib.ExitStack() as ph:
        t = [sb("lc%d" % i, [128, LN], F32, ph) for i in range(6)]
        ap = C.lvec[:, :, 7]
        S.op("scalar", lambda e: e.activation(out=t[0][:], in_=ap, func=AF.Abs), reads=["lvec"], writes=["lc0"])
        S.op("scalar", lambda e: e.activation(out=t[1][:], in_=t[0][:], func=AF.Exp, scale=-1.0), reads=["lc0"], writes=["lc1"])
        S.op("scalar", lambda e: e.activation(out=t[2][:], in_=t[1][:], func=AF.Ln, bias=1.0), reads=["lc1"], writes=["lc2"])
        S.op("vector", lambda e: e.tensor_scalar(out=t[3][:], in0=t[1][:], scalar1=1.0 / 3.0, scalar2=-0.5, op0=ALU.mult, op1=ALU.add), reads=["lc1"], writes=["lc3"])
        S.op("vector", lambda e: e.tensor_tensor(out=t[3][:], in0=t[3][:], in1=t[1][:], op=ALU.mult), reads=["lc3", "lc1"], writes=["lc3"])
        S.op("vector", lambda e: e.tensor_scalar(out=t[3][:], in0=t[3][:], scalar1=1.0, scalar2=None, op0=ALU.add), reads=["lc3"], writes=["lc3"])
        S.op("vector", lambda e: e.tensor_tensor(out=t[3][:], in0=t[3][:], in1=t[1][:], op=ALU.mult), reads=["lc3", "lc1"], writes=["lc3"])
        S.op("vector", lambda e: e.tensor_single_scalar(out=t[4][:], in_=t[1][:], scalar=0.03, op=ALU.is_lt), reads=["lc1"], writes=["lc4"])
        S.op("vector", lambda e: e.tensor_tensor(out=t[3][:], in0=t[3][:], in1=t[2][:], op=ALU.subtract), reads=["lc3", "lc2"], writes=["lc3"])
        S.op("vector", lambda e: e.tensor_tensor(out=t[3][:], in0=t[3][:], in1=t[4][:], op=ALU.mult), reads=["lc3", "lc4"], writes=["lc3"])
        S.op("vector", lambda e: e.tensor_tensor(out=t[3][:], in0=t[3][:], in1=t[2][:], op=ALU.add), reads=["lc3", "lc2"], writes=["lc3"])
        S.op("vector", lambda e: e.tensor_scalar(out=t[5][:], in0=ap, scalar1=-1.0, scalar2=0.0, op0=ALU.mult, op1=ALU.max), reads=["lvec"], writes=["lc5"])
        S.op("vector", lambda e: e.tensor_tensor(out=t[3][:], in0=t[3][:], in1=t[5][:], op=ALU.add), reads=["lc3", "lc5"], writes=["lc3"])
        S.op("vector", lambda e: e.tensor_scalar(out=C.lca[:, :, 0], in0=t[3][:], scalar1=-8.0, scalar2=None, op0=ALU.mult), reads=["lc3"], writes=["lca"])
        S.op("vector", lambda e: e.tensor_scalar(out=C.lca[:, :, 1], in0=t[3][:], scalar1=-16.0, scalar2=None, op0=ALU.mult), reads=["lc3"], writes=["lca"])
        S.barrier()


def rmsnorm(C, gi, final=False):
    S = C.S
    ph = contextlib.ExitStack()
    C.sq = [C.sb("sq%d" % i, [128, TT], F32, ph) for i in range(2)]
    C.rs = C.sb("rs", [128, TT], F32, ph)
    for tt in range(NTT):
        ps, pk = C.nextps()
        for kc in range(KC):
            sq = C.sq[kc % 2]
            S.op("scalar", lambda e: e.activation(out=sq[:], in_=C.hT[:, kc, tsl(tt)], func=AF.Square),
                 reads=[("hT", kc, tt)], writes=[("sq", kc % 2)])
            S.op("tensor", lambda e: e.matmul(ps[:], lhsT=C.ones[:], rhs=sq[:], start=(kc == 0), stop=(kc == KC - 1)),
                 reads=["ones", ("sq", kc % 2)], writes=[pk])
        S.op("vector", lambda e: e.tensor_scalar(out=C.rs[:], in0=ps[:], scalar1=1.0 / D, scalar2=EPS, op0=ALU.mult, op1=ALU.add),
             reads=[pk], writes=["rs"])
        S.op("scalar", lambda e: e.activation(out=C.rs[:], in_=C.rs[:], func=AF.Sqrt), reads=["rs"], writes=["rs"])
        S.op("vector", lambda e: e.reciprocal(out=C.rs[:], in_=C.rs[:]), reads=["rs"], writes=["rs"])
        for kc in range(KC):
            if final:
                S.op("vector", lambda e: e.scalar_tensor_tensor(out=C.hT[:, kc, tsl(tt)], in0=C.hT[:, kc, tsl(tt)], scalar=C.gn[:, gi, kc:kc + 1],
                                                               in1=C.rs[:], op0=ALU.mult, op1=ALU.mult),
                     reads=[("hT", kc, tt), "rs", "gn"], writes=[("hT", kc, tt)])
            else:
                S.op("vector", lambda e: e.scalar_tensor_tensor(out=C.xn[:, kc, tsl(tt)], in0=C.hT[:, kc, tsl(tt)], scalar=C.gn[:, gi, kc:kc + 1],
                                                               in1=C.rs[:], op0=ALU.mult, op1=ALU.mult),
                     reads=[("hT", kc, tt), "rs", "gn"], writes=[("xn", kc, tt)])
    S.barrier()
    ph.close()


def inproj(C, w, col0, tt, evac):
    S = C.S
    ps, pk = C.nextps()
    wt, wk = w
    for kc in range(KC):
        S.op("tensor", lambda e: e.matmul(ps[:], lhsT=wt[:, kc, col0:col0 + 128], rhs=C.xn[:, kc, tsl(tt)], start=(kc == 0), stop=(kc == KC - 1)),
             reads=[wk, ("xn", kc, tt)], writes=[pk])
    evac(ps, pk)


def gelu_inplace(C, x, xk, t1, t1k, tt):
    S = C.S
    sl = tsl(tt)
    S.op("scalar", lambda e: e.activation(out=t1[:, sl], in_=x[:, sl], func=AF.Square), reads=[(xk, tt)], writes=[(t1k, tt)])
    S.op("vector", lambda e: e.tensor_scalar(out=t1[:, sl], in0=t1[:, sl], scalar1=0.044715, scalar2=1.0, op0=ALU.mult, op1=ALU.add),
         reads=[(t1k, tt)], writes=[(t1k, tt)])
    S.op("vector", lambda e: e.tensor_tensor(out=t1[:, sl], in0=t1[:, sl], in1=x[:, sl], op=ALU.mult), reads=[(t1k, tt), (xk, tt)], writes=[(t1k, tt)])
    S.op("scalar", lambda e: e.activation(out=t1[:, sl], in_=t1[:, sl], func=AF.Sigmoid, scale=GELU_K), reads=[(t1k, tt)], writes=[(t1k, tt)])
    S.op("vector", lambda e: e.tensor_tensor(out=x[:, sl], in0=x[:, sl], in1=t1[:, sl], op=ALU.mult), reads=[(t1k, tt), (xk, tt)], writes=[(xk, tt)])


def outproj_group(C, acts, wouts):
    S = C.S
    n = len(acts)
    for tt in range(NTT):
        for m in range(KC):
            ps, pk = C.nextps()
            for i in range(n):
                a_ap, a_k = acts[i]
                wt, wk = wouts[i]
                S.op("tensor", lambda e: e.matmul(ps[:], lhsT=wt[:, m * 128:(m + 1) * 128], rhs=a_ap(tt), start=(i == 0), stop=(i == n - 1)),
                     reads=[wk, a_k(tt)], writes=[pk])
            S.op("vector", lambda e: e.tensor_tensor(out=C.hT[:, m, tsl(tt)], in0=C.hT[:, m, tsl(tt)], in1=ps[:], op=ALU.add),
                 reads=[pk, ("hT", m, tt)], writes=[("hT", m, tt)])


def lru_mixer(C):
    S, sb, nc = C.S, C.sb, C.nc
    G = 2
    with contextlib.ExitStack() as ph:
        win = [sb("l_win%d" % i, [128, KC, 256], BF16, ph) for i in range(2)]
        gw = [sb("l_gw%d" % i, [128, 2, 128], BF16, ph) for i in range(2)]
        wo = [sb("l_wo%d" % i, [128, D], BF16, ph) for i in range(2 * G)]
        hy = [sb("l_hy%d" % i, [128, NT], BF16, ph) for i in range(2 * G)]
        ysbs = [sb("l_ysb%d" % i, [128, NT], F32, ph) for i in range(2)]
        xpads = [sb("l_xpad%d" % i, [128, 3 + NT], F32, ph) for i in range(2)]
        t1 = sb("l_t1", [128, NT], F32, ph)
        xc = sb("l_xc", [128, NT], F32, ph)
        xcb = sb("l_xcb", [128, NT], BF16, ph)
        rr = sb("l_r", [128, NT], F32, ph)
        ig = sb("l_ig", [128, NT], F32, ph)
        hh = sb("l_h", [128, NT], F32, ph)
        for i in range(2):
            S.op("vector", lambda e: e.memset(xpads[i][:, 0:3], 0.0), writes=["l_xpad%d_0" % i])

        def load_and_proj(n):
            b = n % 2
            wi, gwi, woi = win[b], gw[b], wo[n % (2 * G)]
            wik, gwk, wok = "l_win%d" % b, "l_gw%d" % b, "l_wo%d" % (n % (2 * G))
            S.dma("gpsimd", wik, lambda e: e.dma_start(out=wi[:], in_=C.lru_win[n]), writes=[wik])
            S.dma("gpsimd", gwk, lambda e: e.dma_start(out=gwi[:], in_=C.lru_gw[n]), writes=[gwk])
            S.dma("gpsimd", wok, lambda e: e.dma_start(out=woi[:], in_=C.lru_wout[n]), writes=[wok])
            ysb, xpad = ysbs[b], xpads[b]
            for tt in range(NTT):
                inproj(C, (wi, wik), 128, tt, lambda ps, pk: S.op(
                    "scalar", lambda e: e.activation(out=xpad[:, 3 + tt * TT:3 + (tt + 1) * TT], in_=ps[:], func=AF.Identity),
                    reads=[pk], writes=[("l_xpad%d" % b, tt)]))
            for tt in range(NTT):
                inproj(C, (wi, wik), 0, tt, lambda ps, pk: S.op(
                    "scalar", lambda e: e.activation(out=ysb[:, tsl(tt)], in_=ps[:], func=AF.Identity), reads=[pk], writes=[("l_ysb%d" % b, tt)]))

        pending = None
        load_and_proj(0)
        for n in range(LN):
            b = n % 2
            gwi, gwk = gw[b], "l_gw%d" % b
            hyi, hyk = hy[n % (2 * G)], "l_hy%d" % (n % (2 * G))
            ysb, xpad = ysbs[b], xpads[b]
            yk, xk = "l_ysb%d" % b, "l_xpad%d" % b
            T4 = range(NTT)
            for tt in T4:
                rd = [(xk, tt), (xk, tt - 1) if tt > 0 else xk + "_0", "lvec"]
                S.op("vector", lambda e: e.tensor_scalar(out=xc[:, tsl(tt)], in0=xpad[:, 3 + tt * TT:3 + (tt + 1) * TT], scalar1=C.lvec[:, n, 3:4],
                                                        scalar2=C.lvec[:, n, 4:5], op0=ALU.mult, op1=ALU.add), reads=rd, writes=[("l_xc", tt)])
                for k in (2, 1, 0):
                    S.op("vector", lambda e: e.scalar_tensor_tensor(out=xc[:, tsl(tt)], in0=xpad[:, k + tt * TT:k + (tt + 1) * TT], scalar=C.lvec[:, n, k:k + 1],
                                                                   in1=xc[:, tsl(tt)], op0=ALU.mult, op1=ALU.add),
                         reads=rd + [("l_xc", tt)], writes=[("l_xc", tt)])
                S.op("scalar", lambda e: e.activation(out=xcb[:, tsl(tt)], in_=xc[:, tsl(tt)], func=AF.Identity), reads=[("l_xc", tt)], writes=[("l_xcb", tt)])
            if n + 1 < LN:
                load_and_proj(n + 1)
            if pending is not None:
                outproj_group(C, *pending)
                pending = None
            for tt in T4:
                S.op("scalar", lambda e: e.activation(out=t1[:, tsl(tt)], in_=ysb[:, tsl(tt)], func=AF.Square), reads=[(yk, tt)], writes=[("l_t1", tt)])
            for g, (dst, dk) in enumerate(((rr, "l_r"), (ig, "l_ig"))):
                for tt in T4:
                    ps, pk = C.nextps()
                    S.op("tensor", lambda e: e.matmul(ps[:], lhsT=gwi[:, g, :], rhs=xcb[:, tsl(tt)], start=True, stop=True),
                         reads=[gwk, ("l_xcb", tt)], writes=[pk])
                    S.op("scalar", lambda e: e.activation(out=dst[:, tsl(tt)], in_=ps[:], func=AF.Sigmoid, bias=C.lvec[:, n, 5 + g:6 + g]),
                         reads=[pk, "lvec"], writes=[(dk, tt)])
            for tt in T4:
                sl = tsl(tt)
                S.op("vector", lambda e: e.tensor_scalar(out=t1[:, sl], in0=t1[:, sl], scalar1=0.044715, scalar2=1.0, op0=ALU.mult, op1=ALU.add),
                     reads=[("l_t1", tt)], writes=[("l_t1", tt)])
            for tt in T4:
                sl = tsl(tt)
                S.op("vector", lambda e: e.tensor_tensor(out=t1[:, sl], in0=t1[:, sl], in1=ysb[:, sl], op=ALU.mult), reads=[("l_t1", tt), (yk, tt)], writes=[("l_t1", tt)])
            for tt in T4:
                sl = tsl(tt)
                S.op("scalar", lambda e: e.activation(out=t1[:, sl], in_=t1[:, sl], func=AF.Sigmoid, scale=GELU_K), reads=[("l_t1", tt)], writes=[("l_t1", tt)])
            for tt in T4:
                sl = tsl(tt)
                S.op("vector", lambda e: e.tensor_tensor(out=ysb[:, sl], in0=ysb[:, sl], in1=t1[:, sl], op=ALU.mult), reads=[("l_t1", tt), (yk, tt)], writes=[(yk, tt)])
            for tt in T4:
                sl = tsl(tt)
                S.op("vector", lambda e: e.tensor_tensor(out=ig[:, sl], in0=ig[:, sl], in1=xc[:, sl], op=ALU.mult), reads=[("l_ig", tt), ("l_xc", tt)], writes=[("l_ig", tt)])
            for tt in T4:
                sl = tsl(tt)
                S.op("scalar", lambda e: e.activation(out=t1[:, sl], in_=rr[:, sl], func=AF.Exp, scale=C.lca[:, n, 1:2]), reads=[("l_r", tt), "lca"], writes=[("l_t1", tt)])
            for tt in T4:
                sl = tsl(tt)
                S.op("scalar", lambda e: e.activation(out=rr[:, sl], in_=rr[:, sl], func=AF.Exp, scale=C.lca[:, n, 0:1]), reads=[("l_r", tt), "lca"], writes=[("l_r", tt)])
            for tt in T4:
                sl = tsl(tt)
                S.op("scalar", lambda e: e.activation(out=t1[:, sl], in_=t1[:, sl], func=AF.Sqrt, scale=-1.0, bias=1.0), reads=[("l_t1", tt)], writes=[("l_t1", tt)])
            for tt in T4:
                sl = tsl(tt)
                S.op("vector", lambda e: e.tensor_tensor(out=ig[:, sl], in0=ig[:, sl], in1=t1[:, sl], op=ALU.mult), reads=[("l_ig", tt), ("l_t1", tt)], writes=[("l_ig", tt)])
            for tt in T4:
                sl = tsl(tt)
                init = 0.0 if tt == 0 else hh[:, tt * TT - 1:tt * TT]
                S.op("vector", lambda e: e.tensor_tensor_scan(out=hh[:, sl], data0=rr[:, sl], data1=ig[:, sl], initial=init, op0=ALU.mult, op1=ALU.add),
                     reads=[("l_r", tt), ("l_ig", tt)] + ([("l_h", tt - 1)] if tt else []), writes=[("l_h", tt)])
                S.op("vector", lambda e: e.tensor_tensor(out=hyi[:, sl], in0=hh[:, sl], in1=ysb[:, sl], op=ALU.mult),
                     reads=[("l_h", tt), (yk, tt)], writes=[(hyk, tt)])
            if n % G == G - 1:
                idx = [(n - G + 1 + i) % (2 * G) for i in range(G)]
                pending = ([((lambda tt, i=i: hy[i][:, tsl(tt)]), (lambda tt, i=i: ("l_hy%d" % i, tt))) for i in idx],
                           [(wo[i], "l_wo%d" % i) for i in idx])
        outproj_group(C, *pending)
        S.barrier()


def conv_ffn(C, L):
    S, sb, nc = C.S, C.sb, C.nc
    G = 4
    with contextlib.ExitStack() as ph:
        win = [sb("f_win%d" % i, [128, KC, 256], BF16, ph) for i in range(2)]
        wo = [sb("f_wo%d" % i, [128, D], BF16, ph) for i in range(2 * G)]
        act = [sb("f_act%d" % i, [128, NT], BF16, ph) for i in range(2 * G)]
        apads = [sb("f_apad%d" % i, [128, 2 + NT], F32, ph) for i in range(2)]
        bsbs = [sb("f_bsb%d" % i, [128, NT], BF16, ph) for i in range(2)]
        ac = sb("f_ac", [128, NT], F32, ph)
        t1 = sb("f_t1", [128, NT], F32, ph)
        for i in range(2):
            S.op("vector", lambda e: e.memset(apads[i][:, 0:2], 0.0), writes=["f_apad%d_0" % i])

        def load_and_proj(j):
            b = j % 2
            wi, woi = win[b], wo[j % (2 * G)]
            wik, wok = "f_win%d" % b, "f_wo%d" % (j % (2 * G))
            S.dma("gpsimd", wik, lambda e: e.dma_start(out=wi[:], in_=C.ffn_win[L, j]), writes=[wik])
            S.dma("gpsimd", wok, lambda e: e.dma_start(out=woi[:], in_=C.ffn_wout[L, j]), writes=[wok])
            apad, bsb = apads[b], bsbs[b]
            for tt in range(NTT):
                inproj(C, (wi, wik), 0, tt, lambda ps, pk: S.op(
                    "scalar", lambda e: e.activation(out=apad[:, 2 + tt * TT:2 + (tt + 1) * TT], in_=ps[:], func=AF.Identity),
                    reads=[pk], writes=[("f_apad%d" % b, tt)]))
            for tt in range(NTT):
                inproj(C, (wi, wik), 128, tt, lambda ps, pk: S.op(
                    "scalar", lambda e: e.activation(out=bsb[:, tsl(tt)], in_=ps[:], func=AF.Identity), reads=[pk], writes=[("f_bsb%d" % b, tt)]))

        pending = None
        load_and_proj(0)
        for j in range(FJ):
            b = j % 2
            acti, actk = act[j % (2 * G)], "f_act%d" % (j % (2 * G))
            apad, bsb = apads[b], bsbs[b]
            ak, bk = "f_apad%d" % b, "f_bsb%d" % b
            T4 = range(NTT)
            if j + 1 < FJ:
                load_and_proj(j + 1)
            if pending is not None:
                outproj_group(C, *pending)
                pending = None
            for tt in T4:
                sl = tsl(tt)
                rd = [(ak, tt), (ak, tt - 1) if tt > 0 else ak + "_0", "fvec"]
                S.op("vector", lambda e: e.tensor_scalar(out=ac[:, sl], in0=apad[:, 2 + tt * TT:2 + (tt + 1) * TT], scalar1=C.fvec[:, L, j, 2:3],
                                                        scalar2=C.fvec[:, L, j, 3:4], op0=ALU.mult, op1=ALU.add), reads=rd, writes=[("f_ac", tt)])
                for k in (1, 0):
                    S.op("vector", lambda e: e.scalar_tensor_tensor(out=ac[:, sl], in0=apad[:, k + tt * TT:k + (tt + 1) * TT], scalar=C.fvec[:, L, j, k:k + 1],
                                                                   in1=ac[:, sl], op0=ALU.mult, op1=ALU.add),
                         reads=rd + [("f_ac", tt)], writes=[("f_ac", tt)])
                S.op("scalar", lambda e: e.activation(out=t1[:, sl], in_=ac[:, sl], func=AF.Square), reads=[("f_ac", tt)], writes=[("f_t1", tt)])
            for tt in T4:
                sl = tsl(tt)
                S.op("vector", lambda e: e.tensor_scalar(out=t1[:, sl], in0=t1[:, sl], scalar1=0.044715, scalar2=1.0, op0=ALU.mult, op1=ALU.add),
                     reads=[("f_t1", tt)], writes=[("f_t1", tt)])
            for tt in T4:
                sl = tsl(tt)
                S.op("vector", lambda e: e.tensor_tensor(out=t1[:, sl], in0=t1[:, sl], in1=ac[:, sl], op=ALU.mult), reads=[("f_t1", tt), ("f_ac", tt)], writes=[("f_t1", tt)])
                S.op("scalar", lambda e: e.activation(out=t1[:, sl], in_=t1[:, sl], func=AF.Sigmoid, scale=GELU_K), reads=[("f_t1", tt)], writes=[("f_t1", tt)])
            for tt in T4:
                sl = tsl(tt)
                S.op("vector", lambda e: e.tensor_tensor(out=ac[:, sl], in0=ac[:, sl], in1=bsb[:, sl], op=ALU.mult), reads=[("f_ac", tt), (bk, tt)], writes=[("f_ac", tt)])
            for tt in T4:
                sl = tsl(tt)
                S.op("vector", lambda e: e.tensor_tensor(out=acti[:, sl], in0=ac[:, sl], in1=t1[:, sl], op=ALU.mult),
                     reads=[("f_ac", tt), ("f_t1", tt)], writes=[(actk, tt)])
            if j % G == G - 1:
                idx = [(j - G + 1 + i) % (2 * G) for i in range(G)]
                pending = ([((lambda tt, i=i: act[i][:, tsl(tt)]), (lambda tt, i=i: ("f_act%d" % i, tt))) for i in idx],
                           [(wo[i], "f_wo%d" % i) for i in idx])
        outproj_group(C, *pending)
        S.barrier()


SLOPES = [2.0 ** (-8.0 * (h + 1) / 16.0) for h in range(16)]
BIGNEG = 30000.0


def gelu_ap(C, x, t, xk, tk):
    S = C.S
    S.op("scalar", lambda e: e.activation(out=t, in_=x, func=AF.Square), reads=[xk], writes=[tk])
    S.op("vector", lambda e: e.tensor_scalar(out=t, in0=t, scalar1=0.044715, scalar2=1.0, op0=ALU.mult, op1=ALU.add), reads=[tk], writes=[tk])
    S.op("vector", lambda e: e.tensor_tensor(out=t, in0=t, in1=x, op=ALU.mult), reads=[tk, xk], writes=[tk])
    S.op("scalar", lambda e: e.activation(out=t, in_=t, func=AF.Sigmoid, scale=GELU_K), reads=[tk], writes=[tk])
    S.op("vector", lambda e: e.tensor_tensor(out=x, in0=x, in1=t, op=ALU.mult), reads=[tk, xk], writes=[xk])


def nsa_mixer(C):
    S, sb, nc = C.S, C.sb, C.nc
    C.nrot = 6
    po_banks = [(C.psum[6], ("ps", 6)), (C.psum[7], ("ps", 7))]
    po_i = [0]

    def nextpo():
        r = po_banks[po_i[0] % 2]
        po_i[0] += 1
        return r
    with contextlib.ExitStack() as ph:
        KCT = sb("n_KCT", [128, 4, 128], BF16, ph)
        VCa = sb("n_VCa", [128, 4, 97], BF16, ph)
        wch = [None, None]
        wci = [0]

        def alloc_wch(stack):
            for i in range(2):
                wch[i] = sb("n_wch%d" % i, [128, KC, 128], BF16, stack)

        def load_wch(idx):
            i = wci[0] % 2
            wci[0] += 1
            k = "n_wch%d" % i
            S.dma("gpsimd", k, lambda e: e.dma_start(out=wch[i][:], in_=C.nsa_wch[idx]), writes=[k])
            return wch[i], k

        with contextlib.ExitStack() as p1:
            alloc_wch(p1)
            KV0 = sb("n_KV0", [128, 4, NT], BF16, p1)
            W1 = sb("n_W1", [128, 2, 32, 256], BF16, p1)
            posT = sb("n_posT", [128, 2, 32], BF16, p1)
            b1 = sb("n_b1", [128, 2, 2], F32, p1)
            W2k = sb("n_W2k", [128, 2, 128], BF16, p1)
            W2v = sb("n_W2v", [128, 2, 64], BF16, p1)
            ovl = sb("n_ovl", [128, 33], F32, p1)
            hid = sb("n_hid", [128, 2, 128], F32, p1)
            hsc = sb("n_hsc", [128, 2, 128], F32, p1)
            ghb = sb("n_ghb", [128, 2, 128], BF16, p1)
            bcol = sb("n_bcol", [128, 2], F32, p1)
            for kvi in range(2):
                S.dma("gpsimd", "n_W1_%d" % kvi, lambda e: e.dma_start(out=W1[:, kvi], in_=C.nsa_w1[kvi]), writes=[("n_W1", kvi)])
            S.dma("gpsimd", "n_posT", lambda e: e.dma_start(out=posT[:], in_=C.nsa_posT), writes=["n_posT"])
            S.dma("sync", "n_b1", lambda e: e.dma_start(out=b1[:], in_=C.nsa_b1), writes=["n_b1"])
            S.dma("gpsimd", "n_W2k", lambda e: e.dma_start(out=W2k[:], in_=C.nsa_w2k), writes=["n_W2k"])
            S.dma("gpsimd", "n_W2v", lambda e: e.dma_start(out=W2v[:], in_=C.nsa_w2v), writes=["n_W2v"])
            S.dma("sync", "n_ovl", lambda e: e.dma_start(out=ovl[:], in_=C.c_ovl), writes=["n_ovl"])
            S.op("vector", lambda e: e.memset(KCT[:], 0.0), writes=[("n_KCT", g) for g in range(4)])
            S.op("vector", lambda e: e.memset(VCa[:], 0.0), writes=[("n_VCa", g) for g in range(4)])
            for g in range(4):
                S.op("vector", lambda e: e.tensor_copy(out=VCa[:, g, 64:97], in_=ovl[:]), reads=["n_ovl"], writes=[("n_VCa", g)])
            for c4 in range(4):
                wt, wk = load_wch(c4)
                for tt in range(NTT):
                    inproj(C, (wt, wk), 0, tt, lambda ps, pk: S.op(
                        "scalar", lambda e: e.activation(out=KV0[:, c4, tsl(tt)], in_=ps[:], func=AF.Identity), reads=[pk], writes=[("n_KV0", c4)]))
            for kvi in range(2):
                for g in range(4):
                    c4 = kvi * 2 + g // 2
                    rows = slice((g % 2) * 64, (g % 2) * 64 + 64)
                    for mh in range(2):
                        ps, pk = C.nextps()
                        for i in range(32):
                            S.op("tensor", lambda e: e.matmul(ps[:, 0:127], lhsT=W1[rows, kvi, i, mh * 128:(mh + 1) * 128],
                                                              rhs=KV0[rows, c4, i:i + 2017:16], start=(i == 0), stop=False),
                                 reads=[("n_W1", kvi), ("n_KV0", c4)], writes=[pk])
                        for i in range(32):
                            S.op("tensor", lambda e: e.matmul(ps[:, 127:128], lhsT=W1[rows, kvi, i, mh * 128:(mh + 1) * 128],
                                                              rhs=posT[rows, kvi, i:i + 1], start=False, stop=(i == 31)),
                                 reads=[("n_W1", kvi), "n_posT"], writes=[pk])
                        S.op("vector", lambda e: e.tensor_tensor(out=bcol[:, mh:mh + 1], in0=ps[:, 127:128], in1=b1[:, kvi, mh:mh + 1], op=ALU.add),
                             reads=[pk, "n_b1"], writes=[("n_bcol", mh)])
                        S.op("scalar", lambda e: e.activation(out=hid[:, mh, 0:127], in_=ps[:, 0:127], func=AF.Identity, bias=bcol[:, mh:mh + 1]),
                             reads=[pk, ("n_bcol", mh)], writes=[("n_hid", mh)])
                        gelu_ap(C, hid[:, mh, 0:127], hsc[:, mh, 0:127], ("n_hid", mh), ("n_hsc", mh))
                        S.op("scalar", lambda e: e.activation(out=ghb[:, mh, 0:127], in_=hid[:, mh, 0:127], func=AF.Identity),
                             reads=[("n_hid", mh)], writes=[("n_ghb", mh)])
                    ps, pk = C.nextps()
                    if kvi == 0:
                        for mh in range(2):
                            S.op("tensor", lambda e: e.matmul(ps[:, 0:127], lhsT=W2k[:, mh, :], rhs=ghb[:, mh, 0:127], start=(mh == 0), stop=(mh == 1)),
                                 reads=["n_W2k", ("n_ghb", mh)], writes=[pk])
                        S.op("scalar", lambda e: e.activation(out=KCT[rows, g, 0:127], in_=ps[rows, 0:127], func=AF.Identity), reads=[pk], writes=[("n_KCT", g)])
                    else:
                        for mh in range(2):
                            S.op("tensor", lambda e: e.matmul(ps[0:127, 0:64], lhsT=ghb[:, mh, 0:127], rhs=W2v[:, mh, :], start=(mh == 0), stop=(mh == 1)),
                                 reads=["n_W2v", ("n_ghb", mh)], writes=[pk])
                        S.op("scalar", lambda e: e.activation(out=VCa[0:127, g, 0:64], in_=ps[0:127, 0:64], func=AF.Identity), reads=[pk], writes=[("n_VCa", g)])
            S.barrier()

        pA = ph.enter_context(contextlib.ExitStack())
        QT = sb("n_QT", [128, 8, NT], BF16, pA)
        Vaug = sb("n_Vaug", [128, 2, 16, 4, 65], BF16, pA)
        gts = sb("n_gts", [128, 16, 48], F32, pA)
        with contextlib.ExitStack() as p2:
            alloc_wch(p2)
            K12 = sb("n_K12", [128, 2, 2, NT], BF16, p2)
            wtok = sb("n_wtok", [128, KC, 560], BF16, p2)
            S.dma("gpsimd", "n_wtok", lambda e: e.dma_start(out=wtok[:], in_=C.nsa_wtok), writes=["n_wtok"])
            for mq in range(8):
                wt, wk = load_wch(4 + mq)
                for tt in range(NTT):
                    inproj(C, (wt, wk), 0, tt, lambda ps, pk: S.op(
                        "scalar", lambda e: e.activation(out=QT[:, mq, tsl(tt)], in_=ps[:], func=AF.Identity, scale=0.125), reads=[pk], writes=[("n_QT", mq, tt)]))
            for br in range(2):
                for c2 in range(2):
                    wt, wk = load_wch(12 + br * 2 + c2)
                    for tt in range(NTT):
                        inproj(C, (wt, wk), 0, tt, lambda ps, pk: S.op(
                            "scalar", lambda e: e.activation(out=K12[:, br, c2, tsl(tt)], in_=ps[:], func=AF.Identity), reads=[pk], writes=[("n_K12", br, c2, tt)]))
            S.op("vector", lambda e: e.memset(Vaug[:].rearrange("p a b c d -> p (a b c) d")[:, :, 64:65], 1.0), writes=["n_Vone"])
            for t16 in range(16):
                tok = slice(t16 * 128, (t16 + 1) * 128)
                ps, pk = C.nextps()
                for kc in range(KC):
                    S.op("tensor", lambda e: e.matmul(ps[:, 0:512], lhsT=C.xn[:, kc, tok], rhs=wtok[:, kc, 0:512], start=(kc == 0), stop=(kc == KC - 1)),
                         reads=["n_wtok", ("xn", kc, t16 // 4)], writes=[pk])
                for br in range(2):
                    S.op("scalar", lambda e: e.activation(out=Vaug[:, br, t16, :, 0:64], in_=ps[:, br * 256:(br + 1) * 256].rearrange("p (g d) -> p g d", d=64),
                                                          func=AF.Identity), reads=[pk], writes=[("n_Vaug", br, t16)])
                ps, pk = C.nextps()
                for kc in range(KC):
                    S.op("tensor", lambda e: e.matmul(ps[:, 0:48], lhsT=C.xn[:, kc, tok], rhs=wtok[:, kc, 512:560], start=(kc == 0), stop=(kc == KC - 1)),
                         reads=["n_wtok", ("xn", kc, t16 // 4)], writes=[pk])
                S.op("scalar", lambda e: e.activation(out=gts[:, t16, :], in_=ps[:, 0:48], func=AF.Sigmoid), reads=[pk], writes=[("n_gts", t16)])
            S.barrier()
            Kz = C.xn[:].rearrange("p (b g) t -> p b g t", b=2)
            for br in range(2):
                for g in range(4):
                    own = slice((g % 2) * 64, (g % 2) * 64 + 64)
                    oth = slice((1 - g % 2) * 64, (1 - g % 2) * 64 + 64)
                    S.op("gpsimd", lambda e: e.memset(Kz[oth, br, g, :], 0.0), writes=[("n_KzO", br, g)])
                    if g % 2 == 0:
                        S.op("scalar", lambda e: e.activation(out=Kz[own, br, g, :], in_=K12[own, br, g // 2, :], func=AF.Identity), writes=[("n_Kz", br, g)])
                    else:
                        S.op("vector", lambda e: e.tensor_copy(out=Kz[own, br, g, :], in_=K12[own, br, g // 2, :]), writes=[("n_Kz", br, g)])
            S.barrier()

        with contextlib.ExitStack() as p3:
            dtab = sb("n_dtab", [128, 10, TT], mybir.dt.int16, p3)
            etab = sb("n_etab", [128, 16, 128], BF16, p3)
            biasc = sb("n_biasc", [128, 16, 16], F32, p3)
            keep = sb("n_keep", [128, 16, 32], BF16, p3)
            addc = sb("n_addc", [128, 16, 32], BF16, p3)
            ident = sb("n_ident", [128, 128], F32, p3)
            sm = [sb("n_sm%d" % i, [128, TT], F32, p3) for i in range(2)]
            pT = [sb("n_pT%d" % i, [128, TT], BF16, p3) for i in range(3)]
            otoks = [sb("n_otok%d" % i, [128, 4, 256], F32, p3) for i in range(1)]
            valid = sb("n_valid", [128, 16, 1], F32, p3)
            otmp = sb("n_otmp", [128, 4, 64], F32, p3)
            rden = sb("n_rden", [128, 4, 1], F32, p3)
            ff = sb("n_ff", [128, 4, 1], F32, p3)
            imp = sb("n_imp", [128, 4, 32], F32, p3)
            itmp = sb("n_itmp", [128, 4, 32], F32, p3)
            top8 = sb("n_top8", [128, 4, 8], F32, p3)
            selb = sb("n_selb", [128, 4, 32], F32, p3)
            selbT = sb("n_selbT", [128, 4, TT], BF16, p3)
            oTq = sb("n_oTq", [128, KC, TT], BF16, p3)
            alloc_wch(p3)
            S.dma("sync", "n_dtab", lambda e: e.dma_start(out=dtab[:, 0:9, :], in_=C.c_dtab[:, 0:9, :]), writes=["n_dtab"])
            S.op("gpsimd", lambda e: e.memset(etab[:], 0.0), writes=["n_etab"])
            S.op("gpsimd", lambda e: e.memset(selbT[:], 0.0), writes=[("n_selbT", g) for g in range(4)])
            S.dma("gpsimd", "n_etab", lambda e: e.dma_start(out=etab[0:32], in_=C.c_etab), writes=["n_etab"])
            S.dma("sync", "n_biasc", lambda e: e.dma_start(out=biasc[:], in_=C.c_biasc), writes=["n_biasc"])
            S.dma("gpsimd", "n_keep", lambda e: e.dma_start(out=keep[:], in_=C.c_keep), writes=["n_keep"])
            S.dma("gpsimd", "n_addc", lambda e: e.dma_start(out=addc[:], in_=C.c_addc), writes=["n_addc"])
            S.dma("sync", "n_ident", lambda e: e.dma_start(out=ident[:], in_=C.c_ident), writes=["n_ident"])
            S.dma("sync", "n_valid", lambda e: e.dma_start(out=valid[:], in_=C.c_valid), writes=["n_valid"])
            smi = [0]

            def score_tile(mm_fn, mm_reads, dti, scal, bias_ap):
                dkey = "n_dtabc" if dti == 9 else "n_dtab"
                i = smi[0] % 2
                j = smi[0] % 3
                smi[0] += 1
                ps, pk = C.nextps()
                mm_fn(ps, pk)
                S.op("vector", lambda e: e.scalar_tensor_tensor(out=sm[i][:], in0=dtab[:, dti, :], scalar=scal, in1=ps[:], op0=ALU.mult, op1=ALU.add),
                     reads=[pk, dkey], writes=[("n_sm", i)])
                if bias_ap is None:
                    S.op("scalar", lambda e: e.activation(out=pT[j][:], in_=sm[i][:], func=AF.Exp), reads=[("n_sm", i)], writes=[("n_pT", j)])
                else:
                    S.op("scalar", lambda e: e.activation(out=pT[j][:], in_=sm[i][:], func=AF.Exp, bias=bias_ap), reads=[("n_sm", i), "n_biasc"], writes=[("n_pT", j)])
                return pT[j], ("n_pT", j)

            def run_jobs(jobs, LA=2):
                staged = []
                for idx in range(len(jobs) + LA):
                    if idx < len(jobs):
                        jb = jobs[idx]
                        staged.append(score_tile(jb["mm"], None, jb["dti"], jb["scal"], jb["bias"]))
                    k = idx - LA
                    if k >= 0:
                        jobs[k]["pv"](*staged[k])

            def accum_out(po, pok, ncol, otok, okey, r, gate_col, qt, first):
                po3 = po[:, 0:4 * ncol].rearrange("p (s c) -> p s c", c=ncol)
                S.op("vector", lambda e: e.tensor_scalar(out=rden[:], in0=po3[:, :, 64:65], scalar1=1e-30, scalar2=None, op0=ALU.max), reads=[pok], writes=["n_rden"])
                S.op("vector", lambda e: e.reciprocal(out=rden[:], in_=rden[:]), reads=["n_rden"], writes=["n_rden"])
                if first and qt == 0:
                    S.op("vector", lambda e: e.tensor_tensor(out=rden[:], in0=rden[:], in1=valid[:, 0:4, :], op=ALU.mult), reads=["n_rden", "n_valid"], writes=["n_rden"])
                S.op("vector", lambda e: e.tensor_tensor(out=ff[:], in0=rden[:], in1=gts[:, qt * 4:(qt + 1) * 4, gate_col:gate_col + 1], op=ALU.mult),
                     reads=["n_rden"] + [("n_gts", qt * 4 + i) for i in range(4)], writes=["n_ff"])
                dst = otok[:, :, r * 64:(r + 1) * 64]
                if first:
                    S.op("vector", lambda e: e.tensor_tensor(out=dst, in0=po3[:, :, 0:64], in1=ff[:].to_broadcast([128, 4, 64]), op=ALU.mult),
                         reads=[pok, "n_ff"], writes=[(okey, r)])
                else:
                    S.op("vector", lambda e: e.tensor_tensor(out=otmp[:], in0=po3[:, :, 0:64], in1=ff[:].to_broadcast([128, 4, 64]), op=ALU.mult),
                         reads=[pok, "n_ff"], writes=["n_otmp"])
                    S.op("vector", lambda e: e.tensor_tensor(out=dst, in0=dst, in1=otmp[:], op=ALU.add), reads=["n_otmp", (okey, r)], writes=[(okey, r)])

            for qt in range(NTT):
                qs = tsl(qt)
                S.dma("sync", "n_dtabc", lambda e: e.dma_start(out=dtab[:, 9, :], in_=C.c_dtab[:, 9 + qt, :]), writes=["n_dtabc"])
                for g in range(4):
                    half = g % 2
                    rows = slice(half * 64, half * 64 + 64)
                    c2 = g // 2
                    otok = otoks[0]
                    okey = "n_otok0"
                    jobs = []
                    for r in range(4):
                        hh = g * 4 + r
                        mq = (g // 2) * 4 + r

                        def mm(ps, pk, mq=mq):
                            S.op("tensor", lambda e: e.matmul(ps[:], lhsT=KCT[:, g, :], rhs=QT[:, mq, qs], start=True, stop=True),
                                 reads=[("n_KCT", g), ("n_QT", mq, qt)], writes=[pk])

                        def pv(p_t, p_k, r=r, hh=hh):
                            po, pok = nextpo()
                            for sub in range(4):
                                S.op("tensor", lambda e: e.matmul(po[:, sub * 97:(sub + 1) * 97], lhsT=p_t[:, sub * 128:(sub + 1) * 128], rhs=VCa[:, g, :], start=True, stop=True),
                                     reads=[p_k, ("n_VCa", g)], writes=[pok])
                            accum_out(po, pok, 97, otok, okey, r, hh, qt, True)
                            po3 = po[:, 0:388].rearrange("p (s c) -> p s c", c=97)
                            if r == 0:
                                S.op("vector", lambda e: e.tensor_tensor(out=imp[:], in0=po3[:, :, 65:97], in1=rden[:].to_broadcast([128, 4, 32]), op=ALU.mult),
                                     reads=[pok, "n_rden"], writes=["n_imp"])
                            else:
                                S.op("vector", lambda e: e.tensor_tensor(out=itmp[:], in0=po3[:, :, 65:97], in1=rden[:].to_broadcast([128, 4, 32]), op=ALU.mult),
                                     reads=[pok, "n_rden"], writes=["n_itmp"])
                                S.op("vector", lambda e: e.tensor_tensor(out=imp[:], in0=imp[:], in1=itmp[:], op=ALU.add), reads=["n_itmp", "n_imp"], writes=["n_imp"])
                        jobs.append(dict(mm=mm, dti=9, scal=-SLOPES[hh] / 2.0, bias=None, pv=pv))
                    run_jobs(jobs)
                    S.op("vector", lambda e: e.tensor_tensor(out=imp[:], in0=imp[:], in1=keep[:, qt * 4:(qt + 1) * 4, :], op=ALU.mult), reads=["n_imp", "n_keep"], writes=["n_imp"])
                    S.op("vector", lambda e: e.tensor_tensor(out=imp[:], in0=imp[:], in1=addc[:, qt * 4:(qt + 1) * 4, :], op=ALU.add), reads=["n_imp", "n_addc"], writes=["n_imp"])
                    for sub in range(4):
                        S.op("vector", lambda e: e.max(out=top8[:, sub, :], in_=imp[:, sub, :]), reads=["n_imp"], writes=["n_top8"])
                    for sub in range(4):
                        S.op("vector", lambda e: e.tensor_scalar(out=selb[:, sub, :], in0=imp[:, sub, :], scalar1=top8[:, sub, 7:8], scalar2=-BIGNEG, op0=ALU.is_lt, op1=ALU.mult),
                             reads=["n_imp", "n_top8"], writes=["n_selb"])
                    ps, pk = C.nextps()
                    for sub in range(4):
                        S.op("tensor", lambda e: e.transpose(out=ps[0:32, sub * 128:(sub + 1) * 128], in_=selb[:, sub, :], identity=ident[:]),
                             reads=["n_selb", "n_ident"], writes=[pk])
                    S.op("scalar", lambda e: e.activation(out=selbT[0:32, g, :], in_=ps[0:32, :], func=AF.Identity), reads=[pk], writes=[("n_selbT", g)])
                    jobs = []
                    for r in range(4):
                        hh = g * 4 + r
                        mq = (g // 2) * 4 + r
                        for br in range(2):
                            kts = list(range(0, qt * 4 + 4)) if br == 0 else list(range(max(0, qt * 4 - 4), qt * 4 + 4))
                            state = {}
                            for n_k, kt in enumerate(kts):
                                delta = qt * TT - kt * 128
                                bias_ap = None
                                if delta <= 0:
                                    dti = (-delta) // 128
                                elif br == 1:
                                    dti = 3 + delta // 128
                                else:
                                    dti = 8
                                    bias_ap = biasc[:, hh, delta // 128:delta // 128 + 1]

                                def mm(ps, pk, br=br, kt=kt, mq=mq):
                                    ks = slice(kt * 128, (kt + 1) * 128)
                                    S.op("tensor", lambda e: e.matmul(ps[:], lhsT=Kz[:, br, g, ks], rhs=QT[:, mq, qs], start=True, stop=(br == 1)),
                                         reads=[("n_Kz", br, g), ("n_KzO", br, g), ("n_QT", mq, qt)], writes=[pk])
                                    if br == 0:
                                        S.op("tensor", lambda e: e.matmul(ps[:], lhsT=etab[:, kt, :], rhs=selbT[:, g, :], start=False, stop=True),
                                             reads=["n_etab", ("n_selbT", g)], writes=[pk])

                                def pv(p_t, p_k, br=br, kt=kt, n_k=n_k, nk=len(kts), state=state, r=r, hh=hh):
                                    if n_k == 0:
                                        state["po"] = nextpo()
                                    po, pok = state["po"]
                                    for sub in range(4):
                                        S.op("tensor", lambda e: e.matmul(po[:, sub * 65:(sub + 1) * 65], lhsT=p_t[:, sub * 128:(sub + 1) * 128], rhs=Vaug[:, br, kt, g, :],
                                                                          start=(n_k == 0 and sub == 0), stop=(n_k == nk - 1 and sub == 3)),
                                             reads=[p_k, ("n_Vaug", br, kt), "n_Vone"], writes=[pok])
                                    if n_k == nk - 1:
                                        accum_out(po, pok, 65, otok, okey, r, (1 + br) * 16 + hh, qt, False)
                                jobs.append(dict(mm=mm, dti=dti, scal=-SLOPES[hh], bias=bias_ap, pv=pv))
                    run_jobs(jobs)
                    for kk in range(2):
                        kc = 2 * g + kk
                        ps, pk = C.nextps()
                        for sub in range(4):
                            S.op("tensor", lambda e: e.transpose(out=ps[:, sub * 128:(sub + 1) * 128], in_=otok[:, sub, kk * 128:(kk + 1) * 128], identity=ident[:]),
                                 reads=[(okey, 2 * kk), (okey, 2 * kk + 1), "n_ident"], writes=[pk])
                        S.op("scalar", lambda e: e.activation(out=oTq[:, kc, :], in_=ps[:], func=AF.Identity), reads=[pk], writes=[("n_oTq", kc)])
                for m in range(KC):
                    wt, wk = load_wch(16 + m)
                    ps, pk = C.nextps()
                    for kc in range(KC):
                        S.op("tensor", lambda e: e.matmul(ps[:], lhsT=wt[:, kc, :], rhs=oTq[:, kc, :], start=(kc == 0), stop=(kc == KC - 1)),
                             reads=[wk, ("n_oTq", kc)], writes=[pk])
                    S.op("vector", lambda e: e.tensor_tensor(out=C.hT[:, m, qs], in0=C.hT[:, m, qs], in1=ps[:], op=ALU.add),
                         reads=[pk, ("hT", m, qt)], writes=[("hT", m, qt)])
            S.barrier()
        pA.close()
        S.barrier()
    C.nrot = 8


def prep_weights(inp):
    f = np.ascontiguousarray
    w = {}
    g = np.stack([inp["lru_norm_g"][0], inp["ffn_norm_g"][0], inp["nsa_norm_g"][0], inp["ffn_norm_g"][1], inp["final_norm_g"]], 0)
    w["gains"] = f(g.reshape(5, KC, 128).transpose(2, 0, 1))
    wi = inp["lru_w_in"][0].reshape(KC, 128, 2, LN, 128)
    w["lru_win"] = f(wi.transpose(3, 1, 0, 2, 4).reshape(LN, 128, KC, 256))
    w["lru_gw"] = f(inp["lru_gate_w"][0].transpose(1, 2, 0, 3))
    w["lru_wout"] = f(inp["lru_w_out"][0].reshape(LN, 128, D))
    vec = np.concatenate([inp["lru_conv_w"][0], inp["lru_conv_b"][0][None], inp["lru_gate_b"][0], inp["lru_a_param"][0][None]], 0)
    w["lru_vec"] = f(vec.reshape(8, LN, 128).transpose(2, 1, 0))
    fw = inp["ffn_w_in"].reshape(2, KC, 128, 2, FJ, 128)
    w["ffn_win"] = f(fw.transpose(0, 4, 2, 1, 3, 5).reshape(2, FJ, 128, KC, 256))
    w["ffn_wout"] = f(inp["ffn_w_out"].reshape(2, FJ, 128, D))
    fv = np.concatenate([inp["ffn_conv_w"], inp["ffn_conv_b"][:, None]], 1)
    w["ffn_vec"] = f(fv.reshape(2, 4, FJ, 128).transpose(3, 0, 2, 1))
    w.update(nsa_host(inp))
    w.update(const_tables())
    return w


def nsa_host(inp):
    f = np.ascontiguousarray
    w = {}
    W = inp["nsa_w_in"][0]
    Wr = W.reshape(KC, 128, 2608)

    def chunk(cols):
        return Wr[:, :, cols].transpose(1, 0, 2)
    chunks = []
    for c4 in range(4):
        kvi, gp = c4 // 2, c4 % 2
        base = 1024 + kvi * 256 + gp * 128
        chunks.append(chunk(np.arange(base, base + 128)))
    for mq in range(8):
        p, r = mq // 4, mq % 4
        ha, hb = 4 * (2 * p) + r, 4 * (2 * p + 1) + r
        chunks.append(chunk(np.concatenate([np.arange(ha * 64, ha * 64 + 64), np.arange(hb * 64, hb * 64 + 64)])))
    for br in (1, 2):
        for c2 in range(2):
            base = 1024 + br * 512 + c2 * 128
            chunks.append(chunk(np.arange(base, base + 128)))
    Wo = inp["nsa_w_out"][0].reshape(KC, 128, D)
    for m in range(KC):
        chunks.append(Wo[:, :, m * 128:(m + 1) * 128].transpose(1, 0, 2))
    w["nsa_wch"] = f(np.stack(chunks, 0))
    tokcols = np.concatenate([np.arange(1024 + 512 + 256, 1024 + 512 + 512), np.arange(1024 + 1024 + 256, 1024 + 1024 + 512), np.arange(2560, 2608)])
    w["nsa_wtok"] = f(Wr[:, :, tokcols].transpose(1, 0, 2))
    w1 = inp["nsa_cmp_w1"][0].reshape(2, 32, 64, 256).transpose(0, 2, 1, 3)
    w["nsa_w1"] = f(np.concatenate([w1, w1], 1))
    pT = inp["nsa_cmp_pos"][0].transpose(2, 0, 1)
    w["nsa_posT"] = f(np.concatenate([pT, pT], 0))
    w["nsa_b1"] = f(inp["nsa_cmp_b1"][0].reshape(2, 2, 128).transpose(2, 0, 1))
    w2 = inp["nsa_cmp_w2"][0]
    w2k = w2[0].reshape(2, 128, 64).transpose(1, 0, 2)
    w["nsa_w2k"] = f(np.concatenate([w2k, w2k], 2))
    w["nsa_w2v"] = f(w2[1].reshape(2, 128, 64).transpose(1, 0, 2))
    return w


def const_tables():
    c = {}
    HUGE = 30000
    k = np.arange(128)[:, None]
    q = np.arange(TT)[None, :]
    dt = np.zeros((128, 13, TT), np.int64)
    for i in range(4):
        d = -128 * i + q - k
        dt[:, i] = np.where(d >= 0, d, HUGE)
    for i in range(1, 5):
        d = 128 * i + q - k
        dt[:, 3 + i] = np.where(d < 512, d, HUGE)
    dt[:, 8] = q - k
    cc = np.arange(128)[:, None]
    for qt in range(4):
        t = qt * TT + q
        d2 = 2 * t - 32 * cc - 31
        ok = (16 * cc + 31 <= t) & (cc < 127)
        dt[:, 9 + qt] = np.where(ok, d2, HUGE)
    c["c_dtab"] = dt.astype(np.int16)
    et = np.zeros((32, 16, 128), np.float32)
    for kt in range(16):
        for kk in range(128):
            et[(kt * 128 + kk) // 64, kt, kk] = 1.0
    c["c_etab"] = et
    sl = np.array(SLOPES, np.float64)
    bc = -(sl[:, None] * (128.0 * np.arange(16))[None, :])
    c["c_biasc"] = np.ascontiguousarray(np.broadcast_to(bc[None], (128, 16, 16))).astype(np.float32)
    t = (np.arange(16)[None, :, None] * 128 + np.arange(128)[:, None, None])
    j = np.arange(32)[None, None, :]
    cur = t // 64
    forced = (j == 0) | (j == cur) | (j == cur - 1)
    future = j > cur
    c["c_keep"] = np.where(forced | future, 0.0, 1.0).astype(np.float32)
    c["c_addc"] = np.where(forced, 1e4, np.where(future, -1.0, 0.0)).astype(np.float32)
    c["c_ident"] = np.eye(128, dtype=np.float32)
    c["c_valid"] = (t >= 31).astype(np.float32)
    ov = np.zeros((128, 33), np.float32)
    ov[:127, 0] = 1.0
    cs = np.arange(127)[:, None] * 16
    sj = np.arange(32)[None, :]
    ov[:127, 1:] = ((cs < (sj + 1) * 64) & (cs + 32 > sj * 64)).astype(np.float32)
    c["c_ovl"] = ov
    return c


_CACHE = {}


def kernel(**inp):
    inp = {k: np.asarray(v) for k, v in inp.items()}
    ncores, nseq = 8, 2
    if "nc" not in _CACHE:
        _CACHE["nc"] = build(nseq)[0]
    nc = _CACHE["nc"]
    w = prep_weights(inp)
    x = inp["x"]
    xT = np.ascontiguousarray(x.reshape(ncores, nseq, NT, KC, 128).transpose(0, 1, 4, 3, 2))
    in_maps = [dict(w, xT=xT[c]) for c in range(ncores)]
    res = run_bass_kernel_spmd(nc, in_maps, core_ids=list(range(ncores)))
    o = np.stack([r["outT"] for r in res.results], 0)
    return np.ascontiguousarray(o.transpose(0, 1, 4, 3, 2)).reshape(16, NT, D).astype(np.float32)
```

```python
import contextlib
import numpy as np
import concourse.bass as bass
import concourse.mybir as mybir
from concourse.bass_utils import run_bass_kernel_spmd

F32 = mybir.dt.float32
BF16 = mybir.dt.bfloat16
AF = mybir.ActivationFunctionType
ALU = mybir.AluOpType
AX = mybir.AxisListType

D = 1024
KC = 8
NT = 2048
TT = 512
NTT = 4
LW = 1280
LN = 10
DFF = 3072
FJ = 24
EPS = 1e-6
GELU_K = 1.5957691216057308
GELU_C = 0.044715 ** 0.5


class Sched:
    ENGS = ("tensor", "vector", "scalar", "gpsimd", "sync")

    def __init__(self, nc, stack):
        self.nc = nc
        self.stack = stack
        self.eng = {e: getattr(nc, e) for e in self.ENGS}
        self.sem = {}
        self.cnt = {}
        self.known = {e: {} for e in self.ENGS}
        self.res = {}
        self.n_inst = 0
        self.n_wait = 0
        for e in ("tensor", "vector", "scalar", "gpsimd"):
            self._mksem(e)

    def _mksem(self, key):
        if key not in self.sem:
            name = "s_" + key.replace(":", "_")
            self.sem[key] = self.stack.enter_context(self.nc.semaphore(name))
            self.cnt[key] = 0
        return self.sem[key]

    def _deps(self, engine, reads, writes):
        deps = {}

        def add(k, v):
            if v > deps.get(k, 0):
                deps[k] = v
        for r in reads:
            st = self.res.get(r)
            if st and st["w"]:
                add(*st["w"])
        for w in writes:
            st = self.res.get(w)
            if st:
                if st["w"]:
                    add(*st["w"])
                for k, v in st["r"].items():
                    add(k, v)
        kn = self.known[engine]
        for k, v in deps.items():
            if k == engine and engine == "tensor":
                continue
            if kn.get(k, 0) >= v:
                continue
            self.eng[engine].wait_ge(self.sem[k], v)
            self.n_wait += 1
            kn[k] = v

    def _mark(self, key, val, reads, writes):
        for r in reads:
            st = self.res.setdefault(r, {"w": None, "r": {}})
            st["r"][key] = val
        for w in writes:
            self.res[w] = {"w": (key, val), "r": {}}

    def op(self, engine, fn, reads=(), writes=()):
        self._deps(engine, reads, writes)
        ins = fn(self.eng[engine])
        self.cnt[engine] += 1
        ins.then_inc(self.sem[engine], 1)
        self.n_inst += 1
        self._mark(engine, self.cnt[engine], reads, writes)
        return ins

    def dma(self, queue, key, fn, reads=(), writes=()):
        k = "dma:" + key
        self._mksem(k)
        self._deps(queue, reads, writes)
        ins = fn(self.eng[queue])
        self.cnt[k] += 16
        ins.then_inc(self.sem[k], 16)
        self.n_inst += 1
        self._mark(k, self.cnt[k], reads, writes)
        return ins

    def barrier(self):
        for e in self.ENGS:
            for k, v in self.cnt.items():
                if v == 0 or (k == e and e == "tensor"):
                    continue
                if self.known[e].get(k, 0) >= v:
                    continue
                self.eng[e].wait_ge(self.sem[k], v)
                self.known[e][k] = v

    def finish(self):
        for k, v in self.cnt.items():
            if v and self.known["sync"].get(k, 0) < v:
                self.nc.sync.wait_ge(self.sem[k], v)
                self.known["sync"][k] = v


class Ctx:
    pass


def tsl(tt):
    return slice(tt * TT, (tt + 1) * TT)


def build(nseq=2, upto=99, nsa_stop=9):
    nc = bass.Bass("TRN2", target_bir_lowering=False)
    C = Ctx()
    C.nc = nc
    C.nsa_stop = nsa_stop

    def din(name, shape):
        return nc.dram_tensor(name, list(shape), F32, kind="ExternalInput").ap()
    C.xT = din("xT", [nseq, 128, KC, NT])
    C.gains = din("gains", [128, 5, KC])
    C.lru_win = din("lru_win", [LN, 128, KC, 256])
    C.lru_gw = din("lru_gw", [LN, 128, 2, 128])
    C.lru_wout = din("lru_wout", [LN, 128, D])
    C.lru_vec = din("lru_vec", [128, LN, 8])
    C.ffn_win = din("ffn_win", [2, FJ, 128, KC, 256])
    C.ffn_wout = din("ffn_wout", [2, FJ, 128, D])
    C.ffn_vec = din("ffn_vec", [128, 2, FJ, 4])
    C.nsa_wch = din("nsa_wch", [24, 128, KC, 128])
    C.nsa_wtok = din("nsa_wtok", [128, KC, 560])
    C.nsa_w1 = din("nsa_w1", [2, 128, 32, 256])
    C.nsa_posT = din("nsa_posT", [128, 2, 32])
    C.nsa_b1 = din("nsa_b1", [128, 2, 2])
    C.nsa_w2k = din("nsa_w2k", [128, 2, 128])
    C.nsa_w2v = din("nsa_w2v", [128, 2, 64])
    C.c_dtab = nc.dram_tensor("c_dtab", [128, 13, TT], mybir.dt.int16, kind="ExternalInput").ap()
    C.c_etab = din("c_etab", [32, 16, 128])
    C.c_biasc = din("c_biasc", [128, 16, 16])
    C.c_keep = din("c_keep", [128, 16, 32])
    C.c_addc = din("c_addc", [128, 16, 32])
    C.c_ident = din("c_ident", [128, 128])
    C.c_valid = din("c_valid", [128, 16, 1])
    C.c_ovl = din("c_ovl", [128, 33])
    C.outT = nc.dram_tensor("outT", [nseq, 128, KC, NT], F32, kind="ExternalOutput").ap()

    with contextlib.ExitStack() as st:
        S = Sched(nc, st)
        C.S = S

        uid = [0]

        def sb(name, shape, dt=F32, stack=st):
            uid[0] += 1
            return stack.enter_context(nc.sbuf_tensor("%s_u%d" % (name, uid[0]), list(shape), dt))
        C.sb = sb
        C.hT = sb("hT", [128, KC, NT])
        C.xn = sb("xn", [128, KC, NT], BF16)
        C.ones = sb("ones", [128, 128], BF16)
        C.gn = sb("gn", [128, 5, KC])
        C.lvec = sb("lvec", [128, LN, 8])
        C.lca = sb("lca", [128, LN, 2])
        C.fvec = sb("fvec", [128, 2, FJ, 4])
        C.psum = [st.enter_context(nc.psum_tensor("ps%d" % i, [128, TT], F32)) for i in range(8)]
        C.psi = 0
        C.nrot = 8

        def nextps():
            i = C.psi % C.nrot
            C.psi += 1
            return C.psum[i], ("ps", i)
        C.nextps = nextps

        S.op("vector", lambda e: e.memset(C.ones[:], 1.0), writes=["ones"])
        S.dma("sync", "c0", lambda e: e.dma_start(out=C.gn[:], in_=C.gains), writes=["gn"])
        S.dma("sync", "c1", lambda e: e.dma_start(out=C.lvec[:], in_=C.lru_vec), writes=["lvec"])
        S.dma("sync", "c2", lambda e: e.dma_start(out=C.fvec[:], in_=C.ffn_vec), writes=["fvec"])
        lru_consts(C)

        for s in range(nseq):
            for kc in range(KC):
                S.dma("sync", "x%d" % kc, lambda e, kc=kc: e.dma_start(out=C.hT[:, kc, :], in_=C.xT[s, :, kc, :]),
                      writes=[("hT", kc, tt) for tt in range(NTT)])
            if upto >= 1:
                rmsnorm(C, 0)
                lru_mixer(C)
            if upto >= 2:
                rmsnorm(C, 1)
                conv_ffn(C, 0)
            if upto >= 3:
                rmsnorm(C, 2)
                nsa_mixer(C)
            if upto >= 4:
                rmsnorm(C, 3)
                conv_ffn(C, 1)
            if upto >= 5:
                rmsnorm(C, 4, final=True)
            for kc in range(KC):
                S.dma("sync", "o%d" % kc, lambda e, kc=kc: e.dma_start(out=C.outT[s, :, kc, :], in_=C.hT[:, kc, :]),
                      reads=[("hT", kc, tt) for tt in range(NTT)])
        S.finish()
    C.n_inst = S.n_inst
    C.n_wait = S.n_wait
    return nc, C


def lru_consts(C):
    S, sb = C.S, C.sb
    with contextlib.ExitStack() as ph:
        t = [sb("lc%d" % i, [128, LN], F32, ph) for i in range(6)]
        ap = C.lvec[:, :, 7]
        S.op("scalar", lambda e: e.activation(out=t[0][:], in_=ap, func=AF.Abs), reads=["lvec"], writes=["lc0"])
        S.op("scalar", lambda e: e.activation(out=t[1][:], in_=t[0][:], func=AF.Exp, scale=-1.0), reads=["lc0"], writes=["lc1"])
        S.op("scalar", lambda e: e.activation(out=t[2][:], in_=t[1][:], func=AF.Ln, bias=1.0), reads=["lc1"], writes=["lc2"])
        S.op("vector", lambda e: e.tensor_scalar(out=t[3][:], in0=t[1][:], scalar1=1.0 / 3.0, scalar2=-0.5, op0=ALU.mult, op1=ALU.add), reads=["lc1"], writes=["lc3"])
        S.op("vector", lambda e: e.tensor_tensor(out=t[3][:], in0=t[3][:], in1=t[1][:], op=ALU.mult), reads=["lc3", "lc1"], writes=["lc3"])
        S.op("vector", lambda e: e.tensor_scalar(out=t[3][:], in0=t[3][:], scalar1=1.0, scalar2=None, op0=ALU.add), reads=["lc3"], writes=["lc3"])
        S.op("vector", lambda e: e.tensor_tensor(out=t[3][:], in0=t[3][:], in1=t[1][:], op=ALU.mult), reads=["lc3", "lc1"], writes=["lc3"])
        S.op("vector", lambda e: e.tensor_single_scalar(out=t[4][:], in_=t[1][:], scalar=0.03, op=ALU.is_lt), reads=["lc1"], writes=["lc4"])
        S.op("vector", lambda e: e.tensor_tensor(out=t[3][:], in0=t[3][:], in1=t[2][:], op=ALU.subtract), reads=["lc3", "lc2"], writes=["lc3"])
        S.op("vector", lambda e: e.tensor_tensor(out=t[3][:], in0=t[3][:], in1=t[4][:], op=ALU.mult), reads=["lc3", "lc4"], writes=["lc3"])
        S.op("vector", lambda e: e.tensor_tensor(out=t[3][:], in0=t[3][:], in1=t[2][:], op=ALU.add), reads=["lc3", "lc2"], writes=["lc3"])
        S.op("vector", lambda e: e.tensor_scalar(out=t[5][:], in0=ap, scalar1=-1.0, scalar2=0.0, op0=ALU.mult, op1=ALU.max), reads=["lvec"], writes=["lc5"])
        S.op("vector", lambda e: e.tensor_tensor(out=t[3][:], in0=t[3][:], in1=t[5][:], op=ALU.add), reads=["lc3", "lc5"], writes=["lc3"])
        S.op("vector", lambda e: e.tensor_scalar(out=C.lca[:, :, 0], in0=t[3][:], scalar1=-8.0, scalar2=None, op0=ALU.mult), reads=["lc3"], writes=["lca"])
        S.op("vector", lambda e: e.tensor_scalar(out=C.lca[:, :, 1], in0=t[3][:], scalar1=-16.0, scalar2=None, op0=ALU.mult), reads=["lc3"], writes=["lca"])
        S.barrier()


def rmsnorm(C, gi, final=False):
    S = C.S
    ph = contextlib.ExitStack()
    C.sq = [C.sb("sq%d" % i, [128, TT], BF16, ph) for i in range(4)]
    C.rs = C.sb("rs", [128, TT], F32, ph)
    for tt in range(NTT):
        ps, pk = C.nextps()
        for kc in range(KC):
            sq = C.sq[kc % 4]
            S.op("scalar", lambda e: e.activation(out=sq[:], in_=C.hT[:, kc, tsl(tt)], func=AF.Square),
                 reads=[("hT", kc, tt)], writes=[("sq", kc % 4)])
            S.op("tensor", lambda e: e.matmul(ps[:], lhsT=C.ones[:], rhs=sq[:], start=(kc == 0), stop=(kc == KC - 1)),
                 reads=["ones", ("sq", kc % 4)], writes=[pk])
        S.op("vector", lambda e: e.tensor_scalar(out=C.rs[:], in0=ps[:], scalar1=1.0 / D, scalar2=EPS, op0=ALU.mult, op1=ALU.add),
             reads=[pk], writes=["rs"])
        S.op("scalar", lambda e: e.activation(out=C.rs[:], in_=C.rs[:], func=AF.Sqrt), reads=["rs"], writes=["rs"])
        S.op("vector", lambda e: e.reciprocal(out=C.rs[:], in_=C.rs[:]), reads=["rs"], writes=["rs"])
        for kc in range(KC):
            if final:
                S.op("vector", lambda e: e.scalar_tensor_tensor(out=C.hT[:, kc, tsl(tt)], in0=C.hT[:, kc, tsl(tt)], scalar=C.gn[:, gi, kc:kc + 1],
                                                               in1=C.rs[:], op0=ALU.mult, op1=ALU.mult),
                     reads=[("hT", kc, tt), "rs", "gn"], writes=[("hT", kc, tt)])
            else:
                S.op("vector", lambda e: e.scalar_tensor_tensor(out=C.xn[:, kc, tsl(tt)], in0=C.hT[:, kc, tsl(tt)], scalar=C.gn[:, gi, kc:kc + 1],
                                                               in1=C.rs[:], op0=ALU.mult, op1=ALU.mult),
                     reads=[("hT", kc, tt), "rs", "gn"], writes=[("xn", kc, tt)])
    S.barrier()
    ph.close()


def inproj(C, w, col0, tt, evac):
    S = C.S
    ps, pk = C.nextps()
    wt, wk = w
    for kc in range(KC):
        S.op("tensor", lambda e: e.matmul(ps[:], lhsT=wt[:, kc, col0:col0 + 128], rhs=C.xn[:, kc, tsl(tt)], start=(kc == 0), stop=(kc == KC - 1)),
             reads=[wk, ("xn", kc, tt)], writes=[pk])
    evac(ps, pk)


def gelu_inplace(C, x, xk, t1, t1k, tt):
    S = C.S
    sl = tsl(tt)
    S.op("scalar", lambda e: e.activation(out=t1[:, sl], in_=x[:, sl], func=AF.Square), reads=[(xk, tt)], writes=[(t1k, tt)])
    S.op("vector", lambda e: e.tensor_scalar(out=t1[:, sl], in0=t1[:, sl], scalar1=0.044715, scalar2=1.0, op0=ALU.mult, op1=ALU.add),
         reads=[(t1k, tt)], writes=[(t1k, tt)])
    S.op("vector", lambda e: e.tensor_tensor(out=t1[:, sl], in0=t1[:, sl], in1=x[:, sl], op=ALU.mult), reads=[(t1k, tt), (xk, tt)], writes=[(t1k, tt)])
    S.op("scalar", lambda e: e.activation(out=t1[:, sl], in_=t1[:, sl], func=AF.Sigmoid, scale=GELU_K), reads=[(t1k, tt)], writes=[(t1k, tt)])
    S.op("vector", lambda e: e.tensor_tensor(out=x[:, sl], in0=x[:, sl], in1=t1[:, sl], op=ALU.mult), reads=[(t1k, tt), (xk, tt)], writes=[(xk, tt)])


def outproj_group(C, acts, wouts):
    S = C.S
    n = len(acts)
    for tt in range(NTT):
        for m in range(KC):
            ps, pk = C.nextps()
            for i in range(n):
                a_ap, a_k = acts[i]
                wt, wk = wouts[i]
                S.op("tensor", lambda e: e.matmul(ps[:], lhsT=wt[:, m * 128:(m + 1) * 128], rhs=a_ap(tt), start=(i == 0), stop=(i == n - 1)),
                     reads=[wk, a_k(tt)], writes=[pk])
            S.op("vector", lambda e: e.tensor_tensor(out=C.hT[:, m, tsl(tt)], in0=C.hT[:, m, tsl(tt)], in1=ps[:], op=ALU.add),
                 reads=[pk, ("hT", m, tt)], writes=[("hT", m, tt)])


def lru_mixer(C):
    S, sb, nc = C.S, C.sb, C.nc
    G = 2
    with contextlib.ExitStack() as ph:
        win = [sb("l_win%d" % i, [128, KC, 256], BF16, ph) for i in range(2)]
        gw = [sb("l_gw%d" % i, [128, 2, 128], BF16, ph) for i in range(2)]
        wo = [sb("l_wo%d" % i, [128, D], BF16, ph) for i in range(2 * G)]
        hy = [sb("l_hy%d" % i, [128, NT], BF16, ph) for i in range(2 * G)]
        ysbs = [sb("l_ysb%d" % i, [128, NT], F32, ph) for i in range(2)]
        xpads = [sb("l_xpad%d" % i, [128, 3 + NT], F32, ph) for i in range(2)]
        t1 = sb("l_t1", [128, NT], F32, ph)
        xc = sb("l_xc", [128, NT], F32, ph)
        xcb = sb("l_xcb", [128, NT], BF16, ph)
        rr = sb("l_r", [128, NT], F32, ph)
        ig = sb("l_ig", [128, NT], F32, ph)
        hh = sb("l_h", [128, NT], F32, ph)
        for i in range(2):
            S.op("vector", lambda e: e.memset(xpads[i][:, 0:3], 0.0), writes=["l_xpad%d_0" % i])

        def load_and_proj(n):
            b = n % 2
            wi, gwi, woi = win[b], gw[b], wo[n % (2 * G)]
            wik, gwk, wok = "l_win%d" % b, "l_gw%d" % b, "l_wo%d" % (n % (2 * G))
            S.dma("gpsimd", wik, lambda e: e.dma_start(out=wi[:], in_=C.lru_win[n]), writes=[wik])
            S.dma("gpsimd", gwk, lambda e: e.dma_start(out=gwi[:], in_=C.lru_gw[n]), writes=[gwk])
            S.dma("gpsimd", wok, lambda e: e.dma_start(out=woi[:], in_=C.lru_wout[n]), writes=[wok])
            ysb, xpad = ysbs[b], xpads[b]
            for tt in range(NTT):
                inproj(C, (wi, wik), 128, tt, lambda ps, pk: S.op(
                    "scalar", lambda e: e.activation(out=xpad[:, 3 + tt * TT:3 + (tt + 1) * TT], in_=ps[:], func=AF.Identity),
                    reads=[pk], writes=[("l_xpad%d" % b, tt)]))
            for tt in range(NTT):
                inproj(C, (wi, wik), 0, tt, lambda ps, pk: S.op(
                    "scalar", lambda e: e.activation(out=ysb[:, tsl(tt)], in_=ps[:], func=AF.Identity), reads=[pk], writes=[("l_ysb%d" % b, tt)]))

        pending = None
        load_and_proj(0)
        for n in range(LN):
            b = n % 2
            gwi, gwk = gw[b], "l_gw%d" % b
            hyi, hyk = hy[n % (2 * G)], "l_hy%d" % (n % (2 * G))
            ysb, xpad = ysbs[b], xpads[b]
            yk, xk = "l_ysb%d" % b, "l_xpad%d" % b
            T4 = range(NTT)
            for tt in T4:
                rd = [(xk, tt), (xk, tt - 1) if tt > 0 else xk + "_0", "lvec"]
                S.op("vector", lambda e: e.tensor_scalar(out=xc[:, tsl(tt)], in0=xpad[:, 3 + tt * TT:3 + (tt + 1) * TT], scalar1=C.lvec[:, n, 3:4],
                                                        scalar2=C.lvec[:, n, 4:5], op0=ALU.mult, op1=ALU.add), reads=rd, writes=[("l_xc", tt)])
                for k in (2, 1, 0):
                    S.op("vector", lambda e: e.scalar_tensor_tensor(out=xc[:, tsl(tt)], in0=xpad[:, k + tt * TT:k + (tt + 1) * TT], scalar=C.lvec[:, n, k:k + 1],
                                                                   in1=xc[:, tsl(tt)], op0=ALU.mult, op1=ALU.add),
                         reads=rd + [("l_xc", tt)], writes=[("l_xc", tt)])
                S.op("scalar", lambda e: e.activation(out=xcb[:, tsl(tt)], in_=xc[:, tsl(tt)], func=AF.Identity), reads=[("l_xc", tt)], writes=[("l_xcb", tt)])
            if n + 1 < LN:
                load_and_proj(n + 1)
            if pending is not None:
                outproj_group(C, *pending)
                pending = None
            for tt in T4:
                S.op("scalar", lambda e: e.activation(out=t1[:, tsl(tt)], in_=ysb[:, tsl(tt)], func=AF.Square, scale=GELU_C), reads=[(yk, tt)], writes=[("l_t1", tt)])
            for g, (dst, dk) in enumerate(((rr, "l_r"), (ig, "l_ig"))):
                for tt in T4:
                    ps, pk = C.nextps()
                    S.op("tensor", lambda e: e.matmul(ps[:], lhsT=gwi[:, g, :], rhs=xcb[:, tsl(tt)], start=True, stop=True),
                         reads=[gwk, ("l_xcb", tt)], writes=[pk])
                    S.op("scalar", lambda e: e.activation(out=dst[:, tsl(tt)], in_=ps[:], func=AF.Sigmoid, bias=C.lvec[:, n, 5 + g:6 + g]),
                         reads=[pk, "lvec"], writes=[(dk, tt)])
            for tt in T4:
                sl = tsl(tt)
                S.op("vector", lambda e: e.scalar_tensor_tensor(out=t1[:, sl], in0=t1[:, sl], scalar=1.0, in1=ysb[:, sl], op0=ALU.add, op1=ALU.mult),
                     reads=[("l_t1", tt), (yk, tt)], writes=[("l_t1", tt)])
            for tt in T4:
                sl = tsl(tt)
                S.op("scalar", lambda e: e.activation(out=t1[:, sl], in_=t1[:, sl], func=AF.Sigmoid, scale=GELU_K), reads=[("l_t1", tt)], writes=[("l_t1", tt)])
            for tt in T4:
                sl = tsl(tt)
                S.op("vector", lambda e: e.tensor_tensor(out=ysb[:, sl], in0=ysb[:, sl], in1=t1[:, sl], op=ALU.mult), reads=[("l_t1", tt), (yk, tt)], writes=[(yk, tt)])
            for tt in T4:
                sl = tsl(tt)
                S.op("vector", lambda e: e.tensor_tensor(out=ig[:, sl], in0=ig[:, sl], in1=xc[:, sl], op=ALU.mult), reads=[("l_ig", tt), ("l_xc", tt)], writes=[("l_ig", tt)])
            for tt in T4:
                sl = tsl(tt)
                S.op("scalar", lambda e: e.activation(out=t1[:, sl], in_=rr[:, sl], func=AF.Exp, scale=C.lca[:, n, 1:2]), reads=[("l_r", tt), "lca"], writes=[("l_t1", tt)])
            for tt in T4:
                sl = tsl(tt)
                S.op("scalar", lambda e: e.activation(out=rr[:, sl], in_=rr[:, sl], func=AF.Exp, scale=C.lca[:, n, 0:1]), reads=[("l_r", tt), "lca"], writes=[("l_r", tt)])
            for tt in T4:
                sl = tsl(tt)
                S.op("scalar", lambda e: e.activation(out=t1[:, sl], in_=t1[:, sl], func=AF.Sqrt, scale=-1.0, bias=1.0), reads=[("l_t1", tt)], writes=[("l_t1", tt)])
            for tt in T4:
                sl = tsl(tt)
                S.op("vector", lambda e: e.tensor_tensor(out=ig[:, sl], in0=ig[:, sl], in1=t1[:, sl], op=ALU.mult), reads=[("l_ig", tt), ("l_t1", tt)], writes=[("l_ig", tt)])
            for tt in T4:
                sl = tsl(tt)
                init = 0.0 if tt == 0 else hh[:, tt * TT - 1:tt * TT]
                S.op("vector", lambda e: e.tensor_tensor_scan(out=hh[:, sl], data0=rr[:, sl], data1=ig[:, sl], initial=init, op0=ALU.mult, op1=ALU.add),
                     reads=[("l_r", tt), ("l_ig", tt)] + ([("l_h", tt - 1)] if tt else []), writes=[("l_h", tt)])
                S.op("vector", lambda e: e.tensor_tensor(out=hyi[:, sl], in0=hh[:, sl], in1=ysb[:, sl], op=ALU.mult),
                     reads=[("l_h", tt), (yk, tt)], writes=[(hyk, tt)])
            if n % G == G - 1:
                idx = [(n - G + 1 + i) % (2 * G) for i in range(G)]
                pending = ([((lambda tt, i=i: hy[i][:, tsl(tt)]), (lambda tt, i=i: ("l_hy%d" % i, tt))) for i in idx],
                           [(wo[i], "l_wo%d" % i) for i in idx])
        outproj_group(C, *pending)
        S.barrier()


def conv_ffn(C, L):
    S, sb, nc = C.S, C.sb, C.nc
    G = 4
    with contextlib.ExitStack() as ph:
        win = [sb("f_win%d" % i, [128, KC, 256], BF16, ph) for i in range(2)]
        wo = [sb("f_wo%d" % i, [128, D], BF16, ph) for i in range(2 * G)]
        act = [sb("f_act%d" % i, [128, NT], BF16, ph) for i in range(2 * G)]
        apads = [sb("f_apad%d" % i, [128, 2 + NT], F32, ph) for i in range(2)]
        bsbs = [sb("f_bsb%d" % i, [128, NT], BF16, ph) for i in range(2)]
        acs = [sb("f_ac%d" % i, [128, NT], F32, ph) for i in range(2)]
        t1 = sb("f_t1", [128, NT], F32, ph)
        for i in range(2):
            S.op("vector", lambda e: e.memset(apads[i][:, 0:2], 0.0), writes=["f_apad%d_0" % i])

        def load_and_proj(j):
            b = j % 2
            wi, woi = win[b], wo[j % (2 * G)]
            wik, wok = "f_win%d" % b, "f_wo%d" % (j % (2 * G))
            S.dma("gpsimd", wik, lambda e: e.dma_start(out=wi[:], in_=C.ffn_win[L, j]), writes=[wik])
            S.dma("gpsimd", wok, lambda e: e.dma_start(out=woi[:], in_=C.ffn_wout[L, j]), writes=[wok])
            apad, bsb, acb = apads[b], bsbs[b], acs[b]

            def evac_a(ps, pk, tt):
                S.op("scalar", lambda e: e.activation(out=apad[:, 2 + tt * TT:2 + (tt + 1) * TT], in_=ps[:], func=AF.Identity),
                     reads=[pk], writes=[("f_apad%d" % b, tt)])
                S.op("scalar", lambda e: e.activation(out=acb[:, tsl(tt)], in_=ps[:], func=AF.Identity, scale=C.fvec[:, L, j, 2:3], bias=C.fvec[:, L, j, 3:4]),
                     reads=[pk, "fvec"], writes=[("f_ac%d" % b, tt)])
            for tt in range(NTT):
                inproj(C, (wi, wik), 0, tt, lambda ps, pk: evac_a(ps, pk, tt))
            for tt in range(NTT):
                inproj(C, (wi, wik), 128, tt, lambda ps, pk: S.op(
                    "scalar", lambda e: e.activation(out=bsb[:, tsl(tt)], in_=ps[:], func=AF.Identity), reads=[pk], writes=[("f_bsb%d" % b, tt)]))

        pending = None
        load_and_proj(0)
        for j in range(FJ):
            b = j % 2
            acti, actk = act[j % (2 * G)], "f_act%d" % (j % (2 * G))
            apad, bsb, ac = apads[b], bsbs[b], acs[b]
            ak, bk, ack = "f_apad%d" % b, "f_bsb%d" % b, "f_ac%d" % b
            T4 = range(NTT)
            if j + 1 < FJ:
                load_and_proj(j + 1)
            if pending is not None:
                outproj_group(C, *pending)
                pending = None
            for tt in T4:
                sl = tsl(tt)
                rd = [(ak, tt), (ak, tt - 1) if tt > 0 else ak + "_0", "fvec"]
                for k in (1, 0):
                    S.op("vector", lambda e: e.scalar_tensor_tensor(out=ac[:, sl], in0=apad[:, k + tt * TT:k + (tt + 1) * TT], scalar=C.fvec[:, L, j, k:k + 1],
                                                                   in1=ac[:, sl], op0=ALU.mult, op1=ALU.add),
                         reads=rd + [(ack, tt)], writes=[(ack, tt)])
                S.op("scalar", lambda e: e.activation(out=t1[:, sl], in_=ac[:, sl], func=AF.Square, scale=GELU_C), reads=[(ack, tt)], writes=[("f_t1", tt)])
            for tt in T4:
                sl = tsl(tt)
                S.op("vector", lambda e: e.scalar_tensor_tensor(out=t1[:, sl], in0=t1[:, sl], scalar=1.0, in1=ac[:, sl], op0=ALU.add, op1=ALU.mult),
                     reads=[("f_t1", tt), (ack, tt)], writes=[("f_t1", tt)])
                S.op("scalar", lambda e: e.activation(out=t1[:, sl], in_=t1[:, sl], func=AF.Sigmoid, scale=GELU_K), reads=[("f_t1", tt)], writes=[("f_t1", tt)])
                S.op("vector", lambda e: e.tensor_tensor(out=ac[:, sl], in0=ac[:, sl], in1=bsb[:, sl], op=ALU.mult), reads=[(ack, tt), (bk, tt)], writes=[(ack, tt)])
            for tt in T4:
                sl = tsl(tt)
                S.op("vector", lambda e: e.tensor_tensor(out=acti[:, sl], in0=ac[:, sl], in1=t1[:, sl], op=ALU.mult),
                     reads=[(ack, tt), ("f_t1", tt)], writes=[(actk, tt)])
            if j % G == G - 1:
                idx = [(j - G + 1 + i) % (2 * G) for i in range(G)]
                pending = ([((lambda tt, i=i: act[i][:, tsl(tt)]), (lambda tt, i=i: ("f_act%d" % i, tt))) for i in idx],
                           [(wo[i], "f_wo%d" % i) for i in idx])
        outproj_group(C, *pending)
        S.barrier()


SLOPES = [2.0 ** (-8.0 * (h + 1) / 16.0) for h in range(16)]
BIGNEG = 30000.0


def gelu_ap(C, x, t, xk, tk):
    S = C.S
    S.op("scalar", lambda e: e.activation(out=t, in_=x, func=AF.Square), reads=[xk], writes=[tk])
    S.op("vector", lambda e: e.tensor_scalar(out=t, in0=t, scalar1=0.044715, scalar2=1.0, op0=ALU.mult, op1=ALU.add), reads=[tk], writes=[tk])
    S.op("vector", lambda e: e.tensor_tensor(out=t, in0=t, in1=x, op=ALU.mult), reads=[tk, xk], writes=[tk])
    S.op("scalar", lambda e: e.activation(out=t, in_=t, func=AF.Sigmoid, scale=GELU_K), reads=[tk], writes=[tk])
    S.op("vector", lambda e: e.tensor_tensor(out=x, in0=x, in1=t, op=ALU.mult), reads=[tk, xk], writes=[xk])


def nsa_mixer(C):
    S, sb, nc = C.S, C.sb, C.nc
    C.nrot = 6
    po_banks = [(C.psum[6], ("ps", 6)), (C.psum[7], ("ps", 7))]
    po_i = [0]

    def nextpo():
        r = po_banks[po_i[0] % 2]
        po_i[0] += 1
        return r
    with contextlib.ExitStack() as ph:
        KCT = sb("n_KCT", [128, 4, 128], BF16, ph)
        VCa = sb("n_VCa", [128, 4, 97], BF16, ph)
        wch = [None, None]
        wci = [0]

        def alloc_wch(stack):
            for i in range(2):
                wch[i] = sb("n_wch%d" % i, [128, KC, 128], BF16, stack)

        def load_wch(idx):
            i = wci[0] % 2
            wci[0] += 1
            k = "n_wch%d" % i
            S.dma("gpsimd", k, lambda e: e.dma_start(out=wch[i][:], in_=C.nsa_wch[idx]), writes=[k])
            return wch[i], k

        with contextlib.ExitStack() as p1:
            alloc_wch(p1)
            KV0 = sb("n_KV0", [128, 4, NT], BF16, p1)
            W1 = sb("n_W1", [128, 2, 32, 256], BF16, p1)
            posT = sb("n_posT", [128, 2, 32], BF16, p1)
            b1 = sb("n_b1", [128, 2, 2], F32, p1)
            W2k = sb("n_W2k", [128, 2, 128], BF16, p1)
            W2v = sb("n_W2v", [128, 2, 64], BF16, p1)
            ovl = sb("n_ovl", [128, 33], F32, p1)
            hid = sb("n_hid", [128, 2, 128], F32, p1)
            hsc = sb("n_hsc", [128, 2, 128], F32, p1)
            ghb = sb("n_ghb", [128, 2, 128], BF16, p1)
            bcol = sb("n_bcol", [128, 2], F32, p1)
            for kvi in range(2):
                S.dma("gpsimd", "n_W1_%d" % kvi, lambda e: e.dma_start(out=W1[:, kvi], in_=C.nsa_w1[kvi]), writes=[("n_W1", kvi)])
            S.dma("gpsimd", "n_posT", lambda e: e.dma_start(out=posT[:], in_=C.nsa_posT), writes=["n_posT"])
            S.dma("sync", "n_b1", lambda e: e.dma_start(out=b1[:], in_=C.nsa_b1), writes=["n_b1"])
            S.dma("gpsimd", "n_W2k", lambda e: e.dma_start(out=W2k[:], in_=C.nsa_w2k), writes=["n_W2k"])
            S.dma("gpsimd", "n_W2v", lambda e: e.dma_start(out=W2v[:], in_=C.nsa_w2v), writes=["n_W2v"])
            S.dma("sync", "n_ovl", lambda e: e.dma_start(out=ovl[:], in_=C.c_ovl), writes=["n_ovl"])
            S.op("vector", lambda e: e.memset(KCT[:], 0.0), writes=[("n_KCT", g) for g in range(4)])
            S.op("vector", lambda e: e.memset(VCa[:], 0.0), writes=[("n_VCa", g) for g in range(4)])
            for g in range(4):
                S.op("vector", lambda e: e.tensor_copy(out=VCa[:, g, 64:97], in_=ovl[:]), reads=["n_ovl"], writes=[("n_VCa", g)])
            for c4 in range(4):
                wt, wk = load_wch(c4)
                for tt in range(NTT):
                    inproj(C, (wt, wk), 0, tt, lambda ps, pk: S.op(
                        "scalar", lambda e: e.activation(out=KV0[:, c4, tsl(tt)], in_=ps[:], func=AF.Identity), reads=[pk], writes=[("n_KV0", c4)]))
            for kvi in range(2):
                for g in range(4):
                    c4 = kvi * 2 + g // 2
                    rows = slice((g % 2) * 64, (g % 2) * 64 + 64)
                    for mh in range(2):
                        ps, pk = C.nextps()
                        for i in range(32):
                            S.op("tensor", lambda e: e.matmul(ps[:, 0:127], lhsT=W1[rows, kvi, i, mh * 128:(mh + 1) * 128],
                                                              rhs=KV0[rows, c4, i:i + 2017:16], start=(i == 0), stop=False),
                                 reads=[("n_W1", kvi), ("n_KV0", c4)], writes=[pk])
                        for i in range(32):
                            S.op("tensor", lambda e: e.matmul(ps[:, 127:128], lhsT=W1[rows, kvi, i, mh * 128:(mh + 1) * 128],
                                                              rhs=posT[rows, kvi, i:i + 1], start=False, stop=(i == 31)),
                                 reads=[("n_W1", kvi), "n_posT"], writes=[pk])
                        S.op("vector", lambda e: e.tensor_tensor(out=bcol[:, mh:mh + 1], in0=ps[:, 127:128], in1=b1[:, kvi, mh:mh + 1], op=ALU.add),
                             reads=[pk, "n_b1"], writes=[("n_bcol", mh)])
                        S.op("scalar", lambda e: e.activation(out=hid[:, mh, 0:127], in_=ps[:, 0:127], func=AF.Identity, bias=bcol[:, mh:mh + 1]),
                             reads=[pk, ("n_bcol", mh)], writes=[("n_hid", mh)])
                        gelu_ap(C, hid[:, mh, 0:127], hsc[:, mh, 0:127], ("n_hid", mh), ("n_hsc", mh))
                        S.op("scalar", lambda e: e.activation(out=ghb[:, mh, 0:127], in_=hid[:, mh, 0:127], func=AF.Identity),
                             reads=[("n_hid", mh)], writes=[("n_ghb", mh)])
                    ps, pk = C.nextps()
                    if kvi == 0:
                        for mh in range(2):
                            S.op("tensor", lambda e: e.matmul(ps[:, 0:127], lhsT=W2k[:, mh, :], rhs=ghb[:, mh, 0:127], start=(mh == 0), stop=(mh == 1)),
                                 reads=["n_W2k", ("n_ghb", mh)], writes=[pk])
                        S.op("scalar", lambda e: e.activation(out=KCT[rows, g, 0:127], in_=ps[rows, 0:127], func=AF.Identity), reads=[pk], writes=[("n_KCT", g)])
                    else:
                        for mh in range(2):
                            S.op("tensor", lambda e: e.matmul(ps[0:127, 0:64], lhsT=ghb[:, mh, 0:127], rhs=W2v[:, mh, :], start=(mh == 0), stop=(mh == 1)),
                                 reads=["n_W2v", ("n_ghb", mh)], writes=[pk])
                        S.op("scalar", lambda e: e.activation(out=VCa[0:127, g, 0:64], in_=ps[0:127, 0:64], func=AF.Identity), reads=[pk], writes=[("n_VCa", g)])
            S.barrier()

        pA = ph.enter_context(contextlib.ExitStack())
        QT = sb("n_QT", [128, 8, NT], BF16, pA)
        Vaug = sb("n_Vaug", [128, 2, 16, 4, 65], BF16, pA)
        gts = sb("n_gts", [128, 16, 48], F32, pA)
        with contextlib.ExitStack() as p2:
            alloc_wch(p2)
            K12 = sb("n_K12", [128, 2, 2, NT], BF16, p2)
            wtok = sb("n_wtok", [128, KC, 560], BF16, p2)
            S.dma("gpsimd", "n_wtok", lambda e: e.dma_start(out=wtok[:], in_=C.nsa_wtok), writes=["n_wtok"])
            for mq in range(8 if getattr(C, 'nsa_stop', 9) >= 2 else 0):
                wt, wk = load_wch(4 + mq)
                for tt in range(NTT):
                    inproj(C, (wt, wk), 0, tt, lambda ps, pk: S.op(
                        "scalar", lambda e: e.activation(out=QT[:, mq, tsl(tt)], in_=ps[:], func=AF.Identity, scale=0.125), reads=[pk], writes=[("n_QT", mq, tt)]))
            for br in range(2):
                for c2 in range(2):
                    wt, wk = load_wch(12 + br * 2 + c2)
                    for tt in range(NTT):
                        inproj(C, (wt, wk), 0, tt, lambda ps, pk: S.op(
                            "scalar", lambda e: e.activation(out=K12[:, br, c2, tsl(tt)], in_=ps[:], func=AF.Identity), reads=[pk], writes=[("n_K12", br, c2, tt)]))
            S.op("vector", lambda e: e.memset(Vaug[:].rearrange("p a b c d -> p (a b c) d")[:, :, 64:65], 1.0), writes=["n_Vone"])
            for t16 in range(16 if getattr(C, 'nsa_stop', 9) >= 2 else 0):
                tok = slice(t16 * 128, (t16 + 1) * 128)
                ps, pk = C.nextps()
                for kc in range(KC):
                    S.op("tensor", lambda e: e.matmul(ps[:, 0:512], lhsT=C.xn[:, kc, tok], rhs=wtok[:, kc, 0:512], start=(kc == 0), stop=(kc == KC - 1)),
                         reads=["n_wtok", ("xn", kc, t16 // 4)], writes=[pk])
                for br in range(2):
                    S.op("scalar", lambda e: e.activation(out=Vaug[:, br, t16, :, 0:64], in_=ps[:, br * 256:(br + 1) * 256].rearrange("p (g d) -> p g d", d=64),
                                                          func=AF.Identity), reads=[pk], writes=[("n_Vaug", br, t16)])
                ps, pk = C.nextps()
                for kc in range(KC):
                    S.op("tensor", lambda e: e.matmul(ps[:, 0:48], lhsT=C.xn[:, kc, tok], rhs=wtok[:, kc, 512:560], start=(kc == 0), stop=(kc == KC - 1)),
                         reads=["n_wtok", ("xn", kc, t16 // 4)], writes=[pk])
                S.op("scalar", lambda e: e.activation(out=gts[:, t16, :], in_=ps[:, 0:48], func=AF.Sigmoid), reads=[pk], writes=[("n_gts", t16)])
            S.barrier()
            Kz = C.xn[:].rearrange("p (b g) t -> p b g t", b=2)
            for br in range(2):
                for g in range(4):
                    own = slice((g % 2) * 64, (g % 2) * 64 + 64)
                    oth = slice((1 - g % 2) * 64, (1 - g % 2) * 64 + 64)
                    S.op("gpsimd", lambda e: e.memset(Kz[oth, br, g, :], 0.0), writes=[("n_KzO", br, g)])
                    if g % 2 == 0:
                        S.op("scalar", lambda e: e.activation(out=Kz[own, br, g, :], in_=K12[own, br, g // 2, :], func=AF.Identity), writes=[("n_Kz", br, g)])
                    else:
                        S.op("vector", lambda e: e.tensor_copy(out=Kz[own, br, g, :], in_=K12[own, br, g // 2, :]), writes=[("n_Kz", br, g)])
            S.barrier()

        with contextlib.ExitStack() as p3:
            dtab = sb("n_dtab", [128, 10, TT], mybir.dt.int16, p3)
            etab = sb("n_etab", [128, 16, 128], BF16, p3)
            biasc = sb("n_biasc", [128, 16, 16], F32, p3)
            keep = sb("n_keep", [128, 16, 32], BF16, p3)
            addc = sb("n_addc", [128, 16, 32], BF16, p3)
            ident = sb("n_ident", [128, 128], F32, p3)
            sm = [sb("n_sm%d" % i, [128, TT], F32, p3) for i in range(2)]
            pT = [sb("n_pT%d" % i, [128, TT], BF16, p3) for i in range(3)]
            otoks = [sb("n_otok%d" % i, [128, 4, 256], F32, p3) for i in range(1)]
            valid = sb("n_valid", [128, 16, 1], F32, p3)
            otmp = sb("n_otmp", [128, 4, 64], F32, p3)
            rden = sb("n_rden", [128, 4, 1], F32, p3)
            ff = sb("n_ff", [128, 4, 1], F32, p3)
            imp = sb("n_imp", [128, 4, 32], F32, p3)
            itmp = sb("n_itmp", [128, 4, 32], F32, p3)
            top8 = sb("n_top8", [128, 4, 8], F32, p3)
            selb = sb("n_selb", [128, 4, 32], F32, p3)
            selbT = sb("n_selbT", [128, 4, TT], BF16, p3)
            oTq = sb("n_oTq", [128, KC, TT], BF16, p3)
            alloc_wch(p3)
            S.dma("sync", "n_dtab", lambda e: e.dma_start(out=dtab[:, 0:9, :], in_=C.c_dtab[:, 0:9, :]), writes=["n_dtab"])
            S.op("gpsimd", lambda e: e.memset(etab[:], 0.0), writes=["n_etab"])
            S.op("gpsimd", lambda e: e.memset(selbT[:], 0.0), writes=[("n_selbT", g) for g in range(4)])
            S.dma("gpsimd", "n_etab", lambda e: e.dma_start(out=etab[0:32], in_=C.c_etab), writes=["n_etab"])
            S.dma("sync", "n_biasc", lambda e: e.dma_start(out=biasc[:], in_=C.c_biasc), writes=["n_biasc"])
            S.dma("gpsimd", "n_keep", lambda e: e.dma_start(out=keep[:], in_=C.c_keep), writes=["n_keep"])
            S.dma("gpsimd", "n_addc", lambda e: e.dma_start(out=addc[:], in_=C.c_addc), writes=["n_addc"])
            S.dma("sync", "n_ident", lambda e: e.dma_start(out=ident[:], in_=C.c_ident), writes=["n_ident"])
            S.dma("sync", "n_valid", lambda e: e.dma_start(out=valid[:], in_=C.c_valid), writes=["n_valid"])
            smi = [0]

            def score_tile(mm_fn, mm_reads, dti, scal, bias_ap):
                dkey = "n_dtabc" if dti == 9 else "n_dtab"
                i = smi[0] % 2
                j = smi[0] % 3
                smi[0] += 1
                ps, pk = C.nextps()
                mm_fn(ps, pk)
                S.op("vector", lambda e: e.scalar_tensor_tensor(out=sm[i][:], in0=dtab[:, dti, :], scalar=scal, in1=ps[:], op0=ALU.mult, op1=ALU.add),
                     reads=[pk, dkey], writes=[("n_sm", i)])
                if bias_ap is None:
                    S.op("scalar", lambda e: e.activation(out=pT[j][:], in_=sm[i][:], func=AF.Exp), reads=[("n_sm", i)], writes=[("n_pT", j)])
                else:
                    S.op("scalar", lambda e: e.activation(out=pT[j][:], in_=sm[i][:], func=AF.Exp, bias=bias_ap), reads=[("n_sm", i), "n_biasc"], writes=[("n_pT", j)])
                return pT[j], ("n_pT", j)

            def run_jobs(jobs, LA=2):
                staged = []
                for idx in range(len(jobs) + LA):
                    if idx < len(jobs):
                        jb = jobs[idx]
                        staged.append(score_tile(jb["mm"], None, jb["dti"], jb["scal"], jb["bias"]))
                    k = idx - LA
                    if k >= 0:
                        jobs[k]["pv"](*staged[k])

            def accum_out(po, pok, ncol, otok, okey, r, gate_col, qt, first):
                po3 = po[:, 0:4 * ncol].rearrange("p (s c) -> p s c", c=ncol)
                S.op("vector", lambda e: e.tensor_scalar(out=rden[:], in0=po3[:, :, 64:65], scalar1=1e-30, scalar2=None, op0=ALU.max), reads=[pok], writes=["n_rden"])
                S.op("vector", lambda e: e.reciprocal(out=rden[:], in_=rden[:]), reads=["n_rden"], writes=["n_rden"])
                if first and qt == 0:
                    S.op("vector", lambda e: e.tensor_tensor(out=rden[:], in0=rden[:], in1=valid[:, 0:4, :], op=ALU.mult), reads=["n_rden", "n_valid"], writes=["n_rden"])
                S.op("vector", lambda e: e.tensor_tensor(out=ff[:], in0=rden[:], in1=gts[:, qt * 4:(qt + 1) * 4, gate_col:gate_col + 1], op=ALU.mult),
                     reads=["n_rden"] + [("n_gts", qt * 4 + i) for i in range(4)], writes=["n_ff"])
                dst = otok[:, :, r * 64:(r + 1) * 64]
                if first:
                    S.op("vector", lambda e: e.tensor_tensor(out=dst, in0=po3[:, :, 0:64], in1=ff[:].to_broadcast([128, 4, 64]), op=ALU.mult),
                         reads=[pok, "n_ff"], writes=[(okey, r)])
                else:
                    S.op("vector", lambda e: e.tensor_tensor(out=otmp[:], in0=po3[:, :, 0:64], in1=ff[:].to_broadcast([128, 4, 64]), op=ALU.mult),
                         reads=[pok, "n_ff"], writes=["n_otmp"])
                    S.op("vector", lambda e: e.tensor_tensor(out=dst, in0=dst, in1=otmp[:], op=ALU.add), reads=["n_otmp", (okey, r)], writes=[(okey, r)])

            for qt in range(NTT if getattr(C, 'nsa_stop', 9) >= 3 else 0):
                qs = tsl(qt)
                S.dma("sync", "n_dtabc", lambda e: e.dma_start(out=dtab[:, 9, :], in_=C.c_dtab[:, 9 + qt, :]), writes=["n_dtabc"])
                for g in range(4):
                    half = g % 2
                    rows = slice(half * 64, half * 64 + 64)
                    c2 = g // 2
                    otok = otoks[0]
                    okey = "n_otok0"
                    jobs = []
                    for r in range(4):
                        hh = g * 4 + r
                        mq = (g // 2) * 4 + r

                        def mm(ps, pk, mq=mq):
                            S.op("tensor", lambda e: e.matmul(ps[:], lhsT=KCT[:, g, :], rhs=QT[:, mq, qs], start=True, stop=True),
                                 reads=[("n_KCT", g), ("n_QT", mq, qt)], writes=[pk])

                        def pv(p_t, p_k, r=r, hh=hh):
                            po, pok = nextpo()
                            for sub in range(4):
                                S.op("tensor", lambda e: e.matmul(po[:, sub * 97:(sub + 1) * 97], lhsT=p_t[:, sub * 128:(sub + 1) * 128], rhs=VCa[:, g, :], start=True, stop=True),
                                     reads=[p_k, ("n_VCa", g)], writes=[pok])
                            accum_out(po, pok, 97, otok, okey, r, hh, qt, True)
                            po3 = po[:, 0:388].rearrange("p (s c) -> p s c", c=97)
                            if r == 0:
                                S.op("vector", lambda e: e.tensor_tensor(out=imp[:], in0=po3[:, :, 65:97], in1=rden[:].to_broadcast([128, 4, 32]), op=ALU.mult),
                                     reads=[pok, "n_rden"], writes=["n_imp"])
                            else:
                                S.op("vector", lambda e: e.tensor_tensor(out=itmp[:], in0=po3[:, :, 65:97], in1=rden[:].to_broadcast([128, 4, 32]), op=ALU.mult),
                                     reads=[pok, "n_rden"], writes=["n_itmp"])
                                S.op("vector", lambda e: e.tensor_tensor(out=imp[:], in0=imp[:], in1=itmp[:], op=ALU.add), reads=["n_itmp", "n_imp"], writes=["n_imp"])
                        jobs.append(dict(mm=mm, dti=9, scal=-SLOPES[hh] / 2.0, bias=None, pv=pv))
                    run_jobs(jobs)
                    S.op("vector", lambda e: e.tensor_tensor(out=imp[:], in0=imp[:], in1=keep[:, qt * 4:(qt + 1) * 4, :], op=ALU.mult), reads=["n_imp", "n_keep"], writes=["n_imp"])
                    S.op("vector", lambda e: e.tensor_tensor(out=imp[:], in0=imp[:], in1=addc[:, qt * 4:(qt + 1) * 4, :], op=ALU.add), reads=["n_imp", "n_addc"], writes=["n_imp"])
                    for sub in range(4):
                        S.op("vector", lambda e: e.max(out=top8[:, sub, :], in_=imp[:, sub, :]), reads=["n_imp"], writes=["n_top8"])
                    for sub in range(4):
                        S.op("vector", lambda e: e.tensor_scalar(out=selb[:, sub, :], in0=imp[:, sub, :], scalar1=top8[:, sub, 7:8], scalar2=-BIGNEG, op0=ALU.is_lt, op1=ALU.mult),
                             reads=["n_imp", "n_top8"], writes=["n_selb"])
                    ps, pk = C.nextps()
                    for sub in range(4):
                        S.op("tensor", lambda e: e.transpose(out=ps[0:32, sub * 128:(sub + 1) * 128], in_=selb[:, sub, :], identity=ident[:]),
                             reads=["n_selb", "n_ident"], writes=[pk])
                    S.op("scalar", lambda e: e.activation(out=selbT[0:32, g, :], in_=ps[0:32, :], func=AF.Identity), reads=[pk], writes=[("n_selbT", g)])
                    jobs = []
                    for r in range(4):
                        hh = g * 4 + r
                        mq = (g // 2) * 4 + r
                        for br in range(2):
                            kts = list(range(0, qt * 4 + 4)) if br == 0 else list(range(max(0, qt * 4 - 4), qt * 4 + 4))
                            state = {}
                            for n_k, kt in enumerate(kts):
                                delta = qt * TT - kt * 128
                                bias_ap = None
                                if delta <= 0:
                                    dti = (-delta) // 128
                                elif br == 1:
                                    dti = 3 + delta // 128
                                else:
                                    dti = 8
                                    bias_ap = biasc[:, hh, delta // 128:delta // 128 + 1]

                                def mm(ps, pk, br=br, kt=kt, mq=mq):
                                    ks = slice(kt * 128, (kt + 1) * 128)
                                    S.op("tensor", lambda e: e.matmul(ps[:], lhsT=Kz[:, br, g, ks], rhs=QT[:, mq, qs], start=True, stop=(br == 1)),
                                         reads=[("n_Kz", br, g), ("n_KzO", br, g), ("n_QT", mq, qt)], writes=[pk])
                                    if br == 0:
                                        S.op("tensor", lambda e: e.matmul(ps[:], lhsT=etab[:, kt, :], rhs=selbT[:, g, :], start=False, stop=True),
                                             reads=["n_etab", ("n_selbT", g)], writes=[pk])

                                def pv(p_t, p_k, br=br, kt=kt, n_k=n_k, nk=len(kts), state=state, r=r, hh=hh):
                                    if n_k == 0:
                                        state["po"] = nextpo()
                                    po, pok = state["po"]
                                    for sub in range(4):
                                        S.op("tensor", lambda e: e.matmul(po[:, sub * 65:(sub + 1) * 65], lhsT=p_t[:, sub * 128:(sub + 1) * 128], rhs=Vaug[:, br, kt, g, :],
                                                                          start=(n_k == 0 and sub == 0), stop=(n_k == nk - 1 and sub == 3)),
                                             reads=[p_k, ("n_Vaug", br, kt), "n_Vone"], writes=[pok])
                                    if n_k == nk - 1:
                                        accum_out(po, pok, 65, otok, okey, r, (1 + br) * 16 + hh, qt, False)
                                jobs.append(dict(mm=mm, dti=dti, scal=-SLOPES[hh], bias=bias_ap, pv=pv))
                    run_jobs(jobs)
                    for kk in range(2):
                        kc = 2 * g + kk
                        ps, pk = C.nextps()
                        for sub in range(4):
                            S.op("tensor", lambda e: e.transpose(out=ps[:, sub * 128:(sub + 1) * 128], in_=otok[:, sub, kk * 128:(kk + 1) * 128], identity=ident[:]),
                                 reads=[(okey, 2 * kk), (okey, 2 * kk + 1), "n_ident"], writes=[pk])
                        S.op("scalar", lambda e: e.activation(out=oTq[:, kc, :], in_=ps[:], func=AF.Identity), reads=[pk], writes=[("n_oTq", kc)])
                for m in range(KC):
                    wt, wk = load_wch(16 + m)
                    ps, pk = C.nextps()
                    for kc in range(KC):
                        S.op("tensor", lambda e: e.matmul(ps[:], lhsT=wt[:, kc, :], rhs=oTq[:, kc, :], start=(kc == 0), stop=(kc == KC - 1)),
                             reads=[wk, ("n_oTq", kc)], writes=[pk])
                    S.op("vector", lambda e: e.tensor_tensor(out=C.hT[:, m, qs], in0=C.hT[:, m, qs], in1=ps[:], op=ALU.add),
                         reads=[pk, ("hT", m, qt)], writes=[("hT", m, qt)])
            S.barrier()
        pA.close()
        S.barrier()
    C.nrot = 8


def prep_weights(inp):
    f = np.ascontiguousarray
    w = {}
    g = np.stack([inp["lru_norm_g"][0], inp["ffn_norm_g"][0], inp["nsa_norm_g"][0], inp["ffn_norm_g"][1], inp["final_norm_g"]], 0)
    w["gains"] = f(g.reshape(5, KC, 128).transpose(2, 0, 1))
    wi = inp["lru_w_in"][0].reshape(KC, 128, 2, LN, 128)
    w["lru_win"] = f(wi.transpose(3, 1, 0, 2, 4).reshape(LN, 128, KC, 256))
    w["lru_gw"] = f(inp["lru_gate_w"][0].transpose(1, 2, 0, 3))
    w["lru_wout"] = f(inp["lru_w_out"][0].reshape(LN, 128, D))
    vec = np.concatenate([inp["lru_conv_w"][0], inp["lru_conv_b"][0][None], inp["lru_gate_b"][0], inp["lru_a_param"][0][None]], 0)
    w["lru_vec"] = f(vec.reshape(8, LN, 128).transpose(2, 1, 0))
    fw = inp["ffn_w_in"].reshape(2, KC, 128, 2, FJ, 128)
    w["ffn_win"] = f(fw.transpose(0, 4, 2, 1, 3, 5).reshape(2, FJ, 128, KC, 256))
    w["ffn_wout"] = f(inp["ffn_w_out"].reshape(2, FJ, 128, D))
    fv = np.concatenate([inp["ffn_conv_w"], inp["ffn_conv_b"][:, None]], 1)
    w["ffn_vec"] = f(fv.reshape(2, 4, FJ, 128).transpose(3, 0, 2, 1))
    w.update(nsa_host(inp))
    w.update(const_tables())
    return w


def nsa_host(inp):
    f = np.ascontiguousarray
    w = {}
    W = inp["nsa_w_in"][0]
    Wr = W.reshape(KC, 128, 2608)

    def chunk(cols):
        return Wr[:, :, cols].transpose(1, 0, 2)
    chunks = []
    for c4 in range(4):
        kvi, gp = c4 // 2, c4 % 2
        base = 1024 + kvi * 256 + gp * 128
        chunks.append(chunk(np.arange(base, base + 128)))
    for mq in range(8):
        p, r = mq // 4, mq % 4
        ha, hb = 4 * (2 * p) + r, 4 * (2 * p + 1) + r
        chunks.append(chunk(np.concatenate([np.arange(ha * 64, ha * 64 + 64), np.arange(hb * 64, hb * 64 + 64)])))
    for br in (1, 2):
        for c2 in range(2):
            base = 1024 + br * 512 + c2 * 128
            chunks.append(chunk(np.arange(base, base + 128)))
    Wo = inp["nsa_w_out"][0].reshape(KC, 128, D)
    for m in range(KC):
        chunks.append(Wo[:, :, m * 128:(m + 1) * 128].transpose(1, 0, 2))
    w["nsa_wch"] = f(np.stack(chunks, 0))
    tokcols = np.concatenate([np.arange(1024 + 512 + 256, 1024 + 512 + 512), np.arange(1024 + 1024 + 256, 1024 + 1024 + 512), np.arange(2560, 2608)])
    w["nsa_wtok"] = f(Wr[:, :, tokcols].transpose(1, 0, 2))
    w1 = inp["nsa_cmp_w1"][0].reshape(2, 32, 64, 256).transpose(0, 2, 1, 3)
    w["nsa_w1"] = f(np.concatenate([w1, w1], 1))
    pT = inp["nsa_cmp_pos"][0].transpose(2, 0, 1)
    w["nsa_posT"] = f(np.concatenate([pT, pT], 0))
    w["nsa_b1"] = f(inp["nsa_cmp_b1"][0].reshape(2, 2, 128).transpose(2, 0, 1))
    w2 = inp["nsa_cmp_w2"][0]
    w2k = w2[0].reshape(2, 128, 64).transpose(1, 0, 2)
    w["nsa_w2k"] = f(np.concatenate([w2k, w2k], 2))
    w["nsa_w2v"] = f(w2[1].reshape(2, 128, 64).transpose(1, 0, 2))
    return w


def const_tables():
    c = {}
    HUGE = 30000
    k = np.arange(128)[:, None]
    q = np.arange(TT)[None, :]
    dt = np.zeros((128, 13, TT), np.int64)
    for i in range(4):
        d = -128 * i + q - k
        dt[:, i] = np.where(d >= 0, d, HUGE)
    for i in range(1, 5):
        d = 128 * i + q - k
        dt[:, 3 + i] = np.where(d < 512, d, HUGE)
    dt[:, 8] = q - k
    cc = np.arange(128)[:, None]
    for qt in range(4):
        t = qt * TT + q
        d2 = 2 * t - 32 * cc - 31
        ok = (16 * cc + 31 <= t) & (cc < 127)
        dt[:, 9 + qt] = np.where(ok, d2, HUGE)
    c["c_dtab"] = dt.astype(np.int16)
    et = np.zeros((32, 16, 128), np.float32)
    for kt in range(16):
        for kk in range(128):
            et[(kt * 128 + kk) // 64, kt, kk] = 1.0
    c["c_etab"] = et
    sl = np.array(SLOPES, np.float64)
    bc = -(sl[:, None] * (128.0 * np.arange(16))[None, :])
    c["c_biasc"] = np.ascontiguousarray(np.broadcast_to(bc[None], (128, 16, 16))).astype(np.float32)
    t = (np.arange(16)[None, :, None] * 128 + np.arange(128)[:, None, None])
    j = np.arange(32)[None, None, :]
    cur = t // 64
    forced = (j == 0) | (j == cur) | (j == cur - 1)
    future = j > cur
    c["c_keep"] = np.where(forced | future, 0.0, 1.0).astype(np.float32)
    c["c_addc"] = np.where(forced, 1e4, np.where(future, -1.0, 0.0)).astype(np.float32)
    c["c_ident"] = np.eye(128, dtype=np.float32)
    c["c_valid"] = (t >= 31).astype(np.float32)
    ov = np.zeros((128, 33), np.float32)
    ov[:127, 0] = 1.0
    cs = np.arange(127)[:, None] * 16
    sj = np.arange(32)[None, :]
    ov[:127, 1:] = ((cs < (sj + 1) * 64) & (cs + 32 > sj * 64)).astype(np.float32)
    c["c_ovl"] = ov
    return c


_CACHE = {}


def kernel(**inp):
    inp = {k: np.asarray(v) for k, v in inp.items()}
    ncores, nseq = 8, 2
    if "nc" not in _CACHE:
        _CACHE["nc"] = build(nseq)[0]
    nc = _CACHE["nc"]
    w = prep_weights(inp)
    x = inp["x"]
    xT = np.ascontiguousarray(x.reshape(ncores, nseq, NT, KC, 128).transpose(0, 1, 4, 3, 2))
    in_maps = [dict(w, xT=xT[c]) for c in range(ncores)]
    res = run_bass_kernel_spmd(nc, in_maps, core_ids=list(range(ncores)))
    o = np.stack([r["outT"] for r in res.results], 0)
    return np.ascontiguousarray(o.transpose(0, 1, 4, 3, 2)).reshape(16, NT, D).astype(np.float32)
```

```python
import contextlib
import numpy as np
import concourse.bass as bass
import concourse.mybir as mybir
from concourse.bass_utils import run_bass_kernel_spmd

F32 = mybir.dt.float32
BF16 = mybir.dt.bfloat16
AF = mybir.ActivationFunctionType
ALU = mybir.AluOpType
AX = mybir.AxisListType

D = 1024
KC = 8
NT = 2048
TT = 512
NTT = 4
LW = 1280
LN = 10
DFF = 3072
FJ = 24
EPS = 1e-6
GELU_K = 1.5957691216057308
GELU_C = 0.044715 ** 0.5


class Sched:
    ENGS = ("tensor", "vector", "scalar", "gpsimd", "sync")

    def __init__(self, nc, stack):
        self.nc = nc
        self.stack = stack
        self.eng = {e: getattr(nc, e) for e in self.ENGS}
        self.sem = {}
        self.cnt = {}
        self.known = {e: {} for e in self.ENGS}
        self.res = {}
        self.n_inst = 0
        self.n_wait = 0
        for e in ("tensor", "vector", "scalar", "gpsimd"):
            self._mksem(e)

    def _mksem(self, key):
        if key not in self.sem:
            name = "s_" + key.replace(":", "_")
            self.sem[key] = self.stack.enter_context(self.nc.semaphore(name))
            self.cnt[key] = 0
        return self.sem[key]

    def _deps(self, engine, reads, writes):
        deps = {}

        def add(k, v):
            if v > deps.get(k, 0):
                deps[k] = v
        for r in reads:
            st = self.res.get(r)
            if st and st["w"]:
                add(*st["w"])
        for w in writes:
            st = self.res.get(w)
            if st:
                if st["w"]:
                    add(*st["w"])
                for k, v in st["r"].items():
                    add(k, v)
        kn = self.known[engine]
        for k, v in deps.items():
            if k == engine and engine == "tensor":
                continue
            if kn.get(k, 0) >= v:
                continue
            self.eng[engine].wait_ge(self.sem[k], v)
            self.n_wait += 1
            kn[k] = v

    def _mark(self, key, val, reads, writes):
        for r in reads:
            st = self.res.setdefault(r, {"w": None, "r": {}})
            st["r"][key] = val
        for w in writes:
            self.res[w] = {"w": (key, val), "r": {}}

    def op(self, engine, fn, reads=(), writes=()):
        self._deps(engine, reads, writes)
        ins = fn(self.eng[engine])
        self.cnt[engine] += 1
        ins.then_inc(self.sem[engine], 1)
        self.n_inst += 1
        self._mark(engine, self.cnt[engine], reads, writes)
        return ins

    def dma(self, queue, key, fn, reads=(), writes=()):
        k = "dma:" + key
        self._mksem(k)
        self._deps(queue, reads, writes)
        ins = fn(self.eng[queue])
        self.cnt[k] += 16
        ins.then_inc(self.sem[k], 16)
        self.n_inst += 1
        self._mark(k, self.cnt[k], reads, writes)
        return ins

    def barrier(self):
        for e in self.ENGS:
            for k, v in self.cnt.items():
                if v == 0 or (k == e and e == "tensor"):
                    continue
                if self.known[e].get(k, 0) >= v:
                    continue
                self.eng[e].wait_ge(self.sem[k], v)
                self.known[e][k] = v

    def finish(self):
        for k, v in self.cnt.items():
            if v and self.known["sync"].get(k, 0) < v:
                self.nc.sync.wait_ge(self.sem[k], v)
                self.known["sync"][k] = v


class Ctx:
    pass


def tsl(tt):
    return slice(tt * TT, (tt + 1) * TT)


def build(nseq=2, upto=99, nsa_stop=9):
    nc = bass.Bass("TRN2", target_bir_lowering=False)
    C = Ctx()
    C.nc = nc
    C.nsa_stop = nsa_stop

    def din(name, shape):
        return nc.dram_tensor(name, list(shape), F32, kind="ExternalInput").ap()
    C.xT = din("xT", [nseq, 128, KC, NT])
    C.gains = din("gains", [128, 5, KC])
    C.lru_win = din("lru_win", [LN, 128, KC, 256])
    C.lru_gw = din("lru_gw", [LN, 128, 2, 128])
    C.lru_wout = din("lru_wout", [LN, 128, D])
    C.lru_vec = din("lru_vec", [128, LN, 8])
    C.ffn_win = din("ffn_win", [2, FJ, 128, KC, 256])
    C.ffn_wout = din("ffn_wout", [2, FJ, 128, D])
    C.ffn_vec = din("ffn_vec", [128, 2, FJ, 4])
    C.nsa_wch = din("nsa_wch", [24, 128, KC, 128])
    C.nsa_wtok = din("nsa_wtok", [128, KC, 560])
    C.nsa_w1 = din("nsa_w1", [2, 128, 32, 256])
    C.nsa_posT = din("nsa_posT", [128, 2, 32])
    C.nsa_b1 = din("nsa_b1", [128, 2, 2])
    C.nsa_b1row = din("nsa_b1row", [1, 2, 256])
    C.nsa_w2k = din("nsa_w2k", [128, 2, 128])
    C.nsa_w2v = din("nsa_w2v", [128, 2, 64])
    C.c_dtab = nc.dram_tensor("c_dtab", [128, 13, TT], mybir.dt.int16, kind="ExternalInput").ap()
    C.c_etab = din("c_etab", [32, 16, 128])
    C.c_biasc = din("c_biasc", [128, 16, 16])
    C.c_keep = din("c_keep", [128, 16, 32])
    C.c_addc = din("c_addc", [128, 16, 32])
    C.c_ident = din("c_ident", [128, 128])
    C.c_valid = din("c_valid", [128, 16, 1])
    C.c_fq = din("c_fq", [128, 4, 16])
    C.c_ovl = din("c_ovl", [128, 33])
    C.outT = nc.dram_tensor("outT", [nseq, 128, KC, NT], F32, kind="ExternalOutput").ap()

    with contextlib.ExitStack() as st:
        S = Sched(nc, st)
        C.S = S

        uid = [0]

        def sb(name, shape, dt=F32, stack=st):
            uid[0] += 1
            return stack.enter_context(nc.sbuf_tensor("%s_u%d" % (name, uid[0]), list(shape), dt))
        C.sb = sb
        C.hT = sb("hT", [128, KC, NT])
        C.xn = sb("xn", [128, KC, NT], BF16)
        C.ones = sb("ones", [128, 128], BF16)
        C.gn = sb("gn", [128, 5, KC])
        C.lvec = sb("lvec", [128, LN, 8])
        C.lca = sb("lca", [128, LN, 2])
        C.fvec = sb("fvec", [128, 2, FJ, 4])
        C.psum = [st.enter_context(nc.psum_tensor("ps%d" % i, [128, TT], F32)) for i in range(8)]
        C.psi = 0
        C.nrot = 8

        def nextps():
            i = C.psi % C.nrot
            C.psi += 1
            return C.psum[i], ("ps", i)
        C.nextps = nextps

        S.op("vector", lambda e: e.memset(C.ones[:], 1.0), writes=["ones"])
        S.dma("sync", "c0", lambda e: e.dma_start(out=C.gn[:], in_=C.gains), writes=["gn"])
        S.dma("sync", "c1", lambda e: e.dma_start(out=C.lvec[:], in_=C.lru_vec), writes=["lvec"])
        S.dma("sync", "c2", lambda e: e.dma_start(out=C.fvec[:], in_=C.ffn_vec), writes=["fvec"])
        lru_consts(C)

        for s in range(nseq):
            for kc in range(KC):
                S.dma("sync", "x%d" % kc, lambda e, kc=kc: e.dma_start(out=C.hT[:, kc, :], in_=C.xT[s, :, kc, :]),
                      writes=[("hT", kc, tt) for tt in range(NTT)])
            if upto >= 1:
                rmsnorm(C, 0)
                lru_mixer(C)
            if upto >= 2:
                rmsnorm(C, 1)
                conv_ffn(C, 0)
            if upto >= 3:
                rmsnorm(C, 2)
                nsa_mixer(C)
            if upto >= 4:
                rmsnorm(C, 3)
                conv_ffn(C, 1)
            if upto >= 5:
                rmsnorm(C, 4, final=True)
            for kc in range(KC):
                S.dma("sync", "o%d" % kc, lambda e, kc=kc: e.dma_start(out=C.outT[s, :, kc, :], in_=C.hT[:, kc, :]),
                      reads=[("hT", kc, tt) for tt in range(NTT)])
        S.finish()
    C.n_inst = S.n_inst
    C.n_wait = S.n_wait
    return nc, C


def lru_consts(C):
    S, sb = C.S, C.sb
    with contextlib.ExitStack() as ph:
        t = [sb("lc%d" % i, [128, LN], F32, ph) for i in range(6)]
        ap = C.lvec[:, :, 7]
        S.op("scalar", lambda e: e.activation(out=t[0][:], in_=ap, func=AF.Abs), reads=["lvec"], writes=["lc0"])
        S.op("scalar", lambda e: e.activation(out=t[1][:], in_=t[0][:], func=AF.Exp, scale=-1.0), reads=["lc0"], writes=["lc1"])
        S.op("scalar", lambda e: e.activation(out=t[2][:], in_=t[1][:], func=AF.Ln, bias=1.0), reads=["lc1"], writes=["lc2"])
        S.op("vector", lambda e: e.tensor_scalar(out=t[3][:], in0=t[1][:], scalar1=1.0 / 3.0, scalar2=-0.5, op0=ALU.mult, op1=ALU.add), reads=["lc1"], writes=["lc3"])
        S.op("vector", lambda e: e.tensor_tensor(out=t[3][:], in0=t[3][:], in1=t[1][:], op=ALU.mult), reads=["lc3", "lc1"], writes=["lc3"])
        S.op("vector", lambda e: e.tensor_scalar(out=t[3][:], in0=t[3][:], scalar1=1.0, scalar2=None, op0=ALU.add), reads=["lc3"], writes=["lc3"])
        S.op("vector", lambda e: e.tensor_tensor(out=t[3][:], in0=t[3][:], in1=t[1][:], op=ALU.mult), reads=["lc3", "lc1"], writes=["lc3"])
        S.op("vector", lambda e: e.tensor_single_scalar(out=t[4][:], in_=t[1][:], scalar=0.03, op=ALU.is_lt), reads=["lc1"], writes=["lc4"])
        S.op("vector", lambda e: e.tensor_tensor(out=t[3][:], in0=t[3][:], in1=t[2][:], op=ALU.subtract), reads=["lc3", "lc2"], writes=["lc3"])
        S.op("vector", lambda e: e.tensor_tensor(out=t[3][:], in0=t[3][:], in1=t[4][:], op=ALU.mult), reads=["lc3", "lc4"], writes=["lc3"])
        S.op("vector", lambda e: e.tensor_tensor(out=t[3][:], in0=t[3][:], in1=t[2][:], op=ALU.add), reads=["lc3", "lc2"], writes=["lc3"])
        S.op("vector", lambda e: e.tensor_scalar(out=t[5][:], in0=ap, scalar1=-1.0, scalar2=0.0, op0=ALU.mult, op1=ALU.max), reads=["lvec"], writes=["lc5"])
        S.op("vector", lambda e: e.tensor_tensor(out=t[3][:], in0=t[3][:], in1=t[5][:], op=ALU.add), reads=["lc3", "lc5"], writes=["lc3"])
        S.op("vector", lambda e: e.tensor_scalar(out=C.lca[:, :, 0], in0=t[3][:], scalar1=-8.0, scalar2=None, op0=ALU.mult), reads=["lc3"], writes=["lca"])
        S.op("vector", lambda e: e.tensor_scalar(out=C.lca[:, :, 1], in0=t[3][:], scalar1=-16.0, scalar2=None, op0=ALU.mult), reads=["lc3"], writes=["lca"])
        S.barrier()


def rmsnorm(C, gi, final=False):
    S = C.S
    ph = contextlib.ExitStack()
    C.sq = [C.sb("sq%d" % i, [128, TT], BF16, ph) for i in range(4)]
    C.rs = C.sb("rs", [128, TT], F32, ph)
    for tt in range(NTT):
        ps, pk = C.nextps()
        for kc in range(KC):
            sq = C.sq[kc % 4]
            S.op("scalar", lambda e: e.activation(out=sq[:], in_=C.hT[:, kc, tsl(tt)], func=AF.Square),
                 reads=[("hT", kc, tt)], writes=[("sq", kc % 4)])
            S.op("tensor", lambda e: e.matmul(ps[:], lhsT=C.ones[:], rhs=sq[:], start=(kc == 0), stop=(kc == KC - 1)),
                 reads=["ones", ("sq", kc % 4)], writes=[pk])
        S.op("vector", lambda e: e.tensor_scalar(out=C.rs[:], in0=ps[:], scalar1=1.0 / D, scalar2=EPS, op0=ALU.mult, op1=ALU.add),
             reads=[pk], writes=["rs"])
        S.op("scalar", lambda e: e.activation(out=C.rs[:], in_=C.rs[:], func=AF.Sqrt), reads=["rs"], writes=["rs"])
        S.op("vector", lambda e: e.reciprocal(out=C.rs[:], in_=C.rs[:]), reads=["rs"], writes=["rs"])
        for kc in range(KC):
            if final:
                S.op("vector", lambda e: e.scalar_tensor_tensor(out=C.hT[:, kc, tsl(tt)], in0=C.hT[:, kc, tsl(tt)], scalar=C.gn[:, gi, kc:kc + 1],
                                                               in1=C.rs[:], op0=ALU.mult, op1=ALU.mult),
                     reads=[("hT", kc, tt), "rs", "gn"], writes=[("hT", kc, tt)])
            else:
                S.op("vector", lambda e: e.scalar_tensor_tensor(out=C.xn[:, kc, tsl(tt)], in0=C.hT[:, kc, tsl(tt)], scalar=C.gn[:, gi, kc:kc + 1],
                                                               in1=C.rs[:], op0=ALU.mult, op1=ALU.mult),
                     reads=[("hT", kc, tt), "rs", "gn"], writes=[("xn", kc, tt)])
    S.barrier()
    ph.close()


def inproj(C, w, col0, tt, evac):
    S = C.S
    ps, pk = C.nextps()
    wt, wk = w
    for kc in range(KC):
        S.op("tensor", lambda e: e.matmul(ps[:], lhsT=wt[:, kc, col0:col0 + 128], rhs=C.xn[:, kc, tsl(tt)], start=(kc == 0), stop=(kc == KC - 1)),
             reads=[wk, ("xn", kc, tt)], writes=[pk])
    evac(ps, pk)


def gelu_inplace(C, x, xk, t1, t1k, tt):
    S = C.S
    sl = tsl(tt)
    S.op("scalar", lambda e: e.activation(out=t1[:, sl], in_=x[:, sl], func=AF.Square), reads=[(xk, tt)], writes=[(t1k, tt)])
    S.op("vector", lambda e: e.tensor_scalar(out=t1[:, sl], in0=t1[:, sl], scalar1=0.044715, scalar2=1.0, op0=ALU.mult, op1=ALU.add),
         reads=[(t1k, tt)], writes=[(t1k, tt)])
    S.op("vector", lambda e: e.tensor_tensor(out=t1[:, sl], in0=t1[:, sl], in1=x[:, sl], op=ALU.mult), reads=[(t1k, tt), (xk, tt)], writes=[(t1k, tt)])
    S.op("scalar", lambda e: e.activation(out=t1[:, sl], in_=t1[:, sl], func=AF.Sigmoid, scale=GELU_K), reads=[(t1k, tt)], writes=[(t1k, tt)])
    S.op("vector", lambda e: e.tensor_tensor(out=x[:, sl], in0=x[:, sl], in1=t1[:, sl], op=ALU.mult), reads=[(t1k, tt), (xk, tt)], writes=[(xk, tt)])


def outproj_group(C, acts, wouts):
    S = C.S
    n = len(acts)
    for tt in range(NTT):
        for m in range(KC):
            ps, pk = C.nextps()
            for i in range(n):
                a_ap, a_k = acts[i]
                wt, wk = wouts[i]
                S.op("tensor", lambda e: e.matmul(ps[:], lhsT=wt[:, m * 128:(m + 1) * 128], rhs=a_ap(tt), start=(i == 0), stop=(i == n - 1)),
                     reads=[wk, a_k(tt)], writes=[pk])
            S.op("vector", lambda e: e.tensor_tensor(out=C.hT[:, m, tsl(tt)], in0=C.hT[:, m, tsl(tt)], in1=ps[:], op=ALU.add),
                 reads=[pk, ("hT", m, tt)], writes=[("hT", m, tt)])


def lru_mixer(C):
    S, sb, nc = C.S, C.sb, C.nc
    G = 2
    with contextlib.ExitStack() as ph:
        win = [sb("l_win%d" % i, [128, KC, 256], BF16, ph) for i in range(2)]
        gw = [sb("l_gw%d" % i, [128, 2, 128], BF16, ph) for i in range(2)]
        wo = [sb("l_wo%d" % i, [128, D], BF16, ph) for i in range(2 * G)]
        hy = [sb("l_hy%d" % i, [128, NT], BF16, ph) for i in range(2 * G)]
        ysbs = [sb("l_ysb%d" % i, [128, NT], F32, ph) for i in range(2)]
        xpads = [sb("l_xpad%d" % i, [128, 3 + NT], F32, ph) for i in range(2)]
        t1 = sb("l_t1", [128, NT], F32, ph)
        xc = sb("l_xc", [128, NT], F32, ph)
        xcb = sb("l_xcb", [128, NT], BF16, ph)
        rr = sb("l_r", [128, NT], F32, ph)
        ig = sb("l_ig", [128, NT], F32, ph)
        hh = sb("l_h", [128, NT], F32, ph)
        for i in range(2):
            S.op("vector", lambda e: e.memset(xpads[i][:, 0:3], 0.0), writes=["l_xpad%d_0" % i])

        def load_and_proj(n):
            b = n % 2
            wi, gwi, woi = win[b], gw[b], wo[n % (2 * G)]
            wik, gwk, wok = "l_win%d" % b, "l_gw%d" % b, "l_wo%d" % (n % (2 * G))
            S.dma("gpsimd", wik, lambda e: e.dma_start(out=wi[:], in_=C.lru_win[n]), writes=[wik])
            S.dma("gpsimd", gwk, lambda e: e.dma_start(out=gwi[:], in_=C.lru_gw[n]), writes=[gwk])
            S.dma("gpsimd", wok, lambda e: e.dma_start(out=woi[:], in_=C.lru_wout[n]), writes=[wok])
            ysb, xpad = ysbs[b], xpads[b]
            for tt in range(NTT):
                inproj(C, (wi, wik), 128, tt, lambda ps, pk: S.op(
                    "scalar", lambda e: e.activation(out=xpad[:, 3 + tt * TT:3 + (tt + 1) * TT], in_=ps[:], func=AF.Identity),
                    reads=[pk], writes=[("l_xpad%d" % b, tt)]))
            for tt in range(NTT):
                inproj(C, (wi, wik), 0, tt, lambda ps, pk: S.op(
                    "scalar", lambda e: e.activation(out=ysb[:, tsl(tt)], in_=ps[:], func=AF.Identity), reads=[pk], writes=[("l_ysb%d" % b, tt)]))

        pending = None
        load_and_proj(0)
        for n in range(LN):
            b = n % 2
            gwi, gwk = gw[b], "l_gw%d" % b
            hyi, hyk = hy[n % (2 * G)], "l_hy%d" % (n % (2 * G))
            ysb, xpad = ysbs[b], xpads[b]
            yk, xk = "l_ysb%d" % b, "l_xpad%d" % b
            T4 = range(NTT)
            for tt in T4:
                rd = [(xk, tt), (xk, tt - 1) if tt > 0 else xk + "_0", "lvec"]
                S.op("vector", lambda e: e.tensor_scalar(out=xc[:, tsl(tt)], in0=xpad[:, 3 + tt * TT:3 + (tt + 1) * TT], scalar1=C.lvec[:, n, 3:4],
                                                        scalar2=C.lvec[:, n, 4:5], op0=ALU.mult, op1=ALU.add), reads=rd, writes=[("l_xc", tt)])
                for k in (2, 1, 0):
                    S.op("vector", lambda e: e.scalar_tensor_tensor(out=xc[:, tsl(tt)], in0=xpad[:, k + tt * TT:k + (tt + 1) * TT], scalar=C.lvec[:, n, k:k + 1],
                                                                   in1=xc[:, tsl(tt)], op0=ALU.mult, op1=ALU.add),
                         reads=rd + [("l_xc", tt)], writes=[("l_xc", tt)])
                S.op("scalar", lambda e: e.activation(out=xcb[:, tsl(tt)], in_=xc[:, tsl(tt)], func=AF.Identity), reads=[("l_xc", tt)], writes=[("l_xcb", tt)])
            if n + 1 < LN:
                load_and_proj(n + 1)
            if pending is not None:
                outproj_group(C, *pending)
                pending = None
            for tt in T4:
                S.op("scalar", lambda e: e.activation(out=t1[:, tsl(tt)], in_=ysb[:, tsl(tt)], func=AF.Square, scale=GELU_C), reads=[(yk, tt)], writes=[("l_t1", tt)])
            for g, (dst, dk) in enumerate(((rr, "l_r"), (ig, "l_ig"))):
                for tt in T4:
                    ps, pk = C.nextps()
                    S.op("tensor", lambda e: e.matmul(ps[:], lhsT=gwi[:, g, :], rhs=xcb[:, tsl(tt)], start=True, stop=True),
                         reads=[gwk, ("l_xcb", tt)], writes=[pk])
                    S.op("scalar", lambda e: e.activation(out=dst[:, tsl(tt)], in_=ps[:], func=AF.Sigmoid, bias=C.lvec[:, n, 5 + g:6 + g]),
                         reads=[pk, "lvec"], writes=[(dk, tt)])
            for tt in T4:
                sl = tsl(tt)
                S.op("vector", lambda e: e.scalar_tensor_tensor(out=t1[:, sl], in0=t1[:, sl], scalar=1.0, in1=ysb[:, sl], op0=ALU.add, op1=ALU.mult),
                     reads=[("l_t1", tt), (yk, tt)], writes=[("l_t1", tt)])
            for tt in T4:
                sl = tsl(tt)
                S.op("scalar", lambda e: e.activation(out=t1[:, sl], in_=t1[:, sl], func=AF.Sigmoid, scale=GELU_K), reads=[("l_t1", tt)], writes=[("l_t1", tt)])
            for tt in T4:
                sl = tsl(tt)
                S.op("vector", lambda e: e.tensor_tensor(out=ysb[:, sl], in0=ysb[:, sl], in1=t1[:, sl], op=ALU.mult), reads=[("l_t1", tt), (yk, tt)], writes=[(yk, tt)])
            for tt in T4:
                sl = tsl(tt)
                S.op("vector", lambda e: e.tensor_tensor(out=ig[:, sl], in0=ig[:, sl], in1=xc[:, sl], op=ALU.mult), reads=[("l_ig", tt), ("l_xc", tt)], writes=[("l_ig", tt)])
            for tt in T4:
                sl = tsl(tt)
                S.op("scalar", lambda e: e.activation(out=t1[:, sl], in_=rr[:, sl], func=AF.Exp, scale=C.lca[:, n, 1:2]), reads=[("l_r", tt), "lca"], writes=[("l_t1", tt)])
            for tt in T4:
                sl = tsl(tt)
                S.op("scalar", lambda e: e.activation(out=rr[:, sl], in_=rr[:, sl], func=AF.Exp, scale=C.lca[:, n, 0:1]), reads=[("l_r", tt), "lca"], writes=[("l_r", tt)])
            for tt in T4:
                sl = tsl(tt)
                S.op("scalar", lambda e: e.activation(out=t1[:, sl], in_=t1[:, sl], func=AF.Sqrt, scale=-1.0, bias=1.0), reads=[("l_t1", tt)], writes=[("l_t1", tt)])
            for tt in T4:
                sl = tsl(tt)
                S.op("vector", lambda e: e.tensor_tensor(out=ig[:, sl], in0=ig[:, sl], in1=t1[:, sl], op=ALU.mult), reads=[("l_ig", tt), ("l_t1", tt)], writes=[("l_ig", tt)])
            for tt in T4:
                sl = tsl(tt)
                init = 0.0 if tt == 0 else hh[:, tt * TT - 1:tt * TT]
                S.op("vector", lambda e: e.tensor_tensor_scan(out=hh[:, sl], data0=rr[:, sl], data1=ig[:, sl], initial=init, op0=ALU.mult, op1=ALU.add),
                     reads=[("l_r", tt), ("l_ig", tt)] + ([("l_h", tt - 1)] if tt else []), writes=[("l_h", tt)])
                S.op("vector", lambda e: e.tensor_tensor(out=hyi[:, sl], in0=hh[:, sl], in1=ysb[:, sl], op=ALU.mult),
                     reads=[("l_h", tt), (yk, tt)], writes=[(hyk, tt)])
            if n % G == G - 1:
                idx = [(n - G + 1 + i) % (2 * G) for i in range(G)]
                pending = ([((lambda tt, i=i: hy[i][:, tsl(tt)]), (lambda tt, i=i: ("l_hy%d" % i, tt))) for i in idx],
                           [(wo[i], "l_wo%d" % i) for i in idx])
        outproj_group(C, *pending)
        S.barrier()


def conv_ffn(C, L):
    S, sb, nc = C.S, C.sb, C.nc
    G = 4
    with contextlib.ExitStack() as ph:
        win = [sb("f_win%d" % i, [128, KC, 256], BF16, ph) for i in range(2)]
        wo = [sb("f_wo%d" % i, [128, D], BF16, ph) for i in range(2 * G)]
        act = [sb("f_act%d" % i, [128, NT], BF16, ph) for i in range(2 * G)]
        apads = [sb("f_apad%d" % i, [128, 2 + NT], F32, ph) for i in range(2)]
        bsbs = [sb("f_bsb%d" % i, [128, NT], BF16, ph) for i in range(2)]
        acs = [sb("f_ac%d" % i, [128, NT], F32, ph) for i in range(2)]
        t1 = sb("f_t1", [128, NT], F32, ph)
        for i in range(2):
            S.op("vector", lambda e: e.memset(apads[i][:, 0:2], 0.0), writes=["f_apad%d_0" % i])

        def load_and_proj(j):
            b = j % 2
            wi, woi = win[b], wo[j % (2 * G)]
            wik, wok = "f_win%d" % b, "f_wo%d" % (j % (2 * G))
            S.dma("gpsimd", wik, lambda e: e.dma_start(out=wi[:], in_=C.ffn_win[L, j]), writes=[wik])
            S.dma("gpsimd", wok, lambda e: e.dma_start(out=woi[:], in_=C.ffn_wout[L, j]), writes=[wok])
            apad, bsb, acb = apads[b], bsbs[b], acs[b]

            def evac_a(ps, pk, tt):
                S.op("scalar", lambda e: e.activation(out=apad[:, 2 + tt * TT:2 + (tt + 1) * TT], in_=ps[:], func=AF.Identity),
                     reads=[pk], writes=[("f_apad%d" % b, tt)])
                S.op("scalar", lambda e: e.activation(out=acb[:, tsl(tt)], in_=ps[:], func=AF.Identity, scale=C.fvec[:, L, j, 2:3], bias=C.fvec[:, L, j, 3:4]),
                     reads=[pk, "fvec"], writes=[("f_ac%d" % b, tt)])
            for tt in range(NTT):
                inproj(C, (wi, wik), 0, tt, lambda ps, pk: evac_a(ps, pk, tt))
            for tt in range(NTT):
                inproj(C, (wi, wik), 128, tt, lambda ps, pk: S.op(
                    "scalar", lambda e: e.activation(out=bsb[:, tsl(tt)], in_=ps[:], func=AF.Identity), reads=[pk], writes=[("f_bsb%d" % b, tt)]))

        pending = None
        load_and_proj(0)
        for j in range(FJ):
            b = j % 2
            acti, actk = act[j % (2 * G)], "f_act%d" % (j % (2 * G))
            apad, bsb, ac = apads[b], bsbs[b], acs[b]
            ak, bk, ack = "f_apad%d" % b, "f_bsb%d" % b, "f_ac%d" % b
            T4 = range(NTT)
            if j + 1 < FJ:
                load_and_proj(j + 1)
            if pending is not None:
                outproj_group(C, *pending)
                pending = None
            for tt in T4:
                sl = tsl(tt)
                rd = [(ak, tt), (ak, tt - 1) if tt > 0 else ak + "_0", "fvec"]
                for k in (1, 0):
                    S.op("vector", lambda e: e.scalar_tensor_tensor(out=ac[:, sl], in0=apad[:, k + tt * TT:k + (tt + 1) * TT], scalar=C.fvec[:, L, j, k:k + 1],
                                                                   in1=ac[:, sl], op0=ALU.mult, op1=ALU.add),
                         reads=rd + [(ack, tt)], writes=[(ack, tt)])
                S.op("scalar", lambda e: e.activation(out=t1[:, sl], in_=ac[:, sl], func=AF.Square, scale=GELU_C), reads=[(ack, tt)], writes=[("f_t1", tt)])
            for tt in T4:
                sl = tsl(tt)
                S.op("vector", lambda e: e.scalar_tensor_tensor(out=t1[:, sl], in0=t1[:, sl], scalar=1.0, in1=ac[:, sl], op0=ALU.add, op1=ALU.mult),
                     reads=[("f_t1", tt), (ack, tt)], writes=[("f_t1", tt)])
                S.op("scalar", lambda e: e.activation(out=t1[:, sl], in_=t1[:, sl], func=AF.Sigmoid, scale=GELU_K), reads=[("f_t1", tt)], writes=[("f_t1", tt)])
                S.op("vector", lambda e: e.tensor_tensor(out=ac[:, sl], in0=ac[:, sl], in1=bsb[:, sl], op=ALU.mult), reads=[(ack, tt), (bk, tt)], writes=[(ack, tt)])
            for tt in T4:
                sl = tsl(tt)
                S.op("vector", lambda e: e.tensor_tensor(out=acti[:, sl], in0=ac[:, sl], in1=t1[:, sl], op=ALU.mult),
                     reads=[(ack, tt), ("f_t1", tt)], writes=[(actk, tt)])
            if j % G == G - 1:
                idx = [(j - G + 1 + i) % (2 * G) for i in range(G)]
                pending = ([((lambda tt, i=i: act[i][:, tsl(tt)]), (lambda tt, i=i: ("f_act%d" % i, tt))) for i in idx],
                           [(wo[i], "f_wo%d" % i) for i in idx])
        outproj_group(C, *pending)
        S.barrier()


SLOPES = [2.0 ** (-8.0 * (h + 1) / 16.0) for h in range(16)]
BIGNEG = 30000.0
NPT = 5


def gelu_ap(C, x, t, xk, tk):
    S = C.S
    S.op("scalar", lambda e: e.activation(out=t, in_=x, func=AF.Square), reads=[xk], writes=[tk])
    S.op("vector", lambda e: e.tensor_scalar(out=t, in0=t, scalar1=0.044715, scalar2=1.0, op0=ALU.mult, op1=ALU.add), reads=[tk], writes=[tk])
    S.op("vector", lambda e: e.tensor_tensor(out=t, in0=t, in1=x, op=ALU.mult), reads=[tk, xk], writes=[tk])
    S.op("scalar", lambda e: e.activation(out=t, in_=t, func=AF.Sigmoid, scale=GELU_K), reads=[tk], writes=[tk])
    S.op("vector", lambda e: e.tensor_tensor(out=x, in0=x, in1=t, op=ALU.mult), reads=[tk, xk], writes=[xk])


def nsa_mixer(C):
    S, sb, nc = C.S, C.sb, C.nc
    C.nrot = 4
    po_banks = [(C.psum[i], ("ps", i)) for i in (4, 5, 6, 7)]
    po_i = [0]

    def nextpo():
        r = po_banks[po_i[0] % 4]
        po_i[0] += 1
        return r
    with contextlib.ExitStack() as ph:
        KCT = sb("n_KCT", [128, 4, 128], BF16, ph)
        VCa = sb("n_VCa", [128, 4, 97], BF16, ph)
        wch = [None, None]
        wci = [0]

        def alloc_wch(stack):
            for i in range(2):
                wch[i] = sb("n_wch%d" % i, [128, KC, 128], BF16, stack)

        def load_wch(idx):
            i = wci[0] % 2
            wci[0] += 1
            k = "n_wch%d" % i
            S.dma("gpsimd", k, lambda e: e.dma_start(out=wch[i][:], in_=C.nsa_wch[idx]), writes=[k])
            return wch[i], k

        with contextlib.ExitStack() as p1:
            alloc_wch(p1)
            KV0 = sb("n_KV0", [128, 4, NT], BF16, p1)
            W1 = sb("n_W1", [128, 2, 32, 256], BF16, p1)
            posT = sb("n_posT", [128, 2, 32], BF16, p1)
            b1 = sb("n_b1", [128, 2, 2], F32, p1)
            W2k = sb("n_W2k", [128, 2, 128], BF16, p1)
            W2v = sb("n_W2v", [128, 2, 64], BF16, p1)
            ovl = sb("n_ovl", [128, 33], F32, p1)
            hids = [sb("n_hid%d" % i, [128, 256], F32, p1) for i in range(2)]
            hscs = [sb("n_hsc%d" % i, [128, 256], F32, p1) for i in range(2)]
            ghbs = [sb("n_ghb%d" % i, [128, 2, 128], BF16, p1) for i in range(2)]
            cvec = sb("n_cvec", [1, 2, 256], BF16, p1)
            onesr = sb("n_onesr", [1, 128], BF16, p1)
            b1row = sb("n_b1row", [1, 2, 256], F32, p1)
            identc = sb("n_identc", [128, 128], F32, p1)
            S.dma("sync", "n_identc", lambda e: e.dma_start(out=identc[:], in_=C.c_ident), writes=["n_identc"])
            S.dma("sync", "n_b1row", lambda e: e.dma_start(out=b1row[:], in_=C.nsa_b1row), writes=["n_b1row"])
            S.op("vector", lambda e: e.memset(onesr[:], 1.0), writes=["n_onesr"])
            for kvi in range(2):
                S.dma("gpsimd", "n_W1_%d" % kvi, lambda e: e.dma_start(out=W1[:, kvi], in_=C.nsa_w1[kvi]), writes=[("n_W1", kvi)])
            S.dma("gpsimd", "n_posT", lambda e: e.dma_start(out=posT[:], in_=C.nsa_posT), writes=["n_posT"])
            S.dma("sync", "n_b1", lambda e: e.dma_start(out=b1[:], in_=C.nsa_b1), writes=["n_b1"])
            S.dma("gpsimd", "n_W2k", lambda e: e.dma_start(out=W2k[:], in_=C.nsa_w2k), writes=["n_W2k"])
            S.dma("gpsimd", "n_W2v", lambda e: e.dma_start(out=W2v[:], in_=C.nsa_w2v), writes=["n_W2v"])
            S.dma("sync", "n_ovl", lambda e: e.dma_start(out=ovl[:], in_=C.c_ovl), writes=["n_ovl"])
            S.op("vector", lambda e: e.memset(KCT[:], 0.0), writes=[("n_KCT", g) for g in range(4)])
            S.op("vector", lambda e: e.memset(VCa[:], 0.0), writes=[("n_VCa", g) for g in range(4)])
            for g in range(4):
                S.op("vector", lambda e: e.tensor_copy(out=VCa[:, g, 64:97], in_=ovl[:]), reads=["n_ovl"], writes=[("n_VCa", g)])
            for c4 in range(4):
                wt, wk = load_wch(c4)
                for tt in range(NTT):
                    inproj(C, (wt, wk), 0, tt, lambda ps, pk: S.op(
                        "scalar", lambda e: e.activation(out=KV0[:, c4, tsl(tt)], in_=ps[:], func=AF.Identity), reads=[pk], writes=[("n_KV0", c4)]))
            for kvi in range(2):
                ps, pk = C.nextps()
                for i in range(32):
                    S.op("tensor", lambda e: e.matmul(ps[0:1, 0:256], lhsT=posT[0:64, kvi, i:i + 1], rhs=W1[0:64, kvi, i, :], start=(i == 0), stop=(i == 31)),
                         reads=[("n_W1", kvi), "n_posT"], writes=[pk])
                S.op("vector", lambda e: e.tensor_tensor(out=cvec[0:1, kvi, :], in0=ps[0:1, 0:256], in1=b1row[0:1, kvi, :], op=ALU.add),
                     reads=[pk, "n_b1row"], writes=[("n_cvec", kvi)])
            un = 0
            for kvi in range(2):
                for g in range(4):
                    c4 = kvi * 2 + g // 2
                    rows = slice((g % 2) * 64, (g % 2) * 64 + 64)
                    ub = un % 2
                    un += 1
                    hid, hsc, ghb = hids[ub], hscs[ub], ghbs[ub]
                    hk, sk, gk = "n_hid%d" % ub, "n_hsc%d" % ub, "n_ghb%d" % ub
                    ps, pk = C.nextps()
                    for i in range(32):
                        S.op("tensor", lambda e: e.matmul(ps[0:127, 0:256], lhsT=KV0[rows, c4, i:i + 2017:16], rhs=W1[rows, kvi, i, :], start=(i == 0), stop=False),
                             reads=[("n_W1", kvi), ("n_KV0", c4)], writes=[pk])
                    S.op("tensor", lambda e: e.matmul(ps[0:127, 0:256], lhsT=onesr[0:1, 0:127], rhs=cvec[0:1, kvi, :], start=False, stop=True),
                         reads=["n_onesr", ("n_cvec", kvi)], writes=[pk])
                    S.op("scalar", lambda e: e.activation(out=hid[0:127, :], in_=ps[0:127, 0:256], func=AF.Identity), reads=[pk], writes=[hk])
                    gelu_ap(C, hid[0:127, :], hsc[0:127, :], hk, sk)
                    ps, pk = C.nextps()
                    for mh in range(2):
                        S.op("tensor", lambda e: e.transpose(out=ps[:, mh * 128:mh * 128 + 127], in_=hid[0:127, mh * 128:(mh + 1) * 128], identity=identc[0:127, 0:127]),
                             reads=[hk, "n_identc"], writes=[pk])
                    S.op("scalar", lambda e: e.activation(out=ghb[:, :, 0:127], in_=ps[:, 0:256].rearrange("p (m c) -> p m c", c=128)[:, :, 0:127], func=AF.Identity),
                         reads=[pk], writes=[(gk, 0), (gk, 1)])
                    ps, pk = C.nextps()
                    if kvi == 0:
                        for mh in range(2):
                            S.op("tensor", lambda e: e.matmul(ps[:, 0:127], lhsT=W2k[:, mh, :], rhs=ghb[:, mh, 0:127], start=(mh == 0), stop=(mh == 1)),
                                 reads=["n_W2k", (gk, mh)], writes=[pk])
                        S.op("scalar", lambda e: e.activation(out=KCT[rows, g, 0:127], in_=ps[rows, 0:127], func=AF.Identity), reads=[pk], writes=[("n_KCT", g)])
                    else:
                        for mh in range(2):
                            S.op("tensor", lambda e: e.matmul(ps[0:127, 0:64], lhsT=ghb[:, mh, 0:127], rhs=W2v[:, mh, :], start=(mh == 0), stop=(mh == 1)),
                                 reads=["n_W2v", (gk, mh)], writes=[pk])
                        S.op("scalar", lambda e: e.activation(out=VCa[0:127, g, 0:64], in_=ps[0:127, 0:64], func=AF.Identity), reads=[pk], writes=[("n_VCa", g)])
            S.barrier()

        pA = ph.enter_context(contextlib.ExitStack())
        QT = sb("n_QT", [128, 8, NT], BF16, pA)
        Vaug = sb("n_Vaug", [128, 2, 16, 4, 65], BF16, pA)
        gts = sb("n_gts", [128, 16, 48], F32, pA)
        with contextlib.ExitStack() as p2:
            alloc_wch(p2)
            K12 = sb("n_K12", [128, 2, 2, NT], BF16, p2)
            wtok = sb("n_wtok", [128, KC, 560], BF16, p2)
            S.dma("gpsimd", "n_wtok", lambda e: e.dma_start(out=wtok[:], in_=C.nsa_wtok), writes=["n_wtok"])
            for mq in range(8 if getattr(C, 'nsa_stop', 9) >= 2 else 0):
                wt, wk = load_wch(4 + mq)
                for tt in range(NTT):
                    inproj(C, (wt, wk), 0, tt, lambda ps, pk: S.op(
                        "scalar", lambda e: e.activation(out=QT[:, mq, tsl(tt)], in_=ps[:], func=AF.Identity, scale=0.125), reads=[pk], writes=[("n_QT", mq, tt)]))
            for br in range(2):
                for c2 in range(2):
                    wt, wk = load_wch(12 + br * 2 + c2)
                    for tt in range(NTT):
                        inproj(C, (wt, wk), 0, tt, lambda ps, pk: S.op(
                            "scalar", lambda e: e.activation(out=K12[:, br, c2, tsl(tt)], in_=ps[:], func=AF.Identity), reads=[pk], writes=[("n_K12", br, c2, tt)]))
            S.op("vector", lambda e: e.memset(Vaug[:].rearrange("p a b c d -> p (a b c) d")[:, :, 64:65], 1.0), writes=["n_Vone"])
            for t16 in range(16 if getattr(C, 'nsa_stop', 9) >= 2 else 0):
                tok = slice(t16 * 128, (t16 + 1) * 128)
                ps, pk = C.nextps()
                for kc in range(KC):
                    S.op("tensor", lambda e: e.matmul(ps[:, 0:512], lhsT=C.xn[:, kc, tok], rhs=wtok[:, kc, 0:512], start=(kc == 0), stop=(kc == KC - 1)),
                         reads=["n_wtok", ("xn", kc, t16 // 4)], writes=[pk])
                for br in range(2):
                    S.op("scalar", lambda e: e.activation(out=Vaug[:, br, t16, :, 0:64], in_=ps[:, br * 256:(br + 1) * 256].rearrange("p (g d) -> p g d", d=64),
                                                          func=AF.Identity), reads=[pk], writes=[("n_Vaug", br, t16)])
                ps, pk = C.nextps()
                for kc in range(KC):
                    S.op("tensor", lambda e: e.matmul(ps[:, 0:48], lhsT=C.xn[:, kc, tok], rhs=wtok[:, kc, 512:560], start=(kc == 0), stop=(kc == KC - 1)),
                         reads=["n_wtok", ("xn", kc, t16 // 4)], writes=[pk])
                S.op("scalar", lambda e: e.activation(out=gts[:, t16, :], in_=ps[:, 0:48], func=AF.Sigmoid), reads=[pk], writes=[("n_gts", t16)])
            S.barrier()
            Kz = C.xn[:].rearrange("p (b g) t -> p b g t", b=2)
            for br in range(2):
                for g in range(4):
                    own = slice((g % 2) * 64, (g % 2) * 64 + 64)
                    oth = slice((1 - g % 2) * 64, (1 - g % 2) * 64 + 64)
                    S.op("gpsimd", lambda e: e.memset(Kz[oth, br, g, :], 0.0), writes=[("n_KzO", br, g)])
                    if g % 2 == 0:
                        S.op("scalar", lambda e: e.activation(out=Kz[own, br, g, :], in_=K12[own, br, g // 2, :], func=AF.Identity), writes=[("n_Kz", br, g)])
                    else:
                        S.op("vector", lambda e: e.tensor_copy(out=Kz[own, br, g, :], in_=K12[own, br, g // 2, :]), writes=[("n_Kz", br, g)])
            S.barrier()

        with contextlib.ExitStack() as p3:
            dtab = sb("n_dtab", [128, 10, TT], mybir.dt.int16, p3)
            etab = sb("n_etab", [128, 16, 128], BF16, p3)
            biasc = sb("n_biasc", [128, 16, 16], F32, p3)
            keep = sb("n_keep", [128, 16, 32], BF16, p3)
            addc = sb("n_addc", [128, 16, 32], BF16, p3)
            ident = sb("n_ident", [128, 128], F32, p3)
            sm = [sb("n_sm%d" % i, [128, TT], F32, p3) for i in range(2)]
            pT = [sb("n_pT%d" % i, [128, TT], BF16, p3) for i in range(NPT)]
            otoks = [sb("n_otok%d" % i, [128, 4, 256], F32, p3) for i in range(1)]
            valid = sb("n_valid", [128, 16, 1], F32, p3)
            otmp = sb("n_otmp", [128, 4, 64], F32, p3)
            rden = sb("n_rden", [128, 4, 1], F32, p3)
            ff = sb("n_ff", [128, 4, 1], F32, p3)
            comb = sb("n_comb", [128, 4, 65], F32, p3)
            fq = sb("n_fq", [128, 4, 16], F32, p3)
            S.dma("sync", "n_fq", lambda e: e.dma_start(out=fq[:], in_=C.c_fq), writes=["n_fq"])
            imp = sb("n_imp", [128, 4, 32], F32, p3)
            itmp = sb("n_itmp", [128, 4, 32], F32, p3)
            top8 = sb("n_top8", [128, 4, 8], F32, p3)
            selb = sb("n_selb", [128, 4, 32], F32, p3)
            selbT = sb("n_selbT", [128, 4, TT], BF16, p3)
            oTq = sb("n_oTq", [128, KC, TT], BF16, p3)
            alloc_wch(p3)
            S.dma("sync", "n_dtab", lambda e: e.dma_start(out=dtab[:, 0:9, :], in_=C.c_dtab[:, 0:9, :]), writes=["n_dtab"])
            S.op("gpsimd", lambda e: e.memset(etab[:], 0.0), writes=["n_etab"])
            S.op("gpsimd", lambda e: e.memset(selbT[:], 0.0), writes=[("n_selbT", g) for g in range(4)])
            S.dma("gpsimd", "n_etab", lambda e: e.dma_start(out=etab[0:32], in_=C.c_etab), writes=["n_etab"])
            S.dma("sync", "n_biasc", lambda e: e.dma_start(out=biasc[:], in_=C.c_biasc), writes=["n_biasc"])
            S.dma("gpsimd", "n_keep", lambda e: e.dma_start(out=keep[:], in_=C.c_keep), writes=["n_keep"])
            S.dma("gpsimd", "n_addc", lambda e: e.dma_start(out=addc[:], in_=C.c_addc), writes=["n_addc"])
            S.dma("sync", "n_ident", lambda e: e.dma_start(out=ident[:], in_=C.c_ident), writes=["n_ident"])
            S.dma("sync", "n_valid", lambda e: e.dma_start(out=valid[:], in_=C.c_valid), writes=["n_valid"])
            smi = [0]
            pti = [0]

            def score_tile(mm_fn, mm_reads, dti, scal, bias_ap):
                dkey = "n_dtabc" if dti == 9 else "n_dtab"
                i = smi[0] % 2
                smi[0] += 1
                ps, pk = C.nextps()
                mm_fn(ps, pk)
                if dti is None:
                    smi[0] -= 1
                    jj = pti[0] % NPT
                    pti[0] += 1
                    S.op("scalar", lambda e: e.activation(out=pT[jj][:], in_=ps[:], func=AF.Exp, bias=bias_ap), reads=[pk, "n_biasc"], writes=[("n_pT", jj)])
                    return pT[jj], ("n_pT", jj)
                j = pti[0] % NPT
                pti[0] += 1
                S.op("vector", lambda e: e.scalar_tensor_tensor(out=sm[i][:], in0=dtab[:, dti, :], scalar=scal, in1=ps[:], op0=ALU.mult, op1=ALU.add),
                     reads=[pk, dkey], writes=[("n_sm", i)])
                if bias_ap is None:
                    S.op("scalar", lambda e: e.activation(out=pT[j][:], in_=sm[i][:], func=AF.Exp), reads=[("n_sm", i)], writes=[("n_pT", j)])
                else:
                    S.op("scalar", lambda e: e.activation(out=pT[j][:], in_=sm[i][:], func=AF.Exp, bias=bias_ap), reads=[("n_sm", i), "n_biasc"], writes=[("n_pT", j)])
                return pT[j], ("n_pT", j)

            def run_jobs(jobs, LA=NPT - 1):
                staged = []
                for idx in range(len(jobs) + LA):
                    if idx < len(jobs):
                        jb = jobs[idx]
                        staged.append(score_tile(jb["mm"], None, jb["dti"], jb["scal"], jb["bias"]))
                    k = idx - LA
                    if k >= 0:
                        jobs[k]["pv"](*staged[k])

            def accum_out(po, pok, ncol, otok, okey, r, gate_col, qt, first, src3=None):
                po3 = po[:, 0:4 * ncol].rearrange("p (s c) -> p s c", c=ncol) if src3 is None else src3
                S.op("vector", lambda e: e.tensor_scalar(out=rden[:], in0=po3[:, :, 64:65], scalar1=1e-30, scalar2=None, op0=ALU.max), reads=[pok], writes=["n_rden"])
                S.op("vector", lambda e: e.reciprocal(out=rden[:], in_=rden[:]), reads=["n_rden"], writes=["n_rden"])
                if first and qt == 0:
                    S.op("vector", lambda e: e.tensor_tensor(out=rden[:], in0=rden[:], in1=valid[:, 0:4, :], op=ALU.mult), reads=["n_rden", "n_valid"], writes=["n_rden"])
                S.op("vector", lambda e: e.tensor_tensor(out=ff[:], in0=rden[:], in1=gts[:, qt * 4:(qt + 1) * 4, gate_col:gate_col + 1], op=ALU.mult),
                     reads=["n_rden"] + [("n_gts", qt * 4 + i) for i in range(4)], writes=["n_ff"])
                dst = otok[:, :, r * 64:(r + 1) * 64]
                if first:
                    S.op("vector", lambda e: e.tensor_tensor(out=dst, in0=po3[:, :, 0:64], in1=ff[:].to_broadcast([128, 4, 64]), op=ALU.mult),
                         reads=[pok, "n_ff"], writes=[(okey, r)])
                else:
                    S.op("vector", lambda e: e.tensor_tensor(out=otmp[:], in0=po3[:, :, 0:64], in1=ff[:].to_broadcast([128, 4, 64]), op=ALU.mult),
                         reads=[pok, "n_ff"], writes=["n_otmp"])
                    S.op("vector", lambda e: e.tensor_tensor(out=dst, in0=dst, in1=otmp[:], op=ALU.add), reads=["n_otmp", (okey, r)], writes=[(okey, r)])

            for qt in range(NTT if getattr(C, 'nsa_stop', 9) >= 3 else 0):
                qs = tsl(qt)
                S.dma("sync", "n_dtabc", lambda e: e.dma_start(out=dtab[:, 9, :], in_=C.c_dtab[:, 9 + qt, :]), writes=["n_dtabc"])
                for g in range(4):
                    half = g % 2
                    rows = slice(half * 64, half * 64 + 64)
                    c2 = g // 2
                    otok = otoks[0]
                    okey = "n_otok0"
                    jobs = []
                    for r in range(4):
                        hh = g * 4 + r
                        mq = (g // 2) * 4 + r

                        def mm(ps, pk, mq=mq):
                            S.op("tensor", lambda e: e.matmul(ps[:], lhsT=KCT[:, g, :], rhs=QT[:, mq, qs], start=True, stop=True),
                                 reads=[("n_KCT", g), ("n_QT", mq, qt)], writes=[pk])

                        def pv(p_t, p_k, r=r, hh=hh):
                            po, pok = nextpo()
                            for sub in range(4):
                                S.op("tensor", lambda e: e.matmul(po[:, sub * 97:(sub + 1) * 97], lhsT=p_t[:, sub * 128:(sub + 1) * 128], rhs=VCa[:, g, :], start=True, stop=True),
                                     reads=[p_k, ("n_VCa", g)], writes=[pok])
                            accum_out(po, pok, 97, otok, okey, r, hh, qt, True)
                            po3 = po[:, 0:388].rearrange("p (s c) -> p s c", c=97)
                            if r == 0:
                                S.op("vector", lambda e: e.tensor_tensor(out=imp[:], in0=po3[:, :, 65:97], in1=rden[:].to_broadcast([128, 4, 32]), op=ALU.mult),
                                     reads=[pok, "n_rden"], writes=["n_imp"])
                            else:
                                S.op("vector", lambda e: e.tensor_tensor(out=itmp[:], in0=po3[:, :, 65:97], in1=rden[:].to_broadcast([128, 4, 32]), op=ALU.mult),
                                     reads=[pok, "n_rden"], writes=["n_itmp"])
                                S.op("vector", lambda e: e.tensor_tensor(out=imp[:], in0=imp[:], in1=itmp[:], op=ALU.add), reads=["n_itmp", "n_imp"], writes=["n_imp"])
                        jobs.append(dict(mm=mm, dti=9, scal=-SLOPES[hh] / 2.0, bias=None, pv=pv))
                    run_jobs(jobs)
                    S.op("vector", lambda e: e.tensor_tensor(out=imp[:], in0=imp[:], in1=keep[:, qt * 4:(qt + 1) * 4, :], op=ALU.mult), reads=["n_imp", "n_keep"], writes=["n_imp"])
                    S.op("vector", lambda e: e.tensor_tensor(out=imp[:], in0=imp[:], in1=addc[:, qt * 4:(qt + 1) * 4, :], op=ALU.add), reads=["n_imp", "n_addc"], writes=["n_imp"])
                    for sub in range(4):
                        S.op("vector", lambda e: e.max(out=top8[:, sub, :], in_=imp[:, sub, :]), reads=["n_imp"], writes=["n_top8"])
                    for sub in range(4):
                        S.op("vector", lambda e: e.tensor_scalar(out=selb[:, sub, :], in0=imp[:, sub, :], scalar1=top8[:, sub, 7:8], scalar2=-BIGNEG, op0=ALU.is_lt, op1=ALU.mult),
                             reads=["n_imp", "n_top8"], writes=["n_selb"])
                    ps, pk = C.nextps()
                    for sub in range(4):
                        S.op("tensor", lambda e: e.transpose(out=ps[0:32, sub * 128:(sub + 1) * 128], in_=selb[:, sub, :], identity=ident[:]),
                             reads=["n_selb", "n_ident"], writes=[pk])
                    S.op("scalar", lambda e: e.activation(out=selbT[0:32, g, :], in_=ps[0:32, :], func=AF.Identity), reads=[pk], writes=[("n_selbT", g)])
                    jobs = []
                    for r in range(4):
                        hh = g * 4 + r
                        mq = (g // 2) * 4 + r
                        for br in range(2):
                            kts = list(range(0, qt * 4 + 4)) if br == 0 else list(range(max(0, qt * 4 - 4), qt * 4 + 4))
                            state = {}
                            for n_k, kt in enumerate(kts):
                                delta = qt * TT - kt * 128
                                bias_ap = None
                                far = False
                                if delta <= 0:
                                    dti = (-delta) // 128
                                elif br == 1:
                                    dti = 3 + delta // 128
                                else:
                                    dti = None
                                    far = True
                                    bias_ap = biasc[:, hh, delta // 128:delta // 128 + 1]
                                nfar = qt * 4 if br == 0 else 0

                                def mm(ps, pk, br=br, kt=kt, mq=mq):
                                    ks = slice(kt * 128, (kt + 1) * 128)
                                    S.op("tensor", lambda e: e.matmul(ps[:], lhsT=Kz[:, br, g, ks], rhs=QT[:, mq, qs], start=True, stop=(br == 1)),
                                         reads=[("n_Kz", br, g), ("n_KzO", br, g), ("n_QT", mq, qt)], writes=[pk])
                                    if br == 0:
                                        S.op("tensor", lambda e: e.matmul(ps[:], lhsT=etab[:, kt, :], rhs=selbT[:, g, :], start=False, stop=True),
                                             reads=["n_etab", ("n_selbT", g)], writes=[pk])

                                def pv(p_t, p_k, br=br, kt=kt, n_k=n_k, nk=len(kts), state=state, r=r, hh=hh, far=far, nfar=nfar):
                                    if n_k == 0 and nfar:
                                        state["far"] = nextpo()
                                    if n_k == nfar:
                                        state["po"] = nextpo()
                                    po, pok = state["far"] if far else state["po"]
                                    first = (n_k == 0) if far else (n_k == nfar)
                                    last = (n_k == nfar - 1) if far else (n_k == nk - 1)
                                    for sub in range(4):
                                        S.op("tensor", lambda e: e.matmul(po[:, sub * 65:(sub + 1) * 65], lhsT=p_t[:, sub * 128:(sub + 1) * 128], rhs=Vaug[:, br, kt, g, :],
                                                                          start=(first and sub == 0), stop=(last and sub == 3)),
                                             reads=[p_k, ("n_Vaug", br, kt), "n_Vone"], writes=[pok])
                                    if n_k == nk - 1:
                                        if nfar:
                                            pf, pfk = state["far"]
                                            pf3 = pf[:, 0:260].rearrange("p (s c) -> p s c", c=65)
                                            po3 = po[:, 0:260].rearrange("p (s c) -> p s c", c=65)
                                            S.op("vector", lambda e: e.tensor_tensor(out=comb[:], in0=pf3, in1=fq[:, :, hh:hh + 1].to_broadcast([128, 4, 65]), op=ALU.mult),
                                                 reads=[pfk, "n_fq"], writes=["n_comb"])
                                            S.op("vector", lambda e: e.tensor_tensor(out=comb[:], in0=comb[:], in1=po3, op=ALU.add), reads=[pok, "n_comb"], writes=["n_comb"])
                                            accum_out(po, "n_comb", 65, otok, okey, r, (1 + br) * 16 + hh, qt, False, src3=comb[:])
                                        else:
                                            accum_out(po, pok, 65, otok, okey, r, (1 + br) * 16 + hh, qt, False)
                                jobs.append(dict(mm=mm, dti=dti, scal=-SLOPES[hh], bias=bias_ap, pv=pv))
                    run_jobs(jobs)
                    for kk in range(2):
                        kc = 2 * g + kk
                        ps, pk = C.nextps()
                        for sub in range(4):
                            S.op("tensor", lambda e: e.transpose(out=ps[:, sub * 128:(sub + 1) * 128], in_=otok[:, sub, kk * 128:(kk + 1) * 128], identity=ident[:]),
                                 reads=[(okey, 2 * kk), (okey, 2 * kk + 1), "n_ident"], writes=[pk])
                        S.op("scalar", lambda e: e.activation(out=oTq[:, kc, :], in_=ps[:], func=AF.Identity), reads=[pk], writes=[("n_oTq", kc)])
                for m in range(KC):
                    wt, wk = load_wch(16 + m)
                    ps, pk = C.nextps()
                    for kc in range(KC):
                        S.op("tensor", lambda e: e.matmul(ps[:], lhsT=wt[:, kc, :], rhs=oTq[:, kc, :], start=(kc == 0), stop=(kc == KC - 1)),
                             reads=[wk, ("n_oTq", kc)], writes=[pk])
                    S.op("vector", lambda e: e.tensor_tensor(out=C.hT[:, m, qs], in0=C.hT[:, m, qs], in1=ps[:], op=ALU.add),
                         reads=[pk, ("hT", m, qt)], writes=[("hT", m, qt)])
            S.barrier()
        pA.close()
        S.barrier()
    C.nrot = 8


def prep_weights(inp):
    f = np.ascontiguousarray
    w = {}
    g = np.stack([inp["lru_norm_g"][0], inp["ffn_norm_g"][0], inp["nsa_norm_g"][0], inp["ffn_norm_g"][1], inp["final_norm_g"]], 0)
    w["gains"] = f(g.reshape(5, KC, 128).transpose(2, 0, 1))
    wi = inp["lru_w_in"][0].reshape(KC, 128, 2, LN, 128)
    w["lru_win"] = f(wi.transpose(3, 1, 0, 2, 4).reshape(LN, 128, KC, 256))
    w["lru_gw"] = f(inp["lru_gate_w"][0].transpose(1, 2, 0, 3))
    w["lru_wout"] = f(inp["lru_w_out"][0].reshape(LN, 128, D))
    vec = np.concatenate([inp["lru_conv_w"][0], inp["lru_conv_b"][0][None], inp["lru_gate_b"][0], inp["lru_a_param"][0][None]], 0)
    w["lru_vec"] = f(vec.reshape(8, LN, 128).transpose(2, 1, 0))
    fw = inp["ffn_w_in"].reshape(2, KC, 128, 2, FJ, 128)
    w["ffn_win"] = f(fw.transpose(0, 4, 2, 1, 3, 5).reshape(2, FJ, 128, KC, 256))
    w["ffn_wout"] = f(inp["ffn_w_out"].reshape(2, FJ, 128, D))
    fv = np.concatenate([inp["ffn_conv_w"], inp["ffn_conv_b"][:, None]], 1)
    w["ffn_vec"] = f(fv.reshape(2, 4, FJ, 128).transpose(3, 0, 2, 1))
    w.update(nsa_host(inp))
    w.update(const_tables())
    return w


def nsa_host(inp):
    f = np.ascontiguousarray
    w = {}
    W = inp["nsa_w_in"][0]
    Wr = W.reshape(KC, 128, 2608)

    def chunk(cols):
        return Wr[:, :, cols].transpose(1, 0, 2)
    chunks = []
    for c4 in range(4):
        kvi, gp = c4 // 2, c4 % 2
        base = 1024 + kvi * 256 + gp * 128
        chunks.append(chunk(np.arange(base, base + 128)))
    for mq in range(8):
        p, r = mq // 4, mq % 4
        ha, hb = 4 * (2 * p) + r, 4 * (2 * p + 1) + r
        chunks.append(chunk(np.concatenate([np.arange(ha * 64, ha * 64 + 64), np.arange(hb * 64, hb * 64 + 64)])))
    for br in (1, 2):
        for c2 in range(2):
            base = 1024 + br * 512 + c2 * 128
            chunks.append(chunk(np.arange(base, base + 128)))
    Wo = inp["nsa_w_out"][0].reshape(KC, 128, D)
    for m in range(KC):
        chunks.append(Wo[:, :, m * 128:(m + 1) * 128].transpose(1, 0, 2))
    w["nsa_wch"] = f(np.stack(chunks, 0))
    tokcols = np.concatenate([np.arange(1024 + 512 + 256, 1024 + 512 + 512), np.arange(1024 + 1024 + 256, 1024 + 1024 + 512), np.arange(2560, 2608)])
    w["nsa_wtok"] = f(Wr[:, :, tokcols].transpose(1, 0, 2))
    w1 = inp["nsa_cmp_w1"][0].reshape(2, 32, 64, 256).transpose(0, 2, 1, 3)
    w["nsa_w1"] = f(np.concatenate([w1, w1], 1))
    pT = inp["nsa_cmp_pos"][0].transpose(2, 0, 1)
    w["nsa_posT"] = f(np.concatenate([pT, pT], 0))
    w["nsa_b1"] = f(inp["nsa_cmp_b1"][0].reshape(2, 2, 128).transpose(2, 0, 1))
    w["nsa_b1row"] = f(inp["nsa_cmp_b1"][0][None])
    w2 = inp["nsa_cmp_w2"][0]
    w2k = w2[0].reshape(2, 128, 64).transpose(1, 0, 2)
    w["nsa_w2k"] = f(np.concatenate([w2k, w2k], 2))
    w["nsa_w2v"] = f(w2[1].reshape(2, 128, 64).transpose(1, 0, 2))
    return w


def const_tables():
    c = {}
    HUGE = 30000
    k = np.arange(128)[:, None]
    q = np.arange(TT)[None, :]
    dt = np.zeros((128, 13, TT), np.int64)
    for i in range(4):
        d = -128 * i + q - k
        dt[:, i] = np.where(d >= 0, d, HUGE)
    for i in range(1, 5):
        d = 128 * i + q - k
        dt[:, 3 + i] = np.where(d < 512, d, HUGE)
    dt[:, 8] = q - k
    cc = np.arange(128)[:, None]
    for qt in range(4):
        t = qt * TT + q
        d2 = 2 * t - 32 * cc - 31
        ok = (16 * cc + 31 <= t) & (cc < 127)
        dt[:, 9 + qt] = np.where(ok, d2, HUGE)
    c["c_dtab"] = dt.astype(np.int16)
    et = np.zeros((32, 16, 128), np.float32)
    for kt in range(16):
        for kk in range(128):
            et[(kt * 128 + kk) // 64, kt, kk] = 1.0
    c["c_etab"] = et
    sl = np.array(SLOPES, np.float64)
    kk = np.arange(128, dtype=np.float64)[:, None, None]
    bc = -(sl[None, :, None] * ((128.0 * np.arange(16))[None, None, :] - kk))
    c["c_biasc"] = np.ascontiguousarray(bc).astype(np.float32)
    qq = (np.arange(4)[None, :, None] * 128 + np.arange(128)[:, None, None]).astype(np.float64)
    c["c_fq"] = np.exp(-sl[None, None, :] * qq).astype(np.float32)
    t = (np.arange(16)[None, :, None] * 128 + np.arange(128)[:, None, None])
    j = np.arange(32)[None, None, :]
    cur = t // 64
    forced = (j == 0) | (j == cur) | (j == cur - 1)
    future = j > cur
    c["c_keep"] = np.where(forced | future, 0.0, 1.0).astype(np.float32)
    c["c_addc"] = np.where(forced, 1e4, np.where(future, -1.0, 0.0)).astype(np.float32)
    c["c_ident"] = np.eye(128, dtype=np.float32)
    c["c_valid"] = (t >= 31).astype(np.float32)
    ov = np.zeros((128, 33), np.float32)
    ov[:127, 0] = 1.0
    cs = np.arange(127)[:, None] * 16
    sj = np.arange(32)[None, :]
    ov[:127, 1:] = ((cs < (sj + 1) * 64) & (cs + 32 > sj * 64)).astype(np.float32)
    c["c_ovl"] = ov
    return c


_CACHE = {}


def kernel(**inp):
    inp = {k: np.asarray(v) for k, v in inp.items()}
    ncores, nseq = 8, 2
    if "nc" not in _CACHE:
        _CACHE["nc"] = build(nseq)[0]
    nc = _CACHE["nc"]
    w = prep_weights(inp)
    x = inp["x"]
    xT = np.ascontiguousarray(x.reshape(ncores, nseq, NT, KC, 128).transpose(0, 1, 4, 3, 2))
    in_maps = [dict(w, xT=xT[c]) for c in range(ncores)]
    res = run_bass_kernel_spmd(nc, in_maps, core_ids=list(range(ncores)))
    o = np.stack([r["outT"] for r in res.results], 0)
    return np.ascontiguousarray(o.transpose(0, 1, 4, 3, 2)).reshape(16, NT, D).astype(np.float32)
```

```python
import contextlib
import numpy as np
import concourse.bass as bass
import concourse.mybir as mybir
from concourse.bass_utils import run_bass_kernel_spmd

F32 = mybir.dt.float32
BF16 = mybir.dt.bfloat16
AF = mybir.ActivationFunctionType
ALU = mybir.AluOpType
AX = mybir.AxisListType

D = 1024
KC = 8
NT = 2048
TT = 512
NTT = 4
LW = 1280
LN = 10
DFF = 3072
FJ = 24
EPS = 1e-6
GELU_K = 1.5957691216057308
GELU_C = 0.044715 ** 0.5


class Sched:
    ENGS = ("tensor", "vector", "scalar", "gpsimd", "sync")

    def __init__(self, nc, stack):
        self.nc = nc
        self.stack = stack
        self.eng = {e: getattr(nc, e) for e in self.ENGS}
        self.sem = {}
        self.cnt = {}
        self.known = {e: {} for e in self.ENGS}
        self.res = {}
        self.n_inst = 0
        self.n_wait = 0
        for e in ("tensor", "vector", "scalar", "gpsimd"):
            self._mksem(e)

    def _mksem(self, key):
        if key not in self.sem:
            name = "s_" + key.replace(":", "_")
            self.sem[key] = self.stack.enter_context(self.nc.semaphore(name))
            self.cnt[key] = 0
        return self.sem[key]

    def _deps(self, engine, reads, writes):
        deps = {}

        def add(k, v):
            if v > deps.get(k, 0):
                deps[k] = v
        for r in reads:
            st = self.res.get(r)
            if st and st["w"]:
                add(*st["w"])
        for w in writes:
            st = self.res.get(w)
            if st:
                if st["w"]:
                    add(*st["w"])
                for k, v in st["r"].items():
                    add(k, v)
        kn = self.known[engine]
        for k, v in deps.items():
            if k == engine and engine == "tensor":
                continue
            if kn.get(k, 0) >= v:
                continue
            self.eng[engine].wait_ge(self.sem[k], v)
            self.n_wait += 1
            kn[k] = v

    def _mark(self, key, val, reads, writes):
        for r in reads:
            st = self.res.setdefault(r, {"w": None, "r": {}})
            st["r"][key] = val
        for w in writes:
            self.res[w] = {"w": (key, val), "r": {}}

    def op(self, engine, fn, reads=(), writes=()):
        self._deps(engine, reads, writes)
        ins = fn(self.eng[engine])
        self.cnt[engine] += 1
        ins.then_inc(self.sem[engine], 1)
        self.n_inst += 1
        self._mark(engine, self.cnt[engine], reads, writes)
        return ins

    def dma(self, queue, key, fn, reads=(), writes=()):
        k = "dma:" + key
        self._mksem(k)
        self._deps(queue, reads, writes)
        ins = fn(self.eng[queue])
        self.cnt[k] += 16
        ins.then_inc(self.sem[k], 16)
        self.n_inst += 1
        self._mark(k, self.cnt[k], reads, writes)
        return ins

    def barrier(self):
        for e in self.ENGS:
            for k, v in self.cnt.items():
                if v == 0 or (k == e and e == "tensor"):
                    continue
                if self.known[e].get(k, 0) >= v:
                    continue
                self.eng[e].wait_ge(self.sem[k], v)
                self.known[e][k] = v

    def finish(self):
        for k, v in self.cnt.items():
            if v and self.known["sync"].get(k, 0) < v:
                self.nc.sync.wait_ge(self.sem[k], v)
                self.known["sync"][k] = v


class Ctx:
    pass


def tsl(tt):
    return slice(tt * TT, (tt + 1) * TT)


def build(nseq=2, upto=99, nsa_stop=9):
    nc = bass.Bass("TRN2", target_bir_lowering=False)
    C = Ctx()
    C.nc = nc
    C.nsa_stop = nsa_stop

    def din(name, shape):
        return nc.dram_tensor(name, list(shape), F32, kind="ExternalInput").ap()
    C.xT = din("xT", [nseq, 128, KC, NT])
    C.gains = din("gains", [128, 5, KC])
    C.lru_win = din("lru_win", [LN, 128, KC, 256])
    C.lru_gw = din("lru_gw", [LN, 128, 2, 128])
    C.lru_wout = din("lru_wout", [LN, 128, D])
    C.lru_vec = din("lru_vec", [128, LN, 8])
    C.ffn_win = din("ffn_win", [2, FJ, 128, KC, 256])
    C.ffn_wout = din("ffn_wout", [2, FJ, 128, D])
    C.ffn_vec = din("ffn_vec", [128, 2, FJ, 4])
    C.nsa_wch = din("nsa_wch", [24, 128, KC, 128])
    C.nsa_wtok = din("nsa_wtok", [128, KC, 560])
    C.nsa_w1 = din("nsa_w1", [2, 128, 32, 256])
    C.nsa_posT = din("nsa_posT", [128, 2, 32])
    C.nsa_b1 = din("nsa_b1", [128, 2, 2])
    C.nsa_b1row = din("nsa_b1row", [1, 2, 256])
    C.nsa_w2k = din("nsa_w2k", [128, 2, 128])
    C.nsa_w2v = din("nsa_w2v", [128, 2, 64])
    C.c_dtab = nc.dram_tensor("c_dtab", [128, 13, TT], mybir.dt.int16, kind="ExternalInput").ap()
    C.c_etab = din("c_etab", [32, 16, 128])
    C.c_biasc = din("c_biasc", [128, 16, 16])
    C.c_keep = din("c_keep", [128, 16, 32])
    C.c_addc = din("c_addc", [128, 16, 32])
    C.c_ident = din("c_ident", [128, 128])
    C.c_valid = din("c_valid", [128, 16, 1])
    C.c_fq = din("c_fq", [128, 4, 16])
    C.c_ovl = din("c_ovl", [128, 33])
    C.outT = nc.dram_tensor("outT", [nseq, 128, KC, NT], F32, kind="ExternalOutput").ap()

    with contextlib.ExitStack() as st:
        S = Sched(nc, st)
        C.S = S

        uid = [0]

        def sb(name, shape, dt=F32, stack=st):
            uid[0] += 1
            return stack.enter_context(nc.sbuf_tensor("%s_u%d" % (name, uid[0]), list(shape), dt))
        C.sb = sb
        C.hT = sb("hT", [128, KC, NT])
        C.xn = sb("xn", [128, KC, NT], BF16)
        C.ones = sb("ones", [128, 128], BF16)
        C.gn = sb("gn", [128, 5, KC])
        C.lvec = sb("lvec", [128, LN, 8])
        C.lca = sb("lca", [128, LN, 2])
        C.fvec = sb("fvec", [128, 2, FJ, 4])
        C.psum = [st.enter_context(nc.psum_tensor("ps%d" % i, [128, TT], F32)) for i in range(8)]
        C.psi = 0
        C.nrot = 8

        def nextps():
            i = C.psi % C.nrot
            C.psi += 1
            return C.psum[i], ("ps", i)
        C.nextps = nextps

        S.op("vector", lambda e: e.memset(C.ones[:], 1.0), writes=["ones"])
        S.dma("sync", "c0", lambda e: e.dma_start(out=C.gn[:], in_=C.gains), writes=["gn"])
        S.dma("sync", "c1", lambda e: e.dma_start(out=C.lvec[:], in_=C.lru_vec), writes=["lvec"])
        S.dma("sync", "c2", lambda e: e.dma_start(out=C.fvec[:], in_=C.ffn_vec), writes=["fvec"])
        lru_consts(C)

        for s in range(nseq):
            for kc in range(KC):
                S.dma("sync", "x%d" % kc, lambda e, kc=kc: e.dma_start(out=C.hT[:, kc, :], in_=C.xT[s, :, kc, :]),
                      writes=[("hT", kc, tt) for tt in range(NTT)])
            if upto >= 1:
                rmsnorm(C, 0)
                lru_mixer(C)
            if upto >= 2:
                rmsnorm(C, 1)
                conv_ffn(C, 0)
            if upto >= 3:
                rmsnorm(C, 2)
                nsa_mixer(C)
            if upto >= 4:
                rmsnorm(C, 3)
                conv_ffn(C, 1)
            if upto >= 5:
                rmsnorm(C, 4, final=True)
            for kc in range(KC):
                S.dma("sync", "o%d" % kc, lambda e, kc=kc: e.dma_start(out=C.outT[s, :, kc, :], in_=C.hT[:, kc, :]),
                      reads=[("hT", kc, tt) for tt in range(NTT)])
        S.finish()
    C.n_inst = S.n_inst
    C.n_wait = S.n_wait
    return nc, C


def lru_consts(C):
    S, sb = C.S, C.sb
    with contextlib.ExitStack() as ph:
        t = [sb("lc%d" % i, [128, LN], F32, ph) for i in range(6)]
        ap = C.lvec[:, :, 7]
        S.op("scalar", lambda e: e.activation(out=t[0][:], in_=ap, func=AF.Abs), reads=["lvec"], writes=["lc0"])
        S.op("scalar", lambda e: e.activation(out=t[1][:], in_=t[0][:], func=AF.Exp, scale=-1.0), reads=["lc0"], writes=["lc1"])
        S.op("scalar", lambda e: e.activation(out=t[2][:], in_=t[1][:], func=AF.Ln, bias=1.0), reads=["lc1"], writes=["lc2"])
        S.op("vector", lambda e: e.tensor_scalar(out=t[3][:], in0=t[1][:], scalar1=1.0 / 3.0, scalar2=-0.5, op0=ALU.mult, op1=ALU.add), reads=["lc1"], writes=["lc3"])
        S.op("vector", lambda e: e.tensor_tensor(out=t[3][:], in0=t[3][:], in1=t[1][:], op=ALU.mult), reads=["lc3", "lc1"], writes=["lc3"])
        S.op("vector", lambda e: e.tensor_scalar(out=t[3][:], in0=t[3][:], scalar1=1.0, scalar2=None, op0=ALU.add), reads=["lc3"], writes=["lc3"])
        S.op("vector", lambda e: e.tensor_tensor(out=t[3][:], in0=t[3][:], in1=t[1][:], op=ALU.mult), reads=["lc3", "lc1"], writes=["lc3"])
        S.op("vector", lambda e: e.tensor_single_scalar(out=t[4][:], in_=t[1][:], scalar=0.03, op=ALU.is_lt), reads=["lc1"], writes=["lc4"])
        S.op("vector", lambda e: e.tensor_tensor(out=t[3][:], in0=t[3][:], in1=t[2][:], op=ALU.subtract), reads=["lc3", "lc2"], writes=["lc3"])
        S.op("vector", lambda e: e.tensor_tensor(out=t[3][:], in0=t[3][:], in1=t[4][:], op=ALU.mult), reads=["lc3", "lc4"], writes=["lc3"])
        S.op("vector", lambda e: e.tensor_tensor(out=t[3][:], in0=t[3][:], in1=t[2][:], op=ALU.add), reads=["lc3", "lc2"], writes=["lc3"])
        S.op("vector", lambda e: e.tensor_scalar(out=t[5][:], in0=ap, scalar1=-1.0, scalar2=0.0, op0=ALU.mult, op1=ALU.max), reads=["lvec"], writes=["lc5"])
        S.op("vector", lambda e: e.tensor_tensor(out=t[3][:], in0=t[3][:], in1=t[5][:], op=ALU.add), reads=["lc3", "lc5"], writes=["lc3"])
        S.op("vector", lambda e: e.tensor_scalar(out=C.lca[:, :, 0], in0=t[3][:], scalar1=-8.0, scalar2=None, op0=ALU.mult), reads=["lc3"], writes=["lca"])
        S.op("vector", lambda e: e.tensor_scalar(out=C.lca[:, :, 1], in0=t[3][:], scalar1=-16.0, scalar2=None, op0=ALU.mult), reads=["lc3"], writes=["lca"])
        S.barrier()


def rmsnorm(C, gi, final=False):
    S = C.S
    ph = contextlib.ExitStack()
    C.sq = [C.sb("sq%d" % i, [128, TT], BF16, ph) for i in range(4)]
    C.rs = C.sb("rs", [128, TT], F32, ph)
    for tt in range(NTT):
        ps, pk = C.nextps()
        for kc in range(KC):
            sq = C.sq[kc % 4]
            S.op("scalar", lambda e: e.activation(out=sq[:], in_=C.hT[:, kc, tsl(tt)], func=AF.Square),
                 reads=[("hT", kc, tt)], writes=[("sq", kc % 4)])
            S.op("tensor", lambda e: e.matmul(ps[:], lhsT=C.ones[:], rhs=sq[:], start=(kc == 0), stop=(kc == KC - 1)),
                 reads=["ones", ("sq", kc % 4)], writes=[pk])
        S.op("vector", lambda e: e.tensor_scalar(out=C.rs[:], in0=ps[:], scalar1=1.0 / D, scalar2=EPS, op0=ALU.mult, op1=ALU.add),
             reads=[pk], writes=["rs"])
        S.op("scalar", lambda e: e.activation(out=C.rs[:], in_=C.rs[:], func=AF.Sqrt), reads=["rs"], writes=["rs"])
        S.op("vector", lambda e: e.reciprocal(out=C.rs[:], in_=C.rs[:]), reads=["rs"], writes=["rs"])
        for kc in range(KC):
            if final:
                S.op("vector", lambda e: e.scalar_tensor_tensor(out=C.hT[:, kc, tsl(tt)], in0=C.hT[:, kc, tsl(tt)], scalar=C.gn[:, gi, kc:kc + 1],
                                                               in1=C.rs[:], op0=ALU.mult, op1=ALU.mult),
                     reads=[("hT", kc, tt), "rs", "gn"], writes=[("hT", kc, tt)])
            else:
                S.op("vector", lambda e: e.scalar_tensor_tensor(out=C.xn[:, kc, tsl(tt)], in0=C.hT[:, kc, tsl(tt)], scalar=C.gn[:, gi, kc:kc + 1],
                                                               in1=C.rs[:], op0=ALU.mult, op1=ALU.mult),
                     reads=[("hT", kc, tt), "rs", "gn"], writes=[("xn", kc, tt)])
    S.barrier()
    ph.close()


def inproj(C, w, col0, tt, evac):
    S = C.S
    ps, pk = C.nextps()
    wt, wk = w
    for kc in range(KC):
        S.op("tensor", lambda e: e.matmul(ps[:], lhsT=wt[:, kc, col0:col0 + 128], rhs=C.xn[:, kc, tsl(tt)], start=(kc == 0), stop=(kc == KC - 1)),
             reads=[wk, ("xn", kc, tt)], writes=[pk])
    evac(ps, pk)


def gelu_inplace(C, x, xk, t1, t1k, tt):
    S = C.S
    sl = tsl(tt)
    S.op("scalar", lambda e: e.activation(out=t1[:, sl], in_=x[:, sl], func=AF.Square), reads=[(xk, tt)], writes=[(t1k, tt)])
    S.op("vector", lambda e: e.tensor_scalar(out=t1[:, sl], in0=t1[:, sl], scalar1=0.044715, scalar2=1.0, op0=ALU.mult, op1=ALU.add),
         reads=[(t1k, tt)], writes=[(t1k, tt)])
    S.op("vector", lambda e: e.tensor_tensor(out=t1[:, sl], in0=t1[:, sl], in1=x[:, sl], op=ALU.mult), reads=[(t1k, tt), (xk, tt)], writes=[(t1k, tt)])
    S.op("scalar", lambda e: e.activation(out=t1[:, sl], in_=t1[:, sl], func=AF.Sigmoid, scale=GELU_K), reads=[(t1k, tt)], writes=[(t1k, tt)])
    S.op("vector", lambda e: e.tensor_tensor(out=x[:, sl], in0=x[:, sl], in1=t1[:, sl], op=ALU.mult), reads=[(t1k, tt), (xk, tt)], writes=[(xk, tt)])


def outproj_group(C, acts, wouts):
    S = C.S
    n = len(acts)
    for tt in range(NTT):
        for m in range(KC):
            ps, pk = C.nextps()
            for i in range(n):
                a_ap, a_k = acts[i]
                wt, wk = wouts[i]
                S.op("tensor", lambda e: e.matmul(ps[:], lhsT=wt[:, m * 128:(m + 1) * 128], rhs=a_ap(tt), start=(i == 0), stop=(i == n - 1)),
                     reads=[wk, a_k(tt)], writes=[pk])
            S.op("vector", lambda e: e.tensor_tensor(out=C.hT[:, m, tsl(tt)], in0=C.hT[:, m, tsl(tt)], in1=ps[:], op=ALU.add),
                 reads=[pk, ("hT", m, tt)], writes=[("hT", m, tt)])


def lru_mixer(C):
    S, sb, nc = C.S, C.sb, C.nc
    G = 2
    HT = NT // 2
    with contextlib.ExitStack() as ph:
        win = [sb("l_win%d" % i, [128, KC, 256], BF16, ph) for i in range(2)]
        gw = [sb("l_gw%d" % i, [128, 2, 128], BF16, ph) for i in range(2)]
        wo = [sb("l_wo%d" % i, [128, D], BF16, ph) for i in range(2 * G)]
        hy = [sb("l_hy%d" % i, [128, NT], BF16, ph) for i in range(2 * G)]
        ysbs = [sb("l_ysb%d" % i, [128, NT], F32, ph) for i in range(2)]
        xpads = [sb("l_xpad%d" % i, [128, 3 + NT], F32, ph) for i in range(2)]
        t1s = [sb("l_t1%d" % i, [128, HT], F32, ph) for i in range(2)]
        xcs = [sb("l_xc%d" % i, [128, HT], F32, ph) for i in range(2)]
        xcbs = [sb("l_xcb%d" % i, [128, HT], BF16, ph) for i in range(2)]
        rrs = [sb("l_r%d" % i, [128, HT], F32, ph) for i in range(2)]
        igs = [sb("l_ig%d" % i, [128, HT], F32, ph) for i in range(2)]
        hhs = [sb("l_h%d" % i, [128, HT], F32, ph) for i in range(2)]
        for i in range(2):
            S.op("vector", lambda e: e.memset(xpads[i][:, 0:3], 0.0), writes=["l_xpad%d_0" % i])

        def load_w(n):
            b = n % 2
            S.dma("gpsimd", "l_win%d" % b, lambda e: e.dma_start(out=win[b][:], in_=C.lru_win[n]), writes=["l_win%d" % b])
            S.dma("gpsimd", "l_gw%d" % b, lambda e: e.dma_start(out=gw[b][:], in_=C.lru_gw[n]), writes=["l_gw%d" % b])
            S.dma("gpsimd", "l_wo%d" % (n % (2 * G)), lambda e: e.dma_start(out=wo[n % (2 * G)][:], in_=C.lru_wout[n]), writes=["l_wo%d" % (n % (2 * G))])

        def proj(n, u):
            b = n % 2
            wi, wik = win[b], "l_win%d" % b
            ysb, xpad = ysbs[b], xpads[b]
            for tt in (2 * u, 2 * u + 1):
                inproj(C, (wi, wik), 128, tt, lambda ps, pk: S.op(
                    "scalar", lambda e: e.activation(out=xpad[:, 3 + tt * TT:3 + (tt + 1) * TT], in_=ps[:], func=AF.Identity),
                    reads=[pk], writes=[("l_xpad%d" % b, tt)]))
            for tt in (2 * u, 2 * u + 1):
                inproj(C, (wi, wik), 0, tt, lambda ps, pk: S.op(
                    "scalar", lambda e: e.activation(out=ysb[:, tsl(tt)], in_=ps[:], func=AF.Identity), reads=[pk], writes=[("l_ysb%d" % b, tt)]))

        units = [(n, u) for n in range(LN) for u in range(2)]
        def conv_unit(k):
            n, u = units[k]
            b = n % 2
            q = k % 2
            xpad, xk = xpads[b], "l_xpad%d" % b
            xc, xcb = xcs[q], xcbs[q]
            xck, xcbk = "l_xc%d" % q, "l_xcb%d" % q
            TU = (2 * u, 2 * u + 1)

            def lsl(tt):
                return slice((tt - 2 * u) * TT, (tt - 2 * u + 1) * TT)
            for tt in TU:
                rd = [(xk, tt), (xk, tt - 1) if tt > 0 else xk + "_0", "lvec"]
                S.op("vector", lambda e: e.tensor_scalar(out=xc[:, lsl(tt)], in0=xpad[:, 3 + tt * TT:3 + (tt + 1) * TT], scalar1=C.lvec[:, n, 3:4],
                                                        scalar2=C.lvec[:, n, 4:5], op0=ALU.mult, op1=ALU.add), reads=rd, writes=[(xck, tt)])
                for kk in (2, 1, 0):
                    S.op("vector", lambda e: e.scalar_tensor_tensor(out=xc[:, lsl(tt)], in0=xpad[:, kk + tt * TT:kk + (tt + 1) * TT], scalar=C.lvec[:, n, kk:kk + 1],
                                                                   in1=xc[:, lsl(tt)], op0=ALU.mult, op1=ALU.add),
                         reads=rd + [(xck, tt)], writes=[(xck, tt)])
                S.op("scalar", lambda e: e.activation(out=xcb[:, lsl(tt)], in_=xc[:, lsl(tt)], func=AF.Identity), reads=[(xck, tt)], writes=[(xcbk, tt)])

        pending = None
        load_w(0)
        proj(0, 0)
        conv_unit(0)
        for k, (n, u) in enumerate(units):
            b = n % 2
            q = k % 2
            gwi, gwk = gw[b], "l_gw%d" % b
            hyi, hyk = hy[n % (2 * G)], "l_hy%d" % (n % (2 * G))
            ysb, xpad = ysbs[b], xpads[b]
            yk, xk = "l_ysb%d" % b, "l_xpad%d" % b
            t1, xc, xcb, rr, ig, hh = t1s[q], xcs[q], xcbs[q], rrs[q], igs[q], hhs[q]
            t1k, xck, xcbk, rk, igk, hk = "l_t1%d" % q, "l_xc%d" % q, "l_xcb%d" % q, "l_r%d" % q, "l_ig%d" % q, "l_h%d" % q
            TU = (2 * u, 2 * u + 1)

            def lsl(tt):
                return slice((tt - 2 * u) * TT, (tt - 2 * u + 1) * TT)
            if k + 1 < len(units):
                n2, u2 = units[k + 1]
                if u2 == 0:
                    load_w(n2)
                proj(n2, u2)
            if pending is not None and u == 1:
                outproj_group(C, *pending)
                pending = None
            for tt in TU:
                S.op("scalar", lambda e: e.activation(out=t1[:, lsl(tt)], in_=ysb[:, tsl(tt)], func=AF.Square, scale=GELU_C), reads=[(yk, tt)], writes=[(t1k, tt)])
            for g, (dst, dk) in enumerate(((rr, rk), (ig, igk))):
                for tt in TU:
                    ps, pk = C.nextps()
                    S.op("tensor", lambda e: e.matmul(ps[:], lhsT=gwi[:, g, :], rhs=xcb[:, lsl(tt)], start=True, stop=True),
                         reads=[gwk, (xcbk, tt)], writes=[pk])
                    S.op("scalar", lambda e: e.activation(out=dst[:, lsl(tt)], in_=ps[:], func=AF.Sigmoid, bias=C.lvec[:, n, 5 + g:6 + g]),
                         reads=[pk, "lvec"], writes=[(dk, tt)])
            for tt in TU:
                S.op("vector", lambda e: e.scalar_tensor_tensor(out=t1[:, lsl(tt)], in0=t1[:, lsl(tt)], scalar=1.0, in1=ysb[:, tsl(tt)], op0=ALU.add, op1=ALU.mult),
                     reads=[(t1k, tt), (yk, tt)], writes=[(t1k, tt)])
            for tt in TU:
                S.op("scalar", lambda e: e.activation(out=t1[:, lsl(tt)], in_=t1[:, lsl(tt)], func=AF.Sigmoid, scale=GELU_K), reads=[(t1k, tt)], writes=[(t1k, tt)])
            for tt in TU:
                S.op("vector", lambda e: e.tensor_tensor(out=ysb[:, tsl(tt)], in0=ysb[:, tsl(tt)], in1=t1[:, lsl(tt)], op=ALU.mult), reads=[(t1k, tt), (yk, tt)], writes=[(yk, tt)])
            for tt in TU:
                S.op("vector", lambda e: e.tensor_tensor(out=ig[:, lsl(tt)], in0=ig[:, lsl(tt)], in1=xc[:, lsl(tt)], op=ALU.mult), reads=[(igk, tt), (xck, tt)], writes=[(igk, tt)])
            for tt in TU:
                S.op("scalar", lambda e: e.activation(out=t1[:, lsl(tt)], in_=rr[:, lsl(tt)], func=AF.Exp, scale=C.lca[:, n, 1:2]), reads=[(rk, tt), "lca"], writes=[(t1k, tt)])
            for tt in TU:
                S.op("scalar", lambda e: e.activation(out=rr[:, lsl(tt)], in_=rr[:, lsl(tt)], func=AF.Exp, scale=C.lca[:, n, 0:1]), reads=[(rk, tt), "lca"], writes=[(rk, tt)])
            for tt in TU:
                S.op("scalar", lambda e: e.activation(out=t1[:, lsl(tt)], in_=t1[:, lsl(tt)], func=AF.Sqrt, scale=-1.0, bias=1.0), reads=[(t1k, tt)], writes=[(t1k, tt)])
            if k + 1 < len(units):
                conv_unit(k + 1)
            for tt in TU:
                S.op("vector", lambda e: e.tensor_tensor(out=ig[:, lsl(tt)], in0=ig[:, lsl(tt)], in1=t1[:, lsl(tt)], op=ALU.mult), reads=[(igk, tt), (t1k, tt)], writes=[(igk, tt)])
            for tt in TU:
                if tt == 0:
                    init, ird = 0.0, []
                elif tt == 2 * u:
                    init, ird = hhs[1 - q][:, HT - 1:HT], [("l_h%d" % (1 - q), tt - 1)]
                else:
                    init, ird = hh[:, TT - 1:TT], [(hk, tt - 1)]
                S.op("vector", lambda e: e.tensor_tensor_scan(out=hh[:, lsl(tt)], data0=rr[:, lsl(tt)], data1=ig[:, lsl(tt)], initial=init, op0=ALU.mult, op1=ALU.add),
                     reads=[(rk, tt), (igk, tt)] + ird, writes=[(hk, tt)])
                S.op("vector", lambda e: e.tensor_tensor(out=hyi[:, tsl(tt)], in0=hh[:, lsl(tt)], in1=ysb[:, tsl(tt)], op=ALU.mult),
                     reads=[(hk, tt), (yk, tt)], writes=[(hyk, tt)])
            if u == 1 and n % G == G - 1:
                idx = [(n - G + 1 + i) % (2 * G) for i in range(G)]
                pending = ([((lambda tt, i=i: hy[i][:, tsl(tt)]), (lambda tt, i=i: ("l_hy%d" % i, tt))) for i in idx],
                           [(wo[i], "l_wo%d" % i) for i in idx])
        outproj_group(C, *pending)
        S.barrier()


def conv_ffn(C, L):
    S, sb, nc = C.S, C.sb, C.nc
    G = 4
    with contextlib.ExitStack() as ph:
        win = [sb("f_win%d" % i, [128, KC, 256], BF16, ph) for i in range(2)]
        wo = [sb("f_wo%d" % i, [128, D], BF16, ph) for i in range(2 * G)]
        act = [sb("f_act%d" % i, [128, NT], BF16, ph) for i in range(2 * G)]
        apads = [sb("f_apad%d" % i, [128, 2 + NT], F32, ph) for i in range(2)]
        bsbs = [sb("f_bsb%d" % i, [128, NT], BF16, ph) for i in range(2)]
        acs = [sb("f_ac%d" % i, [128, NT], F32, ph) for i in range(2)]
        t1 = sb("f_t1", [128, NT], F32, ph)
        for i in range(2):
            S.op("vector", lambda e: e.memset(apads[i][:, 0:2], 0.0), writes=["f_apad%d_0" % i])

        def load_and_proj(j):
            b = j % 2
            wi, woi = win[b], wo[j % (2 * G)]
            wik, wok = "f_win%d" % b, "f_wo%d" % (j % (2 * G))
            S.dma("gpsimd", wik, lambda e: e.dma_start(out=wi[:], in_=C.ffn_win[L, j]), writes=[wik])
            S.dma("gpsimd", wok, lambda e: e.dma_start(out=woi[:], in_=C.ffn_wout[L, j]), writes=[wok])
            apad, bsb, acb = apads[b], bsbs[b], acs[b]

            def evac_a(ps, pk, tt):
                S.op("scalar", lambda e: e.activation(out=apad[:, 2 + tt * TT:2 + (tt + 1) * TT], in_=ps[:], func=AF.Identity),
                     reads=[pk], writes=[("f_apad%d" % b, tt)])
                S.op("scalar", lambda e: e.activation(out=acb[:, tsl(tt)], in_=ps[:], func=AF.Identity, scale=C.fvec[:, L, j, 2:3], bias=C.fvec[:, L, j, 3:4]),
                     reads=[pk, "fvec"], writes=[("f_ac%d" % b, tt)])
            for tt in range(NTT):
                inproj(C, (wi, wik), 0, tt, lambda ps, pk: evac_a(ps, pk, tt))
            for tt in range(NTT):
                inproj(C, (wi, wik), 128, tt, lambda ps, pk: S.op(
                    "scalar", lambda e: e.activation(out=bsb[:, tsl(tt)], in_=ps[:], func=AF.Identity), reads=[pk], writes=[("f_bsb%d" % b, tt)]))

        pending = None
        load_and_proj(0)
        for j in range(FJ):
            b = j % 2
            acti, actk = act[j % (2 * G)], "f_act%d" % (j % (2 * G))
            apad, bsb, ac = apads[b], bsbs[b], acs[b]
            ak, bk, ack = "f_apad%d" % b, "f_bsb%d" % b, "f_ac%d" % b
            T4 = range(NTT)
            if j + 1 < FJ:
                load_and_proj(j + 1)
            if pending is not None:
                outproj_group(C, *pending)
                pending = None
            for tt in T4:
                sl = tsl(tt)
                rd = [(ak, tt), (ak, tt - 1) if tt > 0 else ak + "_0", "fvec"]
                for k in (1, 0):
                    S.op("vector", lambda e: e.scalar_tensor_tensor(out=ac[:, sl], in0=apad[:, k + tt * TT:k + (tt + 1) * TT], scalar=C.fvec[:, L, j, k:k + 1],
                                                                   in1=ac[:, sl], op0=ALU.mult, op1=ALU.add),
                         reads=rd + [(ack, tt)], writes=[(ack, tt)])
                S.op("scalar", lambda e: e.activation(out=t1[:, sl], in_=ac[:, sl], func=AF.Square, scale=GELU_C), reads=[(ack, tt)], writes=[("f_t1", tt)])
            for tt in T4:
                sl = tsl(tt)
                S.op("vector", lambda e: e.scalar_tensor_tensor(out=t1[:, sl], in0=t1[:, sl], scalar=1.0, in1=ac[:, sl], op0=ALU.add, op1=ALU.mult),
                     reads=[("f_t1", tt), (ack, tt)], writes=[("f_t1", tt)])
                S.op("scalar", lambda e: e.activation(out=t1[:, sl], in_=t1[:, sl], func=AF.Sigmoid, scale=GELU_K), reads=[("f_t1", tt)], writes=[("f_t1", tt)])
                S.op("vector", lambda e: e.tensor_tensor(out=ac[:, sl], in0=ac[:, sl], in1=bsb[:, sl], op=ALU.mult), reads=[(ack, tt), (bk, tt)], writes=[(ack, tt)])
            for tt in T4:
                sl = tsl(tt)
                S.op("vector", lambda e: e.tensor_tensor(out=acti[:, sl], in0=ac[:, sl], in1=t1[:, sl], op=ALU.mult),
                     reads=[(ack, tt), ("f_t1", tt)], writes=[(actk, tt)])
            if j % G == G - 1:
                idx = [(j - G + 1 + i) % (2 * G) for i in range(G)]
                pending = ([((lambda tt, i=i: act[i][:, tsl(tt)]), (lambda tt, i=i: ("f_act%d" % i, tt))) for i in idx],
                           [(wo[i], "f_wo%d" % i) for i in idx])
        outproj_group(C, *pending)
        S.barrier()


SLOPES = [2.0 ** (-8.0 * (h + 1) / 16.0) for h in range(16)]
BIGNEG = 30000.0
NPT = 5


def gelu_ap(C, x, t, xk, tk):
    S = C.S
    S.op("scalar", lambda e: e.activation(out=t, in_=x, func=AF.Square), reads=[xk], writes=[tk])
    S.op("vector", lambda e: e.tensor_scalar(out=t, in0=t, scalar1=0.044715, scalar2=1.0, op0=ALU.mult, op1=ALU.add), reads=[tk], writes=[tk])
    S.op("vector", lambda e: e.tensor_tensor(out=t, in0=t, in1=x, op=ALU.mult), reads=[tk, xk], writes=[tk])
    S.op("scalar", lambda e: e.activation(out=t, in_=t, func=AF.Sigmoid, scale=GELU_K), reads=[tk], writes=[tk])
    S.op("vector", lambda e: e.tensor_tensor(out=x, in0=x, in1=t, op=ALU.mult), reads=[tk, xk], writes=[xk])


def nsa_mixer(C):
    S, sb, nc = C.S, C.sb, C.nc
    C.nrot = 4
    po_banks = [(C.psum[i], ("ps", i)) for i in (4, 5, 6, 7)]
    po_i = [0]

    def nextpo():
        r = po_banks[po_i[0] % 4]
        po_i[0] += 1
        return r
    with contextlib.ExitStack() as ph:
        KCT = sb("n_KCT", [128, 4, 128], BF16, ph)
        VCa = sb("n_VCa", [128, 4, 97], BF16, ph)
        wch = [None, None]
        wci = [0]

        def alloc_wch(stack):
            for i in range(2):
                wch[i] = sb("n_wch%d" % i, [128, KC, 128], BF16, stack)

        def load_wch(idx):
            i = wci[0] % 2
            wci[0] += 1
            k = "n_wch%d" % i
            S.dma("gpsimd", k, lambda e: e.dma_start(out=wch[i][:], in_=C.nsa_wch[idx]), writes=[k])
            return wch[i], k

        with contextlib.ExitStack() as p1:
            alloc_wch(p1)
            KV0 = sb("n_KV0", [128, 4, NT], BF16, p1)
            W1 = sb("n_W1", [128, 2, 32, 256], BF16, p1)
            posT = sb("n_posT", [128, 2, 32], BF16, p1)
            b1 = sb("n_b1", [128, 2, 2], F32, p1)
            W2k = sb("n_W2k", [128, 2, 128], BF16, p1)
            W2v = sb("n_W2v", [128, 2, 64], BF16, p1)
            ovl = sb("n_ovl", [128, 33], F32, p1)
            hids = [sb("n_hid%d" % i, [128, 256], F32, p1) for i in range(2)]
            hscs = [sb("n_hsc%d" % i, [128, 256], F32, p1) for i in range(2)]
            ghbs = [sb("n_ghb%d" % i, [128, 2, 128], BF16, p1) for i in range(2)]
            cvec = sb("n_cvec", [1, 2, 256], BF16, p1)
            onesr = sb("n_onesr", [1, 128], BF16, p1)
            b1row = sb("n_b1row", [1, 2, 256], F32, p1)
            identc = sb("n_identc", [128, 128], F32, p1)
            S.dma("sync", "n_identc", lambda e: e.dma_start(out=identc[:], in_=C.c_ident), writes=["n_identc"])
            S.dma("sync", "n_b1row", lambda e: e.dma_start(out=b1row[:], in_=C.nsa_b1row), writes=["n_b1row"])
            S.op("vector", lambda e: e.memset(onesr[:], 1.0), writes=["n_onesr"])
            for kvi in range(2):
                S.dma("gpsimd", "n_W1_%d" % kvi, lambda e: e.dma_start(out=W1[:, kvi], in_=C.nsa_w1[kvi]), writes=[("n_W1", kvi)])
            S.dma("gpsimd", "n_posT", lambda e: e.dma_start(out=posT[:], in_=C.nsa_posT), writes=["n_posT"])
            S.dma("sync", "n_b1", lambda e: e.dma_start(out=b1[:], in_=C.nsa_b1), writes=["n_b1"])
            S.dma("gpsimd", "n_W2k", lambda e: e.dma_start(out=W2k[:], in_=C.nsa_w2k), writes=["n_W2k"])
            S.dma("gpsimd", "n_W2v", lambda e: e.dma_start(out=W2v[:], in_=C.nsa_w2v), writes=["n_W2v"])
            S.dma("sync", "n_ovl", lambda e: e.dma_start(out=ovl[:], in_=C.c_ovl), writes=["n_ovl"])
            S.op("vector", lambda e: e.memset(KCT[:], 0.0), writes=[("n_KCT", g) for g in range(4)])
            S.op("vector", lambda e: e.memset(VCa[:], 0.0), writes=[("n_VCa", g) for g in range(4)])
            for g in range(4):
                S.op("vector", lambda e: e.tensor_copy(out=VCa[:, g, 64:97], in_=ovl[:]), reads=["n_ovl"], writes=[("n_VCa", g)])
            for c4 in range(4):
                wt, wk = load_wch(c4)
                for tt in range(NTT):
                    inproj(C, (wt, wk), 0, tt, lambda ps, pk: S.op(
                        "scalar", lambda e: e.activation(out=KV0[:, c4, tsl(tt)], in_=ps[:], func=AF.Identity), reads=[pk], writes=[("n_KV0", c4)]))
            for kvi in range(2):
                ps, pk = C.nextps()
                for i in range(32):
                    S.op("tensor", lambda e: e.matmul(ps[0:1, 0:256], lhsT=posT[0:64, kvi, i:i + 1], rhs=W1[0:64, kvi, i, :], start=(i == 0), stop=(i == 31)),
                         reads=[("n_W1", kvi), "n_posT"], writes=[pk])
                S.op("vector", lambda e: e.tensor_tensor(out=cvec[0:1, kvi, :], in0=ps[0:1, 0:256], in1=b1row[0:1, kvi, :], op=ALU.add),
                     reads=[pk, "n_b1row"], writes=[("n_cvec", kvi)])
            un = 0
            for kvi in range(2):
                for g in range(4):
                    c4 = kvi * 2 + g // 2
                    rows = slice((g % 2) * 64, (g % 2) * 64 + 64)
                    ub = un % 2
                    un += 1
                    hid, hsc, ghb = hids[ub], hscs[ub], ghbs[ub]
                    hk, sk, gk = "n_hid%d" % ub, "n_hsc%d" % ub, "n_ghb%d" % ub
                    ps, pk = C.nextps()
                    for i in range(32):
                        S.op("tensor", lambda e: e.matmul(ps[0:127, 0:256], lhsT=KV0[rows, c4, i:i + 2017:16], rhs=W1[rows, kvi, i, :], start=(i == 0), stop=False),
                             reads=[("n_W1", kvi), ("n_KV0", c4)], writes=[pk])
                    S.op("tensor", lambda e: e.matmul(ps[0:127, 0:256], lhsT=onesr[0:1, 0:127], rhs=cvec[0:1, kvi, :], start=False, stop=True),
                         reads=["n_onesr", ("n_cvec", kvi)], writes=[pk])
                    S.op("scalar", lambda e: e.activation(out=hid[0:127, :], in_=ps[0:127, 0:256], func=AF.Identity), reads=[pk], writes=[hk])
                    gelu_ap(C, hid[0:127, :], hsc[0:127, :], hk, sk)
                    ps, pk = C.nextps()
                    for mh in range(2):
                        S.op("tensor", lambda e: e.transpose(out=ps[:, mh * 128:mh * 128 + 127], in_=hid[0:127, mh * 128:(mh + 1) * 128], identity=identc[0:127, 0:127]),
                             reads=[hk, "n_identc"], writes=[pk])
                    S.op("scalar", lambda e: e.activation(out=ghb[:, :, 0:127], in_=ps[:, 0:256].rearrange("p (m c) -> p m c", c=128)[:, :, 0:127], func=AF.Identity),
                         reads=[pk], writes=[(gk, 0), (gk, 1)])
                    ps, pk = C.nextps()
                    if kvi == 0:
                        for mh in range(2):
                            S.op("tensor", lambda e: e.matmul(ps[:, 0:127], lhsT=W2k[:, mh, :], rhs=ghb[:, mh, 0:127], start=(mh == 0), stop=(mh == 1)),
                                 reads=["n_W2k", (gk, mh)], writes=[pk])
                        S.op("scalar", lambda e: e.activation(out=KCT[rows, g, 0:127], in_=ps[rows, 0:127], func=AF.Identity), reads=[pk], writes=[("n_KCT", g)])
                    else:
                        for mh in range(2):
                            S.op("tensor", lambda e: e.matmul(ps[0:127, 0:64], lhsT=ghb[:, mh, 0:127], rhs=W2v[:, mh, :], start=(mh == 0), stop=(mh == 1)),
                                 reads=["n_W2v", (gk, mh)], writes=[pk])
                        S.op("scalar", lambda e: e.activation(out=VCa[0:127, g, 0:64], in_=ps[0:127, 0:64], func=AF.Identity), reads=[pk], writes=[("n_VCa", g)])
            S.barrier()

        pA = ph.enter_context(contextlib.ExitStack())
        QT = sb("n_QT", [128, 8, NT], BF16, pA)
        Vaug = sb("n_Vaug", [128, 2, 16, 4, 65], BF16, pA)
        gts = sb("n_gts", [128, 16, 48], F32, pA)
        with contextlib.ExitStack() as p2:
            alloc_wch(p2)
            K12 = sb("n_K12", [128, 2, 2, NT], BF16, p2)
            wtok = sb("n_wtok", [128, KC, 560], BF16, p2)
            S.dma("gpsimd", "n_wtok", lambda e: e.dma_start(out=wtok[:], in_=C.nsa_wtok), writes=["n_wtok"])
            for mq in range(8 if getattr(C, 'nsa_stop', 9) >= 2 else 0):
                wt, wk = load_wch(4 + mq)
                for tt in range(NTT):
                    inproj(C, (wt, wk), 0, tt, lambda ps, pk: S.op(
                        "scalar", lambda e: e.activation(out=QT[:, mq, tsl(tt)], in_=ps[:], func=AF.Identity, scale=0.125), reads=[pk], writes=[("n_QT", mq, tt)]))
            for br in range(2):
                for c2 in range(2):
                    wt, wk = load_wch(12 + br * 2 + c2)
                    for tt in range(NTT):
                        inproj(C, (wt, wk), 0, tt, lambda ps, pk: S.op(
                            "scalar", lambda e: e.activation(out=K12[:, br, c2, tsl(tt)], in_=ps[:], func=AF.Identity), reads=[pk], writes=[("n_K12", br, c2, tt)]))
            S.op("vector", lambda e: e.memset(Vaug[:].rearrange("p a b c d -> p (a b c) d")[:, :, 64:65], 1.0), writes=["n_Vone"])
            for t16 in range(16 if getattr(C, 'nsa_stop', 9) >= 2 else 0):
                tok = slice(t16 * 128, (t16 + 1) * 128)
                ps, pk = C.nextps()
                for kc in range(KC):
                    S.op("tensor", lambda e: e.matmul(ps[:, 0:512], lhsT=C.xn[:, kc, tok], rhs=wtok[:, kc, 0:512], start=(kc == 0), stop=(kc == KC - 1)),
                         reads=["n_wtok", ("xn", kc, t16 // 4)], writes=[pk])
                for br in range(2):
                    S.op("scalar", lambda e: e.activation(out=Vaug[:, br, t16, :, 0:64], in_=ps[:, br * 256:(br + 1) * 256].rearrange("p (g d) -> p g d", d=64),
                                                          func=AF.Identity), reads=[pk], writes=[("n_Vaug", br, t16)])
                ps, pk = C.nextps()
                for kc in range(KC):
                    S.op("tensor", lambda e: e.matmul(ps[:, 0:48], lhsT=C.xn[:, kc, tok], rhs=wtok[:, kc, 512:560], start=(kc == 0), stop=(kc == KC - 1)),
                         reads=["n_wtok", ("xn", kc, t16 // 4)], writes=[pk])
                S.op("scalar", lambda e: e.activation(out=gts[:, t16, :], in_=ps[:, 0:48], func=AF.Sigmoid), reads=[pk], writes=[("n_gts", t16)])
            S.barrier()
            Kz = C.xn[:].rearrange("p (b g) t -> p b g t", b=2)
            for br in range(2):
                for g in range(4):
                    own = slice((g % 2) * 64, (g % 2) * 64 + 64)
                    oth = slice((1 - g % 2) * 64, (1 - g % 2) * 64 + 64)
                    S.op("gpsimd", lambda e: e.memset(Kz[oth, br, g, :], 0.0), writes=[("n_KzO", br, g)])
                    if g % 2 == 0:
                        S.op("scalar", lambda e: e.activation(out=Kz[own, br, g, :], in_=K12[own, br, g // 2, :], func=AF.Identity), writes=[("n_Kz", br, g)])
                    else:
                        S.op("vector", lambda e: e.tensor_copy(out=Kz[own, br, g, :], in_=K12[own, br, g // 2, :]), writes=[("n_Kz", br, g)])
            S.barrier()

        with contextlib.ExitStack() as p3:
            dtab = sb("n_dtab", [128, 10, TT], mybir.dt.int16, p3)
            etab = sb("n_etab", [128, 16, 128], BF16, p3)
            biasc = sb("n_biasc", [128, 16, 16], F32, p3)
            keep = sb("n_keep", [128, 16, 32], BF16, p3)
            addc = sb("n_addc", [128, 16, 32], BF16, p3)
            ident = sb("n_ident", [128, 128], F32, p3)
            sm = [sb("n_sm%d" % i, [128, TT], F32, p3) for i in range(2)]
            pT = [sb("n_pT%d" % i, [128, TT], BF16, p3) for i in range(NPT)]
            otoks = [sb("n_otok%d" % i, [128, 4, 256], F32, p3) for i in range(2)]
            valid = sb("n_valid", [128, 16, 1], F32, p3)
            otmp = sb("n_otmp", [128, 4, 64], F32, p3)
            rden = sb("n_rden", [128, 4, 1], F32, p3)
            ff = sb("n_ff", [128, 4, 1], F32, p3)
            comb = sb("n_comb", [128, 4, 65], F32, p3)
            fq = sb("n_fq", [128, 4, 16], F32, p3)
            S.dma("sync", "n_fq", lambda e: e.dma_start(out=fq[:], in_=C.c_fq), writes=["n_fq"])
            imp = sb("n_imp", [128, 4, 32], F32, p3)
            itmp = sb("n_itmp", [128, 4, 32], F32, p3)
            top8 = sb("n_top8", [128, 4, 8], F32, p3)
            selb = sb("n_selb", [128, 4, 32], F32, p3)
            selbT = sb("n_selbT", [128, 4, TT], BF16, p3)
            oTq = sb("n_oTq", [128, KC, TT], BF16, p3)
            alloc_wch(p3)
            S.dma("sync", "n_dtab", lambda e: e.dma_start(out=dtab[:, 0:9, :], in_=C.c_dtab[:, 0:9, :]), writes=["n_dtab"])
            S.op("gpsimd", lambda e: e.memset(etab[:], 0.0), writes=["n_etab"])
            S.op("gpsimd", lambda e: e.memset(selbT[:], 0.0), writes=[("n_selbT", g) for g in range(4)])
            S.dma("gpsimd", "n_etab", lambda e: e.dma_start(out=etab[0:32], in_=C.c_etab), writes=["n_etab"])
            S.dma("sync", "n_biasc", lambda e: e.dma_start(out=biasc[:], in_=C.c_biasc), writes=["n_biasc"])
            S.dma("gpsimd", "n_keep", lambda e: e.dma_start(out=keep[:], in_=C.c_keep), writes=["n_keep"])
            S.dma("gpsimd", "n_addc", lambda e: e.dma_start(out=addc[:], in_=C.c_addc), writes=["n_addc"])
            S.dma("sync", "n_ident", lambda e: e.dma_start(out=ident[:], in_=C.c_ident), writes=["n_ident"])
            S.dma("sync", "n_valid", lambda e: e.dma_start(out=valid[:], in_=C.c_valid), writes=["n_valid"])
            smi = [0]
            pti = [0]

            def score_tile(mm_fn, mm_reads, dti, scal, bias_ap):
                dkey = "n_dtabc" if dti == 9 else "n_dtab"
                i = smi[0] % 2
                smi[0] += 1
                ps, pk = C.nextps()
                mm_fn(ps, pk)
                if dti is None:
                    smi[0] -= 1
                    jj = pti[0] % NPT
                    pti[0] += 1
                    S.op("scalar", lambda e: e.activation(out=pT[jj][:], in_=ps[:], func=AF.Exp, bias=bias_ap), reads=[pk, "n_biasc"], writes=[("n_pT", jj)])
                    return pT[jj], ("n_pT", jj)
                j = pti[0] % NPT
                pti[0] += 1
                S.op("vector", lambda e: e.scalar_tensor_tensor(out=sm[i][:], in0=dtab[:, dti, :], scalar=scal, in1=ps[:], op0=ALU.mult, op1=ALU.add),
                     reads=[pk, dkey], writes=[("n_sm", i)])
                if bias_ap is None:
                    S.op("scalar", lambda e: e.activation(out=pT[j][:], in_=sm[i][:], func=AF.Exp), reads=[("n_sm", i)], writes=[("n_pT", j)])
                else:
                    S.op("scalar", lambda e: e.activation(out=pT[j][:], in_=sm[i][:], func=AF.Exp, bias=bias_ap), reads=[("n_sm", i), "n_biasc"], writes=[("n_pT", j)])
                return pT[j], ("n_pT", j)

            def run_jobs(jobs, LA=NPT - 1):
                staged = []
                for idx in range(len(jobs) + LA):
                    if idx < len(jobs):
                        jb = jobs[idx]
                        staged.append(score_tile(jb["mm"], None, jb["dti"], jb["scal"], jb["bias"]))
                    k = idx - LA
                    if k >= 0:
                        jobs[k]["pv"](*staged[k])

            def accum_out(po, pok, ncol, otok, okey, r, gate_col, qt, first, src3=None):
                po3 = po[:, 0:4 * ncol].rearrange("p (s c) -> p s c", c=ncol) if src3 is None else src3
                S.op("vector", lambda e: e.tensor_scalar(out=rden[:], in0=po3[:, :, 64:65], scalar1=1e-30, scalar2=None, op0=ALU.max), reads=[pok], writes=["n_rden"])
                S.op("vector", lambda e: e.reciprocal(out=rden[:], in_=rden[:]), reads=["n_rden"], writes=["n_rden"])
                if first and qt == 0:
                    S.op("vector", lambda e: e.tensor_tensor(out=rden[:], in0=rden[:], in1=valid[:, 0:4, :], op=ALU.mult), reads=["n_rden", "n_valid"], writes=["n_rden"])
                S.op("vector", lambda e: e.tensor_tensor(out=ff[:], in0=rden[:], in1=gts[:, qt * 4:(qt + 1) * 4, gate_col:gate_col + 1], op=ALU.mult),
                     reads=["n_rden"] + [("n_gts", qt * 4 + i) for i in range(4)], writes=["n_ff"])
                dst = otok[:, :, r * 64:(r + 1) * 64]
                if first:
                    S.op("vector", lambda e: e.tensor_tensor(out=dst, in0=po3[:, :, 0:64], in1=ff[:].to_broadcast([128, 4, 64]), op=ALU.mult),
                         reads=[pok, "n_ff"], writes=[(okey, r)])
                else:
                    S.op("vector", lambda e: e.tensor_tensor(out=otmp[:], in0=po3[:, :, 0:64], in1=ff[:].to_broadcast([128, 4, 64]), op=ALU.mult),
                         reads=[pok, "n_ff"], writes=["n_otmp"])
                    S.op("vector", lambda e: e.tensor_tensor(out=dst, in0=dst, in1=otmp[:], op=ALU.add), reads=["n_otmp", (okey, r)], writes=[(okey, r)])

            def group_ctx(qt, g):
                k = qt * 4 + g
                return tsl(qt), slice((g % 2) * 64, (g % 2) * 64 + 64), g // 2, otoks[k % 2], "n_otok%d" % (k % 2)

            def cmp_select(qt, g):
                qs, rows, c2, otok, okey = group_ctx(qt, g)
                if g == 0:
                    S.dma("sync", "n_dtabc", lambda e: e.dma_start(out=dtab[:, 9, :], in_=C.c_dtab[:, 9 + qt, :]), writes=["n_dtabc"])
                jobs = []
                for r in range(4):
                    hh = g * 4 + r
                    mq = (g // 2) * 4 + r

                    def mm(ps, pk, mq=mq):
                        S.op("tensor", lambda e: e.matmul(ps[:], lhsT=KCT[:, g, :], rhs=QT[:, mq, qs], start=True, stop=True),
                             reads=[("n_KCT", g), ("n_QT", mq, qt)], writes=[pk])

                    def pv(p_t, p_k, r=r, hh=hh):
                        po, pok = nextpo()
                        for sub in range(4):
                            S.op("tensor", lambda e: e.matmul(po[:, sub * 97:(sub + 1) * 97], lhsT=p_t[:, sub * 128:(sub + 1) * 128], rhs=VCa[:, g, :], start=True, stop=True),
                                 reads=[p_k, ("n_VCa", g)], writes=[pok])
                        accum_out(po, pok, 97, otok, okey, r, hh, qt, True)
                        po3 = po[:, 0:388].rearrange("p (s c) -> p s c", c=97)
                        if r == 0:
                            S.op("vector", lambda e: e.tensor_tensor(out=imp[:], in0=po3[:, :, 65:97], in1=rden[:].to_broadcast([128, 4, 32]), op=ALU.mult),
                                 reads=[pok, "n_rden"], writes=["n_imp"])
                        else:
                            S.op("vector", lambda e: e.tensor_tensor(out=itmp[:], in0=po3[:, :, 65:97], in1=rden[:].to_broadcast([128, 4, 32]), op=ALU.mult),
                                 reads=[pok, "n_rden"], writes=["n_itmp"])
                            S.op("vector", lambda e: e.tensor_tensor(out=imp[:], in0=imp[:], in1=itmp[:], op=ALU.add), reads=["n_itmp", "n_imp"], writes=["n_imp"])
                    jobs.append(dict(mm=mm, dti=9, scal=-SLOPES[hh] / 2.0, bias=None, pv=pv))
                run_jobs(jobs)
                S.op("vector", lambda e: e.tensor_tensor(out=imp[:], in0=imp[:], in1=keep[:, qt * 4:(qt + 1) * 4, :], op=ALU.mult), reads=["n_imp", "n_keep"], writes=["n_imp"])
                S.op("vector", lambda e: e.tensor_tensor(out=imp[:], in0=imp[:], in1=addc[:, qt * 4:(qt + 1) * 4, :], op=ALU.add), reads=["n_imp", "n_addc"], writes=["n_imp"])
                for sub in range(4):
                    S.op("vector", lambda e: e.max(out=top8[:, sub, :], in_=imp[:, sub, :]), reads=["n_imp"], writes=["n_top8"])
                for sub in range(4):
                    S.op("vector", lambda e: e.tensor_scalar(out=selb[:, sub, :], in0=imp[:, sub, :], scalar1=top8[:, sub, 7:8], scalar2=-BIGNEG, op0=ALU.is_lt, op1=ALU.mult),
                         reads=["n_imp", "n_top8"], writes=["n_selb"])
                ps, pk = C.nextps()
                for sub in range(4):
                    S.op("tensor", lambda e: e.transpose(out=ps[0:32, sub * 128:(sub + 1) * 128], in_=selb[:, sub, :], identity=ident[:]),
                         reads=["n_selb", "n_ident"], writes=[pk])
                S.op("scalar", lambda e: e.activation(out=selbT[0:32, g, :], in_=ps[0:32, :], func=AF.Identity), reads=[pk], writes=[("n_selbT", g)])

            def selwin(qt, g):
                qs, rows, c2, otok, okey = group_ctx(qt, g)
                jobs = []
                for r in range(4):
                    hh = g * 4 + r
                    mq = (g // 2) * 4 + r
                    for br in range(2):
                        kts = list(range(0, qt * 4 + 4)) if br == 0 else list(range(max(0, qt * 4 - 4), qt * 4 + 4))
                        state = {}
                        for n_k, kt in enumerate(kts):
                            delta = qt * TT - kt * 128
                            bias_ap = None
                            far = False
                            if delta <= 0:
                                dti = (-delta) // 128
                            elif br == 1:
                                dti = 3 + delta // 128
                            else:
                                dti = None
                                far = True
                                bias_ap = biasc[:, hh, delta // 128:delta // 128 + 1]
                            nfar = qt * 4 if br == 0 else 0

                            def mm(ps, pk, br=br, kt=kt, mq=mq):
                                ks = slice(kt * 128, (kt + 1) * 128)
                                S.op("tensor", lambda e: e.matmul(ps[:], lhsT=Kz[:, br, g, ks], rhs=QT[:, mq, qs], start=True, stop=(br == 1)),
                                     reads=[("n_Kz", br, g), ("n_KzO", br, g), ("n_QT", mq, qt)], writes=[pk])
                                if br == 0:
                                    S.op("tensor", lambda e: e.matmul(ps[:], lhsT=etab[:, kt, :], rhs=selbT[:, g, :], start=False, stop=True),
                                         reads=["n_etab", ("n_selbT", g)], writes=[pk])

                            def pv(p_t, p_k, br=br, kt=kt, n_k=n_k, nk=len(kts), state=state, r=r, hh=hh, far=far, nfar=nfar):
                                if n_k == 0 and nfar:
                                    state["far"] = nextpo()
                                if n_k == nfar:
                                    state["po"] = nextpo()
                                po, pok = state["far"] if far else state["po"]
                                first = (n_k == 0) if far else (n_k == nfar)
                                last = (n_k == nfar - 1) if far else (n_k == nk - 1)
                                for sub in range(4):
                                    S.op("tensor", lambda e: e.matmul(po[:, sub * 65:(sub + 1) * 65], lhsT=p_t[:, sub * 128:(sub + 1) * 128], rhs=Vaug[:, br, kt, g, :],
                                                                      start=(first and sub == 0), stop=(last and sub == 3)),
                                         reads=[p_k, ("n_Vaug", br, kt), "n_Vone"], writes=[pok])
                                if n_k == nk - 1:
                                    if nfar:
                                        pf, pfk = state["far"]
                                        pf3 = pf[:, 0:260].rearrange("p (s c) -> p s c", c=65)
                                        po3 = po[:, 0:260].rearrange("p (s c) -> p s c", c=65)
                                        S.op("vector", lambda e: e.tensor_tensor(out=comb[:], in0=pf3, in1=fq[:, :, hh:hh + 1].to_broadcast([128, 4, 65]), op=ALU.mult),
                                             reads=[pfk, "n_fq"], writes=["n_comb"])
                                        S.op("vector", lambda e: e.tensor_tensor(out=comb[:], in0=comb[:], in1=po3, op=ALU.add), reads=[pok, "n_comb"], writes=["n_comb"])
                                        accum_out(po, "n_comb", 65, otok, okey, r, (1 + br) * 16 + hh, qt, False, src3=comb[:])
                                    else:
                                        accum_out(po, pok, 65, otok, okey, r, (1 + br) * 16 + hh, qt, False)
                            jobs.append(dict(mm=mm, dti=dti, scal=-SLOPES[hh], bias=bias_ap, pv=pv))
                run_jobs(jobs)
                for kk in range(2):
                    kc = 2 * g + kk
                    ps, pk = C.nextps()
                    for sub in range(4):
                        S.op("tensor", lambda e: e.transpose(out=ps[:, sub * 128:(sub + 1) * 128], in_=otok[:, sub, kk * 128:(kk + 1) * 128], identity=ident[:]),
                             reads=[(okey, 2 * kk), (okey, 2 * kk + 1), "n_ident"], writes=[pk])
                    S.op("scalar", lambda e: e.activation(out=oTq[:, kc, :], in_=ps[:], func=AF.Identity), reads=[pk], writes=[("n_oTq", kc)])
                if g == 3:
                    for m in range(KC):
                        wt, wk = load_wch(16 + m)
                        ps, pk = C.nextps()
                        for kc in range(KC):
                            S.op("tensor", lambda e: e.matmul(ps[:], lhsT=wt[:, kc, :], rhs=oTq[:, kc, :], start=(kc == 0), stop=(kc == KC - 1)),
                                 reads=[wk, ("n_oTq", kc)], writes=[pk])
                        S.op("vector", lambda e: e.tensor_tensor(out=C.hT[:, m, qs], in0=C.hT[:, m, qs], in1=ps[:], op=ALU.add),
                             reads=[pk, ("hT", m, qt)], writes=[("hT", m, qt)])

            steps = [(qt, g) for qt in range(NTT if getattr(C, 'nsa_stop', 9) >= 3 else 0) for g in range(4)]
            if steps:
                cmp_select(*steps[0])
            for k, (qt, g) in enumerate(steps):
                if k + 1 < len(steps):
                    cmp_select(*steps[k + 1])
                selwin(qt, g)
            S.barrier()
        pA.close()
        S.barrier()
    C.nrot = 8


def prep_weights(inp):
    f = np.ascontiguousarray
    w = {}
    g = np.stack([inp["lru_norm_g"][0], inp["ffn_norm_g"][0], inp["nsa_norm_g"][0], inp["ffn_norm_g"][1], inp["final_norm_g"]], 0)
    w["gains"] = f(g.reshape(5, KC, 128).transpose(2, 0, 1))
    wi = inp["lru_w_in"][0].reshape(KC, 128, 2, LN, 128)
    w["lru_win"] = f(wi.transpose(3, 1, 0, 2, 4).reshape(LN, 128, KC, 256))
    w["lru_gw"] = f(inp["lru_gate_w"][0].transpose(1, 2, 0, 3))
    w["lru_wout"] = f(inp["lru_w_out"][0].reshape(LN, 128, D))
    vec = np.concatenate([inp["lru_conv_w"][0], inp["lru_conv_b"][0][None], inp["lru_gate_b"][0], inp["lru_a_param"][0][None]], 0)
    w["lru_vec"] = f(vec.reshape(8, LN, 128).transpose(2, 1, 0))
    fw = inp["ffn_w_in"].reshape(2, KC, 128, 2, FJ, 128)
    w["ffn_win"] = f(fw.transpose(0, 4, 2, 1, 3, 5).reshape(2, FJ, 128, KC, 256))
    w["ffn_wout"] = f(inp["ffn_w_out"].reshape(2, FJ, 128, D))
    fv = np.concatenate([inp["ffn_conv_w"], inp["ffn_conv_b"][:, None]], 1)
    w["ffn_vec"] = f(fv.reshape(2, 4, FJ, 128).transpose(3, 0, 2, 1))
    w.update(nsa_host(inp))
    w.update(const_tables())
    return w


def nsa_host(inp):
    f = np.ascontiguousarray
    w = {}
    W = inp["nsa_w_in"][0]
    Wr = W.reshape(KC, 128, 2608)

    def chunk(cols):
        return Wr[:, :, cols].transpose(1, 0, 2)
    chunks = []
    for c4 in range(4):
        kvi, gp = c4 // 2, c4 % 2
        base = 1024 + kvi * 256 + gp * 128
        chunks.append(chunk(np.arange(base, base + 128)))
    for mq in range(8):
        p, r = mq // 4, mq % 4
        ha, hb = 4 * (2 * p) + r, 4 * (2 * p + 1) + r
        chunks.append(chunk(np.concatenate([np.arange(ha * 64, ha * 64 + 64), np.arange(hb * 64, hb * 64 + 64)])))
    for br in (1, 2):
        for c2 in range(2):
            base = 1024 + br * 512 + c2 * 128
            chunks.append(chunk(np.arange(base, base + 128)))
    Wo = inp["nsa_w_out"][0].reshape(KC, 128, D)
    for m in range(KC):
        chunks.append(Wo[:, :, m * 128:(m + 1) * 128].transpose(1, 0, 2))
    w["nsa_wch"] = f(np.stack(chunks, 0))
    tokcols = np.concatenate([np.arange(1024 + 512 + 256, 1024 + 512 + 512), np.arange(1024 + 1024 + 256, 1024 + 1024 + 512), np.arange(2560, 2608)])
    w["nsa_wtok"] = f(Wr[:, :, tokcols].transpose(1, 0, 2))
    w1 = inp["nsa_cmp_w1"][0].reshape(2, 32, 64, 256).transpose(0, 2, 1, 3)
    w["nsa_w1"] = f(np.concatenate([w1, w1], 1))
    pT = inp["nsa_cmp_pos"][0].transpose(2, 0, 1)
    w["nsa_posT"] = f(np.concatenate([pT, pT], 0))
    w["nsa_b1"] = f(inp["nsa_cmp_b1"][0].reshape(2, 2, 128).transpose(2, 0, 1))
    w["nsa_b1row"] = f(inp["nsa_cmp_b1"][0][None])
    w2 = inp["nsa_cmp_w2"][0]
    w2k = w2[0].reshape(2, 128, 64).transpose(1, 0, 2)
    w["nsa_w2k"] = f(np.concatenate([w2k, w2k], 2))
    w["nsa_w2v"] = f(w2[1].reshape(2, 128, 64).transpose(1, 0, 2))
    return w


def const_tables():
    c = {}
    HUGE = 30000
    k = np.arange(128)[:, None]
    q = np.arange(TT)[None, :]
    dt = np.zeros((128, 13, TT), np.int64)
    for i in range(4):
        d = -128 * i + q - k
        dt[:, i] = np.where(d >= 0, d, HUGE)
    for i in range(1, 5):
        d = 128 * i + q - k
        dt[:, 3 + i] = np.where(d < 512, d, HUGE)
    dt[:, 8] = q - k
    cc = np.arange(128)[:, None]
    for qt in range(4):
        t = qt * TT + q
        d2 = 2 * t - 32 * cc - 31
        ok = (16 * cc + 31 <= t) & (cc < 127)
        dt[:, 9 + qt] = np.where(ok, d2, HUGE)
    c["c_dtab"] = dt.astype(np.int16)
    et = np.zeros((32, 16, 128), np.float32)
    for kt in range(16):
        for kk in range(128):
            et[(kt * 128 + kk) // 64, kt, kk] = 1.0
    c["c_etab"] = et
    sl = np.array(SLOPES, np.float64)
    kk = np.arange(128, dtype=np.float64)[:, None, None]
    bc = -(sl[None, :, None] * ((128.0 * np.arange(16))[None, None, :] - kk))
    c["c_biasc"] = np.ascontiguousarray(bc).astype(np.float32)
    qq = (np.arange(4)[None, :, None] * 128 + np.arange(128)[:, None, None]).astype(np.float64)
    c["c_fq"] = np.exp(-sl[None, None, :] * qq).astype(np.float32)
    t = (np.arange(16)[None, :, None] * 128 + np.arange(128)[:, None, None])
    j = np.arange(32)[None, None, :]
    cur = t // 64
    forced = (j == 0) | (j == cur) | (j == cur - 1)
    future = j > cur
    c["c_keep"] = np.where(forced | future, 0.0, 1.0).astype(np.float32)
    c["c_addc"] = np.where(forced, 1e4, np.where(future, -1.0, 0.0)).astype(np.float32)
    c["c_ident"] = np.eye(128, dtype=np.float32)
    c["c_valid"] = (t >= 31).astype(np.float32)
    ov = np.zeros((128, 33), np.float32)
    ov[:127, 0] = 1.0
    cs = np.arange(127)[:, None] * 16
    sj = np.arange(32)[None, :]
    ov[:127, 1:] = ((cs < (sj + 1) * 64) & (cs + 32 > sj * 64)).astype(np.float32)
    c["c_ovl"] = ov
    return c


_CACHE = {}


def kernel(**inp):
    inp = {k: np.asarray(v) for k, v in inp.items()}
    ncores, nseq = 8, 2
    if "nc" not in _CACHE:
        _CACHE["nc"] = build(nseq)[0]
    nc = _CACHE["nc"]
    w = prep_weights(inp)
    x = inp["x"]
    xT = np.ascontiguousarray(x.reshape(ncores, nseq, NT, KC, 128).transpose(0, 1, 4, 3, 2))
    in_maps = [dict(w, xT=xT[c]) for c in range(ncores)]
    res = run_bass_kernel_spmd(nc, in_maps, core_ids=list(range(ncores)))
    o = np.stack([r["outT"] for r in res.results], 0)
    return np.ascontiguousarray(o.transpose(0, 1, 4, 3, 2)).reshape(16, NT, D).astype(np.float32)
```

```python
import contextlib
import numpy as np
import concourse.bass as bass
import concourse.mybir as mybir
from concourse.bass_utils import run_bass_kernel_spmd

F32 = mybir.dt.float32
BF16 = mybir.dt.bfloat16
AF = mybir.ActivationFunctionType
ALU = mybir.AluOpType
AX = mybir.AxisListType

D = 1024
KC = 8
NT = 2048
TT = 512
NTT = 4
LW = 1280
LN = 10
DFF = 3072
FJ = 24
EPS = 1e-6
GELU_K = 1.5957691216057308
GELU_C = 0.044715 ** 0.5


class Sched:
    ENGS = ("tensor", "vector", "scalar", "gpsimd", "sync")

    def __init__(self, nc, stack):
        self.nc = nc
        self.stack = stack
        self.eng = {e: getattr(nc, e) for e in self.ENGS}
        self.sem = {}
        self.cnt = {}
        self.known = {e: {} for e in self.ENGS}
        self.res = {}
        self.n_inst = 0
        self.n_wait = 0
        for e in ("tensor", "vector", "scalar", "gpsimd"):
            self._mksem(e)

    def _mksem(self, key):
        if key not in self.sem:
            name = "s_" + key.replace(":", "_")
            self.sem[key] = self.stack.enter_context(self.nc.semaphore(name))
            self.cnt[key] = 0
        return self.sem[key]

    def _deps(self, engine, reads, writes):
        deps = {}

        def add(k, v):
            if v > deps.get(k, 0):
                deps[k] = v
        for r in reads:
            st = self.res.get(r)
            if st and st["w"]:
                add(*st["w"])
        for w in writes:
            st = self.res.get(w)
            if st:
                if st["w"]:
                    add(*st["w"])
                for k, v in st["r"].items():
                    add(k, v)
        kn = self.known[engine]
        for k, v in deps.items():
            if k == engine and engine == "tensor":
                continue
            if kn.get(k, 0) >= v:
                continue
            self.eng[engine].wait_ge(self.sem[k], v)
            self.n_wait += 1
            kn[k] = v

    def _mark(self, key, val, reads, writes):
        for r in reads:
            st = self.res.setdefault(r, {"w": None, "r": {}})
            st["r"][key] = val
        for w in writes:
            self.res[w] = {"w": (key, val), "r": {}}

    def op(self, engine, fn, reads=(), writes=()):
        self._deps(engine, reads, writes)
        ins = fn(self.eng[engine])
        self.cnt[engine] += 1
        ins.then_inc(self.sem[engine], 1)
        self.n_inst += 1
        self._mark(engine, self.cnt[engine], reads, writes)
        return ins

    def dma(self, queue, key, fn, reads=(), writes=()):
        k = "dma:" + key
        self._mksem(k)
        self._deps(queue, reads, writes)
        ins = fn(self.eng[queue])
        self.cnt[k] += 16
        ins.then_inc(self.sem[k], 16)
        self.n_inst += 1
        self._mark(k, self.cnt[k], reads, writes)
        return ins

    def barrier(self):
        for e in self.ENGS:
            for k, v in self.cnt.items():
                if v == 0 or (k == e and e == "tensor"):
                    continue
                if self.known[e].get(k, 0) >= v:
                    continue
                self.eng[e].wait_ge(self.sem[k], v)
                self.known[e][k] = v

    def finish(self):
        for k, v in self.cnt.items():
            if v and self.known["sync"].get(k, 0) < v:
                self.nc.sync.wait_ge(self.sem[k], v)
                self.known["sync"][k] = v


class Ctx:
    pass


def tsl(tt):
    return slice(tt * TT, (tt + 1) * TT)


def build(nseq=2, upto=99, nsa_stop=9):
    nc = bass.Bass("TRN2", target_bir_lowering=False)
    C = Ctx()
    C.nc = nc
    C.nsa_stop = nsa_stop

    def din(name, shape):
        return nc.dram_tensor(name, list(shape), F32, kind="ExternalInput").ap()
    C.xT = din("xT", [nseq, 128, KC, NT])
    C.gains = din("gains", [128, 5, KC])
    C.lru_win = din("lru_win", [LN, 128, KC, 256])
    C.lru_gw = din("lru_gw", [LN, 128, 2, 128])
    C.lru_wout = din("lru_wout", [LN, 128, D])
    C.lru_vec = din("lru_vec", [128, LN, 8])
    C.ffn_win = din("ffn_win", [2, FJ, 128, KC, 256])
    C.ffn_wout = din("ffn_wout", [2, FJ, 128, D])
    C.ffn_vec = din("ffn_vec", [128, 2, FJ, 4])
    C.nsa_wch = din("nsa_wch", [24, 128, KC, 128])
    C.nsa_wtok = din("nsa_wtok", [128, KC, 560])
    C.nsa_w1 = din("nsa_w1", [2, 128, 32, 256])
    C.nsa_posT = din("nsa_posT", [128, 2, 32])
    C.nsa_b1 = din("nsa_b1", [128, 2, 2])
    C.nsa_b1row = din("nsa_b1row", [1, 2, 256])
    C.nsa_w2k = din("nsa_w2k", [128, 2, 128])
    C.nsa_w2v = din("nsa_w2v", [128, 2, 64])
    C.c_dtab = nc.dram_tensor("c_dtab", [128, 13, TT], mybir.dt.int16, kind="ExternalInput").ap()
    C.c_etab = din("c_etab", [32, 16, 128])
    C.c_biasc = din("c_biasc", [128, 16, 16])
    C.c_keep = din("c_keep", [128, 16, 32])
    C.c_addc = din("c_addc", [128, 16, 32])
    C.c_ident = din("c_ident", [128, 128])
    C.c_valid = din("c_valid", [128, 16, 1])
    C.c_fq = din("c_fq", [128, 4, 16])
    C.c_ovl = din("c_ovl", [128, 33])
    C.outT = nc.dram_tensor("outT", [nseq, 128, KC, NT], F32, kind="ExternalOutput").ap()

    with contextlib.ExitStack() as st:
        S = Sched(nc, st)
        C.S = S

        uid = [0]

        def sb(name, shape, dt=F32, stack=st):
            uid[0] += 1
            return stack.enter_context(nc.sbuf_tensor("%s_u%d" % (name, uid[0]), list(shape), dt))
        C.sb = sb
        C.hT = sb("hT", [128, KC, NT])
        C.xn = sb("xn", [128, KC, NT], BF16)
        C.ones = sb("ones", [128, 128], BF16)
        C.gn = sb("gn", [128, 5, KC])
        C.lvec = sb("lvec", [128, LN, 8])
        C.lca = sb("lca", [128, LN, 2])
        C.fvec = sb("fvec", [128, 2, FJ, 4])
        C.psum = [st.enter_context(nc.psum_tensor("ps%d" % i, [128, TT], F32)) for i in range(8)]
        C.psi = 0
        C.nrot = 8

        def nextps():
            i = C.psi % C.nrot
            C.psi += 1
            return C.psum[i], ("ps", i)
        C.nextps = nextps

        S.op("vector", lambda e: e.memset(C.ones[:], 1.0), writes=["ones"])
        C.epsc = sb("epsc", [128, 1])
        S.op("vector", lambda e: e.memset(C.epsc[:], EPS), writes=["epsc"])
        S.dma("sync", "c0", lambda e: e.dma_start(out=C.gn[:], in_=C.gains), writes=["gn"])
        S.dma("sync", "c1", lambda e: e.dma_start(out=C.lvec[:], in_=C.lru_vec), writes=["lvec"])
        S.dma("sync", "c2", lambda e: e.dma_start(out=C.fvec[:], in_=C.ffn_vec), writes=["fvec"])
        lru_consts(C)

        for s in range(nseq):
            C.cur_seq = s
            for tt in range(NTT):
                S.dma("sync", "x%d" % tt, lambda e, tt=tt: e.dma_start(out=C.hT[:, :, tsl(tt)], in_=C.xT[s, :, :, tsl(tt)]),
                      writes=[("hT", kc, tt) for kc in range(KC)])
            if upto >= 1:
                rmsnorm(C, 0)
                lru_mixer(C)
            if upto >= 2:
                rmsnorm(C, 1)
                conv_ffn(C, 0)
            if upto >= 3:
                rmsnorm(C, 2)
                nsa_mixer(C)
            if upto >= 4:
                rmsnorm(C, 3)
                conv_ffn(C, 1)
            if upto >= 5:
                rmsnorm(C, 4, final=True)
            if upto < 5:
                for tt in range(NTT):
                    S.dma("sync", "o%d" % tt, lambda e, tt=tt: e.dma_start(out=C.outT[s, :, :, tsl(tt)], in_=C.hT[:, :, tsl(tt)]),
                          reads=[("hT", kc, tt) for kc in range(KC)])
        S.finish()
    C.n_inst = S.n_inst
    C.n_wait = S.n_wait
    return nc, C


def lru_consts(C):
    S, sb = C.S, C.sb
    with contextlib.ExitStack() as ph:
        t = [sb("lc%d" % i, [128, LN], F32, ph) for i in range(6)]
        ap = C.lvec[:, :, 7]
        S.op("scalar", lambda e: e.activation(out=t[0][:], in_=ap, func=AF.Abs), reads=["lvec"], writes=["lc0"])
        S.op("scalar", lambda e: e.activation(out=t[1][:], in_=t[0][:], func=AF.Exp, scale=-1.0), reads=["lc0"], writes=["lc1"])
        S.op("scalar", lambda e: e.activation(out=t[2][:], in_=t[1][:], func=AF.Ln, bias=1.0), reads=["lc1"], writes=["lc2"])
        S.op("vector", lambda e: e.tensor_scalar(out=t[3][:], in0=t[1][:], scalar1=1.0 / 3.0, scalar2=-0.5, op0=ALU.mult, op1=ALU.add), reads=["lc1"], writes=["lc3"])
        S.op("vector", lambda e: e.tensor_tensor(out=t[3][:], in0=t[3][:], in1=t[1][:], op=ALU.mult), reads=["lc3", "lc1"], writes=["lc3"])
        S.op("vector", lambda e: e.tensor_scalar(out=t[3][:], in0=t[3][:], scalar1=1.0, scalar2=None, op0=ALU.add), reads=["lc3"], writes=["lc3"])
        S.op("vector", lambda e: e.tensor_tensor(out=t[3][:], in0=t[3][:], in1=t[1][:], op=ALU.mult), reads=["lc3", "lc1"], writes=["lc3"])
        S.op("vector", lambda e: e.tensor_single_scalar(out=t[4][:], in_=t[1][:], scalar=0.03, op=ALU.is_lt), reads=["lc1"], writes=["lc4"])
        S.op("vector", lambda e: e.tensor_tensor(out=t[3][:], in0=t[3][:], in1=t[2][:], op=ALU.subtract), reads=["lc3", "lc2"], writes=["lc3"])
        S.op("vector", lambda e: e.tensor_tensor(out=t[3][:], in0=t[3][:], in1=t[4][:], op=ALU.mult), reads=["lc3", "lc4"], writes=["lc3"])
        S.op("vector", lambda e: e.tensor_tensor(out=t[3][:], in0=t[3][:], in1=t[2][:], op=ALU.add), reads=["lc3", "lc2"], writes=["lc3"])
        S.op("vector", lambda e: e.tensor_scalar(out=t[5][:], in0=ap, scalar1=-1.0, scalar2=0.0, op0=ALU.mult, op1=ALU.max), reads=["lvec"], writes=["lc5"])
        S.op("vector", lambda e: e.tensor_tensor(out=t[3][:], in0=t[3][:], in1=t[5][:], op=ALU.add), reads=["lc3", "lc5"], writes=["lc3"])
        S.op("vector", lambda e: e.tensor_scalar(out=C.lca[:, :, 0], in0=t[3][:], scalar1=-8.0, scalar2=None, op0=ALU.mult), reads=["lc3"], writes=["lca"])
        S.op("vector", lambda e: e.tensor_scalar(out=C.lca[:, :, 1], in0=t[3][:], scalar1=-16.0, scalar2=None, op0=ALU.mult), reads=["lc3"], writes=["lca"])
        S.barrier()


def rmsnorm(C, gi, final=False):
    S = C.S
    ph = contextlib.ExitStack()
    C.sq = [C.sb("sq%d" % i, [128, TT], BF16, ph) for i in range(4)]
    C.rs = C.sb("rs", [128, TT], F32, ph)
    for tt in range(NTT):
        ps, pk = C.nextps()
        for kc in range(KC):
            sq = C.sq[kc % 4]
            S.op("scalar", lambda e: e.activation(out=sq[:], in_=C.hT[:, kc, tsl(tt)], func=AF.Square),
                 reads=[("hT", kc, tt)], writes=[("sq", kc % 4)])
            S.op("tensor", lambda e: e.matmul(ps[:], lhsT=C.ones[:], rhs=sq[:], start=(kc == 0), stop=(kc == KC - 1)),
                 reads=["ones", ("sq", kc % 4)], writes=[pk])
        S.op("scalar", lambda e: e.activation(out=C.rs[:], in_=ps[:], func=AF.Ln, scale=1.0 / D, bias=C.epsc[:]), reads=[pk, "epsc"], writes=["rs"])
        S.op("scalar", lambda e: e.activation(out=C.rs[:], in_=C.rs[:], func=AF.Exp, scale=-0.5), reads=["rs"], writes=["rs"])
        for kc in range(KC):
            if final:
                S.op("vector", lambda e: e.scalar_tensor_tensor(out=C.hT[:, kc, tsl(tt)], in0=C.hT[:, kc, tsl(tt)], scalar=C.gn[:, gi, kc:kc + 1],
                                                               in1=C.rs[:], op0=ALU.mult, op1=ALU.mult),
                     reads=[("hT", kc, tt), "rs", "gn"], writes=[("hT", kc, tt)])
            else:
                S.op("vector", lambda e: e.scalar_tensor_tensor(out=C.xn[:, kc, tsl(tt)], in0=C.hT[:, kc, tsl(tt)], scalar=C.gn[:, gi, kc:kc + 1],
                                                               in1=C.rs[:], op0=ALU.mult, op1=ALU.mult),
                     reads=[("hT", kc, tt), "rs", "gn"], writes=[("xn", kc, tt)])
        if final:
            S.dma("sync", "o%d" % tt, lambda e: e.dma_start(out=C.outT[C.cur_seq, :, :, tsl(tt)], in_=C.hT[:, :, tsl(tt)]),
                  reads=[("hT", kc, tt) for kc in range(KC)])
    S.barrier()
    ph.close()


def inproj(C, w, col0, tt, evac):
    S = C.S
    ps, pk = C.nextps()
    wt, wk = w
    for kc in range(KC):
        S.op("tensor", lambda e: e.matmul(ps[:], lhsT=wt[:, kc, col0:col0 + 128], rhs=C.xn[:, kc, tsl(tt)], start=(kc == 0), stop=(kc == KC - 1)),
             reads=[wk, ("xn", kc, tt)], writes=[pk])
    evac(ps, pk)


def gelu_inplace(C, x, xk, t1, t1k, tt):
    S = C.S
    sl = tsl(tt)
    S.op("scalar", lambda e: e.activation(out=t1[:, sl], in_=x[:, sl], func=AF.Square), reads=[(xk, tt)], writes=[(t1k, tt)])
    S.op("vector", lambda e: e.tensor_scalar(out=t1[:, sl], in0=t1[:, sl], scalar1=0.044715, scalar2=1.0, op0=ALU.mult, op1=ALU.add),
         reads=[(t1k, tt)], writes=[(t1k, tt)])
    S.op("vector", lambda e: e.tensor_tensor(out=t1[:, sl], in0=t1[:, sl], in1=x[:, sl], op=ALU.mult), reads=[(t1k, tt), (xk, tt)], writes=[(t1k, tt)])
    S.op("scalar", lambda e: e.activation(out=t1[:, sl], in_=t1[:, sl], func=AF.Sigmoid, scale=GELU_K), reads=[(t1k, tt)], writes=[(t1k, tt)])
    S.op("vector", lambda e: e.tensor_tensor(out=x[:, sl], in0=x[:, sl], in1=t1[:, sl], op=ALU.mult), reads=[(t1k, tt), (xk, tt)], writes=[(xk, tt)])


def outproj_group(C, acts, wouts):
    S = C.S
    n = len(acts)
    for tt in range(NTT):
        for m in range(KC):
            ps, pk = C.nextps()
            for i in range(n):
                a_ap, a_k = acts[i]
                wt, wk = wouts[i]
                S.op("tensor", lambda e: e.matmul(ps[:], lhsT=wt[:, m * 128:(m + 1) * 128], rhs=a_ap(tt), start=(i == 0), stop=(i == n - 1)),
                     reads=[wk, a_k(tt)], writes=[pk])
            S.op("vector", lambda e: e.tensor_tensor(out=C.hT[:, m, tsl(tt)], in0=C.hT[:, m, tsl(tt)], in1=ps[:], op=ALU.add),
                 reads=[pk, ("hT", m, tt)], writes=[("hT", m, tt)])


def lru_mixer(C):
    S, sb, nc = C.S, C.sb, C.nc
    G = 2
    HT = NT // 2
    with contextlib.ExitStack() as ph:
        win = [sb("l_win%d" % i, [128, KC, 256], BF16, ph) for i in range(2)]
        gw = [sb("l_gw%d" % i, [128, 2, 128], BF16, ph) for i in range(2)]
        wo = [sb("l_wo%d" % i, [128, D], BF16, ph) for i in range(2 * G)]
        hy = [sb("l_hy%d" % i, [128, NT], BF16, ph) for i in range(2 * G)]
        ysbs = [sb("l_ysb%d" % i, [128, NT], F32, ph) for i in range(2)]
        xpads = [sb("l_xpad%d" % i, [128, 3 + NT], F32, ph) for i in range(2)]
        t1s = [sb("l_t1%d" % i, [128, HT], F32, ph) for i in range(2)]
        xcs = [sb("l_xc%d" % i, [128, HT], F32, ph) for i in range(2)]
        xcbs = [sb("l_xcb%d" % i, [128, HT], BF16, ph) for i in range(2)]
        rrs = [sb("l_r%d" % i, [128, HT], F32, ph) for i in range(2)]
        igs = [sb("l_ig%d" % i, [128, HT], F32, ph) for i in range(2)]
        hhs = [sb("l_h%d" % i, [128, HT], F32, ph) for i in range(2)]
        for i in range(2):
            S.op("vector", lambda e: e.memset(xpads[i][:, 0:3], 0.0), writes=["l_xpad%d_0" % i])

        def load_w(n):
            b = n % 2
            S.dma("gpsimd", "l_win%d" % b, lambda e: e.dma_start(out=win[b][:], in_=C.lru_win[n]), writes=["l_win%d" % b])
            S.dma("gpsimd", "l_gw%d" % b, lambda e: e.dma_start(out=gw[b][:], in_=C.lru_gw[n]), writes=["l_gw%d" % b])
            S.dma("gpsimd", "l_wo%d" % (n % (2 * G)), lambda e: e.dma_start(out=wo[n % (2 * G)][:], in_=C.lru_wout[n]), writes=["l_wo%d" % (n % (2 * G))])

        def proj(n, u):
            b = n % 2
            wi, wik = win[b], "l_win%d" % b
            ysb, xpad = ysbs[b], xpads[b]
            for tt in (2 * u, 2 * u + 1):
                inproj(C, (wi, wik), 128, tt, lambda ps, pk: S.op(
                    "scalar", lambda e: e.activation(out=xpad[:, 3 + tt * TT:3 + (tt + 1) * TT], in_=ps[:], func=AF.Identity),
                    reads=[pk], writes=[("l_xpad%d" % b, tt)]))
            for tt in (2 * u, 2 * u + 1):
                inproj(C, (wi, wik), 0, tt, lambda ps, pk: S.op(
                    "scalar", lambda e: e.activation(out=ysb[:, tsl(tt)], in_=ps[:], func=AF.Identity), reads=[pk], writes=[("l_ysb%d" % b, tt)]))

        units = [(n, u) for n in range(LN) for u in range(2)]
        def conv_unit(k):
            n, u = units[k]
            b = n % 2
            q = k % 2
            xpad, xk = xpads[b], "l_xpad%d" % b
            xc, xcb = xcs[q], xcbs[q]
            xck, xcbk = "l_xc%d" % q, "l_xcb%d" % q
            TU = (2 * u, 2 * u + 1)

            def lsl(tt):
                return slice((tt - 2 * u) * TT, (tt - 2 * u + 1) * TT)
            for tt in TU:
                rd = [(xk, tt), (xk, tt - 1) if tt > 0 else xk + "_0", "lvec"]
                S.op("vector", lambda e: e.tensor_scalar(out=xc[:, lsl(tt)], in0=xpad[:, 3 + tt * TT:3 + (tt + 1) * TT], scalar1=C.lvec[:, n, 3:4],
                                                        scalar2=C.lvec[:, n, 4:5], op0=ALU.mult, op1=ALU.add), reads=rd, writes=[(xck, tt)])
                for kk in (2, 1, 0):
                    S.op("vector", lambda e: e.scalar_tensor_tensor(out=xc[:, lsl(tt)], in0=xpad[:, kk + tt * TT:kk + (tt + 1) * TT], scalar=C.lvec[:, n, kk:kk + 1],
                                                                   in1=xc[:, lsl(tt)], op0=ALU.mult, op1=ALU.add),
                         reads=rd + [(xck, tt)], writes=[(xck, tt)])
                S.op("scalar", lambda e: e.activation(out=xcb[:, lsl(tt)], in_=xc[:, lsl(tt)], func=AF.Identity), reads=[(xck, tt)], writes=[(xcbk, tt)])

        pending = None
        load_w(0)
        proj(0, 0)
        conv_unit(0)
        for k, (n, u) in enumerate(units):
            b = n % 2
            q = k % 2
            gwi, gwk = gw[b], "l_gw%d" % b
            hyi, hyk = hy[n % (2 * G)], "l_hy%d" % (n % (2 * G))
            ysb, xpad = ysbs[b], xpads[b]
            yk, xk = "l_ysb%d" % b, "l_xpad%d" % b
            t1, xc, xcb, rr, ig, hh = t1s[q], xcs[q], xcbs[q], rrs[q], igs[q], hhs[q]
            t1k, xck, xcbk, rk, igk, hk = "l_t1%d" % q, "l_xc%d" % q, "l_xcb%d" % q, "l_r%d" % q, "l_ig%d" % q, "l_h%d" % q
            TU = (2 * u, 2 * u + 1)

            def lsl(tt):
                return slice((tt - 2 * u) * TT, (tt - 2 * u + 1) * TT)
            if k + 1 < len(units):
                n2, u2 = units[k + 1]
                if u2 == 0:
                    load_w(n2)
                proj(n2, u2)
            if pending is not None and u == 1:
                outproj_group(C, *pending)
                pending = None
            for tt in TU:
                S.op("scalar", lambda e: e.activation(out=t1[:, lsl(tt)], in_=ysb[:, tsl(tt)], func=AF.Square, scale=GELU_C), reads=[(yk, tt)], writes=[(t1k, tt)])
            for g, (dst, dk) in enumerate(((rr, rk), (ig, igk))):
                for tt in TU:
                    ps, pk = C.nextps()
                    S.op("tensor", lambda e: e.matmul(ps[:], lhsT=gwi[:, g, :], rhs=xcb[:, lsl(tt)], start=True, stop=True),
                         reads=[gwk, (xcbk, tt)], writes=[pk])
                    S.op("scalar", lambda e: e.activation(out=dst[:, lsl(tt)], in_=ps[:], func=AF.Sigmoid, bias=C.lvec[:, n, 5 + g:6 + g]),
                         reads=[pk, "lvec"], writes=[(dk, tt)])
            for tt in TU:
                S.op("vector", lambda e: e.scalar_tensor_tensor(out=t1[:, lsl(tt)], in0=t1[:, lsl(tt)], scalar=1.0, in1=ysb[:, tsl(tt)], op0=ALU.add, op1=ALU.mult),
                     reads=[(t1k, tt), (yk, tt)], writes=[(t1k, tt)])
            for tt in TU:
                S.op("scalar", lambda e: e.activation(out=t1[:, lsl(tt)], in_=t1[:, lsl(tt)], func=AF.Sigmoid, scale=GELU_K), reads=[(t1k, tt)], writes=[(t1k, tt)])
            for tt in TU:
                S.op("vector", lambda e: e.tensor_tensor(out=ysb[:, tsl(tt)], in0=ysb[:, tsl(tt)], in1=t1[:, lsl(tt)], op=ALU.mult), reads=[(t1k, tt), (yk, tt)], writes=[(yk, tt)])
            for tt in TU:
                S.op("vector", lambda e: e.tensor_tensor(out=ig[:, lsl(tt)], in0=ig[:, lsl(tt)], in1=xc[:, lsl(tt)], op=ALU.mult), reads=[(igk, tt), (xck, tt)], writes=[(igk, tt)])
            for tt in TU:
                S.op("scalar", lambda e: e.activation(out=t1[:, lsl(tt)], in_=rr[:, lsl(tt)], func=AF.Exp, scale=C.lca[:, n, 1:2]), reads=[(rk, tt), "lca"], writes=[(t1k, tt)])
            for tt in TU:
                S.op("scalar", lambda e: e.activation(out=rr[:, lsl(tt)], in_=rr[:, lsl(tt)], func=AF.Exp, scale=C.lca[:, n, 0:1]), reads=[(rk, tt), "lca"], writes=[(rk, tt)])
            for tt in TU:
                S.op("scalar", lambda e: e.activation(out=t1[:, lsl(tt)], in_=t1[:, lsl(tt)], func=AF.Sqrt, scale=-1.0, bias=1.0), reads=[(t1k, tt)], writes=[(t1k, tt)])
            if k + 1 < len(units):
                conv_unit(k + 1)
            for tt in TU:
                S.op("vector", lambda e: e.tensor_tensor(out=ig[:, lsl(tt)], in0=ig[:, lsl(tt)], in1=t1[:, lsl(tt)], op=ALU.mult), reads=[(igk, tt), (t1k, tt)], writes=[(igk, tt)])
            for tt in TU:
                if tt == 0:
                    init, ird = 0.0, []
                elif tt == 2 * u:
                    init, ird = hhs[1 - q][:, HT - 1:HT], [("l_h%d" % (1 - q), tt - 1)]
                else:
                    init, ird = hh[:, TT - 1:TT], [(hk, tt - 1)]
                S.op("vector", lambda e: e.tensor_tensor_scan(out=hh[:, lsl(tt)], data0=rr[:, lsl(tt)], data1=ig[:, lsl(tt)], initial=init, op0=ALU.mult, op1=ALU.add),
                     reads=[(rk, tt), (igk, tt)] + ird, writes=[(hk, tt)])
                S.op("vector", lambda e: e.tensor_tensor(out=hyi[:, tsl(tt)], in0=hh[:, lsl(tt)], in1=ysb[:, tsl(tt)], op=ALU.mult),
                     reads=[(hk, tt), (yk, tt)], writes=[(hyk, tt)])
            if u == 1 and n % G == G - 1:
                idx = [(n - G + 1 + i) % (2 * G) for i in range(G)]
                pending = ([((lambda tt, i=i: hy[i][:, tsl(tt)]), (lambda tt, i=i: ("l_hy%d" % i, tt))) for i in idx],
                           [(wo[i], "l_wo%d" % i) for i in idx])
        outproj_group(C, *pending)
        S.barrier()


def conv_ffn(C, L):
    S, sb, nc = C.S, C.sb, C.nc
    G = 4
    with contextlib.ExitStack() as ph:
        win = [sb("f_win%d" % i, [128, KC, 256], BF16, ph) for i in range(2)]
        wo = [sb("f_wo%d" % i, [128, D], BF16, ph) for i in range(2 * G)]
        act = [sb("f_act%d" % i, [128, NT], BF16, ph) for i in range(2 * G)]
        apads = [sb("f_apad%d" % i, [128, 2 + NT], F32, ph) for i in range(2)]
        bsbs = [sb("f_bsb%d" % i, [128, NT], BF16, ph) for i in range(2)]
        acs = [sb("f_ac%d" % i, [128, NT], F32, ph) for i in range(2)]
        t1 = sb("f_t1", [128, NT], F32, ph)
        for i in range(2):
            S.op("vector", lambda e: e.memset(apads[i][:, 0:2], 0.0), writes=["f_apad%d_0" % i])

        def load_and_proj(j):
            b = j % 2
            wi, woi = win[b], wo[j % (2 * G)]
            wik, wok = "f_win%d" % b, "f_wo%d" % (j % (2 * G))
            S.dma("gpsimd", wik, lambda e: e.dma_start(out=wi[:], in_=C.ffn_win[L, j]), writes=[wik])
            S.dma("gpsimd", wok, lambda e: e.dma_start(out=woi[:], in_=C.ffn_wout[L, j]), writes=[wok])
            apad, bsb, acb = apads[b], bsbs[b], acs[b]

            def evac_a(ps, pk, tt):
                S.op("scalar", lambda e: e.activation(out=apad[:, 2 + tt * TT:2 + (tt + 1) * TT], in_=ps[:], func=AF.Identity),
                     reads=[pk], writes=[("f_apad%d" % b, tt)])
                S.op("scalar", lambda e: e.activation(out=acb[:, tsl(tt)], in_=ps[:], func=AF.Identity, scale=C.fvec[:, L, j, 2:3], bias=C.fvec[:, L, j, 3:4]),
                     reads=[pk, "fvec"], writes=[("f_ac%d" % b, tt)])
            for tt in range(NTT):
                inproj(C, (wi, wik), 0, tt, lambda ps, pk: evac_a(ps, pk, tt))
            for tt in range(NTT):
                inproj(C, (wi, wik), 128, tt, lambda ps, pk: S.op(
                    "scalar", lambda e: e.activation(out=bsb[:, tsl(tt)], in_=ps[:], func=AF.Identity), reads=[pk], writes=[("f_bsb%d" % b, tt)]))

        pending = None
        load_and_proj(0)
        for j in range(FJ):
            b = j % 2
            acti, actk = act[j % (2 * G)], "f_act%d" % (j % (2 * G))
            apad, bsb, ac = apads[b], bsbs[b], acs[b]
            ak, bk, ack = "f_apad%d" % b, "f_bsb%d" % b, "f_ac%d" % b
            T4 = range(NTT)
            if j + 1 < FJ:
                load_and_proj(j + 1)
            if pending is not None:
                outproj_group(C, *pending)
                pending = None
            for tt in T4:
                sl = tsl(tt)
                rd = [(ak, tt), (ak, tt - 1) if tt > 0 else ak + "_0", "fvec"]
                for k in (1, 0):
                    S.op("vector", lambda e: e.scalar_tensor_tensor(out=ac[:, sl], in0=apad[:, k + tt * TT:k + (tt + 1) * TT], scalar=C.fvec[:, L, j, k:k + 1],
                                                                   in1=ac[:, sl], op0=ALU.mult, op1=ALU.add),
                         reads=rd + [(ack, tt)], writes=[(ack, tt)])
                S.op("scalar", lambda e: e.activation(out=t1[:, sl], in_=ac[:, sl], func=AF.Square, scale=GELU_C), reads=[(ack, tt)], writes=[("f_t1", tt)])
            for tt in T4:
                sl = tsl(tt)
                S.op("vector", lambda e: e.scalar_tensor_tensor(out=t1[:, sl], in0=t1[:, sl], scalar=1.0, in1=ac[:, sl], op0=ALU.add, op1=ALU.mult),
                     reads=[("f_t1", tt), (ack, tt)], writes=[("f_t1", tt)])
                S.op("scalar", lambda e: e.activation(out=t1[:, sl], in_=t1[:, sl], func=AF.Sigmoid, scale=GELU_K), reads=[("f_t1", tt)], writes=[("f_t1", tt)])
                S.op("vector", lambda e: e.tensor_tensor(out=ac[:, sl], in0=ac[:, sl], in1=bsb[:, sl], op=ALU.mult), reads=[(ack, tt), (bk, tt)], writes=[(ack, tt)])
            for tt in T4:
                sl = tsl(tt)
                S.op("vector", lambda e: e.tensor_tensor(out=acti[:, sl], in0=ac[:, sl], in1=t1[:, sl], op=ALU.mult),
                     reads=[(ack, tt), ("f_t1", tt)], writes=[(actk, tt)])
            if j % G == G - 1:
                idx = [(j - G + 1 + i) % (2 * G) for i in range(G)]
                pending = ([((lambda tt, i=i: act[i][:, tsl(tt)]), (lambda tt, i=i: ("f_act%d" % i, tt))) for i in idx],
                           [(wo[i], "f_wo%d" % i) for i in idx])
        outproj_group(C, *pending)
        S.barrier()


SLOPES = [2.0 ** (-8.0 * (h + 1) / 16.0) for h in range(16)]
BIGNEG = 30000.0
NPT = 4
NSM = 4


def gelu_ap(C, x, t, xk, tk):
    S = C.S
    S.op("scalar", lambda e: e.activation(out=t, in_=x, func=AF.Square), reads=[xk], writes=[tk])
    S.op("vector", lambda e: e.tensor_scalar(out=t, in0=t, scalar1=0.044715, scalar2=1.0, op0=ALU.mult, op1=ALU.add), reads=[tk], writes=[tk])
    S.op("vector", lambda e: e.tensor_tensor(out=t, in0=t, in1=x, op=ALU.mult), reads=[tk, xk], writes=[tk])
    S.op("scalar", lambda e: e.activation(out=t, in_=t, func=AF.Sigmoid, scale=GELU_K), reads=[tk], writes=[tk])
    S.op("vector", lambda e: e.tensor_tensor(out=x, in0=x, in1=t, op=ALU.mult), reads=[tk, xk], writes=[xk])


def nsa_mixer(C):
    S, sb, nc = C.S, C.sb, C.nc
    C.nrot = 4
    po_banks = [(C.psum[i], ("ps", i)) for i in (4, 5, 6, 7)]
    po_i = [0]

    def nextpo():
        r = po_banks[po_i[0] % 4]
        po_i[0] += 1
        return r
    with contextlib.ExitStack() as ph:
        KCT = sb("n_KCT", [128, 4, 128], BF16, ph)
        VCa = sb("n_VCa", [128, 4, 97], BF16, ph)
        wch = [None, None]
        wci = [0]

        def alloc_wch(stack):
            for i in range(2):
                wch[i] = sb("n_wch%d" % i, [128, KC, 128], BF16, stack)

        def load_wch(idx):
            i = wci[0] % 2
            wci[0] += 1
            k = "n_wch%d" % i
            S.dma("gpsimd", k, lambda e: e.dma_start(out=wch[i][:], in_=C.nsa_wch[idx]), writes=[k])
            return wch[i], k

        with contextlib.ExitStack() as p1:
            alloc_wch(p1)
            KV0 = sb("n_KV0", [128, 4, NT], BF16, p1)
            W1 = sb("n_W1", [128, 2, 32, 256], BF16, p1)
            posT = sb("n_posT", [128, 2, 32], BF16, p1)
            b1 = sb("n_b1", [128, 2, 2], F32, p1)
            W2k = sb("n_W2k", [128, 2, 128], BF16, p1)
            W2v = sb("n_W2v", [128, 2, 64], BF16, p1)
            ovl = sb("n_ovl", [128, 33], F32, p1)
            hids = [sb("n_hid%d" % i, [128, 256], F32, p1) for i in range(2)]
            hscs = [sb("n_hsc%d" % i, [128, 256], F32, p1) for i in range(2)]
            ghbs = [sb("n_ghb%d" % i, [128, 2, 128], BF16, p1) for i in range(2)]
            cvec = sb("n_cvec", [1, 2, 256], BF16, p1)
            onesr = sb("n_onesr", [1, 128], BF16, p1)
            b1row = sb("n_b1row", [1, 2, 256], F32, p1)
            identc = sb("n_identc", [128, 128], F32, p1)
            S.dma("sync", "n_identc", lambda e: e.dma_start(out=identc[:], in_=C.c_ident), writes=["n_identc"])
            S.dma("sync", "n_b1row", lambda e: e.dma_start(out=b1row[:], in_=C.nsa_b1row), writes=["n_b1row"])
            S.op("vector", lambda e: e.memset(onesr[:], 1.0), writes=["n_onesr"])
            for kvi in range(2):
                S.dma("gpsimd", "n_W1_%d" % kvi, lambda e: e.dma_start(out=W1[:, kvi], in_=C.nsa_w1[kvi]), writes=[("n_W1", kvi)])
            S.dma("gpsimd", "n_posT", lambda e: e.dma_start(out=posT[:], in_=C.nsa_posT), writes=["n_posT"])
            S.dma("sync", "n_b1", lambda e: e.dma_start(out=b1[:], in_=C.nsa_b1), writes=["n_b1"])
            S.dma("gpsimd", "n_W2k", lambda e: e.dma_start(out=W2k[:], in_=C.nsa_w2k), writes=["n_W2k"])
            S.dma("gpsimd", "n_W2v", lambda e: e.dma_start(out=W2v[:], in_=C.nsa_w2v), writes=["n_W2v"])
            S.dma("sync", "n_ovl", lambda e: e.dma_start(out=ovl[:], in_=C.c_ovl), writes=["n_ovl"])
            S.op("vector", lambda e: e.memset(KCT[:], 0.0), writes=[("n_KCT", g) for g in range(4)])
            S.op("vector", lambda e: e.memset(VCa[:], 0.0), writes=[("n_VCa", g) for g in range(4)])
            for g in range(4):
                S.op("vector", lambda e: e.tensor_copy(out=VCa[:, g, 64:97], in_=ovl[:]), reads=["n_ovl"], writes=[("n_VCa", g)])
            for c4 in range(4):
                wt, wk = load_wch(c4)
                for tt in range(NTT):
                    inproj(C, (wt, wk), 0, tt, lambda ps, pk: S.op(
                        "scalar", lambda e: e.activation(out=KV0[:, c4, tsl(tt)], in_=ps[:], func=AF.Identity), reads=[pk], writes=[("n_KV0", c4)]))
            for kvi in range(2):
                ps, pk = C.nextps()
                for i in range(32):
                    S.op("tensor", lambda e: e.matmul(ps[0:1, 0:256], lhsT=posT[0:64, kvi, i:i + 1], rhs=W1[0:64, kvi, i, :], start=(i == 0), stop=(i == 31)),
                         reads=[("n_W1", kvi), "n_posT"], writes=[pk])
                S.op("vector", lambda e: e.tensor_tensor(out=cvec[0:1, kvi, :], in0=ps[0:1, 0:256], in1=b1row[0:1, kvi, :], op=ALU.add),
                     reads=[pk, "n_b1row"], writes=[("n_cvec", kvi)])
            un = 0
            for kvi in range(2):
                for g in range(4):
                    c4 = kvi * 2 + g // 2
                    rows = slice((g % 2) * 64, (g % 2) * 64 + 64)
                    ub = un % 2
                    un += 1
                    hid, hsc, ghb = hids[ub], hscs[ub], ghbs[ub]
                    hk, sk, gk = "n_hid%d" % ub, "n_hsc%d" % ub, "n_ghb%d" % ub
                    ps, pk = C.nextps()
                    for i in range(32):
                        S.op("tensor", lambda e: e.matmul(ps[0:127, 0:256], lhsT=KV0[rows, c4, i:i + 2017:16], rhs=W1[rows, kvi, i, :], start=(i == 0), stop=False),
                             reads=[("n_W1", kvi), ("n_KV0", c4)], writes=[pk])
                    S.op("tensor", lambda e: e.matmul(ps[0:127, 0:256], lhsT=onesr[0:1, 0:127], rhs=cvec[0:1, kvi, :], start=False, stop=True),
                         reads=["n_onesr", ("n_cvec", kvi)], writes=[pk])
                    S.op("scalar", lambda e: e.activation(out=hid[0:127, :], in_=ps[0:127, 0:256], func=AF.Identity), reads=[pk], writes=[hk])
                    gelu_ap(C, hid[0:127, :], hsc[0:127, :], hk, sk)
                    ps, pk = C.nextps()
                    for mh in range(2):
                        S.op("tensor", lambda e: e.transpose(out=ps[:, mh * 128:mh * 128 + 127], in_=hid[0:127, mh * 128:(mh + 1) * 128], identity=identc[0:127, 0:127]),
                             reads=[hk, "n_identc"], writes=[pk])
                    S.op("scalar", lambda e: e.activation(out=ghb[:, :, 0:127], in_=ps[:, 0:256].rearrange("p (m c) -> p m c", c=128)[:, :, 0:127], func=AF.Identity),
                         reads=[pk], writes=[(gk, 0), (gk, 1)])
                    ps, pk = C.nextps()
                    if kvi == 0:
                        for mh in range(2):
                            S.op("tensor", lambda e: e.matmul(ps[:, 0:127], lhsT=W2k[:, mh, :], rhs=ghb[:, mh, 0:127], start=(mh == 0), stop=(mh == 1)),
                                 reads=["n_W2k", (gk, mh)], writes=[pk])
                        S.op("scalar", lambda e: e.activation(out=KCT[rows, g, 0:127], in_=ps[rows, 0:127], func=AF.Identity), reads=[pk], writes=[("n_KCT", g)])
                    else:
                        for mh in range(2):
                            S.op("tensor", lambda e: e.matmul(ps[0:127, 0:64], lhsT=ghb[:, mh, 0:127], rhs=W2v[:, mh, :], start=(mh == 0), stop=(mh == 1)),
                                 reads=["n_W2v", (gk, mh)], writes=[pk])
                        S.op("scalar", lambda e: e.activation(out=VCa[0:127, g, 0:64], in_=ps[0:127, 0:64], func=AF.Identity), reads=[pk], writes=[("n_VCa", g)])
            S.barrier()

        pA = ph.enter_context(contextlib.ExitStack())
        QT = sb("n_QT", [128, 8, NT], BF16, pA)
        Vaug = sb("n_Vaug", [128, 2, 16, 4, 65], BF16, pA)
        gts = sb("n_gts", [128, 16, 48], F32, pA)
        with contextlib.ExitStack() as p2:
            alloc_wch(p2)
            K12 = sb("n_K12", [128, 2, 2, NT], BF16, p2)
            wtok = sb("n_wtok", [128, KC, 560], BF16, p2)
            S.dma("gpsimd", "n_wtok", lambda e: e.dma_start(out=wtok[:], in_=C.nsa_wtok), writes=["n_wtok"])
            for mq in range(8 if getattr(C, 'nsa_stop', 9) >= 2 else 0):
                wt, wk = load_wch(4 + mq)
                for tt in range(NTT):
                    inproj(C, (wt, wk), 0, tt, lambda ps, pk: S.op(
                        "scalar", lambda e: e.activation(out=QT[:, mq, tsl(tt)], in_=ps[:], func=AF.Identity, scale=0.125), reads=[pk], writes=[("n_QT", mq, tt)]))
            for br in range(2):
                for c2 in range(2):
                    wt, wk = load_wch(12 + br * 2 + c2)
                    for tt in range(NTT):
                        inproj(C, (wt, wk), 0, tt, lambda ps, pk: S.op(
                            "scalar", lambda e: e.activation(out=K12[:, br, c2, tsl(tt)], in_=ps[:], func=AF.Identity), reads=[pk], writes=[("n_K12", br, c2, tt)]))
            S.op("vector", lambda e: e.memset(Vaug[:].rearrange("p a b c d -> p (a b c) d")[:, :, 64:65], 1.0), writes=["n_Vone"])
            for t16 in range(16 if getattr(C, 'nsa_stop', 9) >= 2 else 0):
                tok = slice(t16 * 128, (t16 + 1) * 128)
                ps, pk = C.nextps()
                for kc in range(KC):
                    S.op("tensor", lambda e: e.matmul(ps[:, 0:512], lhsT=C.xn[:, kc, tok], rhs=wtok[:, kc, 0:512], start=(kc == 0), stop=(kc == KC - 1)),
                         reads=["n_wtok", ("xn", kc, t16 // 4)], writes=[pk])
                for br in range(2):
                    S.op("scalar", lambda e: e.activation(out=Vaug[:, br, t16, :, 0:64], in_=ps[:, br * 256:(br + 1) * 256].rearrange("p (g d) -> p g d", d=64),
                                                          func=AF.Identity), reads=[pk], writes=[("n_Vaug", br, t16)])
                ps, pk = C.nextps()
                for kc in range(KC):
                    S.op("tensor", lambda e: e.matmul(ps[:, 0:48], lhsT=C.xn[:, kc, tok], rhs=wtok[:, kc, 512:560], start=(kc == 0), stop=(kc == KC - 1)),
                         reads=["n_wtok", ("xn", kc, t16 // 4)], writes=[pk])
                S.op("scalar", lambda e: e.activation(out=gts[:, t16, :], in_=ps[:, 0:48], func=AF.Sigmoid), reads=[pk], writes=[("n_gts", t16)])
            S.barrier()
            Kz = C.xn[:].rearrange("p (b g) t -> p b g t", b=2)
            for br in range(2):
                for g in range(4):
                    own = slice((g % 2) * 64, (g % 2) * 64 + 64)
                    oth = slice((1 - g % 2) * 64, (1 - g % 2) * 64 + 64)
                    S.op("gpsimd", lambda e: e.memset(Kz[oth, br, g, :], 0.0), writes=[("n_KzO", br, g)])
                    if g % 2 == 0:
                        S.op("scalar", lambda e: e.activation(out=Kz[own, br, g, :], in_=K12[own, br, g // 2, :], func=AF.Identity), writes=[("n_Kz", br, g)])
                    else:
                        S.op("vector", lambda e: e.tensor_copy(out=Kz[own, br, g, :], in_=K12[own, br, g // 2, :]), writes=[("n_Kz", br, g)])
            S.barrier()

        with contextlib.ExitStack() as p3:
            dtab = sb("n_dtab", [128, 9, TT], mybir.dt.int16, p3)
            etab = sb("n_etab", [128, 16, 128], BF16, p3)
            biasc = sb("n_biasc", [128, 16, 16], F32, p3)
            keep = sb("n_keep", [128, 16, 32], BF16, p3)
            addc = sb("n_addc", [128, 16, 32], BF16, p3)
            ident = sb("n_ident", [128, 128], F32, p3)
            sm = [sb("n_sm%d" % i, [128, TT], F32, p3) for i in range(NSM)]
            pT = [sb("n_pT%d" % i, [128, TT], BF16, p3) for i in range(NPT)]
            otoks = [sb("n_otok%d" % i, [128, 4, 256], F32, p3) for i in range(2)]
            valid = sb("n_valid", [128, 16, 1], F32, p3)
            otmp = sb("n_otmp", [128, 4, 64], F32, p3)
            rden = sb("n_rden", [128, 4, 1], F32, p3)
            ff = sb("n_ff", [128, 4, 1], F32, p3)
            comb = sb("n_comb", [128, 4, 65], F32, p3)
            fq = sb("n_fq", [128, 4, 16], F32, p3)
            S.dma("sync", "n_fq", lambda e: e.dma_start(out=fq[:], in_=C.c_fq), writes=["n_fq"])
            imp = sb("n_imp", [128, 4, 32], F32, p3)
            itmp = sb("n_itmp", [128, 4, 32], F32, p3)
            top8 = sb("n_top8", [128, 4, 8], F32, p3)
            selb = sb("n_selb", [128, 4, 32], F32, p3)
            selbT = sb("n_selbT", [128, 4, TT], BF16, p3)
            oTq = sb("n_oTq", [128, KC, TT], BF16, p3)
            alloc_wch(p3)
            S.dma("sync", "n_dtab", lambda e: e.dma_start(out=dtab[:, 0:8, :], in_=C.c_dtab[:, 0:8, :]), writes=["n_dtab"])
            S.op("gpsimd", lambda e: e.memset(etab[:], 0.0), writes=["n_etab"])
            S.op("gpsimd", lambda e: e.memset(selbT[:], 0.0), writes=[("n_selbT", g) for g in range(4)])
            S.dma("gpsimd", "n_etab", lambda e: e.dma_start(out=etab[0:32], in_=C.c_etab), writes=["n_etab"])
            S.dma("sync", "n_biasc", lambda e: e.dma_start(out=biasc[:], in_=C.c_biasc), writes=["n_biasc"])
            S.dma("gpsimd", "n_keep", lambda e: e.dma_start(out=keep[:], in_=C.c_keep), writes=["n_keep"])
            S.dma("gpsimd", "n_addc", lambda e: e.dma_start(out=addc[:], in_=C.c_addc), writes=["n_addc"])
            S.dma("sync", "n_ident", lambda e: e.dma_start(out=ident[:], in_=C.c_ident), writes=["n_ident"])
            S.dma("sync", "n_valid", lambda e: e.dma_start(out=valid[:], in_=C.c_valid), writes=["n_valid"])
            smi = [0]
            pti = [0]

            def score_tile(mm_fn, mm_reads, dti, scal, bias_ap):
                dkey = "n_dtabc" if dti == 8 else "n_dtab"
                i = smi[0] % NSM
                smi[0] += 1
                ps, pk = C.nextps()
                mm_fn(ps, pk)
                if dti is None:
                    smi[0] -= 1
                    jj = pti[0] % NPT
                    pti[0] += 1
                    S.op("scalar", lambda e: e.activation(out=pT[jj][:], in_=ps[:], func=AF.Exp, bias=bias_ap), reads=[pk, "n_biasc"], writes=[("n_pT", jj)])
                    return pT[jj], ("n_pT", jj)
                j = pti[0] % NPT
                pti[0] += 1
                S.op("vector", lambda e: e.scalar_tensor_tensor(out=sm[i][:], in0=dtab[:, dti, :], scalar=scal, in1=ps[:], op0=ALU.mult, op1=ALU.add),
                     reads=[pk, dkey], writes=[("n_sm", i)])
                if bias_ap is None:
                    S.op("scalar", lambda e: e.activation(out=pT[j][:], in_=sm[i][:], func=AF.Exp), reads=[("n_sm", i)], writes=[("n_pT", j)])
                else:
                    S.op("scalar", lambda e: e.activation(out=pT[j][:], in_=sm[i][:], func=AF.Exp, bias=bias_ap), reads=[("n_sm", i), "n_biasc"], writes=[("n_pT", j)])
                return pT[j], ("n_pT", j)

            def run_jobs(jobs, LA=NPT - 1):
                staged = []
                for idx in range(len(jobs) + LA):
                    if idx < len(jobs):
                        jb = jobs[idx]
                        staged.append(score_tile(jb["mm"], None, jb["dti"], jb["scal"], jb["bias"]))
                    k = idx - LA
                    if k >= 0:
                        jobs[k]["pv"](*staged[k])

            def accum_out(po, pok, ncol, otok, okey, r, gate_col, qt, first, src3=None):
                po3 = po[:, 0:4 * ncol].rearrange("p (s c) -> p s c", c=ncol) if src3 is None else src3
                S.op("vector", lambda e: e.tensor_scalar(out=rden[:], in0=po3[:, :, 64:65], scalar1=1e-30, scalar2=None, op0=ALU.max), reads=[pok], writes=["n_rden"])
                S.op("vector", lambda e: e.reciprocal(out=rden[:], in_=rden[:]), reads=["n_rden"], writes=["n_rden"])
                if first and qt == 0:
                    S.op("vector", lambda e: e.tensor_tensor(out=rden[:], in0=rden[:], in1=valid[:, 0:4, :], op=ALU.mult), reads=["n_rden", "n_valid"], writes=["n_rden"])
                S.op("vector", lambda e: e.tensor_tensor(out=ff[:], in0=rden[:], in1=gts[:, qt * 4:(qt + 1) * 4, gate_col:gate_col + 1], op=ALU.mult),
                     reads=["n_rden"] + [("n_gts", qt * 4 + i) for i in range(4)], writes=["n_ff"])
                dst = otok[:, :, r * 64:(r + 1) * 64]
                if first:
                    S.op("vector", lambda e: e.tensor_tensor(out=dst, in0=po3[:, :, 0:64], in1=ff[:].to_broadcast([128, 4, 64]), op=ALU.mult),
                         reads=[pok, "n_ff"], writes=[(okey, r)])
                else:
                    S.op("vector", lambda e: e.tensor_tensor(out=otmp[:], in0=po3[:, :, 0:64], in1=ff[:].to_broadcast([128, 4, 64]), op=ALU.mult),
                         reads=[pok, "n_ff"], writes=["n_otmp"])
                    S.op("vector", lambda e: e.tensor_tensor(out=dst, in0=dst, in1=otmp[:], op=ALU.add), reads=["n_otmp", (okey, r)], writes=[(okey, r)])

            def group_ctx(qt, g):
                k = qt * 4 + g
                return tsl(qt), slice((g % 2) * 64, (g % 2) * 64 + 64), g // 2, otoks[k % 2], "n_otok%d" % (k % 2)

            def cmp_select(qt, g):
                qs, rows, c2, otok, okey = group_ctx(qt, g)
                if g == 0:
                    S.dma("sync", "n_dtabc", lambda e: e.dma_start(out=dtab[:, 8, :], in_=C.c_dtab[:, 9 + qt, :]), writes=["n_dtabc"])
                jobs = []
                for r in range(4):
                    hh = g * 4 + r
                    mq = (g // 2) * 4 + r

                    def mm(ps, pk, mq=mq):
                        S.op("tensor", lambda e: e.matmul(ps[:], lhsT=KCT[:, g, :], rhs=QT[:, mq, qs], start=True, stop=True),
                             reads=[("n_KCT", g), ("n_QT", mq, qt)], writes=[pk])

                    def pv(p_t, p_k, r=r, hh=hh):
                        po, pok = nextpo()
                        for sub in range(4):
                            S.op("tensor", lambda e: e.matmul(po[:, sub * 97:(sub + 1) * 97], lhsT=p_t[:, sub * 128:(sub + 1) * 128], rhs=VCa[:, g, :], start=True, stop=True),
                                 reads=[p_k, ("n_VCa", g)], writes=[pok])
                        accum_out(po, pok, 97, otok, okey, r, hh, qt, True)
                        po3 = po[:, 0:388].rearrange("p (s c) -> p s c", c=97)
                        if r == 0:
                            S.op("vector", lambda e: e.tensor_tensor(out=imp[:], in0=po3[:, :, 65:97], in1=rden[:].to_broadcast([128, 4, 32]), op=ALU.mult),
                                 reads=[pok, "n_rden"], writes=["n_imp"])
                        else:
                            S.op("vector", lambda e: e.tensor_tensor(out=itmp[:], in0=po3[:, :, 65:97], in1=rden[:].to_broadcast([128, 4, 32]), op=ALU.mult),
                                 reads=[pok, "n_rden"], writes=["n_itmp"])
                            S.op("vector", lambda e: e.tensor_tensor(out=imp[:], in0=imp[:], in1=itmp[:], op=ALU.add), reads=["n_itmp", "n_imp"], writes=["n_imp"])
                    jobs.append(dict(mm=mm, dti=8, scal=-SLOPES[hh] / 2.0, bias=None, pv=pv))
                run_jobs(jobs)
                S.op("vector", lambda e: e.tensor_tensor(out=imp[:], in0=imp[:], in1=keep[:, qt * 4:(qt + 1) * 4, :], op=ALU.mult), reads=["n_imp", "n_keep"], writes=["n_imp"])
                S.op("vector", lambda e: e.tensor_tensor(out=imp[:], in0=imp[:], in1=addc[:, qt * 4:(qt + 1) * 4, :], op=ALU.add), reads=["n_imp", "n_addc"], writes=["n_imp"])
                for sub in range(4):
                    S.op("vector", lambda e: e.max(out=top8[:, sub, :], in_=imp[:, sub, :]), reads=["n_imp"], writes=["n_top8"])
                for sub in range(4):
                    S.op("vector", lambda e: e.tensor_scalar(out=selb[:, sub, :], in0=imp[:, sub, :], scalar1=top8[:, sub, 7:8], scalar2=-BIGNEG, op0=ALU.is_lt, op1=ALU.mult),
                         reads=["n_imp", "n_top8"], writes=["n_selb"])
                ps, pk = C.nextps()
                for sub in range(4):
                    S.op("tensor", lambda e: e.transpose(out=ps[0:32, sub * 128:(sub + 1) * 128], in_=selb[:, sub, :], identity=ident[:]),
                         reads=["n_selb", "n_ident"], writes=[pk])
                S.op("scalar", lambda e: e.activation(out=selbT[0:32, g, :], in_=ps[0:32, :], func=AF.Identity), reads=[pk], writes=[("n_selbT", g)])

            def selwin(qt, g):
                qs, rows, c2, otok, okey = group_ctx(qt, g)
                jobs = []
                for r in range(4):
                    hh = g * 4 + r
                    mq = (g // 2) * 4 + r
                    for br in range(2):
                        kts = list(range(0, qt * 4 + 4)) if br == 0 else list(range(max(0, qt * 4 - 4), qt * 4 + 4))
                        state = {}
                        for n_k, kt in enumerate(kts):
                            delta = qt * TT - kt * 128
                            bias_ap = None
                            far = False
                            if delta <= 0:
                                dti = (-delta) // 128
                            elif br == 1:
                                dti = 3 + delta // 128
                            else:
                                dti = None
                                far = True
                                bias_ap = biasc[:, hh, delta // 128:delta // 128 + 1]
                            nfar = qt * 4 if br == 0 else 0

                            def mm(ps, pk, br=br, kt=kt, mq=mq):
                                ks = slice(kt * 128, (kt + 1) * 128)
                                S.op("tensor", lambda e: e.matmul(ps[:], lhsT=Kz[:, br, g, ks], rhs=QT[:, mq, qs], start=True, stop=(br == 1)),
                                     reads=[("n_Kz", br, g), ("n_KzO", br, g), ("n_QT", mq, qt)], writes=[pk])
                                if br == 0:
                                    S.op("tensor", lambda e: e.matmul(ps[:], lhsT=etab[:, kt, :], rhs=selbT[:, g, :], start=False, stop=True),
                                         reads=["n_etab", ("n_selbT", g)], writes=[pk])

                            def pv(p_t, p_k, br=br, kt=kt, n_k=n_k, nk=len(kts), state=state, r=r, hh=hh, far=far, nfar=nfar):
                                if n_k == 0 and nfar:
                                    state["far"] = nextpo()
                                if n_k == nfar:
                                    state["po"] = nextpo()
                                po, pok = state["far"] if far else state["po"]
                                first = (n_k == 0) if far else (n_k == nfar)
                                last = (n_k == nfar - 1) if far else (n_k == nk - 1)
                                for sub in range(4):
                                    S.op("tensor", lambda e: e.matmul(po[:, sub * 65:(sub + 1) * 65], lhsT=p_t[:, sub * 128:(sub + 1) * 128], rhs=Vaug[:, br, kt, g, :],
                                                                      start=(first and sub == 0), stop=(last and sub == 3)),
                                         reads=[p_k, ("n_Vaug", br, kt), "n_Vone"], writes=[pok])
                                if n_k == nk - 1:
                                    if nfar:
                                        pf, pfk = state["far"]
                                        pf3 = pf[:, 0:260].rearrange("p (s c) -> p s c", c=65)
                                        po3 = po[:, 0:260].rearrange("p (s c) -> p s c", c=65)
                                        S.op("vector", lambda e: e.tensor_tensor(out=comb[:], in0=pf3, in1=fq[:, :, hh:hh + 1].to_broadcast([128, 4, 65]), op=ALU.mult),
                                             reads=[pfk, "n_fq"], writes=["n_comb"])
                                        S.op("vector", lambda e: e.tensor_tensor(out=comb[:], in0=comb[:], in1=po3, op=ALU.add), reads=[pok, "n_comb"], writes=["n_comb"])
                                        accum_out(po, "n_comb", 65, otok, okey, r, (1 + br) * 16 + hh, qt, False, src3=comb[:])
                                    else:
                                        accum_out(po, pok, 65, otok, okey, r, (1 + br) * 16 + hh, qt, False)
                            jobs.append(dict(mm=mm, dti=dti, scal=-SLOPES[hh], bias=bias_ap, pv=pv))
                run_jobs(jobs)
                for kk in range(2):
                    kc = 2 * g + kk
                    ps, pk = C.nextps()
                    for sub in range(4):
                        S.op("tensor", lambda e: e.transpose(out=ps[:, sub * 128:(sub + 1) * 128], in_=otok[:, sub, kk * 128:(kk + 1) * 128], identity=ident[:]),
                             reads=[(okey, 2 * kk), (okey, 2 * kk + 1), "n_ident"], writes=[pk])
                    S.op("scalar", lambda e: e.activation(out=oTq[:, kc, :], in_=ps[:], func=AF.Identity), reads=[pk], writes=[("n_oTq", kc)])
                if g == 3:
                    for m in range(KC):
                        wt, wk = load_wch(16 + m)
                        ps, pk = C.nextps()
                        for kc in range(KC):
                            S.op("tensor", lambda e: e.matmul(ps[:], lhsT=wt[:, kc, :], rhs=oTq[:, kc, :], start=(kc == 0), stop=(kc == KC - 1)),
                                 reads=[wk, ("n_oTq", kc)], writes=[pk])
                        S.op("vector", lambda e: e.tensor_tensor(out=C.hT[:, m, qs], in0=C.hT[:, m, qs], in1=ps[:], op=ALU.add),
                             reads=[pk, ("hT", m, qt)], writes=[("hT", m, qt)])

            steps = [(qt, g) for qt in range(NTT if getattr(C, 'nsa_stop', 9) >= 3 else 0) for g in range(4)]
            if steps:
                cmp_select(*steps[0])
            for k, (qt, g) in enumerate(steps):
                if k + 1 < len(steps):
                    cmp_select(*steps[k + 1])
                selwin(qt, g)
            S.barrier()
        pA.close()
        S.barrier()
    C.nrot = 8


def prep_weights(inp):
    f = np.ascontiguousarray
    w = {}
    g = np.stack([inp["lru_norm_g"][0], inp["ffn_norm_g"][0], inp["nsa_norm_g"][0], inp["ffn_norm_g"][1], inp["final_norm_g"]], 0)
    w["gains"] = f(g.reshape(5, KC, 128).transpose(2, 0, 1))
    wi = inp["lru_w_in"][0].reshape(KC, 128, 2, LN, 128)
    w["lru_win"] = f(wi.transpose(3, 1, 0, 2, 4).reshape(LN, 128, KC, 256))
    w["lru_gw"] = f(inp["lru_gate_w"][0].transpose(1, 2, 0, 3))
    w["lru_wout"] = f(inp["lru_w_out"][0].reshape(LN, 128, D))
    vec = np.concatenate([inp["lru_conv_w"][0], inp["lru_conv_b"][0][None], inp["lru_gate_b"][0], inp["lru_a_param"][0][None]], 0)
    w["lru_vec"] = f(vec.reshape(8, LN, 128).transpose(2, 1, 0))
    fw = inp["ffn_w_in"].reshape(2, KC, 128, 2, FJ, 128)
    w["ffn_win"] = f(fw.transpose(0, 4, 2, 1, 3, 5).reshape(2, FJ, 128, KC, 256))
    w["ffn_wout"] = f(inp["ffn_w_out"].reshape(2, FJ, 128, D))
    fv = np.concatenate([inp["ffn_conv_w"], inp["ffn_conv_b"][:, None]], 1)
    w["ffn_vec"] = f(fv.reshape(2, 4, FJ, 128).transpose(3, 0, 2, 1))
    w.update(nsa_host(inp))
    w.update(const_tables())
    return w


def nsa_host(inp):
    f = np.ascontiguousarray
    w = {}
    W = inp["nsa_w_in"][0]
    Wr = W.reshape(KC, 128, 2608)

    def chunk(cols):
        return Wr[:, :, cols].transpose(1, 0, 2)
    chunks = []
    for c4 in range(4):
        kvi, gp = c4 // 2, c4 % 2
        base = 1024 + kvi * 256 + gp * 128
        chunks.append(chunk(np.arange(base, base + 128)))
    for mq in range(8):
        p, r = mq // 4, mq % 4
        ha, hb = 4 * (2 * p) + r, 4 * (2 * p + 1) + r
        chunks.append(chunk(np.concatenate([np.arange(ha * 64, ha * 64 + 64), np.arange(hb * 64, hb * 64 + 64)])))
    for br in (1, 2):
        for c2 in range(2):
            base = 1024 + br * 512 + c2 * 128
            chunks.append(chunk(np.arange(base, base + 128)))
    Wo = inp["nsa_w_out"][0].reshape(KC, 128, D)
    for m in range(KC):
        chunks.append(Wo[:, :, m * 128:(m + 1) * 128].transpose(1, 0, 2))
    w["nsa_wch"] = f(np.stack(chunks, 0))
    tokcols = np.concatenate([np.arange(1024 + 512 + 256, 1024 + 512 + 512), np.arange(1024 + 1024 + 256, 1024 + 1024 + 512), np.arange(2560, 2608)])
    w["nsa_wtok"] = f(Wr[:, :, tokcols].transpose(1, 0, 2))
    w1 = inp["nsa_cmp_w1"][0].reshape(2, 32, 64, 256).transpose(0, 2, 1, 3)
    w["nsa_w1"] = f(np.concatenate([w1, w1], 1))
    pT = inp["nsa_cmp_pos"][0].transpose(2, 0, 1)
    w["nsa_posT"] = f(np.concatenate([pT, pT], 0))
    w["nsa_b1"] = f(inp["nsa_cmp_b1"][0].reshape(2, 2, 128).transpose(2, 0, 1))
    w["nsa_b1row"] = f(inp["nsa_cmp_b1"][0][None])
    w2 = inp["nsa_cmp_w2"][0]
    w2k = w2[0].reshape(2, 128, 64).transpose(1, 0, 2)
    w["nsa_w2k"] = f(np.concatenate([w2k, w2k], 2))
    w["nsa_w2v"] = f(w2[1].reshape(2, 128, 64).transpose(1, 0, 2))
    return w


def const_tables():
    c = {}
    HUGE = 30000
    k = np.arange(128)[:, None]
    q = np.arange(TT)[None, :]
    dt = np.zeros((128, 13, TT), np.int64)
    for i in range(4):
        d = -128 * i + q - k
        dt[:, i] = np.where(d >= 0, d, HUGE)
    for i in range(1, 5):
        d = 128 * i + q - k
        dt[:, 3 + i] = np.where(d < 512, d, HUGE)
    dt[:, 8] = q - k
    cc = np.arange(128)[:, None]
    for qt in range(4):
        t = qt * TT + q
        d2 = 2 * t - 32 * cc - 31
        ok = (16 * cc + 31 <= t) & (cc < 127)
        dt[:, 9 + qt] = np.where(ok, d2, HUGE)
    c["c_dtab"] = dt.astype(np.int16)
    et = np.zeros((32, 16, 128), np.float32)
    for kt in range(16):
        for kk in range(128):
            et[(kt * 128 + kk) // 64, kt, kk] = 1.0
    c["c_etab"] = et
    sl = np.array(SLOPES, np.float64)
    kk = np.arange(128, dtype=np.float64)[:, None, None]
    bc = -(sl[None, :, None] * ((128.0 * np.arange(16))[None, None, :] - kk))
    c["c_biasc"] = np.ascontiguousarray(bc).astype(np.float32)
    qq = (np.arange(4)[None, :, None] * 128 + np.arange(128)[:, None, None]).astype(np.float64)
    c["c_fq"] = np.exp(-sl[None, None, :] * qq).astype(np.float32)
    t = (np.arange(16)[None, :, None] * 128 + np.arange(128)[:, None, None])
    j = np.arange(32)[None, None, :]
    cur = t // 64
    forced = (j == 0) | (j == cur) | (j == cur - 1)
    future = j > cur
    c["c_keep"] = np.where(forced | future, 0.0, 1.0).astype(np.float32)
    c["c_addc"] = np.where(forced, 1e4, np.where(future, -1.0, 0.0)).astype(np.float32)
    c["c_ident"] = np.eye(128, dtype=np.float32)
    c["c_valid"] = (t >= 31).astype(np.float32)
    ov = np.zeros((128, 33), np.float32)
    ov[:127, 0] = 1.0
    cs = np.arange(127)[:, None] * 16
    sj = np.arange(32)[None, :]
    ov[:127, 1:] = ((cs < (sj + 1) * 64) & (cs + 32 > sj * 64)).astype(np.float32)
    c["c_ovl"] = ov
    return c


_CACHE = {}


def kernel(**inp):
    inp = {k: np.asarray(v) for k, v in inp.items()}
    ncores, nseq = 8, 2
    if "nc" not in _CACHE:
        _CACHE["nc"] = build(nseq)[0]
    nc = _CACHE["nc"]
    w = prep_weights(inp)
    x = inp["x"]
    xT = np.ascontiguousarray(x.reshape(ncores, nseq, NT, KC, 128).transpose(0, 1, 4, 3, 2))
    in_maps = [dict(w, xT=xT[c]) for c in range(ncores)]
    res = run_bass_kernel_spmd(nc, in_maps, core_ids=list(range(ncores)))
    o = np.stack([r["outT"] for r in res.results], 0)
    return np.ascontiguousarray(o.transpose(0, 1, 4, 3, 2)).reshape(16, NT, D).astype(np.float32)
```

```python
import contextlib
import numpy as np
import concourse.bass as bass
import concourse.mybir as mybir
from concourse.bass_utils import run_bass_kernel_spmd

F32 = mybir.dt.float32
BF16 = mybir.dt.bfloat16
AF = mybir.ActivationFunctionType
ALU = mybir.AluOpType
AX = mybir.AxisListType

D = 1024
KC = 8
NT = 2048
TT = 512
NTT = 4
LW = 1280
LN = 10
DFF = 3072
FJ = 24
EPS = 1e-6
GELU_K = 1.5957691216057308
GELU_C = 0.044715 ** 0.5


class Sched:
    ENGS = ("tensor", "vector", "scalar", "gpsimd", "sync")

    def __init__(self, nc, stack):
        self.nc = nc
        self.stack = stack
        self.eng = {e: getattr(nc, e) for e in self.ENGS}
        self.sem = {}
        self.cnt = {}
        self.known = {e: {} for e in self.ENGS}
        self.res = {}
        self.n_inst = 0
        self.n_wait = 0
        for e in ("tensor", "vector", "scalar", "gpsimd"):
            self._mksem(e)

    def _mksem(self, key):
        if key not in self.sem:
            name = "s_" + key.replace(":", "_")
            self.sem[key] = self.stack.enter_context(self.nc.semaphore(name))
            self.cnt[key] = 0
        return self.sem[key]

    def _deps(self, engine, reads, writes):
        deps = {}

        def add(k, v):
            if v > deps.get(k, 0):
                deps[k] = v
        for r in reads:
            st = self.res.get(r)
            if st and st["w"]:
                add(*st["w"])
        for w in writes:
            st = self.res.get(w)
            if st:
                if st["w"]:
                    add(*st["w"])
                for k, v in st["r"].items():
                    add(k, v)
        kn = self.known[engine]
        for k, v in deps.items():
            if k == engine and engine == "tensor":
                continue
            if kn.get(k, 0) >= v:
                continue
            self.eng[engine].wait_ge(self.sem[k], v)
            self.n_wait += 1
            kn[k] = v

    def _mark(self, key, val, reads, writes):
        for r in reads:
            st = self.res.setdefault(r, {"w": None, "r": {}})
            st["r"][key] = val
        for w in writes:
            self.res[w] = {"w": (key, val), "r": {}}

    def op(self, engine, fn, reads=(), writes=()):
        self._deps(engine, reads, writes)
        ins = fn(self.eng[engine])
        self.cnt[engine] += 1
        ins.then_inc(self.sem[engine], 1)
        self.n_inst += 1
        self._mark(engine, self.cnt[engine], reads, writes)
        return ins

    def dma(self, queue, key, fn, reads=(), writes=()):
        k = "dma:" + key
        self._mksem(k)
        self._deps(queue, reads, writes)
        ins = fn(self.eng[queue])
        self.cnt[k] += 16
        ins.then_inc(self.sem[k], 16)
        self.n_inst += 1
        self._mark(k, self.cnt[k], reads, writes)
        return ins

    def barrier(self):
        for e in self.ENGS:
            for k, v in self.cnt.items():
                if v == 0 or (k == e and e == "tensor"):
                    continue
                if self.known[e].get(k, 0) >= v:
                    continue
                self.eng[e].wait_ge(self.sem[k], v)
                self.known[e][k] = v

    def finish(self):
        for k, v in self.cnt.items():
            if v and self.known["sync"].get(k, 0) < v:
                self.nc.sync.wait_ge(self.sem[k], v)
                self.known["sync"][k] = v


class Ctx:
    pass


def tsl(tt):
    return slice(tt * TT, (tt + 1) * TT)


def build(nseq=2, upto=99, nsa_stop=9):
    nc = bass.Bass("TRN2", target_bir_lowering=False)
    C = Ctx()
    C.nc = nc
    C.nsa_stop = nsa_stop

    def din(name, shape):
        return nc.dram_tensor(name, list(shape), F32, kind="ExternalInput").ap()
    C.xT = din("xT", [nseq, 128, KC, NT])
    C.gains = din("gains", [128, 5, KC])
    C.lru_win = din("lru_win", [LN, 128, KC, 256])
    C.lru_gw = din("lru_gw", [LN, 128, 2, 128])
    C.lru_wout = din("lru_wout", [LN, 128, D])
    C.lru_vec = din("lru_vec", [128, LN, 8])
    C.ffn_win = din("ffn_win", [2, FJ, 128, KC, 256])
    C.ffn_wout = din("ffn_wout", [2, FJ, 128, D])
    C.ffn_vec = din("ffn_vec", [128, 2, FJ, 4])
    C.nsa_wch = din("nsa_wch", [24, 128, KC, 128])
    C.nsa_wtok = din("nsa_wtok", [128, KC, 560])
    C.nsa_w1 = din("nsa_w1", [2, 128, 32, 256])
    C.nsa_posT = din("nsa_posT", [128, 2, 32])
    C.nsa_b1 = din("nsa_b1", [128, 2, 2])
    C.nsa_b1row = din("nsa_b1row", [1, 2, 256])
    C.nsa_w2k = din("nsa_w2k", [128, 2, 128])
    C.nsa_w2v = din("nsa_w2v", [128, 2, 64])
    C.c_dtab = nc.dram_tensor("c_dtab", [128, 13, TT], mybir.dt.int16, kind="ExternalInput").ap()
    C.c_etab = din("c_etab", [32, 16, 128])
    C.c_biasc = din("c_biasc", [128, 16, 16])
    C.c_keep = din("c_keep", [128, 16, 32])
    C.c_addc = din("c_addc", [128, 16, 32])
    C.c_ident = din("c_ident", [128, 128])
    C.c_valid = din("c_valid", [128, 16, 1])
    C.c_fq = din("c_fq", [128, 4, 16])
    C.c_ovl = din("c_ovl", [128, 33])
    C.outT = nc.dram_tensor("outT", [nseq, 128, KC, NT], F32, kind="ExternalOutput").ap()

    with contextlib.ExitStack() as st:
        S = Sched(nc, st)
        C.S = S

        uid = [0]

        def sb(name, shape, dt=F32, stack=st):
            uid[0] += 1
            return stack.enter_context(nc.sbuf_tensor("%s_u%d" % (name, uid[0]), list(shape), dt))
        C.sb = sb
        C.hT = sb("hT", [128, KC, NT])
        C.xn = sb("xn", [128, KC, NT], BF16)
        C.ones = sb("ones", [128, 128], BF16)
        C.gn = sb("gn", [128, 5, KC])
        C.lvec = sb("lvec", [128, LN, 8])
        C.lca = sb("lca", [128, LN, 2])
        C.fvec = sb("fvec", [128, 2, FJ, 4])
        C.psum = [st.enter_context(nc.psum_tensor("ps%d" % i, [128, TT], F32)) for i in range(8)]
        C.psi = 0
        C.nrot = 8

        def nextps():
            i = C.psi % C.nrot
            C.psi += 1
            return C.psum[i], ("ps", i)
        C.nextps = nextps

        S.op("vector", lambda e: e.memset(C.ones[:], 1.0), writes=["ones"])
        C.epsc = sb("epsc", [128, 1])
        S.op("vector", lambda e: e.memset(C.epsc[:], EPS), writes=["epsc"])
        S.dma("sync", "c0", lambda e: e.dma_start(out=C.gn[:], in_=C.gains), writes=["gn"])
        S.dma("sync", "c1", lambda e: e.dma_start(out=C.lvec[:], in_=C.lru_vec), writes=["lvec"])
        S.dma("sync", "c2", lambda e: e.dma_start(out=C.fvec[:], in_=C.ffn_vec), writes=["fvec"])
        lru_consts(C)

        for s in range(nseq):
            C.cur_seq = s
            for tt in range(NTT):
                S.dma("sync", "x%d" % tt, lambda e, tt=tt: e.dma_start(out=C.hT[:, :, tsl(tt)], in_=C.xT[s, :, :, tsl(tt)]),
                      writes=[("hT", kc, tt) for kc in range(KC)])
            if upto >= 1:
                rmsnorm(C, 0)
                lru_mixer(C)
            if upto >= 2:
                rmsnorm(C, 1)
                conv_ffn(C, 0)
            if upto >= 3:
                rmsnorm(C, 2)
                nsa_mixer(C)
            if upto >= 4:
                rmsnorm(C, 3)
                conv_ffn(C, 1)
            if upto >= 5:
                rmsnorm(C, 4, final=True)
            if upto < 5:
                for tt in range(NTT):
                    S.dma("sync", "o%d" % tt, lambda e, tt=tt: e.dma_start(out=C.outT[s, :, :, tsl(tt)], in_=C.hT[:, :, tsl(tt)]),
                          reads=[("hT", kc, tt) for kc in range(KC)])
        S.finish()
    C.n_inst = S.n_inst
    C.n_wait = S.n_wait
    return nc, C


def lru_consts(C):
    S, sb = C.S, C.sb
    with contextlib.ExitStack() as ph:
        t = [sb("lc%d" % i, [128, LN], F32, ph) for i in range(6)]
        ap = C.lvec[:, :, 7]
        S.op("scalar", lambda e: e.activation(out=t[0][:], in_=ap, func=AF.Abs), reads=["lvec"], writes=["lc0"])
        S.op("scalar", lambda e: e.activation(out=t[1][:], in_=t[0][:], func=AF.Exp, scale=-1.0), reads=["lc0"], writes=["lc1"])
        S.op("scalar", lambda e: e.activation(out=t[2][:], in_=t[1][:], func=AF.Ln, bias=1.0), reads=["lc1"], writes=["lc2"])
        S.op("vector", lambda e: e.tensor_scalar(out=t[3][:], in0=t[1][:], scalar1=1.0 / 3.0, scalar2=-0.5, op0=ALU.mult, op1=ALU.add), reads=["lc1"], writes=["lc3"])
        S.op("vector", lambda e: e.tensor_tensor(out=t[3][:], in0=t[3][:], in1=t[1][:], op=ALU.mult), reads=["lc3", "lc1"], writes=["lc3"])
        S.op("vector", lambda e: e.tensor_scalar(out=t[3][:], in0=t[3][:], scalar1=1.0, scalar2=None, op0=ALU.add), reads=["lc3"], writes=["lc3"])
        S.op("vector", lambda e: e.tensor_tensor(out=t[3][:], in0=t[3][:], in1=t[1][:], op=ALU.mult), reads=["lc3", "lc1"], writes=["lc3"])
        S.op("vector", lambda e: e.tensor_single_scalar(out=t[4][:], in_=t[1][:], scalar=0.03, op=ALU.is_lt), reads=["lc1"], writes=["lc4"])
        S.op("vector", lambda e: e.tensor_tensor(out=t[3][:], in0=t[3][:], in1=t[2][:], op=ALU.subtract), reads=["lc3", "lc2"], writes=["lc3"])
        S.op("vector", lambda e: e.tensor_tensor(out=t[3][:], in0=t[3][:], in1=t[4][:], op=ALU.mult), reads=["lc3", "lc4"], writes=["lc3"])
        S.op("vector", lambda e: e.tensor_tensor(out=t[3][:], in0=t[3][:], in1=t[2][:], op=ALU.add), reads=["lc3", "lc2"], writes=["lc3"])
        S.op("vector", lambda e: e.tensor_scalar(out=t[5][:], in0=ap, scalar1=-1.0, scalar2=0.0, op0=ALU.mult, op1=ALU.max), reads=["lvec"], writes=["lc5"])
        S.op("vector", lambda e: e.tensor_tensor(out=t[3][:], in0=t[3][:], in1=t[5][:], op=ALU.add), reads=["lc3", "lc5"], writes=["lc3"])
        S.op("vector", lambda e: e.tensor_scalar(out=C.lca[:, :, 0], in0=t[3][:], scalar1=-8.0, scalar2=None, op0=ALU.mult), reads=["lc3"], writes=["lca"])
        S.op("vector", lambda e: e.tensor_scalar(out=C.lca[:, :, 1], in0=t[3][:], scalar1=-16.0, scalar2=None, op0=ALU.mult), reads=["lc3"], writes=["lca"])
        S.barrier()


def rmsnorm(C, gi, final=False):
    S = C.S
    ph = contextlib.ExitStack()
    C.sq = [C.sb("sq%d" % i, [128, TT], BF16, ph) for i in range(4)]
    C.rs = C.sb("rs", [128, TT], F32, ph)
    for tt in range(NTT):
        ps, pk = C.nextps()
        for kc in range(KC):
            sq = C.sq[kc % 4]
            S.op("scalar", lambda e: e.activation(out=sq[:], in_=C.hT[:, kc, tsl(tt)], func=AF.Square),
                 reads=[("hT", kc, tt)], writes=[("sq", kc % 4)])
            S.op("tensor", lambda e: e.matmul(ps[:], lhsT=C.ones[:], rhs=sq[:], start=(kc == 0), stop=(kc == KC - 1)),
                 reads=["ones", ("sq", kc % 4)], writes=[pk])
        S.op("scalar", lambda e: e.activation(out=C.rs[:], in_=ps[:], func=AF.Ln, scale=1.0 / D, bias=C.epsc[:]), reads=[pk, "epsc"], writes=["rs"])
        S.op("scalar", lambda e: e.activation(out=C.rs[:], in_=C.rs[:], func=AF.Exp, scale=-0.5), reads=["rs"], writes=["rs"])
        for kc in range(KC):
            if final:
                S.op("vector", lambda e: e.scalar_tensor_tensor(out=C.hT[:, kc, tsl(tt)], in0=C.hT[:, kc, tsl(tt)], scalar=C.gn[:, gi, kc:kc + 1],
                                                               in1=C.rs[:], op0=ALU.mult, op1=ALU.mult),
                     reads=[("hT", kc, tt), "rs", "gn"], writes=[("hT", kc, tt)])
            else:
                S.op("vector", lambda e: e.scalar_tensor_tensor(out=C.xn[:, kc, tsl(tt)], in0=C.hT[:, kc, tsl(tt)], scalar=C.gn[:, gi, kc:kc + 1],
                                                               in1=C.rs[:], op0=ALU.mult, op1=ALU.mult),
                     reads=[("hT", kc, tt), "rs", "gn"], writes=[("xn", kc, tt)])
        if final:
            S.dma("sync", "o%d" % tt, lambda e: e.dma_start(out=C.outT[C.cur_seq, :, :, tsl(tt)], in_=C.hT[:, :, tsl(tt)]),
                  reads=[("hT", kc, tt) for kc in range(KC)])
    S.barrier()
    ph.close()


def inproj(C, w, col0, tt, evac):
    S = C.S
    ps, pk = C.nextps()
    wt, wk = w
    for kc in range(KC):
        S.op("tensor", lambda e: e.matmul(ps[:], lhsT=wt[:, kc, col0:col0 + 128], rhs=C.xn[:, kc, tsl(tt)], start=(kc == 0), stop=(kc == KC - 1)),
             reads=[wk, ("xn", kc, tt)], writes=[pk])
    evac(ps, pk)


def gelu_inplace(C, x, xk, t1, t1k, tt):
    S = C.S
    sl = tsl(tt)
    S.op("scalar", lambda e: e.activation(out=t1[:, sl], in_=x[:, sl], func=AF.Square), reads=[(xk, tt)], writes=[(t1k, tt)])
    S.op("vector", lambda e: e.tensor_scalar(out=t1[:, sl], in0=t1[:, sl], scalar1=0.044715, scalar2=1.0, op0=ALU.mult, op1=ALU.add),
         reads=[(t1k, tt)], writes=[(t1k, tt)])
    S.op("vector", lambda e: e.tensor_tensor(out=t1[:, sl], in0=t1[:, sl], in1=x[:, sl], op=ALU.mult), reads=[(t1k, tt), (xk, tt)], writes=[(t1k, tt)])
    S.op("scalar", lambda e: e.activation(out=t1[:, sl], in_=t1[:, sl], func=AF.Sigmoid, scale=GELU_K), reads=[(t1k, tt)], writes=[(t1k, tt)])
    S.op("vector", lambda e: e.tensor_tensor(out=x[:, sl], in0=x[:, sl], in1=t1[:, sl], op=ALU.mult), reads=[(t1k, tt), (xk, tt)], writes=[(xk, tt)])


def outproj_group(C, acts, wouts):
    S = C.S
    n = len(acts)
    for tt in range(NTT):
        for m in range(KC):
            ps, pk = C.nextps()
            for i in range(n):
                a_ap, a_k = acts[i]
                wt, wk = wouts[i]
                S.op("tensor", lambda e: e.matmul(ps[:], lhsT=wt[:, m * 128:(m + 1) * 128], rhs=a_ap(tt), start=(i == 0), stop=(i == n - 1)),
                     reads=[wk, a_k(tt)], writes=[pk])
            S.op("vector", lambda e: e.tensor_tensor(out=C.hT[:, m, tsl(tt)], in0=C.hT[:, m, tsl(tt)], in1=ps[:], op=ALU.add),
                 reads=[pk, ("hT", m, tt)], writes=[("hT", m, tt)])


def lru_mixer(C):
    S, sb, nc = C.S, C.sb, C.nc
    G = 2
    HT = NT // 2
    with contextlib.ExitStack() as ph:
        win = [sb("l_win%d" % i, [128, KC, 256], BF16, ph) for i in range(2)]
        gw = [sb("l_gw%d" % i, [128, 2, 128], BF16, ph) for i in range(2)]
        wo = [sb("l_wo%d" % i, [128, D], BF16, ph) for i in range(2 * G)]
        hy = [sb("l_hy%d" % i, [128, NT], BF16, ph) for i in range(2 * G)]
        ysbs = [sb("l_ysb%d" % i, [128, NT], F32, ph) for i in range(2)]
        xpads = [sb("l_xpad%d" % i, [128, 3 + NT], F32, ph) for i in range(2)]
        t1s = [sb("l_t1%d" % i, [128, HT], F32, ph) for i in range(2)]
        xcs = [sb("l_xc%d" % i, [128, HT], F32, ph) for i in range(2)]
        xcbs = [sb("l_xcb%d" % i, [128, HT], BF16, ph) for i in range(2)]
        rrs = [sb("l_r%d" % i, [128, HT], F32, ph) for i in range(2)]
        igs = [sb("l_ig%d" % i, [128, HT], F32, ph) for i in range(2)]
        hhs = [sb("l_h%d" % i, [128, HT], F32, ph) for i in range(2)]
        for i in range(2):
            S.op("vector", lambda e: e.memset(xpads[i][:, 0:3], 0.0), writes=["l_xpad%d_0" % i])

        def load_w(n):
            b = n % 2
            S.dma("gpsimd", "l_win%d" % b, lambda e: e.dma_start(out=win[b][:], in_=C.lru_win[n]), writes=["l_win%d" % b])
            S.dma("gpsimd", "l_gw%d" % b, lambda e: e.dma_start(out=gw[b][:], in_=C.lru_gw[n]), writes=["l_gw%d" % b])
            S.dma("gpsimd", "l_wo%d" % (n % (2 * G)), lambda e: e.dma_start(out=wo[n % (2 * G)][:], in_=C.lru_wout[n]), writes=["l_wo%d" % (n % (2 * G))])

        def proj(n, u):
            b = n % 2
            wi, wik = win[b], "l_win%d" % b
            ysb, xpad = ysbs[b], xpads[b]
            for tt in (2 * u, 2 * u + 1):
                inproj(C, (wi, wik), 128, tt, lambda ps, pk: S.op(
                    "scalar", lambda e: e.activation(out=xpad[:, 3 + tt * TT:3 + (tt + 1) * TT], in_=ps[:], func=AF.Identity),
                    reads=[pk], writes=[("l_xpad%d" % b, tt)]))
            for tt in (2 * u, 2 * u + 1):
                inproj(C, (wi, wik), 0, tt, lambda ps, pk: S.op(
                    "scalar", lambda e: e.activation(out=ysb[:, tsl(tt)], in_=ps[:], func=AF.Identity), reads=[pk], writes=[("l_ysb%d" % b, tt)]))

        units = [(n, u) for n in range(LN) for u in range(2)]
        def conv_unit(k):
            n, u = units[k]
            b = n % 2
            q = k % 2
            xpad, xk = xpads[b], "l_xpad%d" % b
            xc, xcb = xcs[q], xcbs[q]
            xck, xcbk = "l_xc%d" % q, "l_xcb%d" % q
            TU = (2 * u, 2 * u + 1)

            def lsl(tt):
                return slice((tt - 2 * u) * TT, (tt - 2 * u + 1) * TT)
            for tt in TU:
                rd = [(xk, tt), (xk, tt - 1) if tt > 0 else xk + "_0", "lvec"]
                S.op("vector", lambda e: e.tensor_scalar(out=xc[:, lsl(tt)], in0=xpad[:, 3 + tt * TT:3 + (tt + 1) * TT], scalar1=C.lvec[:, n, 3:4],
                                                        scalar2=C.lvec[:, n, 4:5], op0=ALU.mult, op1=ALU.add), reads=rd, writes=[(xck, tt)])
                for kk in (2, 1, 0):
                    S.op("vector", lambda e: e.scalar_tensor_tensor(out=xc[:, lsl(tt)], in0=xpad[:, kk + tt * TT:kk + (tt + 1) * TT], scalar=C.lvec[:, n, kk:kk + 1],
                                                                   in1=xc[:, lsl(tt)], op0=ALU.mult, op1=ALU.add),
                         reads=rd + [(xck, tt)], writes=[(xck, tt)])
                S.op("scalar", lambda e: e.activation(out=xcb[:, lsl(tt)], in_=xc[:, lsl(tt)], func=AF.Identity), reads=[(xck, tt)], writes=[(xcbk, tt)])

        pending = None
        load_w(0)
        proj(0, 0)
        conv_unit(0)
        for k, (n, u) in enumerate(units):
            b = n % 2
            q = k % 2
            gwi, gwk = gw[b], "l_gw%d" % b
            hyi, hyk = hy[n % (2 * G)], "l_hy%d" % (n % (2 * G))
            ysb, xpad = ysbs[b], xpads[b]
            yk, xk = "l_ysb%d" % b, "l_xpad%d" % b
            t1, xc, xcb, rr, ig, hh = t1s[q], xcs[q], xcbs[q], rrs[q], igs[q], hhs[q]
            t1k, xck, xcbk, rk, igk, hk = "l_t1%d" % q, "l_xc%d" % q, "l_xcb%d" % q, "l_r%d" % q, "l_ig%d" % q, "l_h%d" % q
            TU = (2 * u, 2 * u + 1)

            def lsl(tt):
                return slice((tt - 2 * u) * TT, (tt - 2 * u + 1) * TT)
            if k + 1 < len(units):
                n2, u2 = units[k + 1]
                if u2 == 0:
                    load_w(n2)
                proj(n2, u2)
            if pending is not None and u == 1:
                outproj_group(C, *pending)
                pending = None
            for tt in TU:
                S.op("scalar", lambda e: e.activation(out=t1[:, lsl(tt)], in_=ysb[:, tsl(tt)], func=AF.Square, scale=GELU_C), reads=[(yk, tt)], writes=[(t1k, tt)])
            for g, (dst, dk) in enumerate(((rr, rk), (ig, igk))):
                for tt in TU:
                    ps, pk = C.nextps()
                    S.op("tensor", lambda e: e.matmul(ps[:], lhsT=gwi[:, g, :], rhs=xcb[:, lsl(tt)], start=True, stop=True),
                         reads=[gwk, (xcbk, tt)], writes=[pk])
                    S.op("scalar", lambda e: e.activation(out=dst[:, lsl(tt)], in_=ps[:], func=AF.Sigmoid, bias=C.lvec[:, n, 5 + g:6 + g]),
                         reads=[pk, "lvec"], writes=[(dk, tt)])
            for tt in TU:
                S.op("vector", lambda e: e.scalar_tensor_tensor(out=t1[:, lsl(tt)], in0=t1[:, lsl(tt)], scalar=1.0, in1=ysb[:, tsl(tt)], op0=ALU.add, op1=ALU.mult),
                     reads=[(t1k, tt), (yk, tt)], writes=[(t1k, tt)])
            for tt in TU:
                S.op("scalar", lambda e: e.activation(out=t1[:, lsl(tt)], in_=t1[:, lsl(tt)], func=AF.Sigmoid, scale=GELU_K), reads=[(t1k, tt)], writes=[(t1k, tt)])
            for tt in TU:
                S.op("vector", lambda e: e.tensor_tensor(out=ysb[:, tsl(tt)], in0=ysb[:, tsl(tt)], in1=t1[:, lsl(tt)], op=ALU.mult), reads=[(t1k, tt), (yk, tt)], writes=[(yk, tt)])
            for tt in TU:
                S.op("vector", lambda e: e.tensor_tensor(out=ig[:, lsl(tt)], in0=ig[:, lsl(tt)], in1=xc[:, lsl(tt)], op=ALU.mult), reads=[(igk, tt), (xck, tt)], writes=[(igk, tt)])
            for tt in TU:
                S.op("scalar", lambda e: e.activation(out=t1[:, lsl(tt)], in_=rr[:, lsl(tt)], func=AF.Exp, scale=C.lca[:, n, 1:2]), reads=[(rk, tt), "lca"], writes=[(t1k, tt)])
            for tt in TU:
                S.op("scalar", lambda e: e.activation(out=rr[:, lsl(tt)], in_=rr[:, lsl(tt)], func=AF.Exp, scale=C.lca[:, n, 0:1]), reads=[(rk, tt), "lca"], writes=[(rk, tt)])
            for tt in TU:
                S.op("scalar", lambda e: e.activation(out=t1[:, lsl(tt)], in_=t1[:, lsl(tt)], func=AF.Sqrt, scale=-1.0, bias=1.0), reads=[(t1k, tt)], writes=[(t1k, tt)])
            if k + 1 < len(units):
                conv_unit(k + 1)
            for tt in TU:
                S.op("vector", lambda e: e.tensor_tensor(out=ig[:, lsl(tt)], in0=ig[:, lsl(tt)], in1=t1[:, lsl(tt)], op=ALU.mult), reads=[(igk, tt), (t1k, tt)], writes=[(igk, tt)])
            for tt in TU:
                if tt == 0:
                    init, ird = 0.0, []
                elif tt == 2 * u:
                    init, ird = hhs[1 - q][:, HT - 1:HT], [("l_h%d" % (1 - q), tt - 1)]
                else:
                    init, ird = hh[:, TT - 1:TT], [(hk, tt - 1)]
                S.op("vector", lambda e: e.tensor_tensor_scan(out=hh[:, lsl(tt)], data0=rr[:, lsl(tt)], data1=ig[:, lsl(tt)], initial=init, op0=ALU.mult, op1=ALU.add),
                     reads=[(rk, tt), (igk, tt)] + ird, writes=[(hk, tt)])
                S.op("vector", lambda e: e.tensor_tensor(out=hyi[:, tsl(tt)], in0=hh[:, lsl(tt)], in1=ysb[:, tsl(tt)], op=ALU.mult),
                     reads=[(hk, tt), (yk, tt)], writes=[(hyk, tt)])
            if u == 1 and n % G == G - 1:
                idx = [(n - G + 1 + i) % (2 * G) for i in range(G)]
                pending = ([((lambda tt, i=i: hy[i][:, tsl(tt)]), (lambda tt, i=i: ("l_hy%d" % i, tt))) for i in idx],
                           [(wo[i], "l_wo%d" % i) for i in idx])
        outproj_group(C, *pending)
        S.barrier()


def conv_ffn(C, L):
    S, sb, nc = C.S, C.sb, C.nc
    G = 4
    with contextlib.ExitStack() as ph:
        win = [sb("f_win%d" % i, [128, KC, 256], BF16, ph) for i in range(2)]
        wo = [sb("f_wo%d" % i, [128, D], BF16, ph) for i in range(2 * G)]
        act = [sb("f_act%d" % i, [128, NT], BF16, ph) for i in range(2 * G)]
        apads = [sb("f_apad%d" % i, [128, 2 + NT], F32, ph) for i in range(2)]
        bsbs = [sb("f_bsb%d" % i, [128, NT], BF16, ph) for i in range(2)]
        acs = [sb("f_ac%d" % i, [128, NT], F32, ph) for i in range(2)]
        t1 = sb("f_t1", [128, NT], F32, ph)
        for i in range(2):
            S.op("vector", lambda e: e.memset(apads[i][:, 0:2], 0.0), writes=["f_apad%d_0" % i])

        def load_and_proj(j):
            b = j % 2
            wi, woi = win[b], wo[j % (2 * G)]
            wik, wok = "f_win%d" % b, "f_wo%d" % (j % (2 * G))
            S.dma("gpsimd", wik, lambda e: e.dma_start(out=wi[:], in_=C.ffn_win[L, j]), writes=[wik])
            S.dma("gpsimd", wok, lambda e: e.dma_start(out=woi[:], in_=C.ffn_wout[L, j]), writes=[wok])
            apad, bsb, acb = apads[b], bsbs[b], acs[b]

            def evac_a(ps, pk, tt):
                S.op("scalar", lambda e: e.activation(out=apad[:, 2 + tt * TT:2 + (tt + 1) * TT], in_=ps[:], func=AF.Identity),
                     reads=[pk], writes=[("f_apad%d" % b, tt)])
                S.op("scalar", lambda e: e.activation(out=acb[:, tsl(tt)], in_=ps[:], func=AF.Identity, scale=C.fvec[:, L, j, 2:3], bias=C.fvec[:, L, j, 3:4]),
                     reads=[pk, "fvec"], writes=[("f_ac%d" % b, tt)])
            for tt in range(NTT):
                inproj(C, (wi, wik), 0, tt, lambda ps, pk: evac_a(ps, pk, tt))
            for tt in range(NTT):
                inproj(C, (wi, wik), 128, tt, lambda ps, pk: S.op(
                    "scalar", lambda e: e.activation(out=bsb[:, tsl(tt)], in_=ps[:], func=AF.Identity), reads=[pk], writes=[("f_bsb%d" % b, tt)]))

        pending = None
        load_and_proj(0)
        for j in range(FJ):
            b = j % 2
            acti, actk = act[j % (2 * G)], "f_act%d" % (j % (2 * G))
            apad, bsb, ac = apads[b], bsbs[b], acs[b]
            ak, bk, ack = "f_apad%d" % b, "f_bsb%d" % b, "f_ac%d" % b
            T4 = range(NTT)
            if j + 1 < FJ:
                load_and_proj(j + 1)
            if pending is not None:
                outproj_group(C, *pending)
                pending = None
            for tt in T4:
                sl = tsl(tt)
                rd = [(ak, tt), (ak, tt - 1) if tt > 0 else ak + "_0", "fvec"]
                for k in (1, 0):
                    S.op("vector", lambda e: e.scalar_tensor_tensor(out=ac[:, sl], in0=apad[:, k + tt * TT:k + (tt + 1) * TT], scalar=C.fvec[:, L, j, k:k + 1],
                                                                   in1=ac[:, sl], op0=ALU.mult, op1=ALU.add),
                         reads=rd + [(ack, tt)], writes=[(ack, tt)])
                S.op("scalar", lambda e: e.activation(out=t1[:, sl], in_=ac[:, sl], func=AF.Square, scale=GELU_C), reads=[(ack, tt)], writes=[("f_t1", tt)])
            for tt in T4:
                sl = tsl(tt)
                S.op("vector", lambda e: e.scalar_tensor_tensor(out=t1[:, sl], in0=t1[:, sl], scalar=1.0, in1=ac[:, sl], op0=ALU.add, op1=ALU.mult),
                     reads=[("f_t1", tt), (ack, tt)], writes=[("f_t1", tt)])
                S.op("scalar", lambda e: e.activation(out=t1[:, sl], in_=t1[:, sl], func=AF.Sigmoid, scale=GELU_K), reads=[("f_t1", tt)], writes=[("f_t1", tt)])
                S.op("vector", lambda e: e.tensor_tensor(out=ac[:, sl], in0=ac[:, sl], in1=bsb[:, sl], op=ALU.mult), reads=[(ack, tt), (bk, tt)], writes=[(ack, tt)])
            for tt in T4:
                sl = tsl(tt)
                S.op("vector", lambda e: e.tensor_tensor(out=acti[:, sl], in0=ac[:, sl], in1=t1[:, sl], op=ALU.mult),
                     reads=[(ack, tt), ("f_t1", tt)], writes=[(actk, tt)])
            if j % G == G - 1:
                idx = [(j - G + 1 + i) % (2 * G) for i in range(G)]
                pending = ([((lambda tt, i=i: act[i][:, tsl(tt)]), (lambda tt, i=i: ("f_act%d" % i, tt))) for i in idx],
                           [(wo[i], "f_wo%d" % i) for i in idx])
        outproj_group(C, *pending)
        S.barrier()


SLOPES = [2.0 ** (-8.0 * (h + 1) / 16.0) for h in range(16)]
BIGNEG = 30000.0
NPT = 4
NSM = 4


def gelu_ap(C, x, t, xk, tk):
    S = C.S
    S.op("scalar", lambda e: e.activation(out=t, in_=x, func=AF.Square), reads=[xk], writes=[tk])
    S.op("vector", lambda e: e.tensor_scalar(out=t, in0=t, scalar1=0.044715, scalar2=1.0, op0=ALU.mult, op1=ALU.add), reads=[tk], writes=[tk])
    S.op("vector", lambda e: e.tensor_tensor(out=t, in0=t, in1=x, op=ALU.mult), reads=[tk, xk], writes=[tk])
    S.op("scalar", lambda e: e.activation(out=t, in_=t, func=AF.Sigmoid, scale=GELU_K), reads=[tk], writes=[tk])
    S.op("vector", lambda e: e.tensor_tensor(out=x, in0=x, in1=t, op=ALU.mult), reads=[tk, xk], writes=[xk])


def nsa_mixer(C):
    S, sb, nc = C.S, C.sb, C.nc
    C.nrot = 4
    C.nsa_deferred = []
    po_banks = [(C.psum[i], ("ps", i)) for i in (4, 5, 6, 7)]
    po_i = [0]

    def nextpo():
        r = po_banks[po_i[0] % 4]
        dl = C.nsa_deferred
        while any(r[1] in tags for _, tags in dl):
            dl.pop(0)[0]()
        po_i[0] += 1
        return r
    with contextlib.ExitStack() as ph:
        KCT = sb("n_KCT", [128, 4, 128], BF16, ph)
        VCa = sb("n_VCa", [128, 4, 97], BF16, ph)
        wch = [None, None]
        wci = [0]

        def alloc_wch(stack):
            for i in range(2):
                wch[i] = sb("n_wch%d" % i, [128, KC, 128], BF16, stack)

        def load_wch(idx):
            i = wci[0] % 2
            wci[0] += 1
            k = "n_wch%d" % i
            S.dma("gpsimd", k, lambda e: e.dma_start(out=wch[i][:], in_=C.nsa_wch[idx]), writes=[k])
            return wch[i], k

        with contextlib.ExitStack() as p1:
            alloc_wch(p1)
            KV0 = sb("n_KV0", [128, 4, NT], BF16, p1)
            W1 = sb("n_W1", [128, 2, 32, 256], BF16, p1)
            posT = sb("n_posT", [128, 2, 32], BF16, p1)
            b1 = sb("n_b1", [128, 2, 2], F32, p1)
            W2k = sb("n_W2k", [128, 2, 128], BF16, p1)
            W2v = sb("n_W2v", [128, 2, 64], BF16, p1)
            ovl = sb("n_ovl", [128, 33], F32, p1)
            hids = [sb("n_hid%d" % i, [128, 256], F32, p1) for i in range(2)]
            hscs = [sb("n_hsc%d" % i, [128, 256], F32, p1) for i in range(2)]
            ghbs = [sb("n_ghb%d" % i, [128, 2, 128], BF16, p1) for i in range(2)]
            cvec = sb("n_cvec", [1, 2, 256], BF16, p1)
            onesr = sb("n_onesr", [1, 128], BF16, p1)
            b1row = sb("n_b1row", [1, 2, 256], F32, p1)
            identc = sb("n_identc", [128, 128], F32, p1)
            S.dma("sync", "n_identc", lambda e: e.dma_start(out=identc[:], in_=C.c_ident), writes=["n_identc"])
            S.dma("sync", "n_b1row", lambda e: e.dma_start(out=b1row[:], in_=C.nsa_b1row), writes=["n_b1row"])
            S.op("vector", lambda e: e.memset(onesr[:], 1.0), writes=["n_onesr"])
            for kvi in range(2):
                S.dma("gpsimd", "n_W1_%d" % kvi, lambda e: e.dma_start(out=W1[:, kvi], in_=C.nsa_w1[kvi]), writes=[("n_W1", kvi)])
            S.dma("gpsimd", "n_posT", lambda e: e.dma_start(out=posT[:], in_=C.nsa_posT), writes=["n_posT"])
            S.dma("sync", "n_b1", lambda e: e.dma_start(out=b1[:], in_=C.nsa_b1), writes=["n_b1"])
            S.dma("gpsimd", "n_W2k", lambda e: e.dma_start(out=W2k[:], in_=C.nsa_w2k), writes=["n_W2k"])
            S.dma("gpsimd", "n_W2v", lambda e: e.dma_start(out=W2v[:], in_=C.nsa_w2v), writes=["n_W2v"])
            S.dma("sync", "n_ovl", lambda e: e.dma_start(out=ovl[:], in_=C.c_ovl), writes=["n_ovl"])
            S.op("vector", lambda e: e.memset(KCT[:], 0.0), writes=[("n_KCT", g) for g in range(4)])
            S.op("vector", lambda e: e.memset(VCa[:], 0.0), writes=[("n_VCa", g) for g in range(4)])
            for g in range(4):
                S.op("vector", lambda e: e.tensor_copy(out=VCa[:, g, 64:97], in_=ovl[:]), reads=["n_ovl"], writes=[("n_VCa", g)])
            for c4 in range(4):
                wt, wk = load_wch(c4)
                for tt in range(NTT):
                    inproj(C, (wt, wk), 0, tt, lambda ps, pk: S.op(
                        "scalar", lambda e: e.activation(out=KV0[:, c4, tsl(tt)], in_=ps[:], func=AF.Identity), reads=[pk], writes=[("n_KV0", c4)]))
            for kvi in range(2):
                ps, pk = C.nextps()
                for i in range(32):
                    S.op("tensor", lambda e: e.matmul(ps[0:1, 0:256], lhsT=posT[0:64, kvi, i:i + 1], rhs=W1[0:64, kvi, i, :], start=(i == 0), stop=(i == 31)),
                         reads=[("n_W1", kvi), "n_posT"], writes=[pk])
                S.op("vector", lambda e: e.tensor_tensor(out=cvec[0:1, kvi, :], in0=ps[0:1, 0:256], in1=b1row[0:1, kvi, :], op=ALU.add),
                     reads=[pk, "n_b1row"], writes=[("n_cvec", kvi)])
            un = 0
            for kvi in range(2):
                for g in range(4):
                    c4 = kvi * 2 + g // 2
                    rows = slice((g % 2) * 64, (g % 2) * 64 + 64)
                    ub = un % 2
                    un += 1
                    hid, hsc, ghb = hids[ub], hscs[ub], ghbs[ub]
                    hk, sk, gk = "n_hid%d" % ub, "n_hsc%d" % ub, "n_ghb%d" % ub
                    ps, pk = C.nextps()
                    for i in range(32):
                        S.op("tensor", lambda e: e.matmul(ps[0:127, 0:256], lhsT=KV0[rows, c4, i:i + 2017:16], rhs=W1[rows, kvi, i, :], start=(i == 0), stop=False),
                             reads=[("n_W1", kvi), ("n_KV0", c4)], writes=[pk])
                    S.op("tensor", lambda e: e.matmul(ps[0:127, 0:256], lhsT=onesr[0:1, 0:127], rhs=cvec[0:1, kvi, :], start=False, stop=True),
                         reads=["n_onesr", ("n_cvec", kvi)], writes=[pk])
                    S.op("scalar", lambda e: e.activation(out=hid[0:127, :], in_=ps[0:127, 0:256], func=AF.Identity), reads=[pk], writes=[hk])
                    gelu_ap(C, hid[0:127, :], hsc[0:127, :], hk, sk)
                    ps, pk = C.nextps()
                    for mh in range(2):
                        S.op("tensor", lambda e: e.transpose(out=ps[:, mh * 128:mh * 128 + 127], in_=hid[0:127, mh * 128:(mh + 1) * 128], identity=identc[0:127, 0:127]),
                             reads=[hk, "n_identc"], writes=[pk])
                    S.op("scalar", lambda e: e.activation(out=ghb[:, :, 0:127], in_=ps[:, 0:256].rearrange("p (m c) -> p m c", c=128)[:, :, 0:127], func=AF.Identity),
                         reads=[pk], writes=[(gk, 0), (gk, 1)])
                    ps, pk = C.nextps()
                    if kvi == 0:
                        for mh in range(2):
                            S.op("tensor", lambda e: e.matmul(ps[:, 0:127], lhsT=W2k[:, mh, :], rhs=ghb[:, mh, 0:127], start=(mh == 0), stop=(mh == 1)),
                                 reads=["n_W2k", (gk, mh)], writes=[pk])
                        S.op("scalar", lambda e: e.activation(out=KCT[rows, g, 0:127], in_=ps[rows, 0:127], func=AF.Identity), reads=[pk], writes=[("n_KCT", g)])
                    else:
                        for mh in range(2):
                            S.op("tensor", lambda e: e.matmul(ps[0:127, 0:64], lhsT=ghb[:, mh, 0:127], rhs=W2v[:, mh, :], start=(mh == 0), stop=(mh == 1)),
                                 reads=["n_W2v", (gk, mh)], writes=[pk])
                        S.op("scalar", lambda e: e.activation(out=VCa[0:127, g, 0:64], in_=ps[0:127, 0:64], func=AF.Identity), reads=[pk], writes=[("n_VCa", g)])
            S.barrier()

        pA = ph.enter_context(contextlib.ExitStack())
        QT = sb("n_QT", [128, 8, NT], BF16, pA)
        Vaug = sb("n_Vaug", [128, 2, 16, 4, 65], BF16, pA)
        gts = sb("n_gts", [128, 16, 48], F32, pA)
        with contextlib.ExitStack() as p2:
            alloc_wch(p2)
            K12 = sb("n_K12", [128, 2, 2, NT], BF16, p2)
            wtok = sb("n_wtok", [128, KC, 560], BF16, p2)
            S.dma("gpsimd", "n_wtok", lambda e: e.dma_start(out=wtok[:], in_=C.nsa_wtok), writes=["n_wtok"])
            for mq in range(8 if getattr(C, 'nsa_stop', 9) >= 2 else 0):
                wt, wk = load_wch(4 + mq)
                for tt in range(NTT):
                    inproj(C, (wt, wk), 0, tt, lambda ps, pk: S.op(
                        "scalar", lambda e: e.activation(out=QT[:, mq, tsl(tt)], in_=ps[:], func=AF.Identity, scale=0.125), reads=[pk], writes=[("n_QT", mq, tt)]))
            for br in range(2):
                for c2 in range(2):
                    wt, wk = load_wch(12 + br * 2 + c2)
                    for tt in range(NTT):
                        inproj(C, (wt, wk), 0, tt, lambda ps, pk: S.op(
                            "scalar", lambda e: e.activation(out=K12[:, br, c2, tsl(tt)], in_=ps[:], func=AF.Identity), reads=[pk], writes=[("n_K12", br, c2, tt)]))
            S.op("vector", lambda e: e.memset(Vaug[:].rearrange("p a b c d -> p (a b c) d")[:, :, 64:65], 1.0), writes=["n_Vone"])
            for t16 in range(16 if getattr(C, 'nsa_stop', 9) >= 2 else 0):
                tok = slice(t16 * 128, (t16 + 1) * 128)
                ps, pk = C.nextps()
                for kc in range(KC):
                    S.op("tensor", lambda e: e.matmul(ps[:, 0:512], lhsT=C.xn[:, kc, tok], rhs=wtok[:, kc, 0:512], start=(kc == 0), stop=(kc == KC - 1)),
                         reads=["n_wtok", ("xn", kc, t16 // 4)], writes=[pk])
                for br in range(2):
                    S.op("scalar", lambda e: e.activation(out=Vaug[:, br, t16, :, 0:64], in_=ps[:, br * 256:(br + 1) * 256].rearrange("p (g d) -> p g d", d=64),
                                                          func=AF.Identity), reads=[pk], writes=[("n_Vaug", br, t16)])
                ps, pk = C.nextps()
                for kc in range(KC):
                    S.op("tensor", lambda e: e.matmul(ps[:, 0:48], lhsT=C.xn[:, kc, tok], rhs=wtok[:, kc, 512:560], start=(kc == 0), stop=(kc == KC - 1)),
                         reads=["n_wtok", ("xn", kc, t16 // 4)], writes=[pk])
                S.op("scalar", lambda e: e.activation(out=gts[:, t16, :], in_=ps[:, 0:48], func=AF.Sigmoid), reads=[pk], writes=[("n_gts", t16)])
            S.barrier()
            Kz = C.xn[:].rearrange("p (b g) t -> p b g t", b=2)
            for br in range(2):
                for g in range(4):
                    own = slice((g % 2) * 64, (g % 2) * 64 + 64)
                    oth = slice((1 - g % 2) * 64, (1 - g % 2) * 64 + 64)
                    S.op("gpsimd", lambda e: e.memset(Kz[oth, br, g, :], 0.0), writes=[("n_KzO", br, g)])
                    if g % 2 == 0:
                        S.op("scalar", lambda e: e.activation(out=Kz[own, br, g, :], in_=K12[own, br, g // 2, :], func=AF.Identity), writes=[("n_Kz", br, g)])
                    else:
                        S.op("vector", lambda e: e.tensor_copy(out=Kz[own, br, g, :], in_=K12[own, br, g // 2, :]), writes=[("n_Kz", br, g)])
            S.barrier()

        with contextlib.ExitStack() as p3:
            dtab = sb("n_dtab", [128, 9, TT], mybir.dt.int16, p3)
            etab = sb("n_etab", [128, 16, 128], BF16, p3)
            biasc = sb("n_biasc", [128, 16, 16], F32, p3)
            keep = sb("n_keep", [128, 16, 32], BF16, p3)
            addc = sb("n_addc", [128, 16, 32], BF16, p3)
            ident = sb("n_ident", [128, 128], F32, p3)
            sm = [sb("n_sm%d" % i, [128, TT], F32, p3) for i in range(NSM)]
            pT = [sb("n_pT%d" % i, [128, TT], BF16, p3) for i in range(NPT)]
            otoks = [sb("n_otok%d" % i, [128, 4, 256], F32, p3) for i in range(2)]
            valid = sb("n_valid", [128, 16, 1], F32, p3)
            otmp = sb("n_otmp", [128, 4, 64], F32, p3)
            rden = sb("n_rden", [128, 4, 1], F32, p3)
            ff = sb("n_ff", [128, 4, 1], F32, p3)
            comb = sb("n_comb", [128, 4, 65], F32, p3)
            fq = sb("n_fq", [128, 4, 16], F32, p3)
            S.dma("sync", "n_fq", lambda e: e.dma_start(out=fq[:], in_=C.c_fq), writes=["n_fq"])
            imp = sb("n_imp", [128, 4, 32], F32, p3)
            itmp = sb("n_itmp", [128, 4, 32], F32, p3)
            top8 = sb("n_top8", [128, 4, 8], F32, p3)
            selb = sb("n_selb", [128, 4, 32], F32, p3)
            selbT = sb("n_selbT", [128, 4, TT], BF16, p3)
            oTq = sb("n_oTq", [128, KC, TT], BF16, p3)
            alloc_wch(p3)
            S.dma("sync", "n_dtab", lambda e: e.dma_start(out=dtab[:, 0:8, :], in_=C.c_dtab[:, 0:8, :]), writes=["n_dtab"])
            S.op("gpsimd", lambda e: e.memset(etab[:], 0.0), writes=["n_etab"])
            S.op("gpsimd", lambda e: e.memset(selbT[:], 0.0), writes=[("n_selbT", g) for g in range(4)])
            S.dma("gpsimd", "n_etab", lambda e: e.dma_start(out=etab[0:32], in_=C.c_etab), writes=["n_etab"])
            S.dma("sync", "n_biasc", lambda e: e.dma_start(out=biasc[:], in_=C.c_biasc), writes=["n_biasc"])
            S.dma("gpsimd", "n_keep", lambda e: e.dma_start(out=keep[:], in_=C.c_keep), writes=["n_keep"])
            S.dma("gpsimd", "n_addc", lambda e: e.dma_start(out=addc[:], in_=C.c_addc), writes=["n_addc"])
            S.dma("sync", "n_ident", lambda e: e.dma_start(out=ident[:], in_=C.c_ident), writes=["n_ident"])
            S.dma("sync", "n_valid", lambda e: e.dma_start(out=valid[:], in_=C.c_valid), writes=["n_valid"])
            smi = [0]
            pti = [0]

            def score_tile(mm_fn, mm_reads, dti, scal, bias_ap):
                dkey = "n_dtabc" if dti == 8 else "n_dtab"
                i = smi[0] % NSM
                smi[0] += 1
                ps, pk = C.nextps()
                mm_fn(ps, pk)
                if dti is None:
                    smi[0] -= 1
                    jj = pti[0] % NPT
                    pti[0] += 1
                    S.op("scalar", lambda e: e.activation(out=pT[jj][:], in_=ps[:], func=AF.Exp, bias=bias_ap), reads=[pk, "n_biasc"], writes=[("n_pT", jj)])
                    return pT[jj], ("n_pT", jj)
                j = pti[0] % NPT
                pti[0] += 1
                S.op("vector", lambda e: e.scalar_tensor_tensor(out=sm[i][:], in0=dtab[:, dti, :], scalar=scal, in1=ps[:], op0=ALU.mult, op1=ALU.add),
                     reads=[pk, dkey], writes=[("n_sm", i)])
                if bias_ap is None:
                    S.op("scalar", lambda e: e.activation(out=pT[j][:], in_=sm[i][:], func=AF.Exp), reads=[("n_sm", i)], writes=[("n_pT", j)])
                else:
                    S.op("scalar", lambda e: e.activation(out=pT[j][:], in_=sm[i][:], func=AF.Exp, bias=bias_ap), reads=[("n_sm", i), "n_biasc"], writes=[("n_pT", j)])
                return pT[j], ("n_pT", j)

            def run_jobs(jobs, LA=NPT - 1):
                staged = []
                for idx in range(len(jobs) + LA):
                    if idx < len(jobs):
                        jb = jobs[idx]
                        staged.append(score_tile(jb["mm"], None, jb["dti"], jb["scal"], jb["bias"]))
                        if deferred:
                            deferred.pop(0)[0]()
                    k = idx - LA
                    if k >= 0:
                        jobs[k]["pv"](*staged[k])
                while deferred:
                    deferred.pop(0)[0]()

            deferred = C.nsa_deferred

            def accum_out(po, pok, ncol, otok, okey, r, gate_col, qt, first, src3=None, emit=None):
                if emit is None:
                    emit = lambda th: th()
                po3 = po[:, 0:4 * ncol].rearrange("p (s c) -> p s c", c=ncol) if src3 is None else src3
                if first:
                    emit(lambda: S.op("vector", lambda e: e.tensor_scalar(out=rden[:], in0=po3[:, :, 64:65], scalar1=1e-30, scalar2=None, op0=ALU.max), reads=[pok], writes=["n_rden"]))
                    emit(lambda: S.op("vector", lambda e: e.reciprocal(out=rden[:], in_=rden[:]), reads=["n_rden"], writes=["n_rden"]))
                else:
                    emit(lambda: S.op("vector", lambda e: e.reciprocal(out=rden[:], in_=po3[:, :, 64:65]), reads=[pok], writes=["n_rden"]))
                if first and qt == 0:
                    emit(lambda: S.op("vector", lambda e: e.tensor_tensor(out=rden[:], in0=rden[:], in1=valid[:, 0:4, :], op=ALU.mult), reads=["n_rden", "n_valid"], writes=["n_rden"]))
                emit(lambda: S.op("vector", lambda e: e.tensor_tensor(out=ff[:], in0=rden[:], in1=gts[:, qt * 4:(qt + 1) * 4, gate_col:gate_col + 1], op=ALU.mult),
                                  reads=["n_rden"] + [("n_gts", qt * 4 + i) for i in range(4)], writes=["n_ff"]))
                dst = otok[:, :, r * 64:(r + 1) * 64]
                if first:
                    emit(lambda: S.op("vector", lambda e: e.tensor_tensor(out=dst, in0=po3[:, :, 0:64], in1=ff[:].to_broadcast([128, 4, 64]), op=ALU.mult),
                                      reads=[pok, "n_ff"], writes=[(okey, r)]))
                else:
                    emit(lambda: S.op("vector", lambda e: e.tensor_tensor(out=otmp[:], in0=po3[:, :, 0:64], in1=ff[:].to_broadcast([128, 4, 64]), op=ALU.mult),
                                      reads=[pok, "n_ff"], writes=["n_otmp"]))
                    emit(lambda: S.op("vector", lambda e: e.tensor_tensor(out=dst, in0=dst, in1=otmp[:], op=ALU.add), reads=["n_otmp", (okey, r)], writes=[(okey, r)]))

            def group_ctx(qt, g):
                k = qt * 4 + g
                return tsl(qt), slice((g % 2) * 64, (g % 2) * 64 + 64), g // 2, otoks[k % 2], "n_otok%d" % (k % 2)

            def cmp_jobs(qt, g):
                qs, rows, c2, otok, okey = group_ctx(qt, g)
                if g == 0:
                    S.dma("sync", "n_dtabc", lambda e: e.dma_start(out=dtab[:, 8, :], in_=C.c_dtab[:, 9 + qt, :]), writes=["n_dtabc"])
                jobs = []
                dq = lambda th: deferred.append((th, ()))

                def select_ops():
                    dq(lambda: S.op("vector", lambda e: e.tensor_tensor(out=imp[:], in0=imp[:], in1=keep[:, qt * 4:(qt + 1) * 4, :], op=ALU.mult), reads=["n_imp", "n_keep"], writes=["n_imp"]))
                    dq(lambda: S.op("vector", lambda e: e.tensor_tensor(out=imp[:], in0=imp[:], in1=addc[:, qt * 4:(qt + 1) * 4, :], op=ALU.add), reads=["n_imp", "n_addc"], writes=["n_imp"]))
                    for sub in range(4):
                        dq(lambda sub=sub: S.op("vector", lambda e: e.max(out=top8[:, sub, :], in_=imp[:, sub, :]), reads=["n_imp"], writes=["n_top8"]))
                    for sub in range(4):
                        dq(lambda sub=sub: S.op("vector", lambda e: e.tensor_scalar(out=selb[:, sub, :], in0=imp[:, sub, :], scalar1=top8[:, sub, 7:8], scalar2=-BIGNEG, op0=ALU.is_lt, op1=ALU.mult),
                                                reads=["n_imp", "n_top8"], writes=["n_selb"]))

                    def tr():
                        ps, pk = C.nextps()
                        for sub in range(4):
                            S.op("tensor", lambda e: e.transpose(out=ps[0:32, sub * 128:(sub + 1) * 128], in_=selb[:, sub, :], identity=ident[:]),
                                 reads=["n_selb", "n_ident"], writes=[pk])
                        S.op("scalar", lambda e: e.activation(out=selbT[0:32, g, :], in_=ps[0:32, :], func=AF.Identity), reads=[pk], writes=[("n_selbT", g)])
                    dq(tr)

                for r in range(4):
                    hh = g * 4 + r
                    mq = (g // 2) * 4 + r

                    def mm(ps, pk, mq=mq):
                        S.op("tensor", lambda e: e.matmul(ps[:], lhsT=KCT[:, g, :], rhs=QT[:, mq, qs], start=True, stop=True),
                             reads=[("n_KCT", g), ("n_QT", mq, qt)], writes=[pk])

                    def pv(p_t, p_k, r=r, hh=hh):
                        po, pok = nextpo()
                        for sub in range(4):
                            S.op("tensor", lambda e: e.matmul(po[:, sub * 97:(sub + 1) * 97], lhsT=p_t[:, sub * 128:(sub + 1) * 128], rhs=VCa[:, g, :], start=True, stop=True),
                                 reads=[p_k, ("n_VCa", g)], writes=[pok])
                        accum_out(po, pok, 97, otok, okey, r, hh, qt, True)
                        po3 = po[:, 0:388].rearrange("p (s c) -> p s c", c=97)
                        if r == 0:
                            S.op("vector", lambda e: e.tensor_tensor(out=imp[:], in0=po3[:, :, 65:97], in1=rden[:].to_broadcast([128, 4, 32]), op=ALU.mult),
                                 reads=[pok, "n_rden"], writes=["n_imp"])
                        else:
                            S.op("vector", lambda e: e.tensor_tensor(out=itmp[:], in0=po3[:, :, 65:97], in1=rden[:].to_broadcast([128, 4, 32]), op=ALU.mult),
                                 reads=[pok, "n_rden"], writes=["n_itmp"])
                            S.op("vector", lambda e: e.tensor_tensor(out=imp[:], in0=imp[:], in1=itmp[:], op=ALU.add), reads=["n_itmp", "n_imp"], writes=["n_imp"])
                        if r == 3:
                            select_ops()
                    jobs.append(dict(mm=mm, dti=8, scal=-SLOPES[hh] / 2.0, bias=None, pv=pv))
                return jobs

            def selwin(qt, g, pre_jobs):
                qs, rows, c2, otok, okey = group_ctx(qt, g)
                jobs = []
                for r in range(4):
                    hh = g * 4 + r
                    mq = (g // 2) * 4 + r
                    for br in range(2):
                        kts = list(range(0, qt * 4 + 4)) if br == 0 else list(range(max(0, qt * 4 - 4), qt * 4 + 4))
                        state = {}
                        for n_k, kt in enumerate(kts):
                            delta = qt * TT - kt * 128
                            bias_ap = None
                            far = False
                            if delta <= 0:
                                dti = (-delta) // 128
                            elif br == 1:
                                dti = 3 + delta // 128
                            else:
                                dti = None
                                far = True
                                bias_ap = biasc[:, hh, delta // 128:delta // 128 + 1]
                            nfar = qt * 4 if br == 0 else 0

                            def mm(ps, pk, br=br, kt=kt, mq=mq):
                                ks = slice(kt * 128, (kt + 1) * 128)
                                S.op("tensor", lambda e: e.matmul(ps[:], lhsT=Kz[:, br, g, ks], rhs=QT[:, mq, qs], start=True, stop=(br == 1)),
                                     reads=[("n_Kz", br, g), ("n_KzO", br, g), ("n_QT", mq, qt)], writes=[pk])
                                if br == 0:
                                    S.op("tensor", lambda e: e.matmul(ps[:], lhsT=etab[:, kt, :], rhs=selbT[:, g, :], start=False, stop=True),
                                         reads=["n_etab", ("n_selbT", g)], writes=[pk])

                            def pv(p_t, p_k, br=br, kt=kt, n_k=n_k, nk=len(kts), state=state, r=r, hh=hh, far=far, nfar=nfar):
                                if n_k == 0 and nfar:
                                    state["far"] = nextpo()
                                if n_k == nfar:
                                    state["po"] = nextpo()
                                po, pok = state["far"] if far else state["po"]
                                first = (n_k == 0) if far else (n_k == nfar)
                                last = (n_k == nfar - 1) if far else (n_k == nk - 1)
                                for sub in range(4):
                                    S.op("tensor", lambda e: e.matmul(po[:, sub * 65:(sub + 1) * 65], lhsT=p_t[:, sub * 128:(sub + 1) * 128], rhs=Vaug[:, br, kt, g, :],
                                                                      start=(first and sub == 0), stop=(last and sub == 3)),
                                         reads=[p_k, ("n_Vaug", br, kt), "n_Vone"], writes=[pok])
                                if n_k == nk - 1:
                                    if nfar:
                                        pf, pfk = state["far"]
                                        pf3 = pf[:, 0:260].rearrange("p (s c) -> p s c", c=65)
                                        po3 = po[:, 0:260].rearrange("p (s c) -> p s c", c=65)
                                        S.op("vector", lambda e: e.tensor_tensor(out=comb[:], in0=pf3, in1=fq[:, :, hh:hh + 1].to_broadcast([128, 4, 65]), op=ALU.mult),
                                             reads=[pfk, "n_fq"], writes=["n_comb"])
                                        S.op("vector", lambda e: e.tensor_tensor(out=comb[:], in0=comb[:], in1=po3, op=ALU.add), reads=[pok, "n_comb"], writes=["n_comb"])
                                        accum_out(po, "n_comb", 65, otok, okey, r, (1 + br) * 16 + hh, qt, False, src3=comb[:])
                                    else:
                                        accum_out(po, pok, 65, otok, okey, r, (1 + br) * 16 + hh, qt, False)
                            jobs.append(dict(mm=mm, dti=dti, scal=-SLOPES[hh], bias=bias_ap, pv=pv))
                run_jobs(pre_jobs + jobs)
                for kk in range(2):
                    kc = 2 * g + kk
                    ps, pk = C.nextps()
                    for sub in range(4):
                        S.op("tensor", lambda e: e.transpose(out=ps[:, sub * 128:(sub + 1) * 128], in_=otok[:, sub, kk * 128:(kk + 1) * 128], identity=ident[:]),
                             reads=[(okey, 2 * kk), (okey, 2 * kk + 1), "n_ident"], writes=[pk])
                    S.op("scalar", lambda e: e.activation(out=oTq[:, kc, :], in_=ps[:], func=AF.Identity), reads=[pk], writes=[("n_oTq", kc)])
                if g == 3:
                    for m in range(KC):
                        wt, wk = load_wch(16 + m)
                        ps, pk = C.nextps()
                        for kc in range(KC):
                            S.op("tensor", lambda e: e.matmul(ps[:], lhsT=wt[:, kc, :], rhs=oTq[:, kc, :], start=(kc == 0), stop=(kc == KC - 1)),
                                 reads=[wk, ("n_oTq", kc)], writes=[pk])
                        S.op("vector", lambda e: e.tensor_tensor(out=C.hT[:, m, qs], in0=C.hT[:, m, qs], in1=ps[:], op=ALU.add),
                             reads=[pk, ("hT", m, qt)], writes=[("hT", m, qt)])

            steps = [(qt, g) for qt in range(NTT if getattr(C, 'nsa_stop', 9) >= 3 else 0) for g in range(4)]
            if steps:
                run_jobs(cmp_jobs(*steps[0]))
            for k, (qt, g) in enumerate(steps):
                selwin(qt, g, cmp_jobs(*steps[k + 1]) if k + 1 < len(steps) else [])
            S.barrier()
        pA.close()
        S.barrier()
    C.nrot = 8


def prep_weights(inp):
    f = np.ascontiguousarray
    w = {}
    g = np.stack([inp["lru_norm_g"][0], inp["ffn_norm_g"][0], inp["nsa_norm_g"][0], inp["ffn_norm_g"][1], inp["final_norm_g"]], 0)
    w["gains"] = f(g.reshape(5, KC, 128).transpose(2, 0, 1))
    wi = inp["lru_w_in"][0].reshape(KC, 128, 2, LN, 128)
    w["lru_win"] = f(wi.transpose(3, 1, 0, 2, 4).reshape(LN, 128, KC, 256))
    w["lru_gw"] = f(inp["lru_gate_w"][0].transpose(1, 2, 0, 3))
    w["lru_wout"] = f(inp["lru_w_out"][0].reshape(LN, 128, D))
    vec = np.concatenate([inp["lru_conv_w"][0], inp["lru_conv_b"][0][None], inp["lru_gate_b"][0], inp["lru_a_param"][0][None]], 0)
    w["lru_vec"] = f(vec.reshape(8, LN, 128).transpose(2, 1, 0))
    fw = inp["ffn_w_in"].reshape(2, KC, 128, 2, FJ, 128)
    w["ffn_win"] = f(fw.transpose(0, 4, 2, 1, 3, 5).reshape(2, FJ, 128, KC, 256))
    w["ffn_wout"] = f(inp["ffn_w_out"].reshape(2, FJ, 128, D))
    fv = np.concatenate([inp["ffn_conv_w"], inp["ffn_conv_b"][:, None]], 1)
    w["ffn_vec"] = f(fv.reshape(2, 4, FJ, 128).transpose(3, 0, 2, 1))
    w.update(nsa_host(inp))
    w.update(const_tables())
    return w


def nsa_host(inp):
    f = np.ascontiguousarray
    w = {}
    W = inp["nsa_w_in"][0]
    Wr = W.reshape(KC, 128, 2608)

    def chunk(cols):
        return Wr[:, :, cols].transpose(1, 0, 2)
    chunks = []
    for c4 in range(4):
        kvi, gp = c4 // 2, c4 % 2
        base = 1024 + kvi * 256 + gp * 128
        chunks.append(chunk(np.arange(base, base + 128)))
    for mq in range(8):
        p, r = mq // 4, mq % 4
        ha, hb = 4 * (2 * p) + r, 4 * (2 * p + 1) + r
        chunks.append(chunk(np.concatenate([np.arange(ha * 64, ha * 64 + 64), np.arange(hb * 64, hb * 64 + 64)])))
    for br in (1, 2):
        for c2 in range(2):
            base = 1024 + br * 512 + c2 * 128
            chunks.append(chunk(np.arange(base, base + 128)))
    Wo = inp["nsa_w_out"][0].reshape(KC, 128, D)
    for m in range(KC):
        chunks.append(Wo[:, :, m * 128:(m + 1) * 128].transpose(1, 0, 2))
    w["nsa_wch"] = f(np.stack(chunks, 0))
    tokcols = np.concatenate([np.arange(1024 + 512 + 256, 1024 + 512 + 512), np.arange(1024 + 1024 + 256, 1024 + 1024 + 512), np.arange(2560, 2608)])
    w["nsa_wtok"] = f(Wr[:, :, tokcols].transpose(1, 0, 2))
    w1 = inp["nsa_cmp_w1"][0].reshape(2, 32, 64, 256).transpose(0, 2, 1, 3)
    w["nsa_w1"] = f(np.concatenate([w1, w1], 1))
    pT = inp["nsa_cmp_pos"][0].transpose(2, 0, 1)
    w["nsa_posT"] = f(np.concatenate([pT, pT], 0))
    w["nsa_b1"] = f(inp["nsa_cmp_b1"][0].reshape(2, 2, 128).transpose(2, 0, 1))
    w["nsa_b1row"] = f(inp["nsa_cmp_b1"][0][None])
    w2 = inp["nsa_cmp_w2"][0]
    w2k = w2[0].reshape(2, 128, 64).transpose(1, 0, 2)
    w["nsa_w2k"] = f(np.concatenate([w2k, w2k], 2))
    w["nsa_w2v"] = f(w2[1].reshape(2, 128, 64).transpose(1, 0, 2))
    return w


def const_tables():
    c = {}
    HUGE = 30000
    k = np.arange(128)[:, None]
    q = np.arange(TT)[None, :]
    dt = np.zeros((128, 13, TT), np.int64)
    for i in range(4):
        d = -128 * i + q - k
        dt[:, i] = np.where(d >= 0, d, HUGE)
    for i in range(1, 5):
        d = 128 * i + q - k
        dt[:, 3 + i] = np.where(d < 512, d, HUGE)
    dt[:, 8] = q - k
    cc = np.arange(128)[:, None]
    for qt in range(4):
        t = qt * TT + q
        d2 = 2 * t - 32 * cc - 31
        ok = (16 * cc + 31 <= t) & (cc < 127)
        dt[:, 9 + qt] = np.where(ok, d2, HUGE)
    c["c_dtab"] = dt.astype(np.int16)
    et = np.zeros((32, 16, 128), np.float32)
    for kt in range(16):
        for kk in range(128):
            et[(kt * 128 + kk) // 64, kt, kk] = 1.0
    c["c_etab"] = et
    sl = np.array(SLOPES, np.float64)
    kk = np.arange(128, dtype=np.float64)[:, None, None]
    bc = -(sl[None, :, None] * ((128.0 * np.arange(16))[None, None, :] - kk))
    c["c_biasc"] = np.ascontiguousarray(bc).astype(np.float32)
    qq = (np.arange(4)[None, :, None] * 128 + np.arange(128)[:, None, None]).astype(np.float64)
    c["c_fq"] = np.exp(-sl[None, None, :] * qq).astype(np.float32)
    t = (np.arange(16)[None, :, None] * 128 + np.arange(128)[:, None, None])
    j = np.arange(32)[None, None, :]
    cur = t // 64
    forced = (j == 0) | (j == cur) | (j == cur - 1)
    future = j > cur
    c["c_keep"] = np.where(forced | future, 0.0, 1.0).astype(np.float32)
    c["c_addc"] = np.where(forced, 1e4, np.where(future, -1.0, 0.0)).astype(np.float32)
    c["c_ident"] = np.eye(128, dtype=np.float32)
    c["c_valid"] = (t >= 31).astype(np.float32)
    ov = np.zeros((128, 33), np.float32)
    ov[:127, 0] = 1.0
    cs = np.arange(127)[:, None] * 16
    sj = np.arange(32)[None, :]
    ov[:127, 1:] = ((cs < (sj + 1) * 64) & (cs + 32 > sj * 64)).astype(np.float32)
    c["c_ovl"] = ov
    return c


_CACHE = {}


def kernel(**inp):
    inp = {k: np.asarray(v) for k, v in inp.items()}
    ncores, nseq = 8, 2
    if "nc" not in _CACHE:
        _CACHE["nc"] = build(nseq)[0]
    nc = _CACHE["nc"]
    w = prep_weights(inp)
    x = inp["x"]
    xT = np.ascontiguousarray(x.reshape(ncores, nseq, NT, KC, 128).transpose(0, 1, 4, 3, 2))
    in_maps = [dict(w, xT=xT[c]) for c in range(ncores)]
    res = run_bass_kernel_spmd(nc, in_maps, core_ids=list(range(ncores)))
    o = np.stack([r["outT"] for r in res.results], 0)
    return np.ascontiguousarray(o.transpose(0, 1, 4, 3, 2)).reshape(16, NT, D).astype(np.float32)
```

```python
import contextlib
import numpy as np
import concourse.bass as bass
import concourse.mybir as mybir
from concourse.bass_utils import run_bass_kernel_spmd

F32 = mybir.dt.float32
BF16 = mybir.dt.bfloat16
AF = mybir.ActivationFunctionType
ALU = mybir.AluOpType
AX = mybir.AxisListType

D = 1024
KC = 8
NT = 2048
TT = 512
NTT = 4
LW = 1280
LN = 10
DFF = 3072
FJ = 24
EPS = 1e-6
GELU_K = 1.5957691216057308
GELU_C = 0.044715 ** 0.5


class Sched:
    ENGS = ("tensor", "vector", "scalar", "gpsimd", "sync")

    def __init__(self, nc, stack):
        self.nc = nc
        self.stack = stack
        self.eng = {e: getattr(nc, e) for e in self.ENGS}
        self.sem = {}
        self.cnt = {}
        self.known = {e: {} for e in self.ENGS}
        self.res = {}
        self.n_inst = 0
        self.n_wait = 0
        for e in ("tensor", "vector", "scalar", "gpsimd"):
            self._mksem(e)

    def _mksem(self, key):
        if key not in self.sem:
            name = "s_" + key.replace(":", "_")
            self.sem[key] = self.stack.enter_context(self.nc.semaphore(name))
            self.cnt[key] = 0
        return self.sem[key]

    def _deps(self, engine, reads, writes):
        deps = {}

        def add(k, v):
            if v > deps.get(k, 0):
                deps[k] = v
        for r in reads:
            st = self.res.get(r)
            if st and st["w"]:
                add(*st["w"])
        for w in writes:
            st = self.res.get(w)
            if st:
                if st["w"]:
                    add(*st["w"])
                for k, v in st["r"].items():
                    add(k, v)
        kn = self.known[engine]
        for k, v in deps.items():
            if k == engine and engine == "tensor":
                continue
            if kn.get(k, 0) >= v:
                continue
            self.eng[engine].wait_ge(self.sem[k], v)
            self.n_wait += 1
            kn[k] = v

    def _mark(self, key, val, reads, writes):
        for r in reads:
            st = self.res.setdefault(r, {"w": None, "r": {}})
            st["r"][key] = val
        for w in writes:
            self.res[w] = {"w": (key, val), "r": {}}

    def op(self, engine, fn, reads=(), writes=()):
        self._deps(engine, reads, writes)
        ins = fn(self.eng[engine])
        self.cnt[engine] += 1
        ins.then_inc(self.sem[engine], 1)
        self.n_inst += 1
        self._mark(engine, self.cnt[engine], reads, writes)
        return ins

    def dma(self, queue, key, fn, reads=(), writes=()):
        k = "dma:" + key
        self._mksem(k)
        self._deps(queue, reads, writes)
        ins = fn(self.eng[queue])
        self.cnt[k] += 16
        ins.then_inc(self.sem[k], 16)
        self.n_inst += 1
        self._mark(k, self.cnt[k], reads, writes)
        return ins

    def barrier(self):
        for e in self.ENGS:
            for k, v in self.cnt.items():
                if v == 0 or (k == e and e == "tensor"):
                    continue
                if self.known[e].get(k, 0) >= v:
                    continue
                self.eng[e].wait_ge(self.sem[k], v)
                self.known[e][k] = v

    def finish(self):
        for k, v in self.cnt.items():
            if v and self.known["sync"].get(k, 0) < v:
                self.nc.sync.wait_ge(self.sem[k], v)
                self.known["sync"][k] = v


class Ctx:
    pass


def tsl(tt):
    return slice(tt * TT, (tt + 1) * TT)


def build(nseq=2, upto=99, nsa_stop=9):
    nc = bass.Bass("TRN2", target_bir_lowering=False)
    C = Ctx()
    C.nc = nc
    C.nsa_stop = nsa_stop

    def din(name, shape):
        return nc.dram_tensor(name, list(shape), F32, kind="ExternalInput").ap()
    C.xT = din("xT", [nseq, 128, KC, NT])
    C.gains = din("gains", [128, 5, KC])
    C.lru_win = din("lru_win", [LN, 128, KC, 256])
    C.lru_gw = din("lru_gw", [LN, 128, 2, 128])
    C.lru_wout = din("lru_wout", [LN, 128, D])
    C.lru_vec = din("lru_vec", [128, LN, 8])
    C.ffn_win = din("ffn_win", [2, FJ, 128, KC, 256])
    C.ffn_wout = din("ffn_wout", [2, FJ, 128, D])
    C.ffn_vec = din("ffn_vec", [128, 2, FJ, 4])
    C.nsa_wch = din("nsa_wch", [24, 128, KC, 128])
    C.nsa_wtok = din("nsa_wtok", [128, KC, 560])
    C.nsa_w1 = din("nsa_w1", [2, 128, 32, 256])
    C.nsa_posT = din("nsa_posT", [128, 2, 32])
    C.nsa_b1 = din("nsa_b1", [128, 2, 2])
    C.nsa_b1row = din("nsa_b1row", [1, 2, 256])
    C.nsa_w2k = din("nsa_w2k", [128, 2, 128])
    C.nsa_w2v = din("nsa_w2v", [128, 2, 64])
    C.c_dtab = nc.dram_tensor("c_dtab", [128, 13, TT], mybir.dt.int16, kind="ExternalInput").ap()
    C.c_etab = din("c_etab", [32, 16, 128])
    C.c_biasc = din("c_biasc", [128, 16, 16])
    C.c_keep = din("c_keep", [128, 16, 32])
    C.c_addc = din("c_addc", [128, 16, 32])
    C.c_ident = din("c_ident", [128, 128])
    C.c_valid = din("c_valid", [128, 16, 1])
    C.c_fq = din("c_fq", [128, 4, 16])
    C.c_ovl = din("c_ovl", [128, 33])
    C.outT = nc.dram_tensor("outT", [nseq, 128, KC, NT], F32, kind="ExternalOutput").ap()

    with contextlib.ExitStack() as st:
        S = Sched(nc, st)
        C.S = S

        uid = [0]

        def sb(name, shape, dt=F32, stack=st):
            uid[0] += 1
            return stack.enter_context(nc.sbuf_tensor("%s_u%d" % (name, uid[0]), list(shape), dt))
        C.sb = sb
        C.hT = sb("hT", [128, KC, NT])
        C.xn = sb("xn", [128, KC, NT], BF16)
        C.ones = sb("ones", [128, 128], BF16)
        C.gn = sb("gn", [128, 5, KC])
        C.lvec = sb("lvec", [128, LN, 8])
        C.lca = sb("lca", [128, LN, 2])
        C.fvec = sb("fvec", [128, 2, FJ, 4])
        C.psum = [st.enter_context(nc.psum_tensor("ps%d" % i, [128, TT], F32)) for i in range(8)]
        C.psi = 0
        C.nrot = 8

        def nextps():
            i = C.psi % C.nrot
            C.psi += 1
            return C.psum[i], ("ps", i)
        C.nextps = nextps

        S.op("vector", lambda e: e.memset(C.ones[:], 1.0), writes=["ones"])
        C.epsc = sb("epsc", [128, 1])
        S.op("vector", lambda e: e.memset(C.epsc[:], EPS), writes=["epsc"])
        S.dma("sync", "c0", lambda e: e.dma_start(out=C.gn[:], in_=C.gains), writes=["gn"])
        S.dma("sync", "c1", lambda e: e.dma_start(out=C.lvec[:], in_=C.lru_vec), writes=["lvec"])
        S.dma("sync", "c2", lambda e: e.dma_start(out=C.fvec[:], in_=C.ffn_vec), writes=["fvec"])
        lru_consts(C)

        for s in range(nseq):
            C.cur_seq = s
            for tt in range(NTT):
                S.dma("sync", "x%d" % tt, lambda e, tt=tt: e.dma_start(out=C.hT[:, :, tsl(tt)], in_=C.xT[s, :, :, tsl(tt)]),
                      writes=[("hT", kc, tt) for kc in range(KC)])
            if upto >= 1:
                lru_mixer(C, 0)
            if upto >= 2:
                conv_ffn(C, 0, 1)
            if upto >= 3:
                nsa_mixer(C, 2)
            if upto >= 4:
                conv_ffn(C, 1, 3)
            if upto >= 5:
                rmsnorm(C, 4, final=True)
            if upto < 5:
                for tt in range(NTT):
                    S.dma("sync", "o%d" % tt, lambda e, tt=tt: e.dma_start(out=C.outT[s, :, :, tsl(tt)], in_=C.hT[:, :, tsl(tt)]),
                          reads=[("hT", kc, tt) for kc in range(KC)])
        S.finish()
    C.n_inst = S.n_inst
    C.n_wait = S.n_wait
    return nc, C


def lru_consts(C):
    S, sb = C.S, C.sb
    with contextlib.ExitStack() as ph:
        t = [sb("lc%d" % i, [128, LN], F32, ph) for i in range(6)]
        ap = C.lvec[:, :, 7]
        S.op("scalar", lambda e: e.activation(out=t[0][:], in_=ap, func=AF.Abs), reads=["lvec"], writes=["lc0"])
        S.op("scalar", lambda e: e.activation(out=t[1][:], in_=t[0][:], func=AF.Exp, scale=-1.0), reads=["lc0"], writes=["lc1"])
        S.op("scalar", lambda e: e.activation(out=t[2][:], in_=t[1][:], func=AF.Ln, bias=1.0), reads=["lc1"], writes=["lc2"])
        S.op("vector", lambda e: e.tensor_scalar(out=t[3][:], in0=t[1][:], scalar1=1.0 / 3.0, scalar2=-0.5, op0=ALU.mult, op1=ALU.add), reads=["lc1"], writes=["lc3"])
        S.op("vector", lambda e: e.tensor_tensor(out=t[3][:], in0=t[3][:], in1=t[1][:], op=ALU.mult), reads=["lc3", "lc1"], writes=["lc3"])
        S.op("vector", lambda e: e.tensor_scalar(out=t[3][:], in0=t[3][:], scalar1=1.0, scalar2=None, op0=ALU.add), reads=["lc3"], writes=["lc3"])
        S.op("vector", lambda e: e.tensor_tensor(out=t[3][:], in0=t[3][:], in1=t[1][:], op=ALU.mult), reads=["lc3", "lc1"], writes=["lc3"])
        S.op("vector", lambda e: e.tensor_single_scalar(out=t[4][:], in_=t[1][:], scalar=0.03, op=ALU.is_lt), reads=["lc1"], writes=["lc4"])
        S.op("vector", lambda e: e.tensor_tensor(out=t[3][:], in0=t[3][:], in1=t[2][:], op=ALU.subtract), reads=["lc3", "lc2"], writes=["lc3"])
        S.op("vector", lambda e: e.tensor_tensor(out=t[3][:], in0=t[3][:], in1=t[4][:], op=ALU.mult), reads=["lc3", "lc4"], writes=["lc3"])
        S.op("vector", lambda e: e.tensor_tensor(out=t[3][:], in0=t[3][:], in1=t[2][:], op=ALU.add), reads=["lc3", "lc2"], writes=["lc3"])
        S.op("vector", lambda e: e.tensor_scalar(out=t[5][:], in0=ap, scalar1=-1.0, scalar2=0.0, op0=ALU.mult, op1=ALU.max), reads=["lvec"], writes=["lc5"])
        S.op("vector", lambda e: e.tensor_tensor(out=t[3][:], in0=t[3][:], in1=t[5][:], op=ALU.add), reads=["lc3", "lc5"], writes=["lc3"])
        S.op("vector", lambda e: e.tensor_scalar(out=C.lca[:, :, 0], in0=t[3][:], scalar1=-8.0, scalar2=None, op0=ALU.mult), reads=["lc3"], writes=["lca"])
        S.op("vector", lambda e: e.tensor_scalar(out=C.lca[:, :, 1], in0=t[3][:], scalar1=-16.0, scalar2=None, op0=ALU.mult), reads=["lc3"], writes=["lca"])
        S.barrier()


def rmsnorm(C, gi, final=False):
    S = C.S
    ph = contextlib.ExitStack()
    C.sq = [C.sb("sq%d" % i, [128, TT], BF16, ph) for i in range(4)]
    C.rs = C.sb("rs", [128, TT], F32, ph)
    for tt in range(NTT):
        ps, pk = C.nextps()
        for kc in range(KC):
            sq = C.sq[kc % 4]
            S.op("scalar", lambda e: e.activation(out=sq[:], in_=C.hT[:, kc, tsl(tt)], func=AF.Square),
                 reads=[("hT", kc, tt)], writes=[("sq", kc % 4)])
            S.op("tensor", lambda e: e.matmul(ps[:], lhsT=C.ones[:], rhs=sq[:], start=(kc == 0), stop=(kc == KC - 1)),
                 reads=["ones", ("sq", kc % 4)], writes=[pk])
        S.op("scalar", lambda e: e.activation(out=C.rs[:], in_=ps[:], func=AF.Ln, scale=1.0 / D, bias=C.epsc[:]), reads=[pk, "epsc"], writes=["rs"])
        S.op("scalar", lambda e: e.activation(out=C.rs[:], in_=C.rs[:], func=AF.Exp, scale=-0.5), reads=["rs"], writes=["rs"])
        for kc in range(KC):
            if final:
                S.op("vector", lambda e: e.scalar_tensor_tensor(out=C.hT[:, kc, tsl(tt)], in0=C.hT[:, kc, tsl(tt)], scalar=C.gn[:, gi, kc:kc + 1],
                                                               in1=C.rs[:], op0=ALU.mult, op1=ALU.mult),
                     reads=[("hT", kc, tt), "rs", "gn"], writes=[("hT", kc, tt)])
            else:
                S.op("vector", lambda e: e.scalar_tensor_tensor(out=C.xn[:, kc, tsl(tt)], in0=C.hT[:, kc, tsl(tt)], scalar=C.gn[:, gi, kc:kc + 1],
                                                               in1=C.rs[:], op0=ALU.mult, op1=ALU.mult),
                     reads=[("hT", kc, tt), "rs", "gn"], writes=[("xn", kc, tt)])
        if final:
            S.dma("sync", "o%d" % tt, lambda e: e.dma_start(out=C.outT[C.cur_seq, :, :, tsl(tt)], in_=C.hT[:, :, tsl(tt)]),
                  reads=[("hT", kc, tt) for kc in range(KC)])
    S.barrier()
    ph.close()


def inproj(C, w, col0, tt, evac):
    S = C.S
    ps, pk = C.nextps()
    wt, wk = w
    for kc in range(KC):
        S.op("tensor", lambda e: e.matmul(ps[:], lhsT=wt[:, kc, col0:col0 + 128], rhs=C.xn[:, kc, tsl(tt)], start=(kc == 0), stop=(kc == KC - 1)),
             reads=[wk, ("xn", kc, tt)], writes=[pk])
    evac(ps, pk)


def gelu_inplace(C, x, xk, t1, t1k, tt):
    S = C.S
    sl = tsl(tt)
    S.op("scalar", lambda e: e.activation(out=t1[:, sl], in_=x[:, sl], func=AF.Square), reads=[(xk, tt)], writes=[(t1k, tt)])
    S.op("vector", lambda e: e.tensor_scalar(out=t1[:, sl], in0=t1[:, sl], scalar1=0.044715, scalar2=1.0, op0=ALU.mult, op1=ALU.add),
         reads=[(t1k, tt)], writes=[(t1k, tt)])
    S.op("vector", lambda e: e.tensor_tensor(out=t1[:, sl], in0=t1[:, sl], in1=x[:, sl], op=ALU.mult), reads=[(t1k, tt), (xk, tt)], writes=[(t1k, tt)])
    S.op("scalar", lambda e: e.activation(out=t1[:, sl], in_=t1[:, sl], func=AF.Sigmoid, scale=GELU_K), reads=[(t1k, tt)], writes=[(t1k, tt)])
    S.op("vector", lambda e: e.tensor_tensor(out=x[:, sl], in0=x[:, sl], in1=t1[:, sl], op=ALU.mult), reads=[(t1k, tt), (xk, tt)], writes=[(xk, tt)])


def outproj_group(C, acts, wouts):
    S = C.S
    n = len(acts)
    for tt in range(NTT):
        for m in range(KC):
            ps, pk = C.nextps()
            for i in range(n):
                a_ap, a_k = acts[i]
                wt, wk = wouts[i]
                S.op("tensor", lambda e: e.matmul(ps[:], lhsT=wt[:, m * 128:(m + 1) * 128], rhs=a_ap(tt), start=(i == 0), stop=(i == n - 1)),
                     reads=[wk, a_k(tt)], writes=[pk])
            S.op("vector", lambda e: e.tensor_tensor(out=C.hT[:, m, tsl(tt)], in0=C.hT[:, m, tsl(tt)], in1=ps[:], op=ALU.add),
                 reads=[pk, ("hT", m, tt)], writes=[("hT", m, tt)])


def lru_mixer(C, gi):
    S, sb, nc = C.S, C.sb, C.nc
    G = 2
    HT = NT // 2
    with contextlib.ExitStack() as ph:
        win = [sb("l_win%d" % i, [128, KC, 256], BF16, ph) for i in range(2)]
        gw = [sb("l_gw%d" % i, [128, 2, 128], BF16, ph) for i in range(2)]
        wo = [sb("l_wo%d" % i, [128, D], BF16, ph) for i in range(2 * G)]

        def load_w(n):
            b = n % 2
            S.dma("gpsimd", "l_win%d" % b, lambda e: e.dma_start(out=win[b][:], in_=C.lru_win[n]), writes=["l_win%d" % b])
            S.dma("gpsimd", "l_gw%d" % b, lambda e: e.dma_start(out=gw[b][:], in_=C.lru_gw[n]), writes=["l_gw%d" % b])
            S.dma("gpsimd", "l_wo%d" % (n % (2 * G)), lambda e: e.dma_start(out=wo[n % (2 * G)][:], in_=C.lru_wout[n]), writes=["l_wo%d" % (n % (2 * G))])
        load_w(0)
        rmsnorm(C, gi)
        hy = [sb("l_hy%d" % i, [128, NT], BF16, ph) for i in range(2 * G)]
        ysbs = [sb("l_ysb%d" % i, [128, NT], F32, ph) for i in range(2)]
        xpads = [sb("l_xpad%d" % i, [128, 3 + NT], F32, ph) for i in range(2)]
        t1s = [sb("l_t1%d" % i, [128, HT], F32, ph) for i in range(2)]
        xcs = [sb("l_xc%d" % i, [128, HT], F32, ph) for i in range(2)]
        xcbs = [sb("l_xcb%d" % i, [128, HT], BF16, ph) for i in range(2)]
        rrs = [sb("l_r%d" % i, [128, HT], F32, ph) for i in range(2)]
        igs = [sb("l_ig%d" % i, [128, HT], F32, ph) for i in range(2)]
        hhs = [sb("l_h%d" % i, [128, HT], F32, ph) for i in range(2)]
        for i in range(2):
            S.op("vector", lambda e: e.memset(xpads[i][:, 0:3], 0.0), writes=["l_xpad%d_0" % i])

        def proj(n, u, k):
            b = n % 2
            q = k % 2
            wi, wik = win[b], "l_win%d" % b
            ysb, xpad = ysbs[b], xpads[b]

            def evac_x(ps, pk, tt):
                S.op("scalar", lambda e: e.activation(out=xpad[:, 3 + tt * TT:3 + (tt + 1) * TT], in_=ps[:], func=AF.Identity),
                     reads=[pk], writes=[("l_xpad%d" % b, tt)])
                S.op("scalar", lambda e: e.activation(out=xcs[q][:, (tt - 2 * u) * TT:(tt - 2 * u + 1) * TT], in_=ps[:], func=AF.Identity,
                                                      scale=C.lvec[:, n, 3:4], bias=C.lvec[:, n, 4:5]),
                     reads=[pk, "lvec"], writes=[("l_xc%d" % q, tt)])
            for tt in (2 * u, 2 * u + 1):
                inproj(C, (wi, wik), 128, tt, lambda ps, pk: evac_x(ps, pk, tt))
            for tt in (2 * u, 2 * u + 1):
                inproj(C, (wi, wik), 0, tt, lambda ps, pk: S.op(
                    "scalar", lambda e: e.activation(out=ysb[:, tsl(tt)], in_=ps[:], func=AF.Identity), reads=[pk], writes=[("l_ysb%d" % b, tt)]))

        units = [(n, u) for n in range(LN) for u in range(2)]
        def conv_unit(k):
            n, u = units[k]
            b = n % 2
            q = k % 2
            xpad, xk = xpads[b], "l_xpad%d" % b
            xc, xcb = xcs[q], xcbs[q]
            xck, xcbk = "l_xc%d" % q, "l_xcb%d" % q
            TU = (2 * u, 2 * u + 1)

            def lsl(tt):
                return slice((tt - 2 * u) * TT, (tt - 2 * u + 1) * TT)
            for tt in TU:
                rd = [(xk, tt), (xk, tt - 1) if tt > 0 else xk + "_0", "lvec"]
                for kk in (2, 1, 0):
                    S.op("vector", lambda e: e.scalar_tensor_tensor(out=xc[:, lsl(tt)], in0=xpad[:, kk + tt * TT:kk + (tt + 1) * TT], scalar=C.lvec[:, n, kk:kk + 1],
                                                                   in1=xc[:, lsl(tt)], op0=ALU.mult, op1=ALU.add),
                         reads=rd + [(xck, tt)], writes=[(xck, tt)])
                S.op("scalar", lambda e: e.activation(out=xcb[:, lsl(tt)], in_=xc[:, lsl(tt)], func=AF.Identity), reads=[(xck, tt)], writes=[(xcbk, tt)])

        pending = None
        proj(0, 0, 0)
        conv_unit(0)
        for k, (n, u) in enumerate(units):
            b = n % 2
            q = k % 2
            gwi, gwk = gw[b], "l_gw%d" % b
            hyi, hyk = hy[n % (2 * G)], "l_hy%d" % (n % (2 * G))
            ysb, xpad = ysbs[b], xpads[b]
            yk, xk = "l_ysb%d" % b, "l_xpad%d" % b
            t1, xc, xcb, rr, ig, hh = t1s[q], xcs[q], xcbs[q], rrs[q], igs[q], hhs[q]
            t1k, xck, xcbk, rk, igk, hk = "l_t1%d" % q, "l_xc%d" % q, "l_xcb%d" % q, "l_r%d" % q, "l_ig%d" % q, "l_h%d" % q
            TU = (2 * u, 2 * u + 1)

            def lsl(tt):
                return slice((tt - 2 * u) * TT, (tt - 2 * u + 1) * TT)
            if k + 1 < len(units):
                n2, u2 = units[k + 1]
                if u2 == 0:
                    load_w(n2)
                proj(n2, u2, k + 1)
            if pending is not None and u == 1:
                outproj_group(C, *pending)
                pending = None
            for tt in TU:
                S.op("scalar", lambda e: e.activation(out=t1[:, lsl(tt)], in_=ysb[:, tsl(tt)], func=AF.Square, scale=GELU_C), reads=[(yk, tt)], writes=[(t1k, tt)])
            for g, (dst, dk) in enumerate(((rr, rk), (ig, igk))):
                for tt in TU:
                    ps, pk = C.nextps()
                    S.op("tensor", lambda e: e.matmul(ps[:], lhsT=gwi[:, g, :], rhs=xcb[:, lsl(tt)], start=True, stop=True),
                         reads=[gwk, (xcbk, tt)], writes=[pk])
                    S.op("scalar", lambda e: e.activation(out=dst[:, lsl(tt)], in_=ps[:], func=AF.Sigmoid, bias=C.lvec[:, n, 5 + g:6 + g]),
                         reads=[pk, "lvec"], writes=[(dk, tt)])
            for tt in TU:
                S.op("vector", lambda e: e.scalar_tensor_tensor(out=t1[:, lsl(tt)], in0=t1[:, lsl(tt)], scalar=1.0, in1=ysb[:, tsl(tt)], op0=ALU.add, op1=ALU.mult),
                     reads=[(t1k, tt), (yk, tt)], writes=[(t1k, tt)])
            for tt in TU:
                S.op("scalar", lambda e: e.activation(out=t1[:, lsl(tt)], in_=t1[:, lsl(tt)], func=AF.Sigmoid, scale=GELU_K), reads=[(t1k, tt)], writes=[(t1k, tt)])
            for tt in TU:
                S.op("vector", lambda e: e.tensor_tensor(out=ysb[:, tsl(tt)], in0=ysb[:, tsl(tt)], in1=t1[:, lsl(tt)], op=ALU.mult), reads=[(t1k, tt), (yk, tt)], writes=[(yk, tt)])
            for tt in TU:
                S.op("vector", lambda e: e.tensor_tensor(out=ig[:, lsl(tt)], in0=ig[:, lsl(tt)], in1=xc[:, lsl(tt)], op=ALU.mult), reads=[(igk, tt), (xck, tt)], writes=[(igk, tt)])
            for tt in TU:
                S.op("scalar", lambda e: e.activation(out=t1[:, lsl(tt)], in_=rr[:, lsl(tt)], func=AF.Exp, scale=C.lca[:, n, 1:2]), reads=[(rk, tt), "lca"], writes=[(t1k, tt)])
            for tt in TU:
                S.op("scalar", lambda e: e.activation(out=rr[:, lsl(tt)], in_=rr[:, lsl(tt)], func=AF.Exp, scale=C.lca[:, n, 0:1]), reads=[(rk, tt), "lca"], writes=[(rk, tt)])
            for tt in TU:
                S.op("scalar", lambda e: e.activation(out=t1[:, lsl(tt)], in_=t1[:, lsl(tt)], func=AF.Sqrt, scale=-1.0, bias=1.0), reads=[(t1k, tt)], writes=[(t1k, tt)])
            if k + 1 < len(units):
                conv_unit(k + 1)
            for tt in TU:
                S.op("vector", lambda e: e.tensor_tensor(out=ig[:, lsl(tt)], in0=ig[:, lsl(tt)], in1=t1[:, lsl(tt)], op=ALU.mult), reads=[(igk, tt), (t1k, tt)], writes=[(igk, tt)])
            for tt in TU:
                if tt == 0:
                    init, ird = 0.0, []
                elif tt == 2 * u:
                    init, ird = hhs[1 - q][:, HT - 1:HT], [("l_h%d" % (1 - q), tt - 1)]
                else:
                    init, ird = hh[:, TT - 1:TT], [(hk, tt - 1)]
                S.op("vector", lambda e: e.tensor_tensor_scan(out=hh[:, lsl(tt)], data0=rr[:, lsl(tt)], data1=ig[:, lsl(tt)], initial=init, op0=ALU.mult, op1=ALU.add),
                     reads=[(rk, tt), (igk, tt)] + ird, writes=[(hk, tt)])
                S.op("vector", lambda e: e.tensor_tensor(out=hyi[:, tsl(tt)], in0=hh[:, lsl(tt)], in1=ysb[:, tsl(tt)], op=ALU.mult),
                     reads=[(hk, tt), (yk, tt)], writes=[(hyk, tt)])
            if u == 1 and n % G == G - 1:
                idx = [(n - G + 1 + i) % (2 * G) for i in range(G)]
                pending = ([((lambda tt, i=i: hy[i][:, tsl(tt)]), (lambda tt, i=i: ("l_hy%d" % i, tt))) for i in idx],
                           [(wo[i], "l_wo%d" % i) for i in idx])
        outproj_group(C, *pending)
        S.barrier()


def conv_ffn(C, L, gi):
    S, sb, nc = C.S, C.sb, C.nc
    G = 4
    with contextlib.ExitStack() as ph:
        win = [sb("f_win%d" % i, [128, KC, 256], BF16, ph) for i in range(2)]
        wo = [sb("f_wo%d" % i, [128, D], BF16, ph) for i in range(2 * G)]

        def load_w(j):
            b = j % 2
            S.dma("gpsimd", "f_win%d" % b, lambda e: e.dma_start(out=win[b][:], in_=C.ffn_win[L, j]), writes=["f_win%d" % b])
            S.dma("gpsimd", "f_wo%d" % (j % (2 * G)), lambda e: e.dma_start(out=wo[j % (2 * G)][:], in_=C.ffn_wout[L, j]), writes=["f_wo%d" % (j % (2 * G))])
        load_w(0)
        rmsnorm(C, gi)
        act = [sb("f_act%d" % i, [128, NT], BF16, ph) for i in range(2 * G)]
        apads = [sb("f_apad%d" % i, [128, 2 + NT], F32, ph) for i in range(2)]
        bsbs = [sb("f_bsb%d" % i, [128, NT], BF16, ph) for i in range(2)]
        acs = [sb("f_ac%d" % i, [128, NT], F32, ph) for i in range(2)]
        t1 = sb("f_t1", [128, NT], F32, ph)
        for i in range(2):
            S.op("vector", lambda e: e.memset(apads[i][:, 0:2], 0.0), writes=["f_apad%d_0" % i])

        def load_and_proj(j):
            b = j % 2
            wi, woi = win[b], wo[j % (2 * G)]
            wik, wok = "f_win%d" % b, "f_wo%d" % (j % (2 * G))
            if j > 0:
                load_w(j)
            apad, bsb, acb = apads[b], bsbs[b], acs[b]

            def evac_a(ps, pk, tt):
                S.op("scalar", lambda e: e.activation(out=apad[:, 2 + tt * TT:2 + (tt + 1) * TT], in_=ps[:], func=AF.Identity),
                     reads=[pk], writes=[("f_apad%d" % b, tt)])
                S.op("scalar", lambda e: e.activation(out=acb[:, tsl(tt)], in_=ps[:], func=AF.Identity, scale=C.fvec[:, L, j, 2:3], bias=C.fvec[:, L, j, 3:4]),
                     reads=[pk, "fvec"], writes=[("f_ac%d" % b, tt)])
            for tt in range(NTT):
                inproj(C, (wi, wik), 0, tt, lambda ps, pk: evac_a(ps, pk, tt))
            for tt in range(NTT):
                inproj(C, (wi, wik), 128, tt, lambda ps, pk: S.op(
                    "scalar", lambda e: e.activation(out=bsb[:, tsl(tt)], in_=ps[:], func=AF.Identity), reads=[pk], writes=[("f_bsb%d" % b, tt)]))

        pending = None
        load_and_proj(0)
        for j in range(FJ):
            b = j % 2
            acti, actk = act[j % (2 * G)], "f_act%d" % (j % (2 * G))
            apad, bsb, ac = apads[b], bsbs[b], acs[b]
            ak, bk, ack = "f_apad%d" % b, "f_bsb%d" % b, "f_ac%d" % b
            T4 = range(NTT)
            if j + 1 < FJ:
                load_and_proj(j + 1)
            if pending is not None:
                outproj_group(C, *pending)
                pending = None
            for tt in T4:
                sl = tsl(tt)
                rd = [(ak, tt), (ak, tt - 1) if tt > 0 else ak + "_0", "fvec"]
                for k in (1, 0):
                    S.op("vector", lambda e: e.scalar_tensor_tensor(out=ac[:, sl], in0=apad[:, k + tt * TT:k + (tt + 1) * TT], scalar=C.fvec[:, L, j, k:k + 1],
                                                                   in1=ac[:, sl], op0=ALU.mult, op1=ALU.add),
                         reads=rd + [(ack, tt)], writes=[(ack, tt)])
                S.op("scalar", lambda e: e.activation(out=t1[:, sl], in_=ac[:, sl], func=AF.Square, scale=GELU_C), reads=[(ack, tt)], writes=[("f_t1", tt)])
            for tt in T4:
                sl = tsl(tt)
                S.op("vector", lambda e: e.scalar_tensor_tensor(out=t1[:, sl], in0=t1[:, sl], scalar=1.0, in1=ac[:, sl], op0=ALU.add, op1=ALU.mult),
                     reads=[("f_t1", tt), (ack, tt)], writes=[("f_t1", tt)])
                S.op("scalar", lambda e: e.activation(out=t1[:, sl], in_=t1[:, sl], func=AF.Sigmoid, scale=GELU_K), reads=[("f_t1", tt)], writes=[("f_t1", tt)])
                S.op("vector", lambda e: e.tensor_tensor(out=ac[:, sl], in0=ac[:, sl], in1=bsb[:, sl], op=ALU.mult), reads=[(ack, tt), (bk, tt)], writes=[(ack, tt)])
            for tt in T4:
                sl = tsl(tt)
                S.op("vector", lambda e: e.tensor_tensor(out=acti[:, sl], in0=ac[:, sl], in1=t1[:, sl], op=ALU.mult),
                     reads=[(ack, tt), ("f_t1", tt)], writes=[(actk, tt)])
            if j % G == G - 1:
                idx = [(j - G + 1 + i) % (2 * G) for i in range(G)]
                pending = ([((lambda tt, i=i: act[i][:, tsl(tt)]), (lambda tt, i=i: ("f_act%d" % i, tt))) for i in idx],
                           [(wo[i], "f_wo%d" % i) for i in idx])
        outproj_group(C, *pending)
        S.barrier()


SLOPES = [2.0 ** (-8.0 * (h + 1) / 16.0) for h in range(16)]
BIGNEG = 30000.0
NPT = 4
NSM = 4


def gelu_ap(C, x, t, xk, tk):
    S = C.S
    S.op("scalar", lambda e: e.activation(out=t, in_=x, func=AF.Square), reads=[xk], writes=[tk])
    S.op("vector", lambda e: e.tensor_scalar(out=t, in0=t, scalar1=0.044715, scalar2=1.0, op0=ALU.mult, op1=ALU.add), reads=[tk], writes=[tk])
    S.op("vector", lambda e: e.tensor_tensor(out=t, in0=t, in1=x, op=ALU.mult), reads=[tk, xk], writes=[tk])
    S.op("scalar", lambda e: e.activation(out=t, in_=t, func=AF.Sigmoid, scale=GELU_K), reads=[tk], writes=[tk])
    S.op("vector", lambda e: e.tensor_tensor(out=x, in0=x, in1=t, op=ALU.mult), reads=[tk, xk], writes=[xk])


def nsa_mixer(C, gi):
    S, sb, nc = C.S, C.sb, C.nc
    C.nrot = 4
    C.nsa_deferred = []
    po_banks = [(C.psum[i], ("ps", i)) for i in (4, 5, 6, 7)]
    po_i = [0]

    def nextpo():
        r = po_banks[po_i[0] % 4]
        dl = C.nsa_deferred
        while any(r[1] in tags for _, tags in dl):
            dl.pop(0)[0]()
        po_i[0] += 1
        return r
    with contextlib.ExitStack() as ph:
        KCT = sb("n_KCT", [128, 4, 128], BF16, ph)
        VCa = sb("n_VCa", [128, 4, 97], BF16, ph)
        wch = [None, None]
        wci = [0]

        def alloc_wch(stack):
            for i in range(2):
                wch[i] = sb("n_wch%d" % i, [128, KC, 128], BF16, stack)

        def load_wch(idx):
            i = wci[0] % 2
            wci[0] += 1
            k = "n_wch%d" % i
            S.dma("gpsimd", k, lambda e: e.dma_start(out=wch[i][:], in_=C.nsa_wch[idx]), writes=[k])
            return wch[i], k

        with contextlib.ExitStack() as p1:
            alloc_wch(p1)
            KV0 = sb("n_KV0", [128, 4, NT], BF16, p1)
            W1 = sb("n_W1", [128, 2, 32, 256], BF16, p1)
            posT = sb("n_posT", [128, 2, 32], BF16, p1)
            b1 = sb("n_b1", [128, 2, 2], F32, p1)
            W2k = sb("n_W2k", [128, 2, 128], BF16, p1)
            W2v = sb("n_W2v", [128, 2, 64], BF16, p1)
            ovl = sb("n_ovl", [128, 33], F32, p1)
            hids = [sb("n_hid%d" % i, [128, 256], F32, p1) for i in range(2)]
            hscs = [sb("n_hsc%d" % i, [128, 256], F32, p1) for i in range(2)]
            ghbs = [sb("n_ghb%d" % i, [128, 2, 128], BF16, p1) for i in range(2)]
            cvec = sb("n_cvec", [1, 2, 256], BF16, p1)
            onesr = sb("n_onesr", [1, 128], BF16, p1)
            b1row = sb("n_b1row", [1, 2, 256], F32, p1)
            identc = sb("n_identc", [128, 128], F32, p1)
            S.dma("sync", "n_identc", lambda e: e.dma_start(out=identc[:], in_=C.c_ident), writes=["n_identc"])
            S.dma("sync", "n_b1row", lambda e: e.dma_start(out=b1row[:], in_=C.nsa_b1row), writes=["n_b1row"])
            S.op("vector", lambda e: e.memset(onesr[:], 1.0), writes=["n_onesr"])
            for kvi in range(2):
                for i0 in range(0, 32, 8):
                    S.dma("gpsimd", "n_W1_%d_%d" % (kvi, i0), lambda e: e.dma_start(out=W1[:, kvi, i0:i0 + 8, :], in_=C.nsa_w1[kvi, :, i0:i0 + 8, :]), writes=[("n_W1", kvi, i0)])
            S.dma("gpsimd", "n_posT", lambda e: e.dma_start(out=posT[:], in_=C.nsa_posT), writes=["n_posT"])
            S.dma("sync", "n_b1", lambda e: e.dma_start(out=b1[:], in_=C.nsa_b1), writes=["n_b1"])
            S.dma("gpsimd", "n_W2k", lambda e: e.dma_start(out=W2k[:], in_=C.nsa_w2k), writes=["n_W2k"])
            S.dma("gpsimd", "n_W2v", lambda e: e.dma_start(out=W2v[:], in_=C.nsa_w2v), writes=["n_W2v"])
            S.dma("sync", "n_ovl", lambda e: e.dma_start(out=ovl[:], in_=C.c_ovl), writes=["n_ovl"])
            wt0 = load_wch(0)
            rmsnorm(C, gi)
            S.op("vector", lambda e: e.memset(KCT[:], 0.0), writes=[("n_KCT", g) for g in range(4)])
            S.op("vector", lambda e: e.memset(VCa[:], 0.0), writes=[("n_VCa", g) for g in range(4)])
            for g in range(4):
                S.op("vector", lambda e: e.tensor_copy(out=VCa[:, g, 64:97], in_=ovl[:]), reads=["n_ovl"], writes=[("n_VCa", g)])
            for c4 in range(4):
                wt, wk = wt0 if c4 == 0 else load_wch(c4)
                for tt in range(NTT):
                    inproj(C, (wt, wk), 0, tt, lambda ps, pk: S.op(
                        "scalar", lambda e: e.activation(out=KV0[:, c4, tsl(tt)], in_=ps[:], func=AF.Identity), reads=[pk], writes=[("n_KV0", c4)]))
            for kvi in range(2):
                ps, pk = C.nextps()
                for i in range(32):
                    S.op("tensor", lambda e: e.matmul(ps[0:1, 0:256], lhsT=posT[0:64, kvi, i:i + 1], rhs=W1[0:64, kvi, i, :], start=(i == 0), stop=(i == 31)),
                         reads=[("n_W1", kvi, (i // 8) * 8), "n_posT"], writes=[pk])
                S.op("vector", lambda e: e.tensor_tensor(out=cvec[0:1, kvi, :], in0=ps[0:1, 0:256], in1=b1row[0:1, kvi, :], op=ALU.add),
                     reads=[pk, "n_b1row"], writes=[("n_cvec", kvi)])
            un = 0
            for kvi in range(2):
                for g in range(4):
                    c4 = kvi * 2 + g // 2
                    rows = slice((g % 2) * 64, (g % 2) * 64 + 64)
                    ub = un % 2
                    un += 1
                    hid, hsc, ghb = hids[ub], hscs[ub], ghbs[ub]
                    hk, sk, gk = "n_hid%d" % ub, "n_hsc%d" % ub, "n_ghb%d" % ub
                    ps, pk = C.nextps()
                    for i in range(32):
                        S.op("tensor", lambda e: e.matmul(ps[0:127, 0:256], lhsT=KV0[rows, c4, i:i + 2017:16], rhs=W1[rows, kvi, i, :], start=(i == 0), stop=False),
                             reads=[("n_W1", kvi, (i // 8) * 8), ("n_KV0", c4)], writes=[pk])
                    S.op("tensor", lambda e: e.matmul(ps[0:127, 0:256], lhsT=onesr[0:1, 0:127], rhs=cvec[0:1, kvi, :], start=False, stop=True),
                         reads=["n_onesr", ("n_cvec", kvi)], writes=[pk])
                    S.op("scalar", lambda e: e.activation(out=hid[0:127, :], in_=ps[0:127, 0:256], func=AF.Identity), reads=[pk], writes=[hk])
                    gelu_ap(C, hid[0:127, :], hsc[0:127, :], hk, sk)
                    ps, pk = C.nextps()
                    for mh in range(2):
                        S.op("tensor", lambda e: e.transpose(out=ps[:, mh * 128:mh * 128 + 127], in_=hid[0:127, mh * 128:(mh + 1) * 128], identity=identc[0:127, 0:127]),
                             reads=[hk, "n_identc"], writes=[pk])
                    S.op("scalar", lambda e: e.activation(out=ghb[:, :, 0:127], in_=ps[:, 0:256].rearrange("p (m c) -> p m c", c=128)[:, :, 0:127], func=AF.Identity),
                         reads=[pk], writes=[(gk, 0), (gk, 1)])
                    ps, pk = C.nextps()
                    if kvi == 0:
                        for mh in range(2):
                            S.op("tensor", lambda e: e.matmul(ps[:, 0:127], lhsT=W2k[:, mh, :], rhs=ghb[:, mh, 0:127], start=(mh == 0), stop=(mh == 1)),
                                 reads=["n_W2k", (gk, mh)], writes=[pk])
                        S.op("scalar", lambda e: e.activation(out=KCT[rows, g, 0:127], in_=ps[rows, 0:127], func=AF.Identity), reads=[pk], writes=[("n_KCT", g)])
                    else:
                        for mh in range(2):
                            S.op("tensor", lambda e: e.matmul(ps[0:127, 0:64], lhsT=ghb[:, mh, 0:127], rhs=W2v[:, mh, :], start=(mh == 0), stop=(mh == 1)),
                                 reads=["n_W2v", (gk, mh)], writes=[pk])
                        S.op("scalar", lambda e: e.activation(out=VCa[0:127, g, 0:64], in_=ps[0:127, 0:64], func=AF.Identity), reads=[pk], writes=[("n_VCa", g)])
            S.barrier()

        pA = ph.enter_context(contextlib.ExitStack())
        QT = sb("n_QT", [128, 8, NT], BF16, pA)
        Vaug = sb("n_Vaug", [128, 2, 16, 4, 65], BF16, pA)
        gts = sb("n_gts", [128, 16, 48], F32, pA)
        with contextlib.ExitStack() as p2:
            alloc_wch(p2)
            K12 = sb("n_K12", [128, 2, 2, NT], BF16, p2)
            wtok = sb("n_wtok", [128, KC, 560], BF16, p2)
            S.dma("gpsimd", "n_wtok", lambda e: e.dma_start(out=wtok[:], in_=C.nsa_wtok), writes=["n_wtok"])
            for mq in range(8 if getattr(C, 'nsa_stop', 9) >= 2 else 0):
                wt, wk = load_wch(4 + mq)
                for tt in range(NTT):
                    inproj(C, (wt, wk), 0, tt, lambda ps, pk: S.op(
                        "scalar", lambda e: e.activation(out=QT[:, mq, tsl(tt)], in_=ps[:], func=AF.Identity, scale=0.125), reads=[pk], writes=[("n_QT", mq, tt)]))
            for br in range(2):
                for c2 in range(2):
                    wt, wk = load_wch(12 + br * 2 + c2)
                    for tt in range(NTT):
                        inproj(C, (wt, wk), 0, tt, lambda ps, pk: S.op(
                            "scalar", lambda e: e.activation(out=K12[:, br, c2, tsl(tt)], in_=ps[:], func=AF.Identity), reads=[pk], writes=[("n_K12", br, c2, tt)]))
            S.op("vector", lambda e: e.memset(Vaug[:].rearrange("p a b c d -> p (a b c) d")[:, :, 64:65], 1.0), writes=["n_Vone"])
            for t16 in range(16 if getattr(C, 'nsa_stop', 9) >= 2 else 0):
                tok = slice(t16 * 128, (t16 + 1) * 128)
                ps, pk = C.nextps()
                for kc in range(KC):
                    S.op("tensor", lambda e: e.matmul(ps[:, 0:512], lhsT=C.xn[:, kc, tok], rhs=wtok[:, kc, 0:512], start=(kc == 0), stop=(kc == KC - 1)),
                         reads=["n_wtok", ("xn", kc, t16 // 4)], writes=[pk])
                for br in range(2):
                    S.op("scalar", lambda e: e.activation(out=Vaug[:, br, t16, :, 0:64], in_=ps[:, br * 256:(br + 1) * 256].rearrange("p (g d) -> p g d", d=64),
                                                          func=AF.Identity), reads=[pk], writes=[("n_Vaug", br, t16)])
                ps, pk = C.nextps()
                for kc in range(KC):
                    S.op("tensor", lambda e: e.matmul(ps[:, 0:48], lhsT=C.xn[:, kc, tok], rhs=wtok[:, kc, 512:560], start=(kc == 0), stop=(kc == KC - 1)),
                         reads=["n_wtok", ("xn", kc, t16 // 4)], writes=[pk])
                S.op("scalar", lambda e: e.activation(out=gts[:, t16, :], in_=ps[:, 0:48], func=AF.Sigmoid), reads=[pk], writes=[("n_gts", t16)])
            S.barrier()
            Kz = C.xn[:].rearrange("p (b g) t -> p b g t", b=2)
            for br in range(2):
                for g in range(4):
                    own = slice((g % 2) * 64, (g % 2) * 64 + 64)
                    oth = slice((1 - g % 2) * 64, (1 - g % 2) * 64 + 64)
                    S.op("gpsimd", lambda e: e.memset(Kz[oth, br, g, :], 0.0), writes=[("n_KzO", br, g)])
                    if g % 2 == 0:
                        S.op("scalar", lambda e: e.activation(out=Kz[own, br, g, :], in_=K12[own, br, g // 2, :], func=AF.Identity), writes=[("n_Kz", br, g)])
                    else:
                        S.op("vector", lambda e: e.tensor_copy(out=Kz[own, br, g, :], in_=K12[own, br, g // 2, :]), writes=[("n_Kz", br, g)])
            S.barrier()

        with contextlib.ExitStack() as p3:
            dtab = sb("n_dtab", [128, 9, TT], mybir.dt.int16, p3)
            etab = sb("n_etab", [128, 16, 128], BF16, p3)
            biasc = sb("n_biasc", [128, 16, 16], F32, p3)
            keep = sb("n_keep", [128, 16, 32], BF16, p3)
            addc = sb("n_addc", [128, 16, 32], BF16, p3)
            ident = sb("n_ident", [128, 128], F32, p3)
            sm = [sb("n_sm%d" % i, [128, TT], F32, p3) for i in range(NSM)]
            pT = [sb("n_pT%d" % i, [128, TT], BF16, p3) for i in range(NPT)]
            otoks = [sb("n_otok%d" % i, [128, 4, 256], F32, p3) for i in range(2)]
            valid = sb("n_valid", [128, 16, 1], F32, p3)
            otmp = sb("n_otmp", [128, 4, 64], F32, p3)
            rden = sb("n_rden", [128, 4, 1], F32, p3)
            ff = sb("n_ff", [128, 4, 1], F32, p3)
            comb = sb("n_comb", [128, 4, 65], F32, p3)
            fq = sb("n_fq", [128, 4, 16], F32, p3)
            S.dma("sync", "n_fq", lambda e: e.dma_start(out=fq[:], in_=C.c_fq), writes=["n_fq"])
            imp = sb("n_imp", [128, 4, 32], F32, p3)
            itmp = sb("n_itmp", [128, 4, 32], F32, p3)
            top8 = sb("n_top8", [128, 4, 8], F32, p3)
            selb = sb("n_selb", [128, 4, 32], F32, p3)
            selbT = sb("n_selbT", [128, 4, TT], BF16, p3)
            oTq = sb("n_oTq", [128, KC, TT], BF16, p3)
            alloc_wch(p3)
            S.dma("sync", "n_dtab", lambda e: e.dma_start(out=dtab[:, 0:8, :], in_=C.c_dtab[:, 0:8, :]), writes=["n_dtab"])
            S.op("gpsimd", lambda e: e.memset(etab[:], 0.0), writes=["n_etab"])
            S.op("gpsimd", lambda e: e.memset(selbT[:], 0.0), writes=[("n_selbT", g) for g in range(4)])
            S.dma("gpsimd", "n_etab", lambda e: e.dma_start(out=etab[0:32], in_=C.c_etab), writes=["n_etab"])
            S.dma("sync", "n_biasc", lambda e: e.dma_start(out=biasc[:], in_=C.c_biasc), writes=["n_biasc"])
            S.dma("gpsimd", "n_keep", lambda e: e.dma_start(out=keep[:], in_=C.c_keep), writes=["n_keep"])
            S.dma("gpsimd", "n_addc", lambda e: e.dma_start(out=addc[:], in_=C.c_addc), writes=["n_addc"])
            S.dma("sync", "n_ident", lambda e: e.dma_start(out=ident[:], in_=C.c_ident), writes=["n_ident"])
            S.dma("sync", "n_valid", lambda e: e.dma_start(out=valid[:], in_=C.c_valid), writes=["n_valid"])
            smi = [0]
            pti = [0]

            def score_tile(mm_fn, mm_reads, dti, scal, bias_ap):
                dkey = "n_dtabc" if dti == 8 else "n_dtab"
                i = smi[0] % NSM
                smi[0] += 1
                ps, pk = C.nextps()
                mm_fn(ps, pk)
                if dti is None:
                    smi[0] -= 1
                    jj = pti[0] % NPT
                    pti[0] += 1
                    S.op("scalar", lambda e: e.activation(out=pT[jj][:], in_=ps[:], func=AF.Exp, bias=bias_ap), reads=[pk, "n_biasc"], writes=[("n_pT", jj)])
                    return pT[jj], ("n_pT", jj)
                j = pti[0] % NPT
                pti[0] += 1
                S.op("vector", lambda e: e.scalar_tensor_tensor(out=sm[i][:], in0=dtab[:, dti, :], scalar=scal, in1=ps[:], op0=ALU.mult, op1=ALU.add),
                     reads=[pk, dkey], writes=[("n_sm", i)])
                if bias_ap is None:
                    S.op("scalar", lambda e: e.activation(out=pT[j][:], in_=sm[i][:], func=AF.Exp), reads=[("n_sm", i)], writes=[("n_pT", j)])
                else:
                    S.op("scalar", lambda e: e.activation(out=pT[j][:], in_=sm[i][:], func=AF.Exp, bias=bias_ap), reads=[("n_sm", i), "n_biasc"], writes=[("n_pT", j)])
                return pT[j], ("n_pT", j)

            def run_jobs(jobs, LA=NPT - 1):
                staged = []
                for idx in range(len(jobs) + LA):
                    if idx < len(jobs):
                        jb = jobs[idx]
                        staged.append(score_tile(jb["mm"], None, jb["dti"], jb["scal"], jb["bias"]))
                        if deferred:
                            deferred.pop(0)[0]()
                    k = idx - LA
                    if k >= 0:
                        jobs[k]["pv"](*staged[k])
                while deferred:
                    deferred.pop(0)[0]()

            deferred = C.nsa_deferred

            def accum_out(po, pok, ncol, otok, okey, r, gate_col, qt, first, src3=None, emit=None):
                if emit is None:
                    emit = lambda th: th()
                po3 = po[:, 0:4 * ncol].rearrange("p (s c) -> p s c", c=ncol) if src3 is None else src3
                if first:
                    emit(lambda: S.op("vector", lambda e: e.tensor_scalar(out=rden[:], in0=po3[:, :, 64:65], scalar1=1e-30, scalar2=None, op0=ALU.max), reads=[pok], writes=["n_rden"]))
                    emit(lambda: S.op("vector", lambda e: e.reciprocal(out=rden[:], in_=rden[:]), reads=["n_rden"], writes=["n_rden"]))
                else:
                    emit(lambda: S.op("vector", lambda e: e.reciprocal(out=rden[:], in_=po3[:, :, 64:65]), reads=[pok], writes=["n_rden"]))
                if first and qt == 0:
                    emit(lambda: S.op("vector", lambda e: e.tensor_tensor(out=rden[:], in0=rden[:], in1=valid[:, 0:4, :], op=ALU.mult), reads=["n_rden", "n_valid"], writes=["n_rden"]))
                emit(lambda: S.op("vector", lambda e: e.tensor_tensor(out=ff[:], in0=rden[:], in1=gts[:, qt * 4:(qt + 1) * 4, gate_col:gate_col + 1], op=ALU.mult),
                                  reads=["n_rden"] + [("n_gts", qt * 4 + i) for i in range(4)], writes=["n_ff"]))
                dst = otok[:, :, r * 64:(r + 1) * 64]
                if first:
                    emit(lambda: S.op("vector", lambda e: e.tensor_tensor(out=dst, in0=po3[:, :, 0:64], in1=ff[:].to_broadcast([128, 4, 64]), op=ALU.mult),
                                      reads=[pok, "n_ff"], writes=[(okey, r)]))
                else:
                    emit(lambda: S.op("vector", lambda e: e.tensor_tensor(out=otmp[:], in0=po3[:, :, 0:64], in1=ff[:].to_broadcast([128, 4, 64]), op=ALU.mult),
                                      reads=[pok, "n_ff"], writes=["n_otmp"]))
                    emit(lambda: S.op("vector", lambda e: e.tensor_tensor(out=dst, in0=dst, in1=otmp[:], op=ALU.add), reads=["n_otmp", (okey, r)], writes=[(okey, r)]))

            def group_ctx(qt, g):
                k = qt * 4 + g
                return tsl(qt), slice((g % 2) * 64, (g % 2) * 64 + 64), g // 2, otoks[k % 2], "n_otok%d" % (k % 2)

            def cmp_jobs(qt, g):
                qs, rows, c2, otok, okey = group_ctx(qt, g)
                if g == 0:
                    S.dma("sync", "n_dtabc", lambda e: e.dma_start(out=dtab[:, 8, :], in_=C.c_dtab[:, 9 + qt, :]), writes=["n_dtabc"])
                jobs = []
                dq = lambda th: deferred.append((th, ()))

                def select_ops():
                    dq(lambda: S.op("vector", lambda e: e.tensor_tensor(out=imp[:], in0=imp[:], in1=keep[:, qt * 4:(qt + 1) * 4, :], op=ALU.mult), reads=["n_imp", "n_keep"], writes=["n_imp"]))
                    dq(lambda: S.op("vector", lambda e: e.tensor_tensor(out=imp[:], in0=imp[:], in1=addc[:, qt * 4:(qt + 1) * 4, :], op=ALU.add), reads=["n_imp", "n_addc"], writes=["n_imp"]))
                    for sub in range(4):
                        dq(lambda sub=sub: S.op("vector", lambda e: e.max(out=top8[:, sub, :], in_=imp[:, sub, :]), reads=["n_imp"], writes=["n_top8"]))
                    for sub in range(4):
                        dq(lambda sub=sub: S.op("vector", lambda e: e.tensor_scalar(out=selb[:, sub, :], in0=imp[:, sub, :], scalar1=top8[:, sub, 7:8], scalar2=-BIGNEG, op0=ALU.is_lt, op1=ALU.mult),
                                                reads=["n_imp", "n_top8"], writes=["n_selb"]))

                    def tr():
                        ps, pk = C.nextps()
                        for sub in range(4):
                            S.op("tensor", lambda e: e.transpose(out=ps[0:32, sub * 128:(sub + 1) * 128], in_=selb[:, sub, :], identity=ident[:]),
                                 reads=["n_selb", "n_ident"], writes=[pk])
                        S.op("scalar", lambda e: e.activation(out=selbT[0:32, g, :], in_=ps[0:32, :], func=AF.Identity), reads=[pk], writes=[("n_selbT", g)])
                    dq(tr)

                for r in range(4):
                    hh = g * 4 + r
                    mq = (g // 2) * 4 + r

                    def mm(ps, pk, mq=mq):
                        S.op("tensor", lambda e: e.matmul(ps[:], lhsT=KCT[:, g, :], rhs=QT[:, mq, qs], start=True, stop=True),
                             reads=[("n_KCT", g), ("n_QT", mq, qt)], writes=[pk])

                    def pv(p_t, p_k, r=r, hh=hh):
                        po, pok = nextpo()
                        for sub in range(4):
                            S.op("tensor", lambda e: e.matmul(po[:, sub * 97:(sub + 1) * 97], lhsT=p_t[:, sub * 128:(sub + 1) * 128], rhs=VCa[:, g, :], start=True, stop=True),
                                 reads=[p_k, ("n_VCa", g)], writes=[pok])
                        accum_out(po, pok, 97, otok, okey, r, hh, qt, True)
                        po3 = po[:, 0:388].rearrange("p (s c) -> p s c", c=97)
                        if r == 0:
                            S.op("vector", lambda e: e.tensor_tensor(out=imp[:], in0=po3[:, :, 65:97], in1=rden[:].to_broadcast([128, 4, 32]), op=ALU.mult),
                                 reads=[pok, "n_rden"], writes=["n_imp"])
                        else:
                            S.op("vector", lambda e: e.tensor_tensor(out=itmp[:], in0=po3[:, :, 65:97], in1=rden[:].to_broadcast([128, 4, 32]), op=ALU.mult),
                                 reads=[pok, "n_rden"], writes=["n_itmp"])
                            S.op("vector", lambda e: e.tensor_tensor(out=imp[:], in0=imp[:], in1=itmp[:], op=ALU.add), reads=["n_itmp", "n_imp"], writes=["n_imp"])
                        if r == 3:
                            select_ops()
                    jobs.append(dict(mm=mm, dti=8, scal=-SLOPES[hh] / 2.0, bias=None, pv=pv))
                return jobs

            def selwin(qt, g, pre_jobs):
                qs, rows, c2, otok, okey = group_ctx(qt, g)
                jobs = []
                for r in range(4):
                    hh = g * 4 + r
                    mq = (g // 2) * 4 + r
                    for br in range(2):
                        kts = list(range(0, qt * 4 + 4)) if br == 0 else list(range(max(0, qt * 4 - 4), qt * 4 + 4))
                        state = {}
                        for n_k, kt in enumerate(kts):
                            delta = qt * TT - kt * 128
                            bias_ap = None
                            far = False
                            if delta <= 0:
                                dti = (-delta) // 128
                            elif br == 1:
                                dti = 3 + delta // 128
                            else:
                                dti = None
                                far = True
                                bias_ap = biasc[:, hh, delta // 128:delta // 128 + 1]
                            nfar = qt * 4 if br == 0 else 0

                            def mm(ps, pk, br=br, kt=kt, mq=mq):
                                ks = slice(kt * 128, (kt + 1) * 128)
                                S.op("tensor", lambda e: e.matmul(ps[:], lhsT=Kz[:, br, g, ks], rhs=QT[:, mq, qs], start=True, stop=(br == 1)),
                                     reads=[("n_Kz", br, g), ("n_KzO", br, g), ("n_QT", mq, qt)], writes=[pk])
                                if br == 0:
                                    S.op("tensor", lambda e: e.matmul(ps[:], lhsT=etab[:, kt, :], rhs=selbT[:, g, :], start=False, stop=True),
                                         reads=["n_etab", ("n_selbT", g)], writes=[pk])

                            def pv(p_t, p_k, br=br, kt=kt, n_k=n_k, nk=len(kts), state=state, r=r, hh=hh, far=far, nfar=nfar):
                                if n_k == 0 and nfar:
                                    state["far"] = nextpo()
                                if n_k == nfar:
                                    state["po"] = nextpo()
                                po, pok = state["far"] if far else state["po"]
                                first = (n_k == 0) if far else (n_k == nfar)
                                last = (n_k == nfar - 1) if far else (n_k == nk - 1)
                                for sub in range(4):
                                    S.op("tensor", lambda e: e.matmul(po[:, sub * 65:(sub + 1) * 65], lhsT=p_t[:, sub * 128:(sub + 1) * 128], rhs=Vaug[:, br, kt, g, :],
                                                                      start=(first and sub == 0), stop=(last and sub == 3)),
                                         reads=[p_k, ("n_Vaug", br, kt), "n_Vone"], writes=[pok])
                                if n_k == nk - 1:
                                    if nfar:
                                        pf, pfk = state["far"]
                                        pf3 = pf[:, 0:260].rearrange("p (s c) -> p s c", c=65)
                                        po3 = po[:, 0:260].rearrange("p (s c) -> p s c", c=65)
                                        S.op("vector", lambda e: e.tensor_tensor(out=comb[:], in0=pf3, in1=fq[:, :, hh:hh + 1].to_broadcast([128, 4, 65]), op=ALU.mult),
                                             reads=[pfk, "n_fq"], writes=["n_comb"])
                                        S.op("vector", lambda e: e.tensor_tensor(out=comb[:], in0=comb[:], in1=po3, op=ALU.add), reads=[pok, "n_comb"], writes=["n_comb"])
                                        accum_out(po, "n_comb", 65, otok, okey, r, (1 + br) * 16 + hh, qt, False, src3=comb[:])
                                    else:
                                        accum_out(po, pok, 65, otok, okey, r, (1 + br) * 16 + hh, qt, False)
                            jobs.append(dict(mm=mm, dti=dti, scal=-SLOPES[hh], bias=bias_ap, pv=pv))
                run_jobs(pre_jobs + jobs)
                for kk in range(2):
                    kc = 2 * g + kk
                    ps, pk = C.nextps()
                    for sub in range(4):
                        S.op("tensor", lambda e: e.transpose(out=ps[:, sub * 128:(sub + 1) * 128], in_=otok[:, sub, kk * 128:(kk + 1) * 128], identity=ident[:]),
                             reads=[(okey, 2 * kk), (okey, 2 * kk + 1), "n_ident"], writes=[pk])
                    S.op("scalar", lambda e: e.activation(out=oTq[:, kc, :], in_=ps[:], func=AF.Identity), reads=[pk], writes=[("n_oTq", kc)])
                if g == 3:
                    for m in range(KC):
                        wt, wk = load_wch(16 + m)
                        ps, pk = C.nextps()
                        for kc in range(KC):
                            S.op("tensor", lambda e: e.matmul(ps[:], lhsT=wt[:, kc, :], rhs=oTq[:, kc, :], start=(kc == 0), stop=(kc == KC - 1)),
                                 reads=[wk, ("n_oTq", kc)], writes=[pk])
                        S.op("vector", lambda e: e.tensor_tensor(out=C.hT[:, m, qs], in0=C.hT[:, m, qs], in1=ps[:], op=ALU.add),
                             reads=[pk, ("hT", m, qt)], writes=[("hT", m, qt)])

            steps = [(qt, g) for qt in range(NTT if getattr(C, 'nsa_stop', 9) >= 3 else 0) for g in range(4)]
            if steps:
                run_jobs(cmp_jobs(*steps[0]))
            for k, (qt, g) in enumerate(steps):
                selwin(qt, g, cmp_jobs(*steps[k + 1]) if k + 1 < len(steps) else [])
            S.barrier()
        pA.close()
        S.barrier()
    C.nrot = 8


def prep_weights(inp):
    f = np.ascontiguousarray
    w = {}
    g = np.stack([inp["lru_norm_g"][0], inp["ffn_norm_g"][0], inp["nsa_norm_g"][0], inp["ffn_norm_g"][1], inp["final_norm_g"]], 0)
    w["gains"] = f(g.reshape(5, KC, 128).transpose(2, 0, 1))
    wi = inp["lru_w_in"][0].reshape(KC, 128, 2, LN, 128)
    w["lru_win"] = f(wi.transpose(3, 1, 0, 2, 4).reshape(LN, 128, KC, 256))
    w["lru_gw"] = f(inp["lru_gate_w"][0].transpose(1, 2, 0, 3))
    w["lru_wout"] = f(inp["lru_w_out"][0].reshape(LN, 128, D))
    vec = np.concatenate([inp["lru_conv_w"][0], inp["lru_conv_b"][0][None], inp["lru_gate_b"][0], inp["lru_a_param"][0][None]], 0)
    w["lru_vec"] = f(vec.reshape(8, LN, 128).transpose(2, 1, 0))
    fw = inp["ffn_w_in"].reshape(2, KC, 128, 2, FJ, 128)
    w["ffn_win"] = f(fw.transpose(0, 4, 2, 1, 3, 5).reshape(2, FJ, 128, KC, 256))
    w["ffn_wout"] = f(inp["ffn_w_out"].reshape(2, FJ, 128, D))
    fv = np.concatenate([inp["ffn_conv_w"], inp["ffn_conv_b"][:, None]], 1)
    w["ffn_vec"] = f(fv.reshape(2, 4, FJ, 128).transpose(3, 0, 2, 1))
    w.update(nsa_host(inp))
    w.update(const_tables())
    return w


def nsa_host(inp):
    f = np.ascontiguousarray
    w = {}
    W = inp["nsa_w_in"][0]
    Wr = W.reshape(KC, 128, 2608)

    def chunk(cols):
        return Wr[:, :, cols].transpose(1, 0, 2)
    chunks = []
    for c4 in range(4):
        kvi, gp = c4 // 2, c4 % 2
        base = 1024 + kvi * 256 + gp * 128
        chunks.append(chunk(np.arange(base, base + 128)))
    for mq in range(8):
        p, r = mq // 4, mq % 4
        ha, hb = 4 * (2 * p) + r, 4 * (2 * p + 1) + r
        chunks.append(chunk(np.concatenate([np.arange(ha * 64, ha * 64 + 64), np.arange(hb * 64, hb * 64 + 64)])))
    for br in (1, 2):
        for c2 in range(2):
            base = 1024 + br * 512 + c2 * 128
            chunks.append(chunk(np.arange(base, base + 128)))
    Wo = inp["nsa_w_out"][0].reshape(KC, 128, D)
    for m in range(KC):
        chunks.append(Wo[:, :, m * 128:(m + 1) * 128].transpose(1, 0, 2))
    w["nsa_wch"] = f(np.stack(chunks, 0))
    tokcols = np.concatenate([np.arange(1024 + 512 + 256, 1024 + 512 + 512), np.arange(1024 + 1024 + 256, 1024 + 1024 + 512), np.arange(2560, 2608)])
    w["nsa_wtok"] = f(Wr[:, :, tokcols].transpose(1, 0, 2))
    w1 = inp["nsa_cmp_w1"][0].reshape(2, 32, 64, 256).transpose(0, 2, 1, 3)
    w["nsa_w1"] = f(np.concatenate([w1, w1], 1))
    pT = inp["nsa_cmp_pos"][0].transpose(2, 0, 1)
    w["nsa_posT"] = f(np.concatenate([pT, pT], 0))
    w["nsa_b1"] = f(inp["nsa_cmp_b1"][0].reshape(2, 2, 128).transpose(2, 0, 1))
    w["nsa_b1row"] = f(inp["nsa_cmp_b1"][0][None])
    w2 = inp["nsa_cmp_w2"][0]
    w2k = w2[0].reshape(2, 128, 64).transpose(1, 0, 2)
    w["nsa_w2k"] = f(np.concatenate([w2k, w2k], 2))
    w["nsa_w2v"] = f(w2[1].reshape(2, 128, 64).transpose(1, 0, 2))
    return w


def const_tables():
    c = {}
    HUGE = 30000
    k = np.arange(128)[:, None]
    q = np.arange(TT)[None, :]
    dt = np.zeros((128, 13, TT), np.int64)
    for i in range(4):
        d = -128 * i + q - k
        dt[:, i] = np.where(d >= 0, d, HUGE)
    for i in range(1, 5):
        d = 128 * i + q - k
        dt[:, 3 + i] = np.where(d < 512, d, HUGE)
    dt[:, 8] = q - k
    cc = np.arange(128)[:, None]
    for qt in range(4):
        t = qt * TT + q
        d2 = 2 * t - 32 * cc - 31
        ok = (16 * cc + 31 <= t) & (cc < 127)
        dt[:, 9 + qt] = np.where(ok, d2, HUGE)
    c["c_dtab"] = dt.astype(np.int16)
    et = np.zeros((32, 16, 128), np.float32)
    for kt in range(16):
        for kk in range(128):
            et[(kt * 128 + kk) // 64, kt, kk] = 1.0
    c["c_etab"] = et
    sl = np.array(SLOPES, np.float64)
    kk = np.arange(128, dtype=np.float64)[:, None, None]
    bc = -(sl[None, :, None] * ((128.0 * np.arange(16))[None, None, :] - kk))
    c["c_biasc"] = np.ascontiguousarray(bc).astype(np.float32)
    qq = (np.arange(4)[None, :, None] * 128 + np.arange(128)[:, None, None]).astype(np.float64)
    c["c_fq"] = np.exp(-sl[None, None, :] * qq).astype(np.float32)
    t = (np.arange(16)[None, :, None] * 128 + np.arange(128)[:, None, None])
    j = np.arange(32)[None, None, :]
    cur = t // 64
    forced = (j == 0) | (j == cur) | (j == cur - 1)
    future = j > cur
    c["c_keep"] = np.where(forced | future, 0.0, 1.0).astype(np.float32)
    c["c_addc"] = np.where(forced, 1e4, np.where(future, -1.0, 0.0)).astype(np.float32)
    c["c_ident"] = np.eye(128, dtype=np.float32)
    c["c_valid"] = (t >= 31).astype(np.float32)
    ov = np.zeros((128, 33), np.float32)
    ov[:127, 0] = 1.0
    cs = np.arange(127)[:, None] * 16
    sj = np.arange(32)[None, :]
    ov[:127, 1:] = ((cs < (sj + 1) * 64) & (cs + 32 > sj * 64)).astype(np.float32)
    c["c_ovl"] = ov
    return c


_CACHE = {}


def kernel(**inp):
    inp = {k: np.asarray(v) for k, v in inp.items()}
    ncores, nseq = 8, 2
    if "nc" not in _CACHE:
        _CACHE["nc"] = build(nseq)[0]
    nc = _CACHE["nc"]
    w = prep_weights(inp)
    x = inp["x"]
    xT = np.ascontiguousarray(x.reshape(ncores, nseq, NT, KC, 128).transpose(0, 1, 4, 3, 2))
    in_maps = [dict(w, xT=xT[c]) for c in range(ncores)]
    res = run_bass_kernel_spmd(nc, in_maps, core_ids=list(range(ncores)))
    o = np.stack([r["outT"] for r in res.results], 0)
    return np.ascontiguousarray(o.transpose(0, 1, 4, 3, 2)).reshape(16, NT, D).astype(np.float32)
```

```python
import contextlib
import numpy as np
import concourse.bass as bass
import concourse.mybir as mybir
from concourse.bass_utils import run_bass_kernel_spmd

F32 = mybir.dt.float32
BF16 = mybir.dt.bfloat16
AF = mybir.ActivationFunctionType
ALU = mybir.AluOpType
AX = mybir.AxisListType

D = 1024
KC = 8
NT = 2048
TT = 512
NTT = 4
LW = 1280
LN = 10
DFF = 3072
FJ = 24
EPS = 1e-6
GELU_K = 1.5957691216057308
GELU_C = 0.044715 ** 0.5


class Sched:
    ENGS = ("tensor", "vector", "scalar", "gpsimd", "sync")

    def __init__(self, nc, stack):
        self.nc = nc
        self.stack = stack
        self.eng = {e: getattr(nc, e) for e in self.ENGS}
        self.sem = {}
        self.cnt = {}
        self.known = {e: {} for e in self.ENGS}
        self.res = {}
        self.n_inst = 0
        self.n_wait = 0
        for e in ("tensor", "vector", "scalar", "gpsimd"):
            self._mksem(e)

    def _mksem(self, key):
        if key not in self.sem:
            name = "s_" + key.replace(":", "_")
            self.sem[key] = self.stack.enter_context(self.nc.semaphore(name))
            self.cnt[key] = 0
        return self.sem[key]

    def _deps(self, engine, reads, writes):
        deps = {}

        def add(k, v):
            if v > deps.get(k, 0):
                deps[k] = v
        for r in reads:
            st = self.res.get(r)
            if st and st["w"]:
                add(*st["w"])
        for w in writes:
            st = self.res.get(w)
            if st:
                if st["w"]:
                    add(*st["w"])
                for k, v in st["r"].items():
                    add(k, v)
        kn = self.known[engine]
        for k, v in deps.items():
            if k == engine and engine == "tensor":
                continue
            if kn.get(k, 0) >= v:
                continue
            self.eng[engine].wait_ge(self.sem[k], v)
            self.n_wait += 1
            kn[k] = v

    def _mark(self, key, val, reads, writes):
        for r in reads:
            st = self.res.setdefault(r, {"w": None, "r": {}})
            st["r"][key] = val
        for w in writes:
            self.res[w] = {"w": (key, val), "r": {}}

    def op(self, engine, fn, reads=(), writes=()):
        self._deps(engine, reads, writes)
        ins = fn(self.eng[engine])
        self.cnt[engine] += 1
        ins.then_inc(self.sem[engine], 1)
        self.n_inst += 1
        self._mark(engine, self.cnt[engine], reads, writes)
        return ins

    def dma(self, queue, key, fn, reads=(), writes=()):
        k = "dma:" + key
        self._mksem(k)
        self._deps(queue, reads, writes)
        ins = fn(self.eng[queue])
        self.cnt[k] += 16
        ins.then_inc(self.sem[k], 16)
        self.n_inst += 1
        self._mark(k, self.cnt[k], reads, writes)
        return ins

    def barrier(self):
        for e in self.ENGS:
            for k, v in self.cnt.items():
                if v == 0 or (k == e and e == "tensor"):
                    continue
                if self.known[e].get(k, 0) >= v:
                    continue
                self.eng[e].wait_ge(self.sem[k], v)
                self.known[e][k] = v

    def finish(self):
        for k, v in self.cnt.items():
            if v and self.known["sync"].get(k, 0) < v:
                self.nc.sync.wait_ge(self.sem[k], v)
                self.known["sync"][k] = v


class Ctx:
    pass


def tsl(tt):
    return slice(tt * TT, (tt + 1) * TT)


def build(nseq=2, upto=99, nsa_stop=9):
    nc = bass.Bass("TRN2", target_bir_lowering=False)
    C = Ctx()
    C.nc = nc
    C.nsa_stop = nsa_stop

    def din(name, shape):
        return nc.dram_tensor(name, list(shape), F32, kind="ExternalInput").ap()
    C.xT = din("xT", [nseq, 128, KC, NT])
    C.gains = din("gains", [128, 5, KC])
    C.lru_win = din("lru_win", [LN, 128, KC, 256])
    C.lru_gw = din("lru_gw", [LN, 128, 2, 128])
    C.lru_wout = din("lru_wout", [LN, 128, D])
    C.lru_vec = din("lru_vec", [128, LN, 8])
    C.ffn_win = din("ffn_win", [2, FJ, 128, KC, 256])
    C.ffn_wout = din("ffn_wout", [2, FJ, 128, D])
    C.ffn_vec = din("ffn_vec", [128, 2, FJ, 4])
    C.nsa_wch = din("nsa_wch", [24, 128, KC, 128])
    C.nsa_wtok = din("nsa_wtok", [128, KC, 560])
    C.nsa_w1 = din("nsa_w1", [2, 128, 32, 256])
    C.nsa_posT = din("nsa_posT", [128, 2, 32])
    C.nsa_b1 = din("nsa_b1", [128, 2, 2])
    C.nsa_b1row = din("nsa_b1row", [1, 2, 256])
    C.nsa_w2k = din("nsa_w2k", [128, 2, 128])
    C.nsa_w2v = din("nsa_w2v", [128, 2, 64])
    C.c_dtab = nc.dram_tensor("c_dtab", [128, 13, TT], mybir.dt.float16, kind="ExternalInput").ap()
    C.c_etab = din("c_etab", [32, 16, 128])
    C.c_biasc = din("c_biasc", [128, 16, 16])
    C.c_keep = din("c_keep", [128, 16, 32])
    C.c_addc = din("c_addc", [128, 16, 32])
    C.c_ident = din("c_ident", [128, 128])
    C.c_valid = din("c_valid", [128, 16, 1])
    C.c_fq = din("c_fq", [128, 4, 16])
    C.c_ovl = din("c_ovl", [128, 33])
    C.outT = nc.dram_tensor("outT", [nseq, 128, KC, NT], F32, kind="ExternalOutput").ap()

    with contextlib.ExitStack() as st:
        S = Sched(nc, st)
        C.S = S

        uid = [0]

        def sb(name, shape, dt=F32, stack=st):
            uid[0] += 1
            return stack.enter_context(nc.sbuf_tensor("%s_u%d" % (name, uid[0]), list(shape), dt))
        C.sb = sb
        C.hT = sb("hT", [128, KC, NT])
        C.xn = sb("xn", [128, KC, NT], BF16)
        C.ones = sb("ones", [128, 128], BF16)
        C.gn = sb("gn", [128, 5, KC])
        C.lvec = sb("lvec", [128, LN, 8])
        C.lca = sb("lca", [128, LN, 2])
        C.fvec = sb("fvec", [128, 2, FJ, 4])
        C.psum = [st.enter_context(nc.psum_tensor("ps%d" % i, [128, TT], F32)) for i in range(8)]
        C.psi = 0
        C.nrot = 8

        def nextps():
            i = C.psi % C.nrot
            C.psi += 1
            return C.psum[i], ("ps", i)
        C.nextps = nextps

        S.op("vector", lambda e: e.memset(C.ones[:], 1.0), writes=["ones"])
        C.epsc = sb("epsc", [128, 1])
        S.op("vector", lambda e: e.memset(C.epsc[:], EPS), writes=["epsc"])
        S.dma("sync", "c0", lambda e: e.dma_start(out=C.gn[:], in_=C.gains), writes=["gn"])
        S.dma("sync", "c1", lambda e: e.dma_start(out=C.lvec[:], in_=C.lru_vec), writes=["lvec"])
        S.dma("sync", "c2", lambda e: e.dma_start(out=C.fvec[:], in_=C.ffn_vec), writes=["fvec"])
        lru_consts(C)

        for s in range(nseq):
            C.cur_seq = s
            for tt in range(NTT):
                S.dma("sync", "x%d" % tt, lambda e, tt=tt: e.dma_start(out=C.hT[:, :, tsl(tt)], in_=C.xT[s, :, :, tsl(tt)]),
                      writes=[("hT", kc, tt) for kc in range(KC)])
            if upto >= 1:
                lru_mixer(C, 0)
            if upto >= 2:
                conv_ffn(C, 0, 1)
            if upto >= 3:
                nsa_mixer(C, 2)
            if upto >= 4:
                conv_ffn(C, 1, 3)
            if upto >= 5:
                rmsnorm(C, 4, final=True)
            if upto < 5:
                for tt in range(NTT):
                    S.dma("sync", "o%d" % tt, lambda e, tt=tt: e.dma_start(out=C.outT[s, :, :, tsl(tt)], in_=C.hT[:, :, tsl(tt)]),
                          reads=[("hT", kc, tt) for kc in range(KC)])
        S.finish()
    C.n_inst = S.n_inst
    C.n_wait = S.n_wait
    return nc, C


def lru_consts(C):
    S, sb = C.S, C.sb
    with contextlib.ExitStack() as ph:
        t = [sb("lc%d" % i, [128, LN], F32, ph) for i in range(6)]
        ap = C.lvec[:, :, 7]
        S.op("scalar", lambda e: e.activation(out=t[0][:], in_=ap, func=AF.Abs), reads=["lvec"], writes=["lc0"])
        S.op("scalar", lambda e: e.activation(out=t[1][:], in_=t[0][:], func=AF.Exp, scale=-1.0), reads=["lc0"], writes=["lc1"])
        S.op("scalar", lambda e: e.activation(out=t[2][:], in_=t[1][:], func=AF.Ln, bias=1.0), reads=["lc1"], writes=["lc2"])
        S.op("vector", lambda e: e.tensor_scalar(out=t[3][:], in0=t[1][:], scalar1=1.0 / 3.0, scalar2=-0.5, op0=ALU.mult, op1=ALU.add), reads=["lc1"], writes=["lc3"])
        S.op("vector", lambda e: e.tensor_tensor(out=t[3][:], in0=t[3][:], in1=t[1][:], op=ALU.mult), reads=["lc3", "lc1"], writes=["lc3"])
        S.op("vector", lambda e: e.tensor_scalar(out=t[3][:], in0=t[3][:], scalar1=1.0, scalar2=None, op0=ALU.add), reads=["lc3"], writes=["lc3"])
        S.op("vector", lambda e: e.tensor_tensor(out=t[3][:], in0=t[3][:], in1=t[1][:], op=ALU.mult), reads=["lc3", "lc1"], writes=["lc3"])
        S.op("vector", lambda e: e.tensor_single_scalar(out=t[4][:], in_=t[1][:], scalar=0.03, op=ALU.is_lt), reads=["lc1"], writes=["lc4"])
        S.op("vector", lambda e: e.tensor_tensor(out=t[3][:], in0=t[3][:], in1=t[2][:], op=ALU.subtract), reads=["lc3", "lc2"], writes=["lc3"])
        S.op("vector", lambda e: e.tensor_tensor(out=t[3][:], in0=t[3][:], in1=t[4][:], op=ALU.mult), reads=["lc3", "lc4"], writes=["lc3"])
        S.op("vector", lambda e: e.tensor_tensor(out=t[3][:], in0=t[3][:], in1=t[2][:], op=ALU.add), reads=["lc3", "lc2"], writes=["lc3"])
        S.op("vector", lambda e: e.tensor_scalar(out=t[5][:], in0=ap, scalar1=-1.0, scalar2=0.0, op0=ALU.mult, op1=ALU.max), reads=["lvec"], writes=["lc5"])
        S.op("vector", lambda e: e.tensor_tensor(out=t[3][:], in0=t[3][:], in1=t[5][:], op=ALU.add), reads=["lc3", "lc5"], writes=["lc3"])
        S.op("vector", lambda e: e.tensor_scalar(out=C.lca[:, :, 0], in0=t[3][:], scalar1=-8.0, scalar2=None, op0=ALU.mult), reads=["lc3"], writes=["lca"])
        S.op("vector", lambda e: e.tensor_scalar(out=C.lca[:, :, 1], in0=t[3][:], scalar1=-16.0, scalar2=None, op0=ALU.mult), reads=["lc3"], writes=["lca"])
        S.barrier()


def rmsnorm(C, gi, final=False):
    S = C.S
    ph = contextlib.ExitStack()
    C.sq = [C.sb("sq%d" % i, [128, TT], BF16, ph) for i in range(4)]
    C.rs = C.sb("rs", [128, TT], F32, ph)
    for tt in range(NTT):
        ps, pk = C.nextps()
        for kc in range(KC):
            sq = C.sq[kc % 4]
            S.op("scalar", lambda e: e.activation(out=sq[:], in_=C.hT[:, kc, tsl(tt)], func=AF.Square),
                 reads=[("hT", kc, tt)], writes=[("sq", kc % 4)])
            S.op("tensor", lambda e: e.matmul(ps[:], lhsT=C.ones[:], rhs=sq[:], start=(kc == 0), stop=(kc == KC - 1)),
                 reads=["ones", ("sq", kc % 4)], writes=[pk])
        S.op("scalar", lambda e: e.activation(out=C.rs[:], in_=ps[:], func=AF.Ln, scale=1.0 / D, bias=C.epsc[:]), reads=[pk, "epsc"], writes=["rs"])
        S.op("scalar", lambda e: e.activation(out=C.rs[:], in_=C.rs[:], func=AF.Exp, scale=-0.5), reads=["rs"], writes=["rs"])
        for kc in range(KC):
            if final:
                S.op("vector", lambda e: e.scalar_tensor_tensor(out=C.hT[:, kc, tsl(tt)], in0=C.hT[:, kc, tsl(tt)], scalar=C.gn[:, gi, kc:kc + 1],
                                                               in1=C.rs[:], op0=ALU.mult, op1=ALU.mult),
                     reads=[("hT", kc, tt), "rs", "gn"], writes=[("hT", kc, tt)])
            else:
                S.op("vector", lambda e: e.scalar_tensor_tensor(out=C.xn[:, kc, tsl(tt)], in0=C.hT[:, kc, tsl(tt)], scalar=C.gn[:, gi, kc:kc + 1],
                                                               in1=C.rs[:], op0=ALU.mult, op1=ALU.mult),
                     reads=[("hT", kc, tt), "rs", "gn"], writes=[("xn", kc, tt)])
        if final:
            S.dma("sync", "o%d" % tt, lambda e: e.dma_start(out=C.outT[C.cur_seq, :, :, tsl(tt)], in_=C.hT[:, :, tsl(tt)]),
                  reads=[("hT", kc, tt) for kc in range(KC)])
    S.barrier()
    ph.close()


def inproj(C, w, col0, tt, evac):
    S = C.S
    ps, pk = C.nextps()
    wt, wk = w
    for kc in range(KC):
        S.op("tensor", lambda e: e.matmul(ps[:], lhsT=wt[:, kc, col0:col0 + 128], rhs=C.xn[:, kc, tsl(tt)], start=(kc == 0), stop=(kc == KC - 1)),
             reads=[wk, ("xn", kc, tt)], writes=[pk])
    evac(ps, pk)


def gelu_inplace(C, x, xk, t1, t1k, tt):
    S = C.S
    sl = tsl(tt)
    S.op("scalar", lambda e: e.activation(out=t1[:, sl], in_=x[:, sl], func=AF.Square), reads=[(xk, tt)], writes=[(t1k, tt)])
    S.op("vector", lambda e: e.tensor_scalar(out=t1[:, sl], in0=t1[:, sl], scalar1=0.044715, scalar2=1.0, op0=ALU.mult, op1=ALU.add),
         reads=[(t1k, tt)], writes=[(t1k, tt)])
    S.op("vector", lambda e: e.tensor_tensor(out=t1[:, sl], in0=t1[:, sl], in1=x[:, sl], op=ALU.mult), reads=[(t1k, tt), (xk, tt)], writes=[(t1k, tt)])
    S.op("scalar", lambda e: e.activation(out=t1[:, sl], in_=t1[:, sl], func=AF.Sigmoid, scale=GELU_K), reads=[(t1k, tt)], writes=[(t1k, tt)])
    S.op("vector", lambda e: e.tensor_tensor(out=x[:, sl], in0=x[:, sl], in1=t1[:, sl], op=ALU.mult), reads=[(t1k, tt), (xk, tt)], writes=[(xk, tt)])


def outproj_group(C, acts, wouts):
    S = C.S
    n = len(acts)
    for tt in range(NTT):
        for m in range(KC):
            ps, pk = C.nextps()
            for i in range(n):
                a_ap, a_k = acts[i]
                wt, wk = wouts[i]
                S.op("tensor", lambda e: e.matmul(ps[:], lhsT=wt[:, m * 128:(m + 1) * 128], rhs=a_ap(tt), start=(i == 0), stop=(i == n - 1)),
                     reads=[wk, a_k(tt)], writes=[pk])
            S.op("vector", lambda e: e.tensor_tensor(out=C.hT[:, m, tsl(tt)], in0=C.hT[:, m, tsl(tt)], in1=ps[:], op=ALU.add),
                 reads=[pk, ("hT", m, tt)], writes=[("hT", m, tt)])


def lru_mixer(C, gi):
    S, sb, nc = C.S, C.sb, C.nc
    G = 2
    HT = NT // 2
    with contextlib.ExitStack() as ph:
        win = [sb("l_win%d" % i, [128, KC, 256], BF16, ph) for i in range(2)]
        gw = [sb("l_gw%d" % i, [128, 2, 128], BF16, ph) for i in range(2)]
        wo = [sb("l_wo%d" % i, [128, D], BF16, ph) for i in range(2 * G)]

        def load_w(n):
            b = n % 2
            S.dma("gpsimd", "l_win%d" % b, lambda e: e.dma_start(out=win[b][:], in_=C.lru_win[n]), writes=["l_win%d" % b])
            S.dma("gpsimd", "l_gw%d" % b, lambda e: e.dma_start(out=gw[b][:], in_=C.lru_gw[n]), writes=["l_gw%d" % b])
            S.dma("gpsimd", "l_wo%d" % (n % (2 * G)), lambda e: e.dma_start(out=wo[n % (2 * G)][:], in_=C.lru_wout[n]), writes=["l_wo%d" % (n % (2 * G))])
        load_w(0)
        rmsnorm(C, gi)
        hy = [sb("l_hy%d" % i, [128, NT], BF16, ph) for i in range(2 * G)]
        ysbs = [sb("l_ysb%d" % i, [128, NT], F32, ph) for i in range(2)]
        xpads = [sb("l_xpad%d" % i, [128, 3 + NT], F32, ph) for i in range(2)]
        t1s = [sb("l_t1%d" % i, [128, HT], F32, ph) for i in range(2)]
        xcs = [sb("l_xc%d" % i, [128, HT], F32, ph) for i in range(2)]
        xcbs = [sb("l_xcb%d" % i, [128, HT], BF16, ph) for i in range(2)]
        rrs = [sb("l_r%d" % i, [128, HT], F32, ph) for i in range(2)]
        igs = [sb("l_ig%d" % i, [128, HT], F32, ph) for i in range(2)]
        hhs = [sb("l_h%d" % i, [128, HT], F32, ph) for i in range(2)]
        for i in range(2):
            S.op("vector", lambda e: e.memset(xpads[i][:, 0:3], 0.0), writes=["l_xpad%d_0" % i])

        def proj(n, u, k):
            b = n % 2
            q = k % 2
            wi, wik = win[b], "l_win%d" % b
            ysb, xpad = ysbs[b], xpads[b]

            def evac_x(ps, pk, tt):
                S.op("scalar", lambda e: e.activation(out=xpad[:, 3 + tt * TT:3 + (tt + 1) * TT], in_=ps[:], func=AF.Identity),
                     reads=[pk], writes=[("l_xpad%d" % b, tt)])
                S.op("scalar", lambda e: e.activation(out=xcs[q][:, (tt - 2 * u) * TT:(tt - 2 * u + 1) * TT], in_=ps[:], func=AF.Identity,
                                                      scale=C.lvec[:, n, 3:4], bias=C.lvec[:, n, 4:5]),
                     reads=[pk, "lvec"], writes=[("l_xc%d" % q, tt)])
            for tt in (2 * u, 2 * u + 1):
                inproj(C, (wi, wik), 128, tt, lambda ps, pk: evac_x(ps, pk, tt))
            for tt in (2 * u, 2 * u + 1):
                inproj(C, (wi, wik), 0, tt, lambda ps, pk: S.op(
                    "scalar", lambda e: e.activation(out=ysb[:, tsl(tt)], in_=ps[:], func=AF.Identity), reads=[pk], writes=[("l_ysb%d" % b, tt)]))

        units = [(n, u) for n in range(LN) for u in range(2)]
        def conv_unit(k):
            n, u = units[k]
            b = n % 2
            q = k % 2
            xpad, xk = xpads[b], "l_xpad%d" % b
            xc, xcb = xcs[q], xcbs[q]
            xck, xcbk = "l_xc%d" % q, "l_xcb%d" % q
            TU = (2 * u, 2 * u + 1)

            def lsl(tt):
                return slice((tt - 2 * u) * TT, (tt - 2 * u + 1) * TT)
            for tt in TU:
                rd = [(xk, tt), (xk, tt - 1) if tt > 0 else xk + "_0", "lvec"]
                for kk in (2, 1, 0):
                    S.op("vector", lambda e: e.scalar_tensor_tensor(out=xc[:, lsl(tt)], in0=xpad[:, kk + tt * TT:kk + (tt + 1) * TT], scalar=C.lvec[:, n, kk:kk + 1],
                                                                   in1=xc[:, lsl(tt)], op0=ALU.mult, op1=ALU.add),
                         reads=rd + [(xck, tt)], writes=[(xck, tt)])
                S.op("scalar", lambda e: e.activation(out=xcb[:, lsl(tt)], in_=xc[:, lsl(tt)], func=AF.Identity), reads=[(xck, tt)], writes=[(xcbk, tt)])

        pending = None
        proj(0, 0, 0)
        conv_unit(0)
        for k, (n, u) in enumerate(units):
            b = n % 2
            q = k % 2
            gwi, gwk = gw[b], "l_gw%d" % b
            hyi, hyk = hy[n % (2 * G)], "l_hy%d" % (n % (2 * G))
            ysb, xpad = ysbs[b], xpads[b]
            yk, xk = "l_ysb%d" % b, "l_xpad%d" % b
            t1, xc, xcb, rr, ig, hh = t1s[q], xcs[q], xcbs[q], rrs[q], igs[q], hhs[q]
            t1k, xck, xcbk, rk, igk, hk = "l_t1%d" % q, "l_xc%d" % q, "l_xcb%d" % q, "l_r%d" % q, "l_ig%d" % q, "l_h%d" % q
            TU = (2 * u, 2 * u + 1)

            def lsl(tt):
                return slice((tt - 2 * u) * TT, (tt - 2 * u + 1) * TT)
            if k + 1 < len(units):
                n2, u2 = units[k + 1]
                if u2 == 0:
                    load_w(n2)
                proj(n2, u2, k + 1)
            if pending is not None and u == 1:
                outproj_group(C, *pending)
                pending = None
            for tt in TU:
                S.op("scalar", lambda e: e.activation(out=t1[:, lsl(tt)], in_=ysb[:, tsl(tt)], func=AF.Square, scale=GELU_C), reads=[(yk, tt)], writes=[(t1k, tt)])
            for g, (dst, dk) in enumerate(((rr, rk), (ig, igk))):
                for tt in TU:
                    ps, pk = C.nextps()
                    S.op("tensor", lambda e: e.matmul(ps[:], lhsT=gwi[:, g, :], rhs=xcb[:, lsl(tt)], start=True, stop=True),
                         reads=[gwk, (xcbk, tt)], writes=[pk])
                    S.op("scalar", lambda e: e.activation(out=dst[:, lsl(tt)], in_=ps[:], func=AF.Sigmoid, bias=C.lvec[:, n, 5 + g:6 + g]),
                         reads=[pk, "lvec"], writes=[(dk, tt)])
            for tt in TU:
                S.op("vector", lambda e: e.scalar_tensor_tensor(out=t1[:, lsl(tt)], in0=t1[:, lsl(tt)], scalar=1.0, in1=ysb[:, tsl(tt)], op0=ALU.add, op1=ALU.mult),
                     reads=[(t1k, tt), (yk, tt)], writes=[(t1k, tt)])
            for tt in TU:
                S.op("scalar", lambda e: e.activation(out=t1[:, lsl(tt)], in_=t1[:, lsl(tt)], func=AF.Sigmoid, scale=GELU_K), reads=[(t1k, tt)], writes=[(t1k, tt)])
            for tt in TU:
                S.op("vector", lambda e: e.tensor_tensor(out=ysb[:, tsl(tt)], in0=ysb[:, tsl(tt)], in1=t1[:, lsl(tt)], op=ALU.mult), reads=[(t1k, tt), (yk, tt)], writes=[(yk, tt)])
            for tt in TU:
                S.op("vector", lambda e: e.tensor_tensor(out=ig[:, lsl(tt)], in0=ig[:, lsl(tt)], in1=xc[:, lsl(tt)], op=ALU.mult), reads=[(igk, tt), (xck, tt)], writes=[(igk, tt)])
            for tt in TU:
                S.op("scalar", lambda e: e.activation(out=t1[:, lsl(tt)], in_=rr[:, lsl(tt)], func=AF.Exp, scale=C.lca[:, n, 1:2]), reads=[(rk, tt), "lca"], writes=[(t1k, tt)])
            for tt in TU:
                S.op("scalar", lambda e: e.activation(out=rr[:, lsl(tt)], in_=rr[:, lsl(tt)], func=AF.Exp, scale=C.lca[:, n, 0:1]), reads=[(rk, tt), "lca"], writes=[(rk, tt)])
            for tt in TU:
                S.op("scalar", lambda e: e.activation(out=t1[:, lsl(tt)], in_=t1[:, lsl(tt)], func=AF.Sqrt, scale=-1.0, bias=1.0), reads=[(t1k, tt)], writes=[(t1k, tt)])
            if k + 1 < len(units):
                conv_unit(k + 1)
            for tt in TU:
                S.op("vector", lambda e: e.tensor_tensor(out=ig[:, lsl(tt)], in0=ig[:, lsl(tt)], in1=t1[:, lsl(tt)], op=ALU.mult), reads=[(igk, tt), (t1k, tt)], writes=[(igk, tt)])
            for tt in TU:
                if tt == 0:
                    init, ird = 0.0, []
                elif tt == 2 * u:
                    init, ird = hhs[1 - q][:, HT - 1:HT], [("l_h%d" % (1 - q), tt - 1)]
                else:
                    init, ird = hh[:, TT - 1:TT], [(hk, tt - 1)]
                S.op("vector", lambda e: e.tensor_tensor_scan(out=hh[:, lsl(tt)], data0=rr[:, lsl(tt)], data1=ig[:, lsl(tt)], initial=init, op0=ALU.mult, op1=ALU.add),
                     reads=[(rk, tt), (igk, tt)] + ird, writes=[(hk, tt)])
                S.op("vector", lambda e: e.tensor_tensor(out=hyi[:, tsl(tt)], in0=hh[:, lsl(tt)], in1=ysb[:, tsl(tt)], op=ALU.mult),
                     reads=[(hk, tt), (yk, tt)], writes=[(hyk, tt)])
            if u == 1 and n % G == G - 1:
                idx = [(n - G + 1 + i) % (2 * G) for i in range(G)]
                pending = ([((lambda tt, i=i: hy[i][:, tsl(tt)]), (lambda tt, i=i: ("l_hy%d" % i, tt))) for i in idx],
                           [(wo[i], "l_wo%d" % i) for i in idx])
        outproj_group(C, *pending)
        S.barrier()


def conv_ffn(C, L, gi):
    S, sb, nc = C.S, C.sb, C.nc
    G = 4
    with contextlib.ExitStack() as ph:
        win = [sb("f_win%d" % i, [128, KC, 256], BF16, ph) for i in range(2)]
        wo = [sb("f_wo%d" % i, [128, D], BF16, ph) for i in range(2 * G)]

        def load_w(j):
            b = j % 2
            S.dma("gpsimd", "f_win%d" % b, lambda e: e.dma_start(out=win[b][:], in_=C.ffn_win[L, j]), writes=["f_win%d" % b])
            S.dma("gpsimd", "f_wo%d" % (j % (2 * G)), lambda e: e.dma_start(out=wo[j % (2 * G)][:], in_=C.ffn_wout[L, j]), writes=["f_wo%d" % (j % (2 * G))])
        load_w(0)
        rmsnorm(C, gi)
        act = [sb("f_act%d" % i, [128, NT], BF16, ph) for i in range(2 * G)]
        apads = [sb("f_apad%d" % i, [128, 2 + NT], F32, ph) for i in range(2)]
        bsbs = [sb("f_bsb%d" % i, [128, NT], BF16, ph) for i in range(2)]
        acs = [sb("f_ac%d" % i, [128, NT], F32, ph) for i in range(2)]
        t1 = sb("f_t1", [128, NT], F32, ph)
        for i in range(2):
            S.op("vector", lambda e: e.memset(apads[i][:, 0:2], 0.0), writes=["f_apad%d_0" % i])

        def load_and_proj(j):
            b = j % 2
            wi, woi = win[b], wo[j % (2 * G)]
            wik, wok = "f_win%d" % b, "f_wo%d" % (j % (2 * G))
            if j > 0:
                load_w(j)
            apad, bsb, acb = apads[b], bsbs[b], acs[b]

            def evac_a(ps, pk, tt):
                S.op("scalar", lambda e: e.activation(out=apad[:, 2 + tt * TT:2 + (tt + 1) * TT], in_=ps[:], func=AF.Identity),
                     reads=[pk], writes=[("f_apad%d" % b, tt)])
                S.op("scalar", lambda e: e.activation(out=acb[:, tsl(tt)], in_=ps[:], func=AF.Identity, scale=C.fvec[:, L, j, 2:3], bias=C.fvec[:, L, j, 3:4]),
                     reads=[pk, "fvec"], writes=[("f_ac%d" % b, tt)])
            for tt in range(NTT):
                inproj(C, (wi, wik), 0, tt, lambda ps, pk: evac_a(ps, pk, tt))
            for tt in range(NTT):
                inproj(C, (wi, wik), 128, tt, lambda ps, pk: S.op(
                    "scalar", lambda e: e.activation(out=bsb[:, tsl(tt)], in_=ps[:], func=AF.Identity), reads=[pk], writes=[("f_bsb%d" % b, tt)]))

        pending = None
        load_and_proj(0)
        for j in range(FJ):
            b = j % 2
            acti, actk = act[j % (2 * G)], "f_act%d" % (j % (2 * G))
            apad, bsb, ac = apads[b], bsbs[b], acs[b]
            ak, bk, ack = "f_apad%d" % b, "f_bsb%d" % b, "f_ac%d" % b
            T4 = range(NTT)
            if j + 1 < FJ:
                load_and_proj(j + 1)
            if pending is not None:
                outproj_group(C, *pending)
                pending = None
            for tt in T4:
                sl = tsl(tt)
                rd = [(ak, tt), (ak, tt - 1) if tt > 0 else ak + "_0", "fvec"]
                for k in (1, 0):
                    S.op("vector", lambda e: e.scalar_tensor_tensor(out=ac[:, sl], in0=apad[:, k + tt * TT:k + (tt + 1) * TT], scalar=C.fvec[:, L, j, k:k + 1],
                                                                   in1=ac[:, sl], op0=ALU.mult, op1=ALU.add),
                         reads=rd + [(ack, tt)], writes=[(ack, tt)])
                S.op("scalar", lambda e: e.activation(out=t1[:, sl], in_=ac[:, sl], func=AF.Square, scale=GELU_C), reads=[(ack, tt)], writes=[("f_t1", tt)])
            for tt in T4:
                sl = tsl(tt)
                S.op("vector", lambda e: e.scalar_tensor_tensor(out=t1[:, sl], in0=t1[:, sl], scalar=1.0, in1=ac[:, sl], op0=ALU.add, op1=ALU.mult),
                     reads=[("f_t1", tt), (ack, tt)], writes=[("f_t1", tt)])
                S.op("scalar", lambda e: e.activation(out=t1[:, sl], in_=t1[:, sl], func=AF.Sigmoid, scale=GELU_K), reads=[("f_t1", tt)], writes=[("f_t1", tt)])
                S.op("vector", lambda e: e.tensor_tensor(out=ac[:, sl], in0=ac[:, sl], in1=bsb[:, sl], op=ALU.mult), reads=[(ack, tt), (bk, tt)], writes=[(ack, tt)])
            for tt in T4:
                sl = tsl(tt)
                S.op("vector", lambda e: e.tensor_tensor(out=acti[:, sl], in0=ac[:, sl], in1=t1[:, sl], op=ALU.mult),
                     reads=[(ack, tt), ("f_t1", tt)], writes=[(actk, tt)])
            if j % G == G - 1:
                idx = [(j - G + 1 + i) % (2 * G) for i in range(G)]
                pending = ([((lambda tt, i=i: act[i][:, tsl(tt)]), (lambda tt, i=i: ("f_act%d" % i, tt))) for i in idx],
                           [(wo[i], "f_wo%d" % i) for i in idx])
        outproj_group(C, *pending)
        S.barrier()


SLOPES = [2.0 ** (-8.0 * (h + 1) / 16.0) for h in range(16)]
BIGNEG = 30000.0
NPT = 4
NSM = 4


def gelu_ap(C, x, t, xk, tk):
    S = C.S
    S.op("scalar", lambda e: e.activation(out=t, in_=x, func=AF.Square), reads=[xk], writes=[tk])
    S.op("vector", lambda e: e.tensor_scalar(out=t, in0=t, scalar1=0.044715, scalar2=1.0, op0=ALU.mult, op1=ALU.add), reads=[tk], writes=[tk])
    S.op("vector", lambda e: e.tensor_tensor(out=t, in0=t, in1=x, op=ALU.mult), reads=[tk, xk], writes=[tk])
    S.op("scalar", lambda e: e.activation(out=t, in_=t, func=AF.Sigmoid, scale=GELU_K), reads=[tk], writes=[tk])
    S.op("vector", lambda e: e.tensor_tensor(out=x, in0=x, in1=t, op=ALU.mult), reads=[tk, xk], writes=[xk])


def nsa_mixer(C, gi):
    S, sb, nc = C.S, C.sb, C.nc
    C.nrot = 4
    C.nsa_deferred = []
    po_banks = [(C.psum[i], ("ps", i)) for i in (4, 5, 6, 7)]
    po_i = [0]

    def nextpo():
        r = po_banks[po_i[0] % 4]
        dl = C.nsa_deferred
        while any(r[1] in tags for _, tags in dl):
            dl.pop(0)[0]()
        po_i[0] += 1
        return r
    with contextlib.ExitStack() as ph:
        KCT = sb("n_KCT", [128, 4, 128], BF16, ph)
        VCa = sb("n_VCa", [128, 4, 97], BF16, ph)
        wch = [None, None]
        wci = [0]

        def alloc_wch(stack):
            for i in range(2):
                wch[i] = sb("n_wch%d" % i, [128, KC, 128], BF16, stack)

        def load_wch(idx):
            i = wci[0] % 2
            wci[0] += 1
            k = "n_wch%d" % i
            S.dma("gpsimd", k, lambda e: e.dma_start(out=wch[i][:], in_=C.nsa_wch[idx]), writes=[k])
            return wch[i], k

        with contextlib.ExitStack() as p1:
            alloc_wch(p1)
            KV0 = sb("n_KV0", [128, 4, NT], BF16, p1)
            W1 = sb("n_W1", [128, 2, 32, 256], BF16, p1)
            posT = sb("n_posT", [128, 2, 32], BF16, p1)
            b1 = sb("n_b1", [128, 2, 2], F32, p1)
            W2k = sb("n_W2k", [128, 2, 128], BF16, p1)
            W2v = sb("n_W2v", [128, 2, 64], BF16, p1)
            ovl = sb("n_ovl", [128, 33], F32, p1)
            hids = [sb("n_hid%d" % i, [128, 256], F32, p1) for i in range(2)]
            hscs = [sb("n_hsc%d" % i, [128, 256], F32, p1) for i in range(2)]
            ghbs = [sb("n_ghb%d" % i, [128, 2, 128], BF16, p1) for i in range(2)]
            cvec = sb("n_cvec", [1, 2, 256], BF16, p1)
            onesr = sb("n_onesr", [1, 128], BF16, p1)
            b1row = sb("n_b1row", [1, 2, 256], F32, p1)
            identc = sb("n_identc", [128, 128], F32, p1)
            S.dma("sync", "n_identc", lambda e: e.dma_start(out=identc[:], in_=C.c_ident), writes=["n_identc"])
            S.dma("sync", "n_b1row", lambda e: e.dma_start(out=b1row[:], in_=C.nsa_b1row), writes=["n_b1row"])
            S.op("vector", lambda e: e.memset(onesr[:], 1.0), writes=["n_onesr"])
            for kvi in range(2):
                for i0 in range(0, 32, 8):
                    S.dma("gpsimd", "n_W1_%d_%d" % (kvi, i0), lambda e: e.dma_start(out=W1[:, kvi, i0:i0 + 8, :], in_=C.nsa_w1[kvi, :, i0:i0 + 8, :]), writes=[("n_W1", kvi, i0)])
            S.dma("gpsimd", "n_posT", lambda e: e.dma_start(out=posT[:], in_=C.nsa_posT), writes=["n_posT"])
            S.dma("sync", "n_b1", lambda e: e.dma_start(out=b1[:], in_=C.nsa_b1), writes=["n_b1"])
            S.dma("gpsimd", "n_W2k", lambda e: e.dma_start(out=W2k[:], in_=C.nsa_w2k), writes=["n_W2k"])
            S.dma("gpsimd", "n_W2v", lambda e: e.dma_start(out=W2v[:], in_=C.nsa_w2v), writes=["n_W2v"])
            S.dma("sync", "n_ovl", lambda e: e.dma_start(out=ovl[:], in_=C.c_ovl), writes=["n_ovl"])
            wt0 = load_wch(0)
            rmsnorm(C, gi)
            S.op("vector", lambda e: e.memset(KCT[:], 0.0), writes=[("n_KCT", g) for g in range(4)])
            S.op("vector", lambda e: e.memset(VCa[:], 0.0), writes=[("n_VCa", g) for g in range(4)])
            for g in range(4):
                S.op("vector", lambda e: e.tensor_copy(out=VCa[:, g, 64:97], in_=ovl[:]), reads=["n_ovl"], writes=[("n_VCa", g)])
            for c4 in range(4):
                wt, wk = wt0 if c4 == 0 else load_wch(c4)
                for tt in range(NTT):
                    inproj(C, (wt, wk), 0, tt, lambda ps, pk: S.op(
                        "scalar", lambda e: e.activation(out=KV0[:, c4, tsl(tt)], in_=ps[:], func=AF.Identity), reads=[pk], writes=[("n_KV0", c4)]))
            for kvi in range(2):
                ps, pk = C.nextps()
                for i in range(32):
                    S.op("tensor", lambda e: e.matmul(ps[0:1, 0:256], lhsT=posT[0:64, kvi, i:i + 1], rhs=W1[0:64, kvi, i, :], start=(i == 0), stop=(i == 31)),
                         reads=[("n_W1", kvi, (i // 8) * 8), "n_posT"], writes=[pk])
                S.op("vector", lambda e: e.tensor_tensor(out=cvec[0:1, kvi, :], in0=ps[0:1, 0:256], in1=b1row[0:1, kvi, :], op=ALU.add),
                     reads=[pk, "n_b1row"], writes=[("n_cvec", kvi)])
            un = 0
            for kvi in range(2):
                for g in range(4):
                    c4 = kvi * 2 + g // 2
                    rows = slice((g % 2) * 64, (g % 2) * 64 + 64)
                    ub = un % 2
                    un += 1
                    hid, hsc, ghb = hids[ub], hscs[ub], ghbs[ub]
                    hk, sk, gk = "n_hid%d" % ub, "n_hsc%d" % ub, "n_ghb%d" % ub
                    ps, pk = C.nextps()
                    for i in range(32):
                        S.op("tensor", lambda e: e.matmul(ps[0:127, 0:256], lhsT=KV0[rows, c4, i:i + 2017:16], rhs=W1[rows, kvi, i, :], start=(i == 0), stop=False),
                             reads=[("n_W1", kvi, (i // 8) * 8), ("n_KV0", c4)], writes=[pk])
                    S.op("tensor", lambda e: e.matmul(ps[0:127, 0:256], lhsT=onesr[0:1, 0:127], rhs=cvec[0:1, kvi, :], start=False, stop=True),
                         reads=["n_onesr", ("n_cvec", kvi)], writes=[pk])
                    S.op("scalar", lambda e: e.activation(out=hid[0:127, :], in_=ps[0:127, 0:256], func=AF.Identity), reads=[pk], writes=[hk])
                    gelu_ap(C, hid[0:127, :], hsc[0:127, :], hk, sk)
                    ps, pk = C.nextps()
                    for mh in range(2):
                        S.op("tensor", lambda e: e.transpose(out=ps[:, mh * 128:mh * 128 + 127], in_=hid[0:127, mh * 128:(mh + 1) * 128], identity=identc[0:127, 0:127]),
                             reads=[hk, "n_identc"], writes=[pk])
                    S.op("scalar", lambda e: e.activation(out=ghb[:, :, 0:127], in_=ps[:, 0:256].rearrange("p (m c) -> p m c", c=128)[:, :, 0:127], func=AF.Identity),
                         reads=[pk], writes=[(gk, 0), (gk, 1)])
                    ps, pk = C.nextps()
                    if kvi == 0:
                        for mh in range(2):
                            S.op("tensor", lambda e: e.matmul(ps[:, 0:127], lhsT=W2k[:, mh, :], rhs=ghb[:, mh, 0:127], start=(mh == 0), stop=(mh == 1)),
                                 reads=["n_W2k", (gk, mh)], writes=[pk])
                        S.op("scalar", lambda e: e.activation(out=KCT[rows, g, 0:127], in_=ps[rows, 0:127], func=AF.Identity), reads=[pk], writes=[("n_KCT", g)])
                    else:
                        for mh in range(2):
                            S.op("tensor", lambda e: e.matmul(ps[0:127, 0:64], lhsT=ghb[:, mh, 0:127], rhs=W2v[:, mh, :], start=(mh == 0), stop=(mh == 1)),
                                 reads=["n_W2v", (gk, mh)], writes=[pk])
                        S.op("scalar", lambda e: e.activation(out=VCa[0:127, g, 0:64], in_=ps[0:127, 0:64], func=AF.Identity), reads=[pk], writes=[("n_VCa", g)])
            S.barrier()

        pA = ph.enter_context(contextlib.ExitStack())
        QT = sb("n_QT", [128, 8, NT], BF16, pA)
        Vaug = sb("n_Vaug", [128, 2, 16, 4, 65], BF16, pA)
        gts = sb("n_gts", [128, 16, 48], F32, pA)
        with contextlib.ExitStack() as p2:
            alloc_wch(p2)
            K12 = sb("n_K12", [128, 2, 2, NT], BF16, p2)
            wtok = sb("n_wtok", [128, KC, 560], BF16, p2)
            S.dma("gpsimd", "n_wtok", lambda e: e.dma_start(out=wtok[:], in_=C.nsa_wtok), writes=["n_wtok"])
            for mq in range(8 if getattr(C, 'nsa_stop', 9) >= 2 else 0):
                wt, wk = load_wch(4 + mq)
                for tt in range(NTT):
                    inproj(C, (wt, wk), 0, tt, lambda ps, pk: S.op(
                        "scalar", lambda e: e.activation(out=QT[:, mq, tsl(tt)], in_=ps[:], func=AF.Identity, scale=0.125), reads=[pk], writes=[("n_QT", mq, tt)]))
            for br in range(2):
                for c2 in range(2):
                    wt, wk = load_wch(12 + br * 2 + c2)
                    for tt in range(NTT):
                        inproj(C, (wt, wk), 0, tt, lambda ps, pk: S.op(
                            "scalar", lambda e: e.activation(out=K12[:, br, c2, tsl(tt)], in_=ps[:], func=AF.Identity), reads=[pk], writes=[("n_K12", br, c2, tt)]))
            S.op("vector", lambda e: e.memset(Vaug[:].rearrange("p a b c d -> p (a b c) d")[:, :, 64:65], 1.0), writes=["n_Vone"])
            for t16 in range(16 if getattr(C, 'nsa_stop', 9) >= 2 else 0):
                tok = slice(t16 * 128, (t16 + 1) * 128)
                ps, pk = C.nextps()
                for kc in range(KC):
                    S.op("tensor", lambda e: e.matmul(ps[:, 0:512], lhsT=C.xn[:, kc, tok], rhs=wtok[:, kc, 0:512], start=(kc == 0), stop=(kc == KC - 1)),
                         reads=["n_wtok", ("xn", kc, t16 // 4)], writes=[pk])
                for br in range(2):
                    S.op("scalar", lambda e: e.activation(out=Vaug[:, br, t16, :, 0:64], in_=ps[:, br * 256:(br + 1) * 256].rearrange("p (g d) -> p g d", d=64),
                                                          func=AF.Identity), reads=[pk], writes=[("n_Vaug", br, t16)])
                ps, pk = C.nextps()
                for kc in range(KC):
                    S.op("tensor", lambda e: e.matmul(ps[:, 0:48], lhsT=C.xn[:, kc, tok], rhs=wtok[:, kc, 512:560], start=(kc == 0), stop=(kc == KC - 1)),
                         reads=["n_wtok", ("xn", kc, t16 // 4)], writes=[pk])
                S.op("scalar", lambda e: e.activation(out=gts[:, t16, :], in_=ps[:, 0:48], func=AF.Sigmoid), reads=[pk], writes=[("n_gts", t16)])
            S.barrier()
            Kz = C.xn[:].rearrange("p (b g) t -> p b g t", b=2)
            for br in range(2):
                for g in range(4):
                    own = slice((g % 2) * 64, (g % 2) * 64 + 64)
                    oth = slice((1 - g % 2) * 64, (1 - g % 2) * 64 + 64)
                    S.op("gpsimd", lambda e: e.memset(Kz[oth, br, g, :], 0.0), writes=[("n_KzO", br, g)])
                    if g % 2 == 0:
                        S.op("scalar", lambda e: e.activation(out=Kz[own, br, g, :], in_=K12[own, br, g // 2, :], func=AF.Identity), writes=[("n_Kz", br, g)])
                    else:
                        S.op("vector", lambda e: e.tensor_copy(out=Kz[own, br, g, :], in_=K12[own, br, g // 2, :]), writes=[("n_Kz", br, g)])
            S.barrier()

        with contextlib.ExitStack() as p3:
            dtab = sb("n_dtab", [128, 9, TT], mybir.dt.float16, p3)
            etab = sb("n_etab", [128, 16, 128], BF16, p3)
            biasc = sb("n_biasc", [128, 16, 16], F32, p3)
            keep = sb("n_keep", [128, 16, 32], BF16, p3)
            addc = sb("n_addc", [128, 16, 32], BF16, p3)
            ident = sb("n_ident", [128, 128], F32, p3)
            sm = [sb("n_sm%d" % i, [128, TT], F32, p3) for i in range(NSM)]
            pT = [sb("n_pT%d" % i, [128, TT], BF16, p3) for i in range(NPT)]
            otoks = [sb("n_otok%d" % i, [128, 4, 256], F32, p3) for i in range(2)]
            valid = sb("n_valid", [128, 16, 1], F32, p3)
            otmp = sb("n_otmp", [128, 4, 64], F32, p3)
            rden = sb("n_rden", [128, 4, 1], F32, p3)
            ff = sb("n_ff", [128, 4, 1], F32, p3)
            comb = sb("n_comb", [128, 4, 65], F32, p3)
            fq = sb("n_fq", [128, 4, 16], F32, p3)
            S.dma("sync", "n_fq", lambda e: e.dma_start(out=fq[:], in_=C.c_fq), writes=["n_fq"])
            imp = sb("n_imp", [128, 4, 32], F32, p3)
            itmp = sb("n_itmp", [128, 4, 32], F32, p3)
            top8 = sb("n_top8", [128, 4, 8], F32, p3)
            selb = sb("n_selb", [128, 4, 32], F32, p3)
            selbT = sb("n_selbT", [128, 4, TT], BF16, p3)
            oTq = sb("n_oTq", [128, KC, TT], BF16, p3)
            alloc_wch(p3)
            S.dma("sync", "n_dtab", lambda e: e.dma_start(out=dtab[:, 0:8, :], in_=C.c_dtab[:, 0:8, :]), writes=["n_dtab"])
            S.op("gpsimd", lambda e: e.memset(etab[:], 0.0), writes=["n_etab"])
            S.op("gpsimd", lambda e: e.memset(selbT[:], 0.0), writes=[("n_selbT", g) for g in range(4)])
            S.dma("gpsimd", "n_etab", lambda e: e.dma_start(out=etab[0:32], in_=C.c_etab), writes=["n_etab"])
            S.dma("sync", "n_biasc", lambda e: e.dma_start(out=biasc[:], in_=C.c_biasc), writes=["n_biasc"])
            S.dma("gpsimd", "n_keep", lambda e: e.dma_start(out=keep[:], in_=C.c_keep), writes=["n_keep"])
            S.dma("gpsimd", "n_addc", lambda e: e.dma_start(out=addc[:], in_=C.c_addc), writes=["n_addc"])
            S.dma("sync", "n_ident", lambda e: e.dma_start(out=ident[:], in_=C.c_ident), writes=["n_ident"])
            S.dma("sync", "n_valid", lambda e: e.dma_start(out=valid[:], in_=C.c_valid), writes=["n_valid"])
            smi = [0]
            pti = [0]

            def score_tile(mm_fn, mm_reads, dti, scal, bias_ap):
                dkey = "n_dtabc" if dti == 8 else "n_dtab"
                i = smi[0] % NSM
                smi[0] += 1
                ps, pk = C.nextps()
                mm_fn(ps, pk)
                if dti is None:
                    smi[0] -= 1
                    jj = pti[0] % NPT
                    pti[0] += 1
                    S.op("scalar", lambda e: e.activation(out=pT[jj][:], in_=ps[:], func=AF.Exp, bias=bias_ap), reads=[pk, "n_biasc"], writes=[("n_pT", jj)])
                    return pT[jj], ("n_pT", jj)
                j = pti[0] % NPT
                pti[0] += 1
                S.op("vector", lambda e: e.scalar_tensor_tensor(out=sm[i][:], in0=dtab[:, dti, :], scalar=scal, in1=ps[:], op0=ALU.mult, op1=ALU.add),
                     reads=[pk, dkey], writes=[("n_sm", i)])
                if bias_ap is None:
                    S.op("scalar", lambda e: e.activation(out=pT[j][:], in_=sm[i][:], func=AF.Exp), reads=[("n_sm", i)], writes=[("n_pT", j)])
                else:
                    S.op("scalar", lambda e: e.activation(out=pT[j][:], in_=sm[i][:], func=AF.Exp, bias=bias_ap), reads=[("n_sm", i), "n_biasc"], writes=[("n_pT", j)])
                return pT[j], ("n_pT", j)

            def run_jobs(jobs, LA=NPT - 1):
                staged = []
                for idx in range(len(jobs) + LA):
                    if idx < len(jobs):
                        jb = jobs[idx]
                        staged.append(score_tile(jb["mm"], None, jb["dti"], jb["scal"], jb["bias"]))
                        if deferred:
                            deferred.pop(0)[0]()
                    k = idx - LA
                    if k >= 0:
                        jobs[k]["pv"](*staged[k])
                while deferred:
                    deferred.pop(0)[0]()

            deferred = C.nsa_deferred

            def accum_out(po, pok, ncol, otok, okey, r, gate_col, qt, first, src3=None, emit=None):
                if emit is None:
                    emit = lambda th: th()
                po3 = po[:, 0:4 * ncol].rearrange("p (s c) -> p s c", c=ncol) if src3 is None else src3
                if first:
                    emit(lambda: S.op("vector", lambda e: e.tensor_scalar(out=rden[:], in0=po3[:, :, 64:65], scalar1=1e-30, scalar2=None, op0=ALU.max), reads=[pok], writes=["n_rden"]))
                    emit(lambda: S.op("vector", lambda e: e.reciprocal(out=rden[:], in_=rden[:]), reads=["n_rden"], writes=["n_rden"]))
                else:
                    emit(lambda: S.op("vector", lambda e: e.reciprocal(out=rden[:], in_=po3[:, :, 64:65]), reads=[pok], writes=["n_rden"]))
                if first and qt == 0:
                    emit(lambda: S.op("vector", lambda e: e.tensor_tensor(out=rden[:], in0=rden[:], in1=valid[:, 0:4, :], op=ALU.mult), reads=["n_rden", "n_valid"], writes=["n_rden"]))
                emit(lambda: S.op("vector", lambda e: e.tensor_tensor(out=ff[:], in0=rden[:], in1=gts[:, qt * 4:(qt + 1) * 4, gate_col:gate_col + 1], op=ALU.mult),
                                  reads=["n_rden"] + [("n_gts", qt * 4 + i) for i in range(4)], writes=["n_ff"]))
                dst = otok[:, :, r * 64:(r + 1) * 64]
                if first:
                    emit(lambda: S.op("vector", lambda e: e.tensor_tensor(out=dst, in0=po3[:, :, 0:64], in1=ff[:].to_broadcast([128, 4, 64]), op=ALU.mult),
                                      reads=[pok, "n_ff"], writes=[(okey, r)]))
                else:
                    emit(lambda: S.op("vector", lambda e: e.tensor_tensor(out=otmp[:], in0=po3[:, :, 0:64], in1=ff[:].to_broadcast([128, 4, 64]), op=ALU.mult),
                                      reads=[pok, "n_ff"], writes=["n_otmp"]))
                    emit(lambda: S.op("vector", lambda e: e.tensor_tensor(out=dst, in0=dst, in1=otmp[:], op=ALU.add), reads=["n_otmp", (okey, r)], writes=[(okey, r)]))

            def group_ctx(qt, g):
                k = qt * 4 + g
                return tsl(qt), slice((g % 2) * 64, (g % 2) * 64 + 64), g // 2, otoks[k % 2], "n_otok%d" % (k % 2)

            def cmp_jobs(qt, g):
                qs, rows, c2, otok, okey = group_ctx(qt, g)
                if g == 0:
                    S.dma("sync", "n_dtabc", lambda e: e.dma_start(out=dtab[:, 8, :], in_=C.c_dtab[:, 9 + qt, :]), writes=["n_dtabc"])
                jobs = []
                dq = lambda th: deferred.append((th, ()))

                def select_ops():
                    dq(lambda: S.op("vector", lambda e: e.tensor_tensor(out=imp[:], in0=imp[:], in1=keep[:, qt * 4:(qt + 1) * 4, :], op=ALU.mult), reads=["n_imp", "n_keep"], writes=["n_imp"]))
                    dq(lambda: S.op("vector", lambda e: e.tensor_tensor(out=imp[:], in0=imp[:], in1=addc[:, qt * 4:(qt + 1) * 4, :], op=ALU.add), reads=["n_imp", "n_addc"], writes=["n_imp"]))
                    for sub in range(4):
                        dq(lambda sub=sub: S.op("vector", lambda e: e.max(out=top8[:, sub, :], in_=imp[:, sub, :]), reads=["n_imp"], writes=["n_top8"]))
                    for sub in range(4):
                        dq(lambda sub=sub: S.op("vector", lambda e: e.tensor_scalar(out=selb[:, sub, :], in0=imp[:, sub, :], scalar1=top8[:, sub, 7:8], scalar2=-BIGNEG, op0=ALU.is_lt, op1=ALU.mult),
                                                reads=["n_imp", "n_top8"], writes=["n_selb"]))

                    def tr():
                        ps, pk = C.nextps()
                        for sub in range(4):
                            S.op("tensor", lambda e: e.transpose(out=ps[0:32, sub * 128:(sub + 1) * 128], in_=selb[:, sub, :], identity=ident[:]),
                                 reads=["n_selb", "n_ident"], writes=[pk])
                        S.op("scalar", lambda e: e.activation(out=selbT[0:32, g, :], in_=ps[0:32, :], func=AF.Identity), reads=[pk], writes=[("n_selbT", g)])
                    dq(tr)

                for r in range(4):
                    hh = g * 4 + r
                    mq = (g // 2) * 4 + r

                    def mm(ps, pk, mq=mq):
                        S.op("tensor", lambda e: e.matmul(ps[:], lhsT=KCT[:, g, :], rhs=QT[:, mq, qs], start=True, stop=True),
                             reads=[("n_KCT", g), ("n_QT", mq, qt)], writes=[pk])

                    def pv(p_t, p_k, r=r, hh=hh):
                        po, pok = nextpo()
                        for sub in range(4):
                            S.op("tensor", lambda e: e.matmul(po[:, sub * 97:(sub + 1) * 97], lhsT=p_t[:, sub * 128:(sub + 1) * 128], rhs=VCa[:, g, :], start=True, stop=True),
                                 reads=[p_k, ("n_VCa", g)], writes=[pok])
                        accum_out(po, pok, 97, otok, okey, r, hh, qt, True)
                        po3 = po[:, 0:388].rearrange("p (s c) -> p s c", c=97)
                        if r == 0:
                            S.op("vector", lambda e: e.tensor_tensor(out=imp[:], in0=po3[:, :, 65:97], in1=rden[:].to_broadcast([128, 4, 32]), op=ALU.mult),
                                 reads=[pok, "n_rden"], writes=["n_imp"])
                        else:
                            S.op("vector", lambda e: e.tensor_tensor(out=itmp[:], in0=po3[:, :, 65:97], in1=rden[:].to_broadcast([128, 4, 32]), op=ALU.mult),
                                 reads=[pok, "n_rden"], writes=["n_itmp"])
                            S.op("vector", lambda e: e.tensor_tensor(out=imp[:], in0=imp[:], in1=itmp[:], op=ALU.add), reads=["n_itmp", "n_imp"], writes=["n_imp"])
                        if r == 3:
                            select_ops()
                    jobs.append(dict(mm=mm, dti=8, scal=-SLOPES[hh] / 2.0, bias=biasc[:, hh, 0:1], pv=pv))
                return jobs

            def selwin(qt, g, pre_jobs):
                qs, rows, c2, otok, okey = group_ctx(qt, g)
                jobs = []
                for r in range(4):
                    hh = g * 4 + r
                    mq = (g // 2) * 4 + r
                    for br in range(2):
                        kts = list(range(0, qt * 4 + 4)) if br == 0 else list(range(max(0, qt * 4 - 4), qt * 4 + 4))
                        state = {}
                        for n_k, kt in enumerate(kts):
                            delta = qt * TT - kt * 128
                            bias_ap = None
                            far = False
                            if delta <= 0:
                                dti = (-delta) // 128
                            elif br == 1:
                                dti = 3 + delta // 128
                            else:
                                dti = None
                                far = True
                                bias_ap = biasc[:, hh, delta // 128:delta // 128 + 1]
                            nfar = qt * 4 if br == 0 else 0

                            def mm(ps, pk, br=br, kt=kt, mq=mq):
                                ks = slice(kt * 128, (kt + 1) * 128)
                                S.op("tensor", lambda e: e.matmul(ps[:], lhsT=Kz[:, br, g, ks], rhs=QT[:, mq, qs], start=True, stop=(br == 1)),
                                     reads=[("n_Kz", br, g), ("n_KzO", br, g), ("n_QT", mq, qt)], writes=[pk])
                                if br == 0:
                                    S.op("tensor", lambda e: e.matmul(ps[:], lhsT=etab[:, kt, :], rhs=selbT[:, g, :], start=False, stop=True),
                                         reads=["n_etab", ("n_selbT", g)], writes=[pk])

                            def pv(p_t, p_k, br=br, kt=kt, n_k=n_k, nk=len(kts), state=state, r=r, hh=hh, far=far, nfar=nfar):
                                if n_k == 0 and nfar:
                                    state["far"] = nextpo()
                                if n_k == nfar:
                                    state["po"] = nextpo()
                                po, pok = state["far"] if far else state["po"]
                                first = (n_k == 0) if far else (n_k == nfar)
                                last = (n_k == nfar - 1) if far else (n_k == nk - 1)
                                for sub in range(4):
                                    S.op("tensor", lambda e: e.matmul(po[:, sub * 65:(sub + 1) * 65], lhsT=p_t[:, sub * 128:(sub + 1) * 128], rhs=Vaug[:, br, kt, g, :],
                                                                      start=(first and sub == 0), stop=(last and sub == 3)),
                                         reads=[p_k, ("n_Vaug", br, kt), "n_Vone"], writes=[pok])
                                if n_k == nk - 1:
                                    if nfar:
                                        pf, pfk = state["far"]
                                        pf3 = pf[:, 0:260].rearrange("p (s c) -> p s c", c=65)
                                        po3 = po[:, 0:260].rearrange("p (s c) -> p s c", c=65)
                                        S.op("vector", lambda e: e.tensor_tensor(out=comb[:], in0=pf3, in1=fq[:, :, hh:hh + 1].to_broadcast([128, 4, 65]), op=ALU.mult),
                                             reads=[pfk, "n_fq"], writes=["n_comb"])
                                        S.op("vector", lambda e: e.tensor_tensor(out=comb[:], in0=comb[:], in1=po3, op=ALU.add), reads=[pok, "n_comb"], writes=["n_comb"])
                                        accum_out(po, "n_comb", 65, otok, okey, r, (1 + br) * 16 + hh, qt, False, src3=comb[:])
                                    else:
                                        accum_out(po, pok, 65, otok, okey, r, (1 + br) * 16 + hh, qt, False)
                            jobs.append(dict(mm=mm, dti=dti, scal=-SLOPES[hh], bias=bias_ap, pv=pv))
                run_jobs(pre_jobs + jobs)
                for kk in range(2):
                    kc = 2 * g + kk
                    ps, pk = C.nextps()
                    for sub in range(4):
                        S.op("tensor", lambda e: e.transpose(out=ps[:, sub * 128:(sub + 1) * 128], in_=otok[:, sub, kk * 128:(kk + 1) * 128], identity=ident[:]),
                             reads=[(okey, 2 * kk), (okey, 2 * kk + 1), "n_ident"], writes=[pk])
                    S.op("scalar", lambda e: e.activation(out=oTq[:, kc, :], in_=ps[:], func=AF.Identity), reads=[pk], writes=[("n_oTq", kc)])
                if g == 3:
                    for m in range(KC):
                        wt, wk = load_wch(16 + m)
                        ps, pk = C.nextps()
                        for kc in range(KC):
                            S.op("tensor", lambda e: e.matmul(ps[:], lhsT=wt[:, kc, :], rhs=oTq[:, kc, :], start=(kc == 0), stop=(kc == KC - 1)),
                                 reads=[wk, ("n_oTq", kc)], writes=[pk])
                        S.op("vector", lambda e: e.tensor_tensor(out=C.hT[:, m, qs], in0=C.hT[:, m, qs], in1=ps[:], op=ALU.add),
                             reads=[pk, ("hT", m, qt)], writes=[("hT", m, qt)])

            steps = [(qt, g) for qt in range(NTT if getattr(C, 'nsa_stop', 9) >= 3 else 0) for g in range(4)]
            if steps:
                run_jobs(cmp_jobs(*steps[0]))
            for k, (qt, g) in enumerate(steps):
                selwin(qt, g, cmp_jobs(*steps[k + 1]) if k + 1 < len(steps) else [])
            S.barrier()
        pA.close()
        S.barrier()
    C.nrot = 8


def prep_weights(inp):
    f = np.ascontiguousarray
    w = {}
    g = np.stack([inp["lru_norm_g"][0], inp["ffn_norm_g"][0], inp["nsa_norm_g"][0], inp["ffn_norm_g"][1], inp["final_norm_g"]], 0)
    w["gains"] = f(g.reshape(5, KC, 128).transpose(2, 0, 1))
    wi = inp["lru_w_in"][0].reshape(KC, 128, 2, LN, 128)
    w["lru_win"] = f(wi.transpose(3, 1, 0, 2, 4).reshape(LN, 128, KC, 256))
    w["lru_gw"] = f(inp["lru_gate_w"][0].transpose(1, 2, 0, 3))
    w["lru_wout"] = f(inp["lru_w_out"][0].reshape(LN, 128, D))
    vec = np.concatenate([inp["lru_conv_w"][0], inp["lru_conv_b"][0][None], inp["lru_gate_b"][0], inp["lru_a_param"][0][None]], 0)
    w["lru_vec"] = f(vec.reshape(8, LN, 128).transpose(2, 1, 0))
    fw = inp["ffn_w_in"].reshape(2, KC, 128, 2, FJ, 128)
    w["ffn_win"] = f(fw.transpose(0, 4, 2, 1, 3, 5).reshape(2, FJ, 128, KC, 256))
    w["ffn_wout"] = f(inp["ffn_w_out"].reshape(2, FJ, 128, D))
    fv = np.concatenate([inp["ffn_conv_w"], inp["ffn_conv_b"][:, None]], 1)
    w["ffn_vec"] = f(fv.reshape(2, 4, FJ, 128).transpose(3, 0, 2, 1))
    w.update(nsa_host(inp))
    w.update(const_tables())
    return w


def nsa_host(inp):
    f = np.ascontiguousarray
    w = {}
    W = inp["nsa_w_in"][0]
    Wr = W.reshape(KC, 128, 2608)

    def chunk(cols):
        return Wr[:, :, cols].transpose(1, 0, 2)
    chunks = []
    for c4 in range(4):
        kvi, gp = c4 // 2, c4 % 2
        base = 1024 + kvi * 256 + gp * 128
        chunks.append(chunk(np.arange(base, base + 128)))
    for mq in range(8):
        p, r = mq // 4, mq % 4
        ha, hb = 4 * (2 * p) + r, 4 * (2 * p + 1) + r
        chunks.append(chunk(np.concatenate([np.arange(ha * 64, ha * 64 + 64), np.arange(hb * 64, hb * 64 + 64)])))
    for br in (1, 2):
        for c2 in range(2):
            base = 1024 + br * 512 + c2 * 128
            chunks.append(chunk(np.arange(base, base + 128)))
    Wo = inp["nsa_w_out"][0].reshape(KC, 128, D)
    for m in range(KC):
        chunks.append(Wo[:, :, m * 128:(m + 1) * 128].transpose(1, 0, 2))
    w["nsa_wch"] = f(np.stack(chunks, 0))
    tokcols = np.concatenate([np.arange(1024 + 512 + 256, 1024 + 512 + 512), np.arange(1024 + 1024 + 256, 1024 + 1024 + 512), np.arange(2560, 2608)])
    w["nsa_wtok"] = f(Wr[:, :, tokcols].transpose(1, 0, 2))
    w1 = inp["nsa_cmp_w1"][0].reshape(2, 32, 64, 256).transpose(0, 2, 1, 3)
    w["nsa_w1"] = f(np.concatenate([w1, w1], 1))
    pT = inp["nsa_cmp_pos"][0].transpose(2, 0, 1)
    w["nsa_posT"] = f(np.concatenate([pT, pT], 0))
    w["nsa_b1"] = f(inp["nsa_cmp_b1"][0].reshape(2, 2, 128).transpose(2, 0, 1))
    w["nsa_b1row"] = f(inp["nsa_cmp_b1"][0][None])
    w2 = inp["nsa_cmp_w2"][0]
    w2k = w2[0].reshape(2, 128, 64).transpose(1, 0, 2)
    w["nsa_w2k"] = f(np.concatenate([w2k, w2k], 2))
    w["nsa_w2v"] = f(w2[1].reshape(2, 128, 64).transpose(1, 0, 2))
    return w


def const_tables():
    c = {}
    HUGE = 30000
    k = np.arange(128)[:, None]
    q = np.arange(TT)[None, :]
    dt = np.zeros((128, 13, TT), np.int64)
    for i in range(4):
        d = -128 * i + q - k
        dt[:, i] = np.where(d >= 0, d, HUGE)
    for i in range(1, 5):
        d = 128 * i + q - k
        dt[:, 3 + i] = np.where(d < 512, d, HUGE)
    dt[:, 8] = q - k
    cc = np.arange(128)[:, None]
    for qt in range(4):
        t = qt * TT + q
        d2 = 2 * t - 32 * cc - 31 - 2048
        ok = (16 * cc + 31 <= t) & (cc < 127)
        dt[:, 9 + qt] = np.where(ok, d2, HUGE)
    assert np.array_equal(dt.astype(np.float16).astype(np.int64)[dt < HUGE], dt[dt < HUGE])
    c["c_dtab"] = dt.astype(np.float16)
    et = np.zeros((32, 16, 128), np.float32)
    for kt in range(16):
        for kk in range(128):
            et[(kt * 128 + kk) // 64, kt, kk] = 1.0
    c["c_etab"] = et
    sl = np.array(SLOPES, np.float64)
    kk = np.arange(128, dtype=np.float64)[:, None, None]
    bc = -(sl[None, :, None] * ((128.0 * np.arange(16))[None, None, :] - kk))
    bc[:, :, 0] = -sl[None, :] * 1024.0
    c["c_biasc"] = np.ascontiguousarray(bc).astype(np.float32)
    qq = (np.arange(4)[None, :, None] * 128 + np.arange(128)[:, None, None]).astype(np.float64)
    c["c_fq"] = np.exp(-sl[None, None, :] * qq).astype(np.float32)
    t = (np.arange(16)[None, :, None] * 128 + np.arange(128)[:, None, None])
    j = np.arange(32)[None, None, :]
    cur = t // 64
    forced = (j == 0) | (j == cur) | (j == cur - 1)
    future = j > cur
    c["c_keep"] = np.where(forced | future, 0.0, 1.0).astype(np.float32)
    c["c_addc"] = np.where(forced, 1e4, np.where(future, -1.0, 0.0)).astype(np.float32)
    c["c_ident"] = np.eye(128, dtype=np.float32)
    c["c_valid"] = (t >= 31).astype(np.float32)
    ov = np.zeros((128, 33), np.float32)
    ov[:127, 0] = 1.0
    cs = np.arange(127)[:, None] * 16
    sj = np.arange(32)[None, :]
    ov[:127, 1:] = ((cs < (sj + 1) * 64) & (cs + 32 > sj * 64)).astype(np.float32)
    c["c_ovl"] = ov
    return c


_CACHE = {}


def kernel(**inp):
    inp = {k: np.asarray(v) for k, v in inp.items()}
    ncores, nseq = 8, 2
    if "nc" not in _CACHE:
        _CACHE["nc"] = build(nseq)[0]
    nc = _CACHE["nc"]
    w = prep_weights(inp)
    x = inp["x"]
    xT = np.ascontiguousarray(x.reshape(ncores, nseq, NT, KC, 128).transpose(0, 1, 4, 3, 2))
    in_maps = [dict(w, xT=xT[c]) for c in range(ncores)]
    res = run_bass_kernel_spmd(nc, in_maps, core_ids=list(range(ncores)))
    o = np.stack([r["outT"] for r in res.results], 0)
    return np.ascontiguousarray(o.transpose(0, 1, 4, 3, 2)).reshape(16, NT, D).astype(np.float32)
```
